# Optimizing a Trainium2 kernel written in Bass

```python
import math
import jax
import jax.numpy as jnp
from jax import lax
import numpy as np

D_MODEL = 1024
BATCH = 4
SEQ = 8192
DEPTH = 2

HEAD_DIM = 64
SB_HEADS = 4
MOBA_HEADS = 4
NSA_HEADS = 8
NSA_KV_GROUPS = 2
NSA_HPG = NSA_HEADS // NSA_KV_GROUPS
N_BRANCHES = 3
SB_WIDTH = SB_HEADS * HEAD_DIM
MOBA_WIDTH = MOBA_HEADS * HEAD_DIM
NSA_WIDTH = NSA_HEADS * HEAD_DIM
NSA_KV_WIDTH = NSA_KV_GROUPS * HEAD_DIM
MIX_WIDTH = SB_WIDTH + MOBA_WIDTH + NSA_WIDTH
N_IN = 3 * SB_WIDTH + 3 * MOBA_WIDTH + NSA_WIDTH + 6 * NSA_KV_WIDTH + N_BRANCHES * NSA_HEADS + N_BRANCHES * D_MODEL

SB_Q_BLOCK = 128
SPARSE_Q_BLOCK = 64
MOBA_BLOCK = 256
MOBA_TOPK = 3
CMP_BLOCK = 32
CMP_STRIDE = 16
CMP_HIDDEN = 4 * HEAD_DIM
SLC_BLOCK = 64
SLC_TOPN = 16
WINDOW = 512
N_BUCKETS = 32
REL_MAX_DISTANCE = 128
N_BIAS_HEADS = MOBA_HEADS + NSA_HEADS
D_FF = -(-8 * D_MODEL // (3 * 256)) * 256
NORM_EPS = 1e-6
NEG = -1e30
BIG = 1e30
TINY = 1e-30

kernel_name = 'hybrid_stickbreak_moba_nsa_block'


def rms_norm(x, g):
    xf = x.astype(jnp.float32)
    y = xf * lax.rsqrt(jnp.mean(xf * xf, axis=-1, keepdims=True) + NORM_EPS)
    return (y * g.astype(jnp.float32)).astype(x.dtype)


def masked_softmax(logits, mask):
    l = jnp.where(mask, logits.astype(jnp.float32), NEG)
    p = jnp.where(mask, jnp.exp(l - jnp.max(l, axis=-1, keepdims=True)), 0.0)
    return p / jnp.maximum(jnp.sum(p, axis=-1, keepdims=True), TINY)


def t5_bucket(dist):
    n = jnp.maximum(dist, 0)
    max_exact = N_BUCKETS // 2
    nf = jnp.maximum(n, 1).astype(jnp.float32)
    large = max_exact + (jnp.log(nf / max_exact) / math.log(REL_MAX_DISTANCE / max_exact) * (N_BUCKETS - max_exact)).astype(jnp.int32)
    large = jnp.minimum(large, N_BUCKETS - 1)
    return jnp.where(n < max_exact, n, large)


def to_heads(t, n_heads):
    b, s, _ = t.shape
    return t.reshape(b, s, n_heads, HEAD_DIM).transpose(0, 2, 1, 3)


def from_heads(t):
    b, h, s, d = t.shape
    return t.transpose(0, 2, 1, 3).reshape(b, s, h * d)


def stick_breaking_attention(q, k, v):
    s_len = q.shape[2]
    scale = HEAD_DIM ** -0.5
    outs = []
    for blk in range(s_len // SB_Q_BLOCK):
        q0 = blk * SB_Q_BLOCK
        kl = q0 + SB_Q_BLOCK
        z = jnp.einsum('bhqd,bhkd->bhqk', q[:, :, q0:kl], k[:, :, :kl]).astype(jnp.float32) * scale
        causal = jnp.arange(kl)[None, :] < (q0 + jnp.arange(SB_Q_BLOCK))[:, None]
        log_stay = jnp.where(causal, jax.nn.log_sigmoid(-z), 0.0)
        log_after = lax.cumsum(log_stay, axis=3, reverse=True) - log_stay
        a = jnp.where(causal, jnp.exp(jax.nn.log_sigmoid(z) + log_after), 0.0)
        outs.append(jnp.einsum('bhqk,bhkd->bhqd', a.astype(v.dtype), v[:, :, :kl]))
    return jnp.concatenate(outs, axis=2)


def moba_attention(q, k, v, tbl):
    b, h, s_len, d = q.shape
    qb = SPARSE_Q_BLOCK
    nblk = -(-s_len // MOBA_BLOCK)
    pad = nblk * MOBA_BLOCK - s_len
    kp = jnp.pad(k, ((0, 0), (0, 0), (0, pad), (0, 0)))
    vp = jnp.pad(v, ((0, 0), (0, 0), (0, pad), (0, 0)))
    kb = kp.reshape(b, h, nblk, MOBA_BLOCK, d)
    vb = vp.reshape(b, h, nblk, MOBA_BLOCK, d)
    kmean = jnp.mean(kb.astype(jnp.float32), axis=3).astype(k.dtype)
    topk = min(MOBA_TOPK, nblk)
    scale = HEAD_DIM ** -0.5
    bi = jnp.arange(b)[:, None, None, None]
    hi = jnp.arange(h)[None, :, None, None]
    hb = jnp.arange(h)[None, :, None, None, None]
    in_blk = jnp.arange(MOBA_BLOCK)
    blk_ids = jnp.arange(nblk)

    def chunk(c):
        q0 = c * qb
        tpos = q0 + jnp.arange(qb)
        cur = q0 // MOBA_BLOCK
        qc = lax.dynamic_slice_in_dim(q, q0, qb, axis=2)
        score = jnp.einsum('bhqd,bhnd->bhqn', qc, kmean).astype(jnp.float32)
        score = jnp.where(blk_ids < cur, score, NEG)
        _, idx = lax.top_k(score, topk)
        k_sel = kb[bi, hi, idx]
        v_sel = vb[bi, hi, idx]
        pos_sel = idx[..., None] * MOBA_BLOCK + in_blk
        l_sel = jnp.einsum('bhqd,bhqnkd->bhqnk', qc, k_sel).astype(jnp.float32) * scale
        l_sel = l_sel + tbl[hb, t5_bucket(tpos[:, None, None] - pos_sel)]
        own0 = cur * MOBA_BLOCK
        k_own = lax.dynamic_slice_in_dim(kp, own0, MOBA_BLOCK, axis=2)
        v_own = lax.dynamic_slice_in_dim(vp, own0, MOBA_BLOCK, axis=2)
        dist_own = tpos[:, None] - (own0 + in_blk)[None, :]
        l_own = jnp.einsum('bhqd,bhkd->bhqk', qc, k_own).astype(jnp.float32) * scale + tbl[:, t5_bucket(dist_own)]
        m_sel = jnp.broadcast_to((jnp.arange(topk) < cur)[:, None], (topk, MOBA_BLOCK)).reshape(topk * MOBA_BLOCK)
        mask = jnp.concatenate([jnp.broadcast_to(m_sel, (qb, topk * MOBA_BLOCK)), dist_own >= 0], axis=-1)
        logits = jnp.concatenate([l_sel.reshape(b, h, qb, topk * MOBA_BLOCK), l_own], axis=-1)
        p = masked_softmax(logits, mask)
        p_sel = p[..., :topk * MOBA_BLOCK].reshape(b, h, qb, topk, MOBA_BLOCK).astype(v.dtype)
        p_own = p[..., topk * MOBA_BLOCK:].astype(v.dtype)
        return jnp.einsum('bhqnk,bhqnkd->bhqd', p_sel, v_sel) + jnp.einsum('bhqk,bhkd->bhqd', p_own, v_own)

    o = lax.map(chunk, jnp.arange(s_len // qb))
    return o.transpose(1, 2, 0, 3, 4).reshape(b, h, s_len, d)


def nsa_compress(t, pos, w1, w2):
    b, g, s_len, d = t.shape
    n_cmp = (s_len - CMP_BLOCK) // CMP_STRIDE + 1
    idx = np.arange(n_cmp)[:, None] * CMP_STRIDE + np.arange(CMP_BLOCK)[None, :]
    blocks = (t[:, :, idx] + pos).reshape(b, g, n_cmp, CMP_BLOCK * d)
    return jax.nn.gelu(blocks @ w1) @ w2


def nsa_attention(q, k_cmp, v_cmp, k_slc, v_slc, k_win, v_win, gates, k_cmp_norm, cmp_pos, cmp_w1, cmp_w2, tbl):
    b, g, hpg, s_len, d = q.shape
    qb = SPARSE_Q_BLOCK
    scale = HEAD_DIM ** -0.5
    n_cmp = (s_len - CMP_BLOCK) // CMP_STRIDE + 1
    cmp_start = np.arange(n_cmp) * CMP_STRIDE
    cmp_end = jnp.asarray(cmp_start + CMP_BLOCK - 1)
    kc = rms_norm(nsa_compress(k_cmp, cmp_pos[0], cmp_w1[0], cmp_w2[0]), k_cmp_norm)
    vc = nsa_compress(v_cmp, cmp_pos[1], cmp_w1[1], cmp_w2[1])
    n_slc = s_len // SLC_BLOCK
    slc_start = np.arange(n_slc) * SLC_BLOCK
    overlap = jnp.asarray(((cmp_start[:, None] < slc_start[None, :] + SLC_BLOCK)
                           & (cmp_start[:, None] + CMP_BLOCK > slc_start[None, :])).astype(np.float32))
    n_sel = min(SLC_TOPN, n_slc)
    ks_blk = k_slc.reshape(b, g, n_slc, SLC_BLOCK, d)
    vs_blk = v_slc.reshape(b, g, n_slc, SLC_BLOCK, d)
    kw_pad = jnp.pad(k_win, ((0, 0), (0, 0), (WINDOW, 0), (0, 0)))
    vw_pad = jnp.pad(v_win, ((0, 0), (0, 0), (WINDOW, 0), (0, 0)))
    bi = jnp.arange(b)[:, None, None, None]
    gi = jnp.arange(g)[None, :, None, None]
    gb = jnp.arange(g)[None, :, None, None, None, None]
    hb = jnp.arange(hpg)[None, None, :, None, None, None]
    blk_ids = jnp.arange(n_slc)
    in_blk = jnp.arange(SLC_BLOCK)
    win_off = jnp.arange(qb + WINDOW)

    def chunk(c):
        q0 = c * qb
        tpos = q0 + jnp.arange(qb)
        qc = lax.dynamic_slice_in_dim(q, q0, qb, axis=3)
        dist_c = tpos[:, None] - cmp_end[None, :]
        l_c = jnp.einsum('bghqd,bgnd->bghqn', qc, kc).astype(jnp.float32) * scale + tbl[:, :, t5_bucket(dist_c)]
        p_c = masked_softmax(l_c, dist_c >= 0)
        o_c = jnp.einsum('bghqn,bgnd->bghqd', p_c.astype(vc.dtype), vc)
        imp = jnp.einsum('bghqn,nm->bgqm', p_c, overlap)
        cur = tpos // SLC_BLOCK
        forced = (blk_ids == 0) | (blk_ids == cur[:, None]) | (blk_ids == cur[:, None] - 1)
        score = jnp.where(forced, BIG, imp)
        score = jnp.where(blk_ids <= cur[:, None], score, NEG)
        _, sidx = lax.top_k(score, n_sel)
        k_sel = ks_blk[bi, gi, sidx]
        v_sel = vs_blk[bi, gi, sidx]
        dist_s = tpos[:, None, None] - (sidx[..., None] * SLC_BLOCK + in_blk)
        l_s = jnp.einsum('bghqd,bgqnkd->bghqnk', qc, k_sel).astype(jnp.float32) * scale
        l_s = l_s + tbl[gb, hb, t5_bucket(dist_s)[:, :, None]]
        p_s = masked_softmax(l_s.reshape(b, g, hpg, qb, n_sel * SLC_BLOCK),
                             (dist_s >= 0).reshape(b, g, 1, qb, n_sel * SLC_BLOCK))
        o_s = jnp.einsum('bghqnk,bgqnkd->bghqd', p_s.reshape(b, g, hpg, qb, n_sel, SLC_BLOCK).astype(v_sel.dtype), v_sel)
        kwin = lax.dynamic_slice_in_dim(kw_pad, q0, qb + WINDOW, axis=2)
        vwin = lax.dynamic_slice_in_dim(vw_pad, q0, qb + WINDOW, axis=2)
        pos_w = q0 - WINDOW + win_off
        dist_w = tpos[:, None] - pos_w[None, :]
        m_w = (dist_w >= 0) & (dist_w < WINDOW) & (pos_w >= 0)[None, :]
        l_w = jnp.einsum('bghqd,bgkd->bghqk', qc, kwin).astype(jnp.float32) * scale + tbl[:, :, t5_bucket(dist_w)]
        p_w = masked_softmax(l_w, m_w)
        o_w = jnp.einsum('bghqk,bgkd->bghqd', p_w.astype(vwin.dtype), vwin)
        return o_c, o_s, o_w

    o_c, o_s, o_w = lax.map(chunk, jnp.arange(s_len // qb))

    def unchunk(o):
        return o.transpose(1, 2, 3, 0, 4, 5).reshape(b, g, hpg, s_len, d)

    return gates[0] * unchunk(o_c) + gates[1] * unchunk(o_s) + gates[2] * unchunk(o_w)


def split_points():
    widths = [SB_WIDTH] * 3 + [MOBA_WIDTH] * 3 + [NSA_WIDTH] + [NSA_KV_WIDTH] * 6 + [N_BRANCHES * NSA_HEADS, N_BRANCHES * D_MODEL]
    return [int(p) for p in np.cumsum(widths)[:-1]]


def hybrid_layer(x, rel_bias, attn_norm, w_in, moba_q_norm, moba_k_norm, nsa_q_norm, nsa_k_norm,
                 nsa_cmp_pos, nsa_cmp_w1, nsa_cmp_w2, w_branch, w_out, ffn_norm, w_gate_up, w_down):
    b, s_len, _ = x.shape
    h = rms_norm(x, attn_norm)
    (sb_q, sb_k, sb_v, mb_q, mb_k, mb_v, ns_q, ns_kc, ns_vc, ns_ks, ns_vs, ns_kw, ns_vw,
     ns_gate, br_gate) = jnp.split(h @ w_in, split_points(), axis=-1)
    tbl = rel_bias.T
    o_a = from_heads(stick_breaking_attention(to_heads(sb_q, SB_HEADS), to_heads(sb_k, SB_HEADS), to_heads(sb_v, SB_HEADS)))
    o_b = from_heads(moba_attention(rms_norm(to_heads(mb_q, MOBA_HEADS), moba_q_norm),
                                    rms_norm(to_heads(mb_k, MOBA_HEADS), moba_k_norm),
                                    to_heads(mb_v, MOBA_HEADS), tbl[:MOBA_HEADS]))
    nq = rms_norm(ns_q.reshape(b, s_len, NSA_KV_GROUPS, NSA_HPG, HEAD_DIM).transpose(0, 2, 3, 1, 4), nsa_q_norm)

    def kv(t):
        return t.reshape(b, s_len, NSA_KV_GROUPS, HEAD_DIM).transpose(0, 2, 1, 3)

    nsa_gates = jax.nn.sigmoid(ns_gate.reshape(b, s_len, 3, NSA_KV_GROUPS, NSA_HPG)).transpose(2, 0, 3, 4, 1)[..., None]
    o_c = nsa_attention(nq, kv(ns_kc), kv(ns_vc), rms_norm(kv(ns_ks), nsa_k_norm[1]), kv(ns_vs),
                        rms_norm(kv(ns_kw), nsa_k_norm[2]), kv(ns_vw), nsa_gates, nsa_k_norm[0],
                        nsa_cmp_pos, nsa_cmp_w1, nsa_cmp_w2,
                        tbl[MOBA_HEADS:].reshape(NSA_KV_GROUPS, NSA_HPG, N_BUCKETS))
    o_c = o_c.transpose(0, 3, 1, 2, 4).reshape(b, s_len, NSA_WIDTH)
    g = jax.nn.sigmoid(br_gate.reshape(b, s_len, N_BRANCHES, D_MODEL))
    mix = (g[:, :, 0] * (o_a @ w_branch[:SB_WIDTH])
           + g[:, :, 1] * (o_b @ w_branch[SB_WIDTH:SB_WIDTH + MOBA_WIDTH])
           + g[:, :, 2] * (o_c @ w_branch[SB_WIDTH + MOBA_WIDTH:]))
    x = x + mix @ w_out
    gate, up = jnp.split(rms_norm(x, ffn_norm) @ w_gate_up, 2, axis=-1)
    return x + (jax.nn.silu(gate) * up) @ w_down


def setup_inputs(seed: int = 0) -> dict:
    key = jax.random.key(seed)
    ks = jax.random.split(key, 18)

    def nrm(k, shape, scale):
        return jax.random.normal(k, shape, jnp.float32) * scale

    w_branch = jnp.concatenate([
        nrm(ks[11], (DEPTH, SB_WIDTH, D_MODEL), SB_WIDTH ** -0.5),
        nrm(ks[16], (DEPTH, MOBA_WIDTH, D_MODEL), MOBA_WIDTH ** -0.5),
        nrm(ks[17], (DEPTH, NSA_WIDTH, D_MODEL), NSA_WIDTH ** -0.5)], axis=1)
    return {
        'x': nrm(ks[0], (BATCH, SEQ, D_MODEL), 1.0),
        'rel_bias': nrm(ks[1], (N_BUCKETS, N_BIAS_HEADS), 0.2),
        'attn_norm': 1.0 + nrm(ks[2], (DEPTH, D_MODEL), 0.02),
        'w_in': nrm(ks[3], (DEPTH, D_MODEL, N_IN), D_MODEL ** -0.5),
        'moba_q_norm': 1.0 + nrm(ks[4], (DEPTH, HEAD_DIM), 0.02),
        'moba_k_norm': 1.0 + nrm(ks[5], (DEPTH, HEAD_DIM), 0.02),
        'nsa_q_norm': 1.0 + nrm(ks[6], (DEPTH, HEAD_DIM), 0.02),
        'nsa_k_norm': 1.0 + nrm(ks[7], (DEPTH, 3, HEAD_DIM), 0.02),
        'nsa_cmp_pos': nrm(ks[8], (DEPTH, 2, CMP_BLOCK, HEAD_DIM), 0.1),
        'nsa_cmp_w1': nrm(ks[9], (DEPTH, 2, CMP_BLOCK * HEAD_DIM, CMP_HIDDEN), (CMP_BLOCK * HEAD_DIM) ** -0.5),
        'nsa_cmp_w2': nrm(ks[10], (DEPTH, 2, CMP_HIDDEN, HEAD_DIM), CMP_HIDDEN ** -0.5),
        'w_branch': w_branch,
        'w_out': nrm(ks[12], (DEPTH, D_MODEL, D_MODEL), D_MODEL ** -0.5),
        'ffn_norm': 1.0 + nrm(ks[13], (DEPTH, D_MODEL), 0.02),
        'w_gate_up': nrm(ks[14], (DEPTH, D_MODEL, 2 * D_FF), D_MODEL ** -0.5),
        'w_down': nrm(ks[15], (DEPTH, D_FF, D_MODEL), D_FF ** -0.5),
    }


def reference(x, rel_bias, attn_norm, w_in, moba_q_norm, moba_k_norm, nsa_q_norm, nsa_k_norm,
              nsa_cmp_pos, nsa_cmp_w1, nsa_cmp_w2, w_branch, w_out, ffn_norm, w_gate_up, w_down):
    for layer in range(DEPTH):
        x = hybrid_layer(x, rel_bias, attn_norm[layer], w_in[layer], moba_q_norm[layer], moba_k_norm[layer],
                         nsa_q_norm[layer], nsa_k_norm[layer], nsa_cmp_pos[layer], nsa_cmp_w1[layer],
                         nsa_cmp_w2[layer], w_branch[layer], w_out[layer], ffn_norm[layer],
                         w_gate_up[layer], w_down[layer])
    return x
```

```python
import math
from contextlib import ExitStack

import numpy as np
import ml_dtypes
import concourse.bass as bass
import concourse.mybir as mybir
from concourse.bass_types import AP
from concourse.bass_utils import run_bass_kernel_spmd

F32 = mybir.dt.float32
BF16 = mybir.dt.bfloat16
AF = mybir.ActivationFunctionType
ALU = mybir.AluOpType
AX = mybir.AxisListType
ENGS = ('pe', 'act', 'dve', 'pool', 'sp')
NDMA = 8
STQ = 'sp'

D = 1024
NIN = 5912
NQKV = 2840
DFF = 2816
EPS = 1e-6
NEG = -1e30
BIG = 1e30
TINY = 1e-30
GOFF = 2304
GL = 4864
FM_COLS = [0, 128, 256, 384, 768, 896, 1024, 1152, 1536, 1664, 1792, 1920, 2048, 2176, 2304, 2560]
FM_KIND = [None, None, None, None, 0, 0, 1, 1, 2, 2, 2, 2, None, None, 4, 5]


class Res:
    __slots__ = ('name', 'lw', 'rd')

    def __init__(self, name=''):
        self.name = name
        self.lw = None
        self.rd = {}


class Ring:
    def __init__(self, items):
        self.items = items
        self.i = 0

    def next(self):
        it = self.items[self.i]
        self.i = (self.i + 1) % len(self.items)
        return it


class Sched:
    def __init__(self, nc, stack):
        self.nc = nc
        self.q = {e: [] for e in ENGS}
        self.cnt = {e: 0 for e in ENGS}
        self.waited = {e: {} for e in ENGS}
        self.sem = {}
        for e in ENGS:
            self.sem[('e', e)] = stack.enter_context(nc.semaphore('s_' + e))
        self.dcnt = {}
        self.drr = {}
        for e in ('sp', 'act', 'pool'):
            self.drr[e] = 0
            for i in range(NDMA):
                k = ('d', e, i)
                self.sem[k] = stack.enter_context(nc.semaphore('d_%s%d' % (e, i)))
                self.dcnt[k] = 0

    def _waits(self, eng, reads, writes, extra=()):
        deps = {}

        def add(ev):
            if ev is None:
                return
            k, v = ev
            if deps.get(k, 0) < v:
                deps[k] = v
        for r in reads:
            add(r.lw)
        for w in writes:
            add(w.lw)
            for k, v in w.rd.items():
                add((k, v))
        for ev in extra:
            add(ev)
        out = []
        wd = self.waited[eng]
        for k, v in deps.items():
            if k == ('e', 'pe') and eng == 'pe':
                continue
            if wd.get(k, 0) >= v:
                continue
            wd[k] = v
            out.append((k, v))
        return out

    def _mark(self, ev, reads, writes):
        k, v = ev
        for r in reads:
            if r.rd.get(k, 0) < v:
                r.rd[k] = v
        for w in writes:
            w.lw = ev
            w.rd = {}

    def op(self, eng, fn, reads=(), writes=()):
        waits = self._waits(eng, reads, writes)
        self.cnt[eng] += 1
        ev = (('e', eng), self.cnt[eng])
        self.q[eng].append((fn, waits, ('e', eng), 1))
        self._mark(ev, reads, writes)
        return ev

    def dma(self, eng, out, in_, reads=(), writes=(), **kw):
        i = self.drr[eng]
        self.drr[eng] = (i + 1) % NDMA
        k = ('d', eng, i)
        prev = (k, 16 * self.dcnt[k]) if self.dcnt[k] else None
        waits = self._waits(eng, reads, writes, extra=(prev,) if prev else ())
        self.dcnt[k] += 1
        ev = (k, 16 * self.dcnt[k])
        self.q[eng].append((lambda e: e.dma_start(out=out, in_=in_, **kw), waits, k, 16))
        self._mark(ev, reads, writes)
        return ev

    def barrier(self):
        evs = [(('e', e), self.cnt[e]) for e in ENGS if self.cnt[e] > 0]
        evs += [(k, 16 * c) for k, c in self.dcnt.items() if c > 0]
        for e in ENGS:
            wd = self.waited[e]
            waits = []
            for k, v in evs:
                if k == ('e', e):
                    continue
                if wd.get(k, 0) >= v:
                    continue
                wd[k] = v
                waits.append((k, v))
            self.q[e].append((None, waits, None, 0))

    def emit(self):
        nc = self.nc
        with nc.Block() as block:
            def run(name):
                def f(e):
                    for fn, waits, k, inc in self.q[name]:
                        for wk, wv in waits:
                            e.wait_ge(self.sem[wk], wv)
                        if fn is not None:
                            fn(e).then_inc(self.sem[k], inc)
                return f
            block.tensor(run('pe'))
            block.scalar(run('act'))
            block.vector(run('dve'))
            block.gpsimd(run('pool'))
            block.sync(run('sp'))

    def mm(self, out, lhsT, rhs, start=True, stop=True, reads=(), writes=()):
        return self.op('pe', lambda e: e.matmul(out, lhsT, rhs, start=start, stop=stop), reads, writes)

    def tr(self, out, in_, ident, reads=(), writes=()):
        return self.op('pe', lambda e: e.transpose(out, in_, ident), reads, writes)

    def act(self, out, in_, func, reads=(), writes=(), **kw):
        return self.op('act', lambda e: e.activation(out, in_, func, **kw), reads, writes)

    def tt(self, eng, out, in0, in1, op, reads=(), writes=()):
        return self.op(eng, lambda e: e.tensor_tensor(out, in0, in1, op), reads, writes)

    def ts(self, eng, out, in0, s1, s2, op0, op1=None, reads=(), writes=()):
        if op1 is None:
            return self.op(eng, lambda e: e.tensor_scalar(out, in0, s1, s2, op0), reads, writes)
        return self.op(eng, lambda e: e.tensor_scalar(out, in0, s1, s2, op0, op1), reads, writes)

    def stt(self, eng, out, in0, scalar, in1, op0, op1, reads=(), writes=()):
        return self.op(eng, lambda e: e.scalar_tensor_tensor(out, in0, scalar, in1, op0, op1), reads, writes)

    def cp(self, eng, out, in_, reads=(), writes=()):
        if eng == 'act':
            return self.op('act', lambda e: e.copy(out, in_), reads, writes)
        return self.op(eng, lambda e: e.tensor_copy(out, in_), reads, writes)

    def memset(self, eng, ap, val, writes=()):
        return self.op(eng, lambda e: e.memset(ap, val), (), writes)


class Arena:
    def __init__(self, ap, size):
        self.ap = ap
        self.size = size
        self.top = 0

    def bf(self, n):
        a = self.top
        self.top += (n + 31) // 32 * 32
        assert self.top <= self.size, ('arena overflow', self.top, self.size)
        return self.ap[:, a:a + n]

    def f32(self, n):
        a = self.top
        self.top += (2 * n + 31) // 32 * 32
        assert self.top <= self.size, ('arena overflow', self.top, self.size)
        return self.ap[:, a:a + 2 * n].bitcast(F32)


def v3(ap, a):
    return ap.rearrange("p (a b) -> p a b", a=a)


def t5_bucket_np(d):
    n = np.maximum(d, 0)
    nf = np.maximum(n, 1).astype(np.float32)
    large = 16 + (np.log(nf / np.float32(16)) / np.float32(math.log(128 / 16)) * np.float32(16)).astype(np.int32)
    large = np.minimum(large, 31)
    return np.where(n < 16, n, large)


def host_consts(S):
    k = np.arange(128)[:, None]
    q = np.arange(128)[None, :]
    ident = (k == q)
    J = (k + q == 127)
    uincl = (k >= q)
    ones = np.ones((128, 128), bool)
    blk = (k // 64 == q // 64)
    maskL = (k < q)
    m512 = (q < k)
    zeros = np.zeros((128, 512), bool)
    n_cmp = (S - 32) // 16 + 1
    nct = (n_cmp + 127) // 128
    n = np.arange(nct * 128)[:, None]
    m = np.arange(128)[None, :]
    ovl = ((16 * n < 64 * m + 64) & (16 * n + 32 > 64 * m) & (n < n_cmp))
    ovl = ovl.reshape(nct, 128, 128).transpose(1, 0, 2).reshape(128, nct * 128)
    mm_ = np.arange(128)[:, None]
    xx = np.arange(S)[None, :]
    B = (xx // 64 == mm_)
    cb = np.concatenate([ident, J, uincl, ones, blk, maskL, m512, zeros, ovl, B], axis=1).astype(np.float32)
    cb = cb.astype(ml_dtypes.bfloat16)
    negmask = np.where(k >= q, -1e4, 0.0).astype(np.float32)
    y = np.arange(256)[None, :]
    rel = y - 126 - (k >= 64)
    cs = np.where((rel == 0) | (rel == -1), BIG, np.where(rel > 0, NEG, 0.0)).astype(np.float32)
    oh = np.zeros((128, 128), np.float32)
    b = t5_bucket_np(np.arange(128))
    oh[b, np.arange(128)] = 1.0
    cf = np.concatenate([negmask, cs, oh], axis=1).astype(np.float32)
    return cb, cf, n_cmp, nct


def build(S, depth, debug=False, phases=None, dbg=None, inject=False):
    NT = S // 128
    NC = S // 512
    cbh, cfh, n_cmp, NCT = host_consts(S)
    NCB = cbh.shape[1]
    NCF = cfh.shape[1]
    nc = bass.Bass("TRN2", target_bir_lowering=False)
    dt_in = lambda name, shape, dt=F32: nc.dram_tensor(name, shape, dt, kind="ExternalInput").ap()
    dbg_kind = "ExternalOutput" if debug else "Internal"
    scr = lambda name, shape, dt: nc.dram_tensor(name, shape, dt, kind=dbg_kind).ap()
    x_in = dt_in("x", [S, D])
    rel_bias = dt_in("rel_bias", [32, 12])
    attn_norm = dt_in("attn_norm", [depth, D])
    w_in = dt_in("w_in", [depth, D, NIN])
    gains = dt_in("gains", [depth, 6, 64])
    cmp_pos = dt_in("nsa_cmp_pos", [depth, 2, 32, 64])
    cmp_w1 = dt_in("nsa_cmp_w1", [depth, 2, 2048, 256])
    cmp_w2 = dt_in("nsa_cmp_w2", [depth, 2, 256, 64])
    w_branch = dt_in("w_branch", [depth, D, D])
    w_out = dt_in("w_out", [depth, D, D])
    ffn_norm = dt_in("ffn_norm", [depth, D])
    w_gate_up = dt_in("w_gate_up", [depth, D, 2 * DFF])
    w_down = dt_in("w_down", [depth, DFF, D])
    cb_in = dt_in("cb", [128, NCB], BF16)
    cf_in = dt_in("cf", [128, NCF])
    out = nc.dram_tensor("out", [S, D], F32, kind="ExternalOutput").ap()

    hTd = scr("hTd", [D, S], BF16)
    fmd = scr("fmd", [2048, S], BF16)
    tmvd = scr("tmvd", [S, 832], BF16)
    nsgd = scr("nsgd", [S, 32], F32)
    oTd = scr("oTd", [D, S], BF16)
    x1d = scr("x1d", [S, D], F32)
    xmd = scr("xmd", [S, D], F32)
    gvd = scr("gvd", [12, GL], BF16)
    oT_in = dt_in("oT_in", [D, S], BF16) if inject else None

    with ExitStack() as st:
        sc = Sched(nc, st)
        ASZ = 98304
        arena_ap = nc.alloc_sbuf_tensor("arena", [128, ASZ], BF16).ap()
        ar = Arena(arena_ap, ASZ)
        banks = [nc.alloc_psum_tensor("bank%d" % i, [128, 512], F32).ap() for i in range(8)]
        bres = [Res("bank%d" % i) for i in range(8)]

        CB_GLOBAL = 8 * 128 + 384
        cb = ar.bf(CB_GLOBAL)
        ident = cb[:, 0:128]
        Jm = cb[:, 128:256]
        uincl = cb[:, 256:384]
        onesb = cb[:, 384:512]
        blkones = cb[:, 512:640]
        maskL = cb[:, 640:768]
        m512 = cb[:, 768:896]
        zerob = cb[:, 896:1408]
        cf = ar.f32(384)
        negmask = cf[:, 0:128]
        cslide = cf[:, 128:384]
        Etab = ar.bf(24 * 128)
        r_const = Res('const')
        sc.dma('sp', cb, cb_in[:, 0:CB_GLOBAL], writes=[r_const])
        sc.dma('sp', cf, cf_in[:, 0:384], writes=[r_const])
        base_top = ar.top

        def E0(h):
            return Etab[:, (2 * h) * 128:(2 * h + 1) * 128]

        def E128(h):
            return Etab[:, (2 * h + 1) * 128:(2 * h + 2) * 128]

        def prologue():
            ar.top = base_top
            tb = ar.f32(12)
            oh = ar.f32(128)
            br = ar.f32(128)
            negc = ar.f32(1)
            gsb = ar.bf(GL)
            hk = ar.bf(128)
            r = Res('pro')
            rb = bres[0]
            sc.dma('sp', tb[0:32, :], rel_bias, writes=[r])
            sc.dma('sp', oh[0:32, :], cf_in[0:32, 384:512], writes=[r])
            sc.mm(banks[0][0:12, 0:128], tb[0:32, 0:12], oh[0:32, 0:128], reads=[r], writes=[rb])
            sc.cp('dve', br[0:12, :], banks[0][0:12, 0:128], reads=[rb], writes=[r])
            sc.ts('dve', negc[0:12, :], br[0:12, 127:128], -1.0, None, ALU.mult, reads=[r], writes=[r])
            sc.memset('pool', gsb[0:12, 0:GOFF], 0.0, writes=[r])
            sc.memset('pool', gsb[0:12, GOFF + 128:GL], 1.0, writes=[r])
            sc.act(gsb[0:12, GOFF:GOFF + 128], br[0:12, :], AF.Exp, reads=[r], writes=[r], bias=negc[0:12, :])
            rg = Res('gvd')
            sc.dma('sp', gvd, gsb[0:12, :], reads=[r], writes=[rg])
            hkr = Res('hk')
            for h in range(12):
                for di, dl in enumerate((0, 128)):
                    src = AP(gvd.tensor, h * GL + GOFF + dl - 127, [[1, 128], [1, 128]])
                    sc.dma('sp', hk, src, reads=[rg], writes=[hkr])
                    sc.mm(banks[1][:, 0:128], Jm, hk, reads=[hkr, r_const], writes=[bres[1]])
                    sc.cp('dve', Etab[:, (2 * h + di) * 128:(2 * h + di + 1) * 128], banks[1][:, 0:128],
                          reads=[bres[1]], writes=[r_const])
            sc.barrier()

        def load_cast(dst, src, n, stage_ring, rdst):
            o = 0
            while o < n:
                w = min(2048, n - o)
                sg, rs = stage_ring.next()
                sc.dma('sp', sg[:, 0:w], src[:, o:o + w], writes=[rs])
                sc.cp('pool', dst[:, o:o + w], sg[:, 0:w], reads=[rs], writes=[rdst])
                o += w

        def rmsnorm_tm(xs, rx, gB, rg, hb, rh, ntile, sq, ssq, rstd, rtmp):
            for t in range(ntile):
                sc.act(sq, xs[:, t, :], AF.Square, reads=[rx], writes=[rtmp])
                sc.op('dve', lambda e, t=t: e.tensor_reduce(ssq[:, t:t + 1], sq, AX.X, ALU.add), reads=[rtmp], writes=[rtmp])
            sc.act(rstd[:, 0:ntile], ssq[:, 0:ntile], AF.Ln, reads=[rtmp], writes=[rtmp], scale=1.0 / D, bias=EPS)
            sc.act(rstd[:, 0:ntile], rstd[:, 0:ntile], AF.Exp, reads=[rtmp], writes=[rtmp], scale=-0.5)
            for t in range(ntile):
                sc.stt('dve', hb[:, t, :], xs[:, t, :], rstd[:, t:t + 1], gB, ALU.mult, ALU.mult,
                       reads=[rx, rtmp, rg], writes=[rh])

        def transpose_to(hb, rh, hT, rhT, ntile, bank_ring, evac_engs):
            for c in range(8):
                bk, rb = bank_ring.next()
                pb = bk.bitcast(BF16)
                for t in range(ntile):
                    sc.tr(pb[:, t * 128:(t + 1) * 128], hb[:, t, c * 128:(c + 1) * 128], ident,
                          reads=[rh, r_const], writes=[rb])
                sc.cp(evac_engs[c % len(evac_engs)], hT[:, c, :], pb[:, 0:ntile * 128], reads=[rb], writes=[rhT])

        def phase_A(l, xsrc):
            ar.top = base_top
            W = v3(ar.bf(8 * NQKV), 8)
            gA = ar.f32(D)
            gcol = ar.f32(8)
            rW = Res('W')
            mark = ar.top
            stg = [(ar.f32(2048), Res('stg%d' % i)) for i in range(2)]
            sring = Ring(stg)
            for c in range(8):
                load_cast(W[:, c, :], w_in[l, c * 128:(c + 1) * 128, 0:NQKV], NQKV, sring, rW)
            sc.dma('sp', gA, AP(attn_norm.tensor, l * D, [[0, 128], [1, D]]), writes=[rW])
            for gi in range(6):
                for half in range(2):
                    sc.dma('sp', gcol[half * 64:(half + 1) * 64, gi:gi + 1],
                           AP(gains.tensor, (l * 6 + gi) * 64, [[1, 64], [1, 1]]), writes=[rW])
            sc.barrier()
            if dbg == 'W':
                return
            ar.top = mark
            xs_r = Ring([(v3(ar.f32(4 * D), 4), Res('xs%d' % i)) for i in range(2)])
            hb = v3(ar.bf(4 * D), 4)
            rhb = Res('hb')
            sq = ar.f32(D)
            ssq = ar.f32(4)
            rstd = ar.f32(4)
            rtmp = Res('tmp')
            hT_r = Ring([(v3(ar.bf(8 * 512), 8), Res('hT%d' % i)) for i in range(2)])
            sqb_r = Ring([(ar.bf(512), Res('sqb%d' % i)) for i in range(2)])
            rs_r = Ring([(ar.f32(512), Res('rs%d' % i)) for i in range(2)])
            fst_r = Ring([(ar.bf(512), Res('fst%d' % i)) for i in range(3)])
            tst_r = Ring([(ar.bf(832), Res('tst%d' % i)) for i in range(2)])
            gst_r = Ring([(ar.f32(24), Res('gst%d' % i)) for i in range(2)])
            tb_r = Ring([(banks[i], bres[i]) for i in (0, 1)])
            pb_r = Ring([(banks[i], bres[i]) for i in (2, 3, 4)])
            nb_r = Ring([(banks[i], bres[i]) for i in (5,)])
            vb_r = Ring([(banks[i], bres[i]) for i in (6, 7)])
            r_hTd, r_fmd, r_tmvd, r_nsgd = Res(), Res(), Res(), Res()
            for ci in range(NC):
                xs, rx = xs_r.next()
                sc.dma('sp', xs, xsrc[ci * 512:(ci + 1) * 512, :].rearrange("(t p) d -> p t d", p=128), writes=[rx])
                if dbg == 'ld':
                    continue
                rmsnorm_tm(xs, rx, gA, rW, hb, rhb, 4, sq, ssq, rstd, rtmp)
                if dbg == 'norm':
                    continue
                hT, rhT = hT_r.next()
                transpose_to(hb, rhb, hT, rhT, 4, tb_r, ('dve', 'act'))
                if dbg == 'tr':
                    continue
                sc.dma(STQ, hTd.rearrange("(c p) s -> p c s", p=128)[:, :, ci * 512:(ci + 1) * 512], hT,
                       reads=[rhT], writes=[r_hTd])
                for j in range(16):
                    col = FM_COLS[j]
                    bk, rb = pb_r.next()
                    for c in range(8):
                        sc.mm(bk, W[:, c, col:col + 128], hT[:, c, :], start=(c == 0), stop=(c == 7),
                              reads=[rW, rhT], writes=[rb])
                    fs, rf = fst_r.next()
                    if FM_KIND[j] is None:
                        sc.cp('act' if j % 2 else 'dve', fs, bk, reads=[rb], writes=[rf])
                    else:
                        gi = FM_KIND[j]
                        sqb, rsq = sqb_r.next()
                        sc.act(sqb, bk, AF.Square, reads=[rb], writes=[rsq])
                        b2, rb2 = nb_r.next()
                        sc.mm(b2, blkones, sqb, reads=[rsq, r_const], writes=[rb2])
                        rs, rrs = rs_r.next()
                        sc.act(rs, b2, AF.Ln, reads=[rb2], writes=[rrs], scale=1.0 / 64, bias=EPS)
                        sc.act(rs, rs, AF.Exp, reads=[rrs], writes=[rrs], scale=-0.5)
                        sc.stt('dve', fs, bk, gcol[:, gi:gi + 1], rs, ALU.mult, ALU.mult, reads=[rb, rrs, rW], writes=[rf])
                    sc.dma(STQ, fmd[j * 128:(j + 1) * 128, ci * 512:(ci + 1) * 512], fs, reads=[rf], writes=[r_fmd])
                if dbg == 'fm':
                    continue
                for t in range(4):
                    b1, rb1 = vb_r.next()
                    b2, rb2 = vb_r.next()
                    lt = lambda c: hT[:, c, t * 128:(t + 1) * 128]
                    for (bk, rb, o, c0, w) in ((b1, rb1, 0, 512, 256), (b1, rb1, 256, 1280, 256),
                                               (b2, rb2, 0, 2432, 128), (b2, rb2, 128, 2688, 152)):
                        for c in range(8):
                            sc.mm(bk[:, o:o + w], lt(c), W[:, c, c0:c0 + w], start=(c == 0), stop=(c == 7),
                                  reads=[rW, rhT], writes=[rb])
                    ts_, rts = tst_r.next()
                    sc.cp('act', ts_[:, 0:512], b1, reads=[rb1], writes=[rts])
                    sc.cp('dve', ts_[:, 512:768], b2[:, 0:256], reads=[rb2], writes=[rts])
                    sc.act(ts_[:, 768:816].bitcast(F32), b2[:, 256:280], AF.Sigmoid, reads=[rb2], writes=[rts])
                    r0 = (ci * 4 + t) * 128
                    sc.dma(STQ, tmvd[r0:r0 + 128, 0:816], ts_[:, 0:816], reads=[rts], writes=[r_tmvd])
            sc.barrier()

        def phase_SB(l):
            ar.top = base_top
            kT = ar.bf(S)
            qT = ar.bf(S)
            v = v3(ar.bf(NT * 64), NT)
            rk = Res('kqv')
            R = ar.bf(512)
            rR = Res('R')
            zc_r = Ring([(ar.f32(512), Res()) for i in range(2)])
            u_r = Ring([(ar.f32(512), Res()) for i in range(2)])
            sp_r = Ring([(ar.bf(512), Res()) for i in range(3)])
            w_r = Ring([(ar.f32(512), Res()) for i in range(2)])
            a_r = Ring([(ar.bf(512), Res()) for i in range(3)])
            os_r = Ring([(ar.bf(512), Res()) for i in range(2)])
            zb_r = Ring([(banks[i], bres[i]) for i in (0, 1)])
            tb_r = Ring([(banks[i], bres[i]) for i in (2, 3)])
            ob_r = Ring([(banks[i], bres[i]) for i in (4, 5)])
            r_oTd = Res()
            for h in range(4):
                sc.dma('sp', kT[0:64, :], fmd[256 + 64 * h:256 + 64 * h + 64, :], writes=[rk])
                sc.dma('sp', qT[0:64, :], fmd[64 * h:64 * h + 64, :], writes=[rk])
                sc.dma('sp', v, tmvd[:, 64 * h:64 * h + 64].rearrange("(t p) d -> p t d", p=128), writes=[rk])
                for c in range(NC):
                    sc.memset('pool', R, 0.0, writes=[rR])
                    ob, rob = ob_r.next()
                    sc.mm(ob[0:64, :], zerob[:, 0:64], zerob, start=True, stop=False, reads=[r_const], writes=[rob])
                    last = 4 * c + 3
                    for kt in range(last, -1, -1):
                        rel = kt - 4 * c
                        c0 = 128 * max(rel, 0)
                        zb, rzb = zb_r.next()
                        sc.mm(zb[:, c0:512], kT[0:64, kt * 128:(kt + 1) * 128], qT[0:64, c * 512 + c0:(c + 1) * 512],
                              reads=[rk], writes=[rzb])
                        zc, rzc = zc_r.next()
                        sc.ts('dve', zc[:, c0:512], zb[:, c0:512], 0.125, 40.0, ALU.mult, ALU.min, reads=[rzb], writes=[rzc])
                        u, ru = u_r.next()
                        sc.act(u[:, c0:512], zc[:, c0:512], AF.Exp, reads=[rzc], writes=[ru])
                        sp, rsp = sp_r.next()
                        sc.act(sp[:, c0:512], u[:, c0:512], AF.Ln, reads=[ru], writes=[rsp], bias=1.0)
                        if rel >= 0:
                            sc.tt('pool', sp[:, c0:c0 + 128], sp[:, c0:c0 + 128], maskL, ALU.mult,
                                  reads=[rsp, r_const], writes=[rsp])
                        tb, rtb = tb_r.next()
                        sc.mm(tb[:, c0:512], uincl, sp[:, c0:512], start=True, stop=(kt == last),
                              reads=[rsp, r_const], writes=[rtb])
                        if kt != last:
                            sc.mm(tb[:, c0:512], onesb, R[:, c0:512], start=False, stop=True,
                                  reads=[rR, r_const], writes=[rtb])
                        w, rw = w_r.next()
                        sc.tt('dve', w[:, c0:512], zc[:, c0:512], tb[:, c0:512], ALU.subtract, reads=[rzc, rtb], writes=[rw])
                        if rel >= 0:
                            sc.tt('pool', w[:, c0:c0 + 128], w[:, c0:c0 + 128], negmask, ALU.add,
                                  reads=[rw, r_const], writes=[rw])
                        a, ra = a_r.next()
                        sc.act(a[:, c0:512], w[:, c0:512], AF.Exp, reads=[rw], writes=[ra])
                        if kt > 0:
                            sc.tt('pool', R[:, c0:512], R[:, c0:512], sp[:, c0:512], ALU.add, reads=[rR, rsp], writes=[rR])
                        sc.mm(ob[0:64, c0:512], v[:, kt, :], a[:, c0:512], start=False, stop=(kt == 0),
                              reads=[rk, ra], writes=[rob])
                    os_, ros = os_r.next()
                    sc.cp('act', os_[0:64, :], ob[0:64, :], reads=[rob], writes=[ros])
                    sc.dma('sp', oTd[64 * h:64 * h + 64, c * 512:(c + 1) * 512], os_[0:64, :], reads=[ros], writes=[r_oTd])
            sc.barrier()


        def phase_MB(l):
            ar.top = base_top
            NBLK = S // 256
            kT = ar.bf(S)
            qT = ar.bf(S)
            va = v3(ar.bf(NT * 65), NT)
            rk = Res('kqv')
            km = ar.f32(32)
            rkm = Res('km')
            qf_r = Ring([(ar.f32(512), Res()) for i in range(2)])
            scv = v3(ar.f32(4 * 32), 4)
            m8 = ar.f32(32)
            selw = v3(ar.f32(4 * 32), 4)
            rsel = Res('sel')
            acc = v3(ar.f32(4 * 65), 4)
            racc = Res('acc')
            rl = v3(ar.f32(4), 4)
            o_tok = v3(ar.bf(NT * 256), NT)
            rot = Res('otok')
            p_r = Ring([(ar.bf(512), Res()) for i in range(3)])
            st_r = Ring([(ar.bf(512), Res()) for i in range(2)])
            lb_r = Ring([(banks[i], bres[i]) for i in (0, 1)])
            ob_r = Ring([(banks[i], bres[i]) for i in (2, 3)])
            sb_r = Ring([(banks[i], bres[i]) for i in (4,)])
            tb_r = Ring([(banks[i], bres[i]) for i in (5, 6)])
            r_oTd = Res()
            sc.memset('pool', va[:, :, 64:65], 1.0, writes=[rk])
            for h in range(4):
                sc.dma('sp', kT[0:64, :], fmd[768 + 64 * h:768 + 64 * h + 64, :], writes=[rk])
                sc.dma('sp', qT[0:64, :], fmd[512 + 64 * h:512 + 64 * h + 64, :], writes=[rk])
                sc.dma('sp', va[:, :, 0:64], tmvd[:, 256 + 64 * h:256 + 64 * h + 64].rearrange("(t p) d -> p t d", p=128), writes=[rk])
                sc.op('dve', lambda e: e.tensor_reduce(km[0:64, 0:NBLK], kT[0:64, :].rearrange("p (n k) -> p n k", k=256), AX.X, ALU.add),
                      reads=[rk], writes=[rkm])
                sc.ts('dve', km[0:64, 0:NBLK], km[0:64, 0:NBLK], 1.0 / 256, None, ALU.mult, reads=[rkm], writes=[rkm])
                for c in range(NC):
                    qf, rqf = qf_r.next()
                    sc.cp('dve', qf[0:64, :], qT[0:64, c * 512:(c + 1) * 512], reads=[rk], writes=[rqf])
                    sc.memset('pool', scv, NEG, writes=[rsel])
                    sb, rsb = sb_r.next()
                    for j in range(4):
                        cur = (4 * c + j) // 2
                        if cur > 0:
                            sc.mm(sb[:, j * 32:j * 32 + cur], qf[0:64, j * 128:(j + 1) * 128], km[0:64, 0:cur],
                                  reads=[rqf, rkm], writes=[rsb])
                    for j in range(4):
                        cur = (4 * c + j) // 2
                        if cur > 0:
                            sc.cp('dve', scv[:, j, 0:cur], sb[:, j * 32:j * 32 + cur], reads=[rsb], writes=[rsel])
                    for j in range(4):
                        sc.op('dve', lambda e, j=j: e.max(out=m8[:, j * 8:(j + 1) * 8], in_=scv[:, j, :]), reads=[rsel], writes=[rsel])
                        sc.ts('dve', selw[:, j, :], scv[:, j, :], m8[:, j * 8 + 2:j * 8 + 3], None, ALU.is_ge, reads=[rsel], writes=[rsel])
                    sc.memset('pool', acc, 0.0, writes=[racc])
                    ob, rob = None, None
                    for kt in range(4 * c + 4):
                        n = kt // 2
                        rel = kt - 4 * c
                        j0 = max(rel, 0)
                        c0 = 128 * j0
                        lb, rlb = lb_r.next()
                        sc.mm(lb[:, c0:512], kT[0:64, kt * 128:(kt + 1) * 128], qT[0:64, c * 512 + c0:(c + 1) * 512],
                              reads=[rk], writes=[rlb])
                        p, rp = p_r.next()
                        sc.act(p[:, c0:512], lb[:, c0:512], AF.Exp, reads=[rlb], writes=[rp], scale=0.125)
                        for j in range(j0, 4):
                            d = 4 * c + j - kt
                            if d == 0:
                                sc.tt('pool', p[:, j * 128:(j + 1) * 128], p[:, j * 128:(j + 1) * 128], E0(h), ALU.mult,
                                      reads=[rp, r_const], writes=[rp])
                            elif d == 1:
                                sc.tt('pool', p[:, j * 128:(j + 1) * 128], p[:, j * 128:(j + 1) * 128], E128(h), ALU.mult,
                                      reads=[rp, r_const], writes=[rp])
                        if kt % 2 == 0:
                            ob, rob = ob_r.next()
                        done = []
                        for j in range(j0, 4):
                            qt = 4 * c + j
                            stop = (kt % 2 == 1) or (kt == qt)
                            sc.mm(ob[:, j * 128:j * 128 + 65], p[:, j * 128:(j + 1) * 128], va[:, kt, :],
                                  start=(kt % 2 == 0 and j == j0), stop=stop, reads=[rp, rk], writes=[rob])
                            if stop:
                                done.append(j)
                        for j in done:
                            qt = 4 * c + j
                            if n == qt // 2:
                                sc.tt('dve', acc[:, j, :], ob[:, j * 128:j * 128 + 65], acc[:, j, :], ALU.add,
                                      reads=[rob, racc], writes=[racc])
                            else:
                                sc.stt('dve', acc[:, j, :], ob[:, j * 128:j * 128 + 65], selw[:, j, n:n + 1], acc[:, j, :],
                                       ALU.mult, ALU.add, reads=[rob, racc, rsel], writes=[racc])
                    sc.ts('dve', rl, acc[:, :, 64:65], TINY, None, ALU.max, reads=[racc], writes=[racc])
                    sc.op('dve', lambda e: e.reciprocal(rl, rl), reads=[racc], writes=[racc])
                    for j in range(4):
                        sc.ts('dve', o_tok[:, 4 * c + j, h * 64:(h + 1) * 64], acc[:, j, 0:64], rl[:, j, :], None, ALU.mult,
                              reads=[racc], writes=[rot])
            for c in range(NC):
                for fc in range(2):
                    tb, rtb = tb_r.next()
                    pb = tb.bitcast(BF16)
                    for j in range(4):
                        sc.tr(pb[:, j * 128:(j + 1) * 128], o_tok[:, 4 * c + j, fc * 128:(fc + 1) * 128], ident,
                              reads=[rot, r_const], writes=[rtb])
                    stg, rst = st_r.next()
                    sc.cp('act' if fc else 'dve', stg, pb[:, 0:512], reads=[rtb], writes=[rst])
                    sc.dma(STQ, oTd[256 + fc * 128:256 + (fc + 1) * 128, c * 512:(c + 1) * 512], stg, reads=[rst], writes=[r_oTd])
            sc.barrier()


        def phase_NSA(l):
            ar.top = base_top
            NCW = NCT * 128
            kcT2 = [ar.bf(NCW) for g in range(2)]
            vca = [v3(ar.bf(NCT * 65), NCT) for g in range(2)]
            gcol = ar.f32(8)
            r_cmp = Res('cmp')
            mark_p = ar.top
            w1sb = v3(ar.bf(32 * 256), 32)
            w2sb = v3(ar.bf(2 * 64), 2)
            TT = ar.bf(S)
            posf = ar.f32(64)
            posb = ar.bf(64)
            posT = ar.bf(32)
            pbias = ar.f32(2)
            xb = ar.f32(512)
            x2 = ar.f32(512)
            th = ar.f32(512)
            ghT = v3(ar.bf(2 * 512), 2)
            sqb = ar.bf(512)
            rs = ar.f32(512)
            sring = Ring([(ar.f32(2048), Res()) for i in range(2)])
            rw, rt, rx, rgh = Res('w'), Res('TT'), Res('x'), Res('gh')
            n = n_cmp
            for gi in range(6):
                for half in range(2):
                    sc.dma('sp', gcol[half * 64:(half + 1) * 64, gi:gi + 1],
                           AP(gains.tensor, (l * 6 + gi) * 64, [[1, 64], [1, 1]]), writes=[r_cmp])
            for g in range(2):
                sc.memset('pool', kcT2[g], 0.0, writes=[r_cmp])
                sc.memset('pool', vca[g], 0.0, writes=[r_cmp])
            for kv in range(2):
                w1v = cmp_w1[l, kv].rearrange("(i d) h -> d i h", d=64)
                for i0 in range(0, 32, 8):
                    sg, rsg = sring.next()
                    sc.dma('sp', v3(sg, 8)[0:64], w1v[:, i0:i0 + 8, :], writes=[rsg])
                    sc.cp('pool', w1sb[0:64, i0:i0 + 8, :], v3(sg, 8)[0:64], reads=[rsg], writes=[rw])
                sg, rsg = sring.next()
                sc.dma('sp', v3(sg[:, 0:128], 2), cmp_w2[l, kv].rearrange("(c p) d -> p c d", p=128), writes=[rsg])
                sc.cp('pool', w2sb, v3(sg[:, 0:128], 2), reads=[rsg], writes=[rw])
                sc.dma('sp', posf[0:32, :], cmp_pos[l, kv], writes=[rw])
                sc.cp('dve', posb[0:32, :], posf[0:32, :], reads=[rw], writes=[rw])
                pbk = banks[5].bitcast(BF16)
                sc.tr(pbk[0:64, 0:32], posb[0:32, 0:64], ident[0:32, 0:32], reads=[rw, r_const], writes=[bres[5]])
                sc.cp('dve', posT[0:64, :], pbk[0:64, 0:32], reads=[bres[5]], writes=[rw])
                for hc in range(2):
                    for i in range(32):
                        sc.mm(banks[6][:, hc:hc + 1], w1sb[0:64, i, hc * 128:(hc + 1) * 128], posT[0:64, i:i + 1],
                              start=(i == 0 and hc == 0), stop=(i == 31), reads=[rw], writes=[bres[6]])
                sc.cp('dve', pbias, banks[6][:, 0:2], reads=[bres[6]], writes=[rw])
                for g in range(2):
                    base = (1536 if kv == 0 else 1664) + 64 * g
                    sc.dma('sp', TT[0:64, :], fmd[base:base + 64, :], writes=[rt])
                    for hc in range(2):
                        bk, rb = banks[hc], bres[hc]
                        for i in range(32):
                            sc.mm(bk[:, 0:n], w1sb[0:64, i, hc * 128:(hc + 1) * 128], TT[0:64, i:i + 16 * (n - 1) + 1:16],
                                  start=(i == 0), stop=(i == 31), reads=[rw, rt], writes=[rb])
                        sc.ts('dve', xb[:, 0:n], bk[:, 0:n], pbias[:, hc:hc + 1], None, ALU.add, reads=[rb, rw], writes=[rx])
                        sc.tt('dve', x2[:, 0:n], xb[:, 0:n], xb[:, 0:n], ALU.mult, reads=[rx], writes=[rx])
                        sc.ts('dve', x2[:, 0:n], x2[:, 0:n], 0.044715, 1.0, ALU.mult, ALU.add, reads=[rx], writes=[rx])
                        sc.tt('dve', x2[:, 0:n], x2[:, 0:n], xb[:, 0:n], ALU.mult, reads=[rx], writes=[rx])
                        sc.act(th[:, 0:n], x2[:, 0:n], AF.Tanh, reads=[rx], writes=[rx], scale=0.7978845608028654)
                        sc.ts('dve', xb[:, 0:n], xb[:, 0:n], 0.5, None, ALU.mult, reads=[rx], writes=[rx])
                        sc.stt('dve', ghT[:, hc, 0:n], th[:, 0:n], 1.0, xb[:, 0:n], ALU.add, ALU.mult, reads=[rx], writes=[rgh])
                    if kv == 0:
                        for hc in range(2):
                            sc.mm(banks[2][0:64, 0:n], w2sb[:, hc, :], ghT[:, hc, 0:n], start=(hc == 0), stop=(hc == 1),
                                  reads=[rw, rgh], writes=[bres[2]])
                        sc.act(sqb[0:64, 0:n], banks[2][0:64, 0:n], AF.Square, reads=[bres[2]], writes=[rx])
                        sc.mm(banks[3][0:64, 0:n], onesb[0:64, 0:64], sqb[0:64, 0:n], reads=[rx, r_const], writes=[bres[3]])
                        sc.act(rs[0:64, 0:n], banks[3][0:64, 0:n], AF.Ln, reads=[bres[3]], writes=[rx], scale=1.0 / 64, bias=EPS)
                        sc.act(rs[0:64, 0:n], rs[0:64, 0:n], AF.Exp, reads=[rx], writes=[rx], scale=-0.5)
                        sc.stt('dve', kcT2[g][0:64, 0:n], banks[2][0:64, 0:n], gcol[0:64, 3:4], rs[0:64, 0:n], ALU.mult, ALU.mult,
                               reads=[bres[2], rx, r_cmp], writes=[r_cmp])
                        sc.dma('sp', kcT2[g][64:128, :], kcT2[g][0:64, :], reads=[r_cmp], writes=[r_cmp])
                    else:
                        for nt in range(NCT):
                            rows = min(128, n - nt * 128)
                            for hc in range(2):
                                sc.mm(banks[2][0:rows, 0:64], ghT[:, hc, nt * 128:nt * 128 + rows], w2sb[:, hc, :],
                                      start=(hc == 0), stop=(hc == 1), reads=[rw, rgh], writes=[bres[2]])
                            sc.cp('dve', vca[g][0:rows, nt, 0:64], banks[2][0:rows, 0:64], reads=[bres[2]], writes=[r_cmp])
                            sc.memset('pool', vca[g][0:rows, nt, 64:65], 1.0, writes=[r_cmp])
            sc.barrier()
            for g in range(2):
                ar.top = mark_p
                qT = v3(ar.bf(2 * S), 2)
                ksT2 = ar.bf(S)
                kwT2 = ar.bf(S)
                vsa = v3(ar.bf(NT * 65), NT)
                vwa = v3(ar.bf(NT * 65), NT)
                G = [ar.bf(2560) for hh in range(4)]
                Bm = ar.bf(S)
                ovl = v3(ar.bf(NCW), NCT)
                rk = Res('kqv')
                hk_r = Ring([(ar.bf(512), Res()) for i in range(2)])
                gtb = ar.bf(4 * 48)
                gt3 = v3(gtb.bitcast(F32), 4)
                rgt = Res('gt')
                impacc = v3(ar.f32(512), 4)
                rimp = Res('imp')
                itmp = v3(ar.f32(512), 4)
                ritmp = Res()
                score = v3(ar.f32(512), 4)
                wk = v3(ar.f32(512), 4)
                m8a = ar.f32(32)
                m8b = ar.f32(32)
                selb = v3(ar.bf(512), 4)
                selT = ar.bf(512)
                rsel = Res('sel')
                oc = ar.f32(4 * 4 * 64)
                roc = Res('oc')
                rlc = ar.f32(16)
                rls = v3(ar.f32(4), 4)
                rlw = v3(ar.f32(4), 4)
                cfs = v3(ar.f32(4), 4)
                cfw = v3(ar.f32(4), 4)
                rcoef = Res('coef')
                ot1 = v3(ar.f32(256), 4)
                ot2 = v3(ar.f32(256), 4)
                ot3 = v3(ar.f32(256), 4)
                rot1, rot2, rot3 = Res(), Res(), Res()
                o_tok = v3(ar.bf(4 * 256), 4)
                rotk = Res('otok')
                p_r = Ring([(ar.bf(512), Res()) for i in range(3)])
                st_r = Ring([(ar.bf(512), Res()) for i in range(2)])
                lb_r = Ring([(banks[i], bres[i]) for i in (0, 1)])
                mb_r = Ring([(banks[i], bres[i]) for i in (2,)])
                a1_r = Ring([(banks[i], bres[i]) for i in (3, 6)])
                a2_r = Ring([(banks[i], bres[i]) for i in (4, 7)])
                tb_r = Ring([(banks[i], bres[i]) for i in (5,)])
                r_oTd = Res()
                for hh in range(4):
                    half, a = hh // 2, hh % 2
                    r0 = 1024 + 64 * (4 * g + hh)
                    sc.dma('sp', qT[half * 64:(half + 1) * 64, a, :], fmd[r0:r0 + 64, :], writes=[rk])
                for half in range(2):
                    sc.dma('sp', ksT2[half * 64:(half + 1) * 64, :], fmd[1792 + 64 * g:1792 + 64 * g + 64, :], writes=[rk])
                    sc.dma('sp', kwT2[half * 64:(half + 1) * 64, :], fmd[1920 + 64 * g:1920 + 64 * g + 64, :], writes=[rk])
                sc.dma('sp', vsa[:, :, 0:64], tmvd[:, 512 + 64 * g:512 + 64 * g + 64].rearrange("(t p) d -> p t d", p=128), writes=[rk])
                sc.dma('sp', vwa[:, :, 0:64], tmvd[:, 640 + 64 * g:640 + 64 * g + 64].rearrange("(t p) d -> p t d", p=128), writes=[rk])
                sc.memset('pool', vsa[:, :, 64:65], 1.0, writes=[rk])
                sc.memset('pool', vwa[:, :, 64:65], 1.0, writes=[rk])
                sc.dma('sp', ovl, v3(cb_in[:, CB_GLOBAL:CB_GLOBAL + NCW], NCT), writes=[rk])
                sc.dma('sp', Bm, cb_in[:, CB_GLOBAL + NCW:CB_GLOBAL + NCW + S], writes=[rk])
                for hh in range(4):
                    hrow = 4 + 4 * g + hh
                    for idx in range(5):
                        hk, rhk = hk_r.next()
                        src = AP(gvd.tensor, hrow * GL + GOFF - 2063 + 512 * idx, [[16, 128], [1, 512]])
                        sc.dma('sp', hk, src, writes=[rhk])
                        tb, rtb = tb_r.next()
                        sc.mm(tb, Jm, hk, reads=[rhk, r_const], writes=[rtb])
                        sc.cp('dve', G[hh][:, idx * 512:(idx + 1) * 512], tb, reads=[rtb], writes=[rk])
                for c in range(NC):
                    sc.dma('sp', v3(gtb, 4), tmvd[c * 512:(c + 1) * 512, 768:816].rearrange("(t p) w -> p t w", p=128), writes=[rgt])
                    sc.memset('pool', impacc, 0.0, writes=[rimp])
                    nts = [nt for nt in range(NCT) if c - 4 * nt >= 0]
                    oc4 = oc.rearrange("p (h j d) -> p h j d", h=4, j=4)
                    rlc4 = rlc.rearrange("p (h j o) -> p h j o", h=4, o=1)
                    for hh in range(4):
                        half, a = hh // 2, hh % 2
                        hs = slice(half * 64, half * 64 + 64)
                        ocb, rocb = a1_r.next()
                        ib, rib = a2_r.next()
                        first = True
                        for nt in nts:
                            lb, rlb = lb_r.next()
                            sc.mm(lb, kcT2[g][hs, nt * 128:(nt + 1) * 128], qT[hs, a, c * 512:(c + 1) * 512], reads=[r_cmp, rk], writes=[rlb])
                            pc, rpc = p_r.next()
                            sc.act(pc, lb, AF.Exp, reads=[rlb], writes=[rpc], scale=0.125)
                            idx = c - 4 * nt
                            if idx < 5:
                                sc.tt('pool', pc, pc, G[hh][:, idx * 512:(idx + 1) * 512], ALU.mult, reads=[rpc, rk], writes=[rpc])
                            for j in range(4):
                                sc.mm(ocb[:, j * 128:j * 128 + 65], pc[:, j * 128:(j + 1) * 128], vca[g][:, nt, :],
                                      start=(first and j == 0), stop=(nt == nts[-1]), reads=[rpc, r_cmp], writes=[rocb])
                            for j in range(4):
                                sc.mm(ib[:, j * 128:(j + 1) * 128], pc[:, j * 128:(j + 1) * 128], ovl[:, nt, :],
                                      start=(first and j == 0), stop=(nt == nts[-1]), reads=[rpc, rk], writes=[rib])
                            first = False
                        ocb3 = v3(ocb, 4)
                        sc.ts('dve', rlc4[:, hh], ocb3[:, :, 64:65], TINY, None, ALU.max, reads=[rocb], writes=[roc])
                        sc.op('dve', lambda e, hh=hh: e.reciprocal(rlc4[:, hh], rlc4[:, hh]), reads=[roc], writes=[roc])
                        sc.tt('dve', oc4[:, hh], ocb3[:, :, 0:64], rlc4[:, hh].to_broadcast([128, 4, 64]), ALU.mult,
                              reads=[rocb, roc], writes=[roc])
                        sc.tt('dve', itmp, v3(ib, 4), rlc4[:, hh].to_broadcast([128, 4, 128]), ALU.mult, reads=[rib, roc], writes=[ritmp])
                        sc.tt('pool', impacc, impacc, itmp, ALU.add, reads=[rimp, ritmp], writes=[rimp])
                    for j in range(4):
                        off = 126 - 2 * (4 * c + j)
                        sc.tt('dve', score[:, j, :], impacc[:, j, :], cslide[:, off:off + 128], ALU.add, reads=[rimp, r_const], writes=[rsel])
                    sc.memset('dve', score[:, :, 0:1], BIG, writes=[rsel])
                    for j in range(4):
                        sc.op('dve', lambda e, j=j: e.max(out=m8a[:, j * 8:(j + 1) * 8], in_=score[:, j, :]), reads=[rsel], writes=[rsel])
                        sc.op('dve', lambda e, j=j: e.match_replace(out=wk[:, j, :], in_to_replace=m8a[:, j * 8:(j + 1) * 8],
                                                                    in_values=score[:, j, :], imm_value=-3e38), reads=[rsel], writes=[rsel])
                        sc.op('dve', lambda e, j=j: e.max(out=m8b[:, j * 8:(j + 1) * 8], in_=wk[:, j, :]), reads=[rsel], writes=[rsel])
                        sc.ts('dve', selb[:, j, :], score[:, j, :], m8b[:, j * 8 + 7:j * 8 + 8], None, ALU.is_ge, reads=[rsel], writes=[rsel])
                    tb, rtb = tb_r.next()
                    pb = tb.bitcast(BF16)
                    for j in range(4):
                        sc.tr(pb[:, j * 128:(j + 1) * 128], selb[:, j, :], ident, reads=[rsel, r_const], writes=[rtb])
                    sc.cp('dve', selT, pb[:, 0:512], reads=[rtb], writes=[rsel])
                    for hh in range(4):
                        half, a = hh // 2, hh % 2
                        hs = slice(half * 64, half * 64 + 64)
                        hrow = 4 + 4 * g + hh
                        osb, rosb = a1_r.next()
                        owb, rowb = a2_r.next()
                        first = True
                        for kt in range(4 * c + 4):
                            j0 = max(kt - 4 * c, 0)
                            c0 = 128 * j0
                            lb, rlb = lb_r.next()
                            sc.mm(lb[:, c0:512], ksT2[hs, kt * 128:(kt + 1) * 128], qT[hs, a, c * 512 + c0:(c + 1) * 512], reads=[rk], writes=[rlb])
                            mb, rmb = mb_r.next()
                            sc.mm(mb[:, c0:512], Bm[:, kt * 128:(kt + 1) * 128], selT[:, c0:512], reads=[rk, rsel], writes=[rmb])
                            ps, rps = p_r.next()
                            sc.act(ps[:, c0:512], lb[:, c0:512], AF.Exp, reads=[rlb], writes=[rps], scale=0.125)
                            sc.tt('dve', ps[:, c0:512], ps[:, c0:512], mb[:, c0:512], ALU.mult, reads=[rps, rmb], writes=[rps])
                            for j in range(j0, 4):
                                d = 4 * c + j - kt
                                if d in (0, 1):
                                    sc.tt('pool', ps[:, j * 128:(j + 1) * 128], ps[:, j * 128:(j + 1) * 128],
                                          E0(hrow) if d == 0 else E128(hrow), ALU.mult, reads=[rps, r_const], writes=[rps])
                            for j in range(j0, 4):
                                sc.mm(osb[:, j * 128:j * 128 + 65], ps[:, j * 128:(j + 1) * 128], vsa[:, kt, :],
                                      start=(first and j == j0), stop=(kt == 4 * c + j), reads=[rps, rk], writes=[rosb])
                            first = False
                        first = True
                        for kt in range(max(4 * c - 4, 0), 4 * c + 4):
                            jlo = max(kt - 4 * c, 0)
                            jhi = min(kt + 4 - 4 * c, 3)
                            cs_ = slice(128 * jlo, 128 * (jhi + 1))
                            lb, rlb = lb_r.next()
                            sc.mm(lb[:, cs_], kwT2[hs, kt * 128:(kt + 1) * 128], qT[hs, a, c * 512 + 128 * jlo:c * 512 + 128 * (jhi + 1)],
                                  reads=[rk], writes=[rlb])
                            pw, rpw = p_r.next()
                            sc.act(pw[:, cs_], lb[:, cs_], AF.Exp, reads=[rlb], writes=[rpw], scale=0.125)
                            for j in range(jlo, jhi + 1):
                                d = 4 * c + j - kt
                                if d in (0, 1, 4):
                                    mk = E0(hrow) if d == 0 else (E128(hrow) if d == 1 else m512)
                                    sc.tt('pool', pw[:, j * 128:(j + 1) * 128], pw[:, j * 128:(j + 1) * 128], mk, ALU.mult,
                                          reads=[rpw, r_const], writes=[rpw])
                            for j in range(jlo, jhi + 1):
                                sc.mm(owb[:, j * 128:j * 128 + 65], pw[:, j * 128:(j + 1) * 128], vwa[:, kt, :],
                                      start=(first and j == jlo), stop=(kt == 4 * c + j), reads=[rpw, rk], writes=[rowb])
                            first = False
                        osb3 = v3(osb, 4)
                        owb3 = v3(owb, 4)
                        sc.ts('dve', rls, osb3[:, :, 64:65], TINY, None, ALU.max, reads=[rosb], writes=[rcoef])
                        sc.op('dve', lambda e: e.reciprocal(rls, rls), reads=[rcoef], writes=[rcoef])
                        sc.ts('dve', rlw, owb3[:, :, 64:65], TINY, None, ALU.max, reads=[rowb], writes=[rcoef])
                        sc.op('dve', lambda e: e.reciprocal(rlw, rlw), reads=[rcoef], writes=[rcoef])
                        gi = g * 4 + hh
                        sc.tt('dve', cfs, rls, gt3[:, :, 8 + gi:9 + gi], ALU.mult, reads=[rcoef, rgt], writes=[rcoef])
                        sc.tt('dve', cfw, rlw, gt3[:, :, 16 + gi:17 + gi], ALU.mult, reads=[rcoef, rgt], writes=[rcoef])
                        sc.tt('dve', ot1, oc4[:, hh], gt3[:, :, gi:gi + 1].to_broadcast([128, 4, 64]), ALU.mult, reads=[roc, rgt], writes=[rot1])
                        sc.tt('dve', ot2, osb3[:, :, 0:64], cfs.to_broadcast([128, 4, 64]), ALU.mult, reads=[rosb, rcoef], writes=[rot2])
                        sc.tt('dve', ot3, owb3[:, :, 0:64], cfw.to_broadcast([128, 4, 64]), ALU.mult, reads=[rowb, rcoef], writes=[rot3])
                        sc.tt('pool', ot1, ot1, ot2, ALU.add, reads=[rot1, rot2], writes=[rot1])
                        sc.tt('pool', o_tok[:, :, hh * 64:(hh + 1) * 64], ot1, ot3, ALU.add, reads=[rot1, rot3], writes=[rotk])
                    for fc in range(2):
                        tb, rtb = tb_r.next()
                        pb = tb.bitcast(BF16)
                        for j in range(4):
                            sc.tr(pb[:, j * 128:(j + 1) * 128], o_tok[:, j, fc * 128:(fc + 1) * 128], ident, reads=[rotk, r_const], writes=[rtb])
                        stg, rst = st_r.next()
                        sc.cp('act', stg, pb[:, 0:512], reads=[rtb], writes=[rst])
                        r0 = 512 + g * 256 + fc * 128
                        sc.dma(STQ, oTd[r0:r0 + 128, c * 512:(c + 1) * 512], stg, reads=[rst], writes=[r_oTd])
                sc.barrier()

        def phase_C1(l, xsrc):
            ar.top = base_top
            Wg = v3(ar.bf(8 * 3072), 8)
            Wb = v3(ar.bf(8 * D), 8)
            Wo = v3(ar.bf(8 * D), 8)
            rW = Res('W')
            mark = ar.top
            sring = Ring([(ar.f32(2048), Res()) for i in range(2)])
            for c in range(8):
                load_cast(Wg[:, c, :], w_in[l, c * 128:(c + 1) * 128, NQKV:NIN], 3072, sring, rW)
                load_cast(Wb[:, c, :], w_branch[l, c * 128:(c + 1) * 128, :], D, sring, rW)
                load_cast(Wo[:, c, :], w_out[l, c * 128:(c + 1) * 128, :], D, sring, rW)
            sc.barrier()
            ar.top = mark
            xs_r = Ring([(v3(ar.f32(4 * D), 4), Res()) for i in range(2)])
            hT_r = Ring([(v3(ar.bf(8 * 512), 8), Res()) for i in range(2)])
            oT_r = Ring([(v3(ar.bf(8 * 512), 8), Res()) for i in range(2)])
            mixT = v3(ar.bf(8 * 512), 8)
            rmix = Res('mix')
            g_r = Ring([(ar.f32(512), Res()) for i in range(3)])
            t_r = Ring([(ar.f32(512), Res()) for i in range(4)])
            gb_r = Ring([(banks[i], bres[i]) for i in (0, 1, 2)])
            bb_r = Ring([(banks[i], bres[i]) for i in (3, 4, 5)])
            yb_r = Ring([(banks[i], bres[i]) for i in (6, 7)])
            r_x1d = Res()
            branch_k = ((0, 1), (2, 3), (4, 5, 6, 7))
            for ci in range(NC):
                xs, rx = xs_r.next()
                sc.dma('sp', xs, xsrc[ci * 512:(ci + 1) * 512, :].rearrange("(t p) d -> p t d", p=128), writes=[rx])
                hT, rhT = hT_r.next()
                sc.dma('sp', hT, hTd.rearrange("(c p) s -> p c s", p=128)[:, :, ci * 512:(ci + 1) * 512], writes=[rhT])
                oT, roT = oT_r.next()
                sc.dma('sp', oT, oTd.rearrange("(c p) s -> p c s", p=128)[:, :, ci * 512:(ci + 1) * 512], writes=[roT])
                for dm in range(8):
                    ts_ = []
                    for br in range(3):
                        gb, rgb = gb_r.next()
                        col = br * D + dm * 128
                        for c in range(8):
                            sc.mm(gb, Wg[:, c, col:col + 128], hT[:, c, :], start=(c == 0), stop=(c == 7),
                                  reads=[rW, rhT], writes=[rgb])
                        g, rg = g_r.next()
                        sc.act(g, gb, AF.Sigmoid, reads=[rgb], writes=[rg])
                        bb, rbb = bb_r.next()
                        ks = branch_k[br]
                        for i, k in enumerate(ks):
                            sc.mm(bb, Wb[:, k, dm * 128:(dm + 1) * 128], oT[:, k, :], start=(i == 0), stop=(i == len(ks) - 1),
                                  reads=[rW, roT], writes=[rbb])
                        t, rt = t_r.next()
                        sc.tt('dve', t, bb, g, ALU.mult, reads=[rbb, rg], writes=[rt])
                        ts_.append((t, rt))
                    sc.tt('pool', ts_[0][0], ts_[0][0], ts_[1][0], ALU.add, reads=[ts_[0][1], ts_[1][1]], writes=[ts_[0][1]])
                    sc.tt('pool', mixT[:, dm, :], ts_[0][0], ts_[2][0], ALU.add, reads=[ts_[0][1], ts_[2][1]], writes=[rmix])
                for t in range(4):
                    for half in range(2):
                        yb, ryb = yb_r.next()
                        for k in range(8):
                            sc.mm(yb, mixT[:, k, t * 128:(t + 1) * 128], Wo[:, k, half * 512:(half + 1) * 512],
                                  start=(k == 0), stop=(k == 7), reads=[rmix, rW], writes=[ryb])
                        sc.tt('dve', xs[:, t, half * 512:(half + 1) * 512], xs[:, t, half * 512:(half + 1) * 512], yb, ALU.add,
                              reads=[rx, ryb], writes=[rx])
                sc.dma(STQ, x1d[ci * 512:(ci + 1) * 512, :].rearrange("(t p) d -> p t d", p=128), xs, reads=[rx], writes=[r_x1d])
            sc.barrier()

        def phase_C2(l, xdst):
            ar.top = base_top
            Wgu = v3(ar.bf(8 * 2 * DFF), 8)
            Wd = v3(ar.bf(22 * D), 22)
            gF = ar.f32(D)
            rW = Res('W')
            mark = ar.top
            sring = Ring([(ar.f32(2048), Res()) for i in range(2)])
            for c in range(8):
                load_cast(Wgu[:, c, :], w_gate_up[l, c * 128:(c + 1) * 128, :], 2 * DFF, sring, rW)
            for f in range(22):
                load_cast(Wd[:, f, :], w_down[l, f * 128:(f + 1) * 128, :], D, sring, rW)
            sc.dma('sp', gF, AP(ffn_norm.tensor, l * D, [[0, 128], [1, D]]), writes=[rW])
            sc.barrier()
            ar.top = mark
            xs_r = Ring([(v3(ar.f32(2 * D), 2), Res()) for i in range(2)])
            hb = v3(ar.bf(2 * D), 2)
            rhb = Res()
            sq = ar.f32(D)
            ssq = ar.f32(4)
            rstd = ar.f32(4)
            rtmp = Res()
            hT = v3(ar.bf(8 * 256), 8)
            rhT = Res()
            actT = v3(ar.bf(22 * 256), 22)
            ract = Res()
            sg_r = Ring([(ar.f32(256), Res()) for i in range(2)])
            tb_r = Ring([(banks[i], bres[i]) for i in (0, 1)])
            fb_r = Ring([(banks[i], bres[i]) for i in (2, 3, 4)])
            yb_r = Ring([(banks[i], bres[i]) for i in (5, 6, 7)])
            r_out = Res()
            for ci in range(S // 256):
                xs, rx = xs_r.next()
                sc.dma('sp', xs, x1d[ci * 256:(ci + 1) * 256, :].rearrange("(t p) d -> p t d", p=128), writes=[rx])
                rmsnorm_tm(xs, rx, gF, rW, hb, rhb, 2, sq, ssq, rstd, rtmp)
                transpose_to(hb, rhb, hT, rhT, 2, tb_r, ('dve', 'act'))
                for f in range(22):
                    fb, rfb = fb_r.next()
                    for half, col in ((0, f * 128), (1, DFF + f * 128)):
                        for c in range(8):
                            sc.mm(fb[:, half * 256:(half + 1) * 256], Wgu[:, c, col:col + 128], hT[:, c, :],
                                  start=(c == 0), stop=(c == 7), reads=[rW, rhT], writes=[rfb])
                    sg, rsg = sg_r.next()
                    sc.act(sg, fb[:, 0:256], AF.Silu, reads=[rfb], writes=[rsg])
                    sc.tt('dve', actT[:, f, :], sg, fb[:, 256:512], ALU.mult, reads=[rsg, rfb], writes=[ract])
                for t in range(2):
                    for half in range(2):
                        yb, ryb = yb_r.next()
                        for f in range(22):
                            sc.mm(yb, actT[:, f, t * 128:(t + 1) * 128], Wd[:, f, half * 512:(half + 1) * 512],
                                  start=(f == 0), stop=(f == 21), reads=[ract, rW], writes=[ryb])
                        sc.tt('dve', xs[:, t, half * 512:(half + 1) * 512], xs[:, t, half * 512:(half + 1) * 512], yb, ALU.add,
                              reads=[rx, ryb], writes=[rx])
                sc.dma(STQ, xdst[ci * 256:(ci + 1) * 256, :].rearrange("(t p) d -> p t d", p=128), xs, reads=[rx], writes=[r_out])
            sc.barrier()

        PH = {}
        exec_phases = phases
        prologue()
        for l in range(depth):
            xsrc = x_in if l == 0 else xmd
            xdst = out if l == depth - 1 else xmd
            if exec_phases is None or 'A' in exec_phases:
                phase_A(l, xsrc)
            if exec_phases is None or 'SB' in exec_phases:
                phase_SB(l)
            if exec_phases is None or 'MB' in exec_phases:
                phase_MB(l)
            if exec_phases is None or 'NSA' in exec_phases:
                phase_NSA(l)
            if inject:
                sc.dma('sp', oTd, oT_in, writes=[Res()])
                sc.barrier()
            if exec_phases is None or 'C1' in exec_phases:
                phase_C1(l, xsrc)
            if exec_phases is None or 'C2' in exec_phases:
                phase_C2(l, xdst)
        sc.barrier()
        sc.emit()
    return nc


def core_inputs(inputs, b, S, depth, cbh, cfh):
    f = lambda a: np.ascontiguousarray(np.asarray(a, dtype=np.float32))
    gains = np.stack([f(inputs['moba_q_norm']), f(inputs['moba_k_norm']), f(inputs['nsa_q_norm']),
                      f(inputs['nsa_k_norm'])[:, 0], f(inputs['nsa_k_norm'])[:, 1], f(inputs['nsa_k_norm'])[:, 2]], axis=1)
    m = {
        'x': f(inputs['x'][b, :S]),
        'rel_bias': f(inputs['rel_bias']),
        'attn_norm': f(inputs['attn_norm'])[:depth],
        'w_in': f(inputs['w_in'])[:depth],
        'gains': np.ascontiguousarray(gains[:depth]),
        'nsa_cmp_pos': f(inputs['nsa_cmp_pos'])[:depth],
        'nsa_cmp_w1': f(inputs['nsa_cmp_w1'])[:depth],
        'nsa_cmp_w2': f(inputs['nsa_cmp_w2'])[:depth],
        'w_branch': f(inputs['w_branch'])[:depth],
        'w_out': f(inputs['w_out'])[:depth],
        'ffn_norm': f(inputs['ffn_norm'])[:depth],
        'w_gate_up': f(inputs['w_gate_up'])[:depth],
        'w_down': f(inputs['w_down'])[:depth],
        'cb': cbh,
        'cf': cfh,
    }
    return m


def kernel(**inputs):
    x = np.asarray(inputs['x'])
    B, S, _ = x.shape
    depth = int(np.asarray(inputs['w_in']).shape[0])
    cbh, cfh, _, _ = host_consts(S)
    nc = build(S, depth)
    in_maps = [core_inputs(inputs, b, S, depth, cbh, cfh) for b in range(B)]
    res = run_bass_kernel_spmd(nc, in_maps, core_ids=list(range(B)))
    return np.stack([np.asarray(r['out'], dtype=np.float32) for r in res.results], axis=0)
```

```python
import math
from contextlib import ExitStack

import numpy as np
import ml_dtypes
import concourse.bass as bass
import concourse.mybir as mybir
from concourse.bass_types import AP
from concourse.bass_utils import run_bass_kernel_spmd

F32 = mybir.dt.float32
BF16 = mybir.dt.bfloat16
AF = mybir.ActivationFunctionType
ALU = mybir.AluOpType
AX = mybir.AxisListType
ENGS = ('pe', 'act', 'dve', 'pool', 'sp')
NDMA = 8
STQ = 'sp'

D = 1024
NIN = 5912
NQKV = 2840
DFF = 2816
EPS = 1e-6
NEG = -1e30
BIG = 1e30
TINY = 1e-30
GOFF = 2304
GL = 4864
FM_COLS = [0, 128, 256, 384, 768, 896, 1024, 1152, 1536, 1664, 1792, 1920, 2048, 2176, 2304, 2560]
FM_KIND = [None, None, None, None, 0, 0, 1, 1, 2, 2, 2, 2, None, None, 4, 5]


class Res:
    __slots__ = ('name', 'lw', 'rd')

    def __init__(self, name=''):
        self.name = name
        self.lw = None
        self.rd = {}


class Ring:
    def __init__(self, items):
        self.items = items
        self.i = 0

    def next(self):
        it = self.items[self.i]
        self.i = (self.i + 1) % len(self.items)
        return it


class Sched:
    def __init__(self, nc, stack):
        self.nc = nc
        self.q = {e: [] for e in ENGS}
        self.cnt = {e: 0 for e in ENGS}
        self.waited = {e: {} for e in ENGS}
        self.sem = {}
        for e in ENGS:
            self.sem[('e', e)] = stack.enter_context(nc.semaphore('s_' + e))
        self.dcnt = {}
        self.drr = {}
        for e in ('sp', 'act', 'pool'):
            self.drr[e] = 0
            for i in range(NDMA):
                k = ('d', e, i)
                self.sem[k] = stack.enter_context(nc.semaphore('d_%s%d' % (e, i)))
                self.dcnt[k] = 0

    def _waits(self, eng, reads, writes, extra=()):
        deps = {}

        def add(ev):
            if ev is None:
                return
            k, v = ev
            if deps.get(k, 0) < v:
                deps[k] = v
        for r in reads:
            add(r.lw)
        for w in writes:
            add(w.lw)
            for k, v in w.rd.items():
                add((k, v))
        for ev in extra:
            add(ev)
        out = []
        wd = self.waited[eng]
        for k, v in deps.items():
            if k == ('e', 'pe') and eng == 'pe':
                continue
            if wd.get(k, 0) >= v:
                continue
            wd[k] = v
            out.append((k, v))
        return out

    def _mark(self, ev, reads, writes):
        k, v = ev
        for r in reads:
            if r.rd.get(k, 0) < v:
                r.rd[k] = v
        for w in writes:
            w.lw = ev
            w.rd = {}

    def op(self, eng, fn, reads=(), writes=()):
        waits = self._waits(eng, reads, writes)
        self.cnt[eng] += 1
        ev = (('e', eng), self.cnt[eng])
        self.q[eng].append((fn, waits, ('e', eng), 1))
        self._mark(ev, reads, writes)
        return ev

    def dma(self, eng, out, in_, reads=(), writes=(), **kw):
        i = self.drr[eng]
        self.drr[eng] = (i + 1) % NDMA
        k = ('d', eng, i)
        prev = (k, 16 * self.dcnt[k]) if self.dcnt[k] else None
        waits = self._waits(eng, reads, writes, extra=(prev,) if prev else ())
        self.dcnt[k] += 1
        ev = (k, 16 * self.dcnt[k])
        self.q[eng].append((lambda e: e.dma_start(out=out, in_=in_, **kw), waits, k, 16))
        self._mark(ev, reads, writes)
        return ev

    def barrier(self):
        evs = [(('e', e), self.cnt[e]) for e in ENGS if self.cnt[e] > 0]
        evs += [(k, 16 * c) for k, c in self.dcnt.items() if c > 0]
        for e in ENGS:
            wd = self.waited[e]
            waits = []
            for k, v in evs:
                if k == ('e', e):
                    continue
                if wd.get(k, 0) >= v:
                    continue
                wd[k] = v
                waits.append((k, v))
            self.q[e].append((None, waits, None, 0))

    def emit(self):
        nc = self.nc
        with nc.Block() as block:
            def run(name):
                def f(e):
                    for fn, waits, k, inc in self.q[name]:
                        for wk, wv in waits:
                            e.wait_ge(self.sem[wk], wv)
                        if fn is not None:
                            fn(e).then_inc(self.sem[k], inc)
                return f
            block.tensor(run('pe'))
            block.scalar(run('act'))
            block.vector(run('dve'))
            block.gpsimd(run('pool'))
            block.sync(run('sp'))

    def mm(self, out, lhsT, rhs, start=True, stop=True, reads=(), writes=()):
        return self.op('pe', lambda e: e.matmul(out, lhsT, rhs, start=start, stop=stop), reads, writes)

    def tr(self, out, in_, ident, reads=(), writes=()):
        return self.op('pe', lambda e: e.transpose(out, in_, ident), reads, writes)

    def act(self, out, in_, func, reads=(), writes=(), **kw):
        return self.op('act', lambda e: e.activation(out, in_, func, **kw), reads, writes)

    def tt(self, eng, out, in0, in1, op, reads=(), writes=()):
        return self.op(eng, lambda e: e.tensor_tensor(out, in0, in1, op), reads, writes)

    def ts(self, eng, out, in0, s1, s2, op0, op1=None, reads=(), writes=()):
        if op1 is None:
            return self.op(eng, lambda e: e.tensor_scalar(out, in0, s1, s2, op0), reads, writes)
        return self.op(eng, lambda e: e.tensor_scalar(out, in0, s1, s2, op0, op1), reads, writes)

    def stt(self, eng, out, in0, scalar, in1, op0, op1, reads=(), writes=()):
        return self.op(eng, lambda e: e.scalar_tensor_tensor(out, in0, scalar, in1, op0, op1), reads, writes)

    def cp(self, eng, out, in_, reads=(), writes=()):
        if eng == 'act':
            return self.op('act', lambda e: e.copy(out, in_), reads, writes)
        return self.op(eng, lambda e: e.tensor_copy(out, in_), reads, writes)

    def memset(self, eng, ap, val, writes=()):
        return self.op(eng, lambda e: e.memset(ap, val), (), writes)


class Arena:
    def __init__(self, ap, size):
        self.ap = ap
        self.size = size
        self.top = 0

    def bf(self, n):
        a = self.top
        self.top += (n + 31) // 32 * 32
        assert self.top <= self.size, ('arena overflow', self.top, self.size)
        return self.ap[:, a:a + n]

    def f32(self, n):
        a = self.top
        self.top += (2 * n + 31) // 32 * 32
        assert self.top <= self.size, ('arena overflow', self.top, self.size)
        return self.ap[:, a:a + 2 * n].bitcast(F32)


def v3(ap, a):
    return ap.rearrange("p (a b) -> p a b", a=a)


LAG = 2


def pipe(n, *stages, lag=LAG):
    ns = len(stages)
    st = [None] * n
    for t in range(n + lag * (ns - 1)):
        for k, f in enumerate(stages):
            i = t - k * lag
            if 0 <= i < n:
                st[i] = f(i, st[i]) if k else f(i)


def t5_bucket_np(d):
    n = np.maximum(d, 0)
    nf = np.maximum(n, 1).astype(np.float32)
    large = 16 + (np.log(nf / np.float32(16)) / np.float32(math.log(128 / 16)) * np.float32(16)).astype(np.int32)
    large = np.minimum(large, 31)
    return np.where(n < 16, n, large)


def host_consts(S):
    k = np.arange(128)[:, None]
    q = np.arange(128)[None, :]
    ident = (k == q)
    J = (k + q == 127)
    uincl = (k >= q)
    ones = np.ones((128, 128), bool)
    blk = (k // 64 == q // 64)
    maskL = (k < q)
    m512 = (q < k)
    zeros = np.zeros((128, 512), bool)
    n_cmp = (S - 32) // 16 + 1
    nct = (n_cmp + 127) // 128
    n = np.arange(nct * 128)[:, None]
    m = np.arange(128)[None, :]
    ovl = ((16 * n < 64 * m + 64) & (16 * n + 32 > 64 * m) & (n < n_cmp))
    ovl = ovl.reshape(nct, 128, 128).transpose(1, 0, 2).reshape(128, nct * 128)
    mm_ = np.arange(128)[:, None]
    xx = np.arange(S)[None, :]
    B = (xx // 64 == mm_)
    cb = np.concatenate([ident, J, uincl, ones, blk, maskL, m512, zeros, ovl, B], axis=1).astype(np.float32)
    cb = cb.astype(ml_dtypes.bfloat16)
    negmask = np.where(k >= q, -1e4, 0.0).astype(np.float32)
    y = np.arange(256)[None, :]
    rel = y - 126 - (k >= 64)
    cs = np.where((rel == 0) | (rel == -1), BIG, np.where(rel > 0, NEG, 0.0)).astype(np.float32)
    oh = np.zeros((128, 128), np.float32)
    b = t5_bucket_np(np.arange(128))
    oh[b, np.arange(128)] = 1.0
    cf = np.concatenate([negmask, cs, oh], axis=1).astype(np.float32)
    return cb, cf, n_cmp, nct


def build(S, depth, debug=False, phases=None, dbg=None, inject=False):
    NT = S // 128
    NC = S // 512
    cbh, cfh, n_cmp, NCT = host_consts(S)
    NCB = cbh.shape[1]
    NCF = cfh.shape[1]
    nc = bass.Bass("TRN2", target_bir_lowering=False)
    dt_in = lambda name, shape, dt=F32: nc.dram_tensor(name, shape, dt, kind="ExternalInput").ap()
    dbg_kind = "ExternalOutput" if debug else "Internal"
    scr = lambda name, shape, dt: nc.dram_tensor(name, shape, dt, kind=dbg_kind).ap()
    x_in = dt_in("x", [S, D])
    rel_bias = dt_in("rel_bias", [32, 12])
    attn_norm = dt_in("attn_norm", [depth, D])
    w_in = dt_in("w_in", [depth, D, NIN])
    gains = dt_in("gains", [depth, 6, 64])
    cmp_pos = dt_in("nsa_cmp_pos", [depth, 2, 32, 64])
    cmp_w1 = dt_in("nsa_cmp_w1", [depth, 2, 2048, 256])
    cmp_w2 = dt_in("nsa_cmp_w2", [depth, 2, 256, 64])
    w_branch = dt_in("w_branch", [depth, D, D])
    w_out = dt_in("w_out", [depth, D, D])
    ffn_norm = dt_in("ffn_norm", [depth, D])
    w_gate_up = dt_in("w_gate_up", [depth, D, 2 * DFF])
    w_down = dt_in("w_down", [depth, DFF, D])
    cb_in = dt_in("cb", [128, NCB], BF16)
    cf_in = dt_in("cf", [128, NCF])
    out = nc.dram_tensor("out", [S, D], F32, kind="ExternalOutput").ap()

    hTd = scr("hTd", [D, S], BF16)
    fmd = scr("fmd", [2048, S], BF16)
    tmvd = scr("tmvd", [S, 832], BF16)
    nsgd = scr("nsgd", [S, 32], F32)
    oTd = scr("oTd", [D, S], BF16)
    x1d = scr("x1d", [S, D], F32)
    xmd = scr("xmd", [S, D], F32)
    gvd = scr("gvd", [12, GL], BF16)
    oT_in = dt_in("oT_in", [D, S], BF16) if inject else None

    with ExitStack() as st:
        sc = Sched(nc, st)
        ASZ = 98304
        arena_ap = nc.alloc_sbuf_tensor("arena", [128, ASZ], BF16).ap()
        ar = Arena(arena_ap, ASZ)
        banks = [nc.alloc_psum_tensor("bank%d" % i, [128, 512], F32).ap() for i in range(8)]
        bres = [Res("bank%d" % i) for i in range(8)]

        CB_GLOBAL = 8 * 128 + 384
        cb = ar.bf(CB_GLOBAL)
        ident = cb[:, 0:128]
        Jm = cb[:, 128:256]
        uincl = cb[:, 256:384]
        onesb = cb[:, 384:512]
        blkones = cb[:, 512:640]
        maskL = cb[:, 640:768]
        m512 = cb[:, 768:896]
        zerob = cb[:, 896:1408]
        cf = ar.f32(384)
        negmask = cf[:, 0:128]
        cslide = cf[:, 128:384]
        Etab = ar.bf(24 * 128)
        r_const = Res('const')
        sc.dma('sp', cb, cb_in[:, 0:CB_GLOBAL], writes=[r_const])
        sc.dma('sp', cf, cf_in[:, 0:384], writes=[r_const])
        base_top = ar.top

        def E0(h):
            return Etab[:, (2 * h) * 128:(2 * h + 1) * 128]

        def E128(h):
            return Etab[:, (2 * h + 1) * 128:(2 * h + 2) * 128]

        def prologue():
            ar.top = base_top
            tb = ar.f32(12)
            oh = ar.f32(128)
            br = ar.f32(128)
            negc = ar.f32(1)
            gsb = ar.bf(GL)
            hk = ar.bf(128)
            r = Res('pro')
            rb = bres[0]
            sc.dma('sp', tb[0:32, :], rel_bias, writes=[r])
            sc.dma('sp', oh[0:32, :], cf_in[0:32, 384:512], writes=[r])
            sc.mm(banks[0][0:12, 0:128], tb[0:32, 0:12], oh[0:32, 0:128], reads=[r], writes=[rb])
            sc.cp('dve', br[0:12, :], banks[0][0:12, 0:128], reads=[rb], writes=[r])
            sc.ts('dve', negc[0:12, :], br[0:12, 127:128], -1.0, None, ALU.mult, reads=[r], writes=[r])
            sc.memset('pool', gsb[0:12, 0:GOFF], 0.0, writes=[r])
            sc.memset('pool', gsb[0:12, GOFF + 128:GL], 1.0, writes=[r])
            sc.act(gsb[0:12, GOFF:GOFF + 128], br[0:12, :], AF.Exp, reads=[r], writes=[r], bias=negc[0:12, :])
            rg = Res('gvd')
            sc.dma('sp', gvd, gsb[0:12, :], reads=[r], writes=[rg])
            hkr = Res('hk')
            for h in range(12):
                for di, dl in enumerate((0, 128)):
                    src = AP(gvd.tensor, h * GL + GOFF + dl - 127, [[1, 128], [1, 128]])
                    sc.dma('sp', hk, src, reads=[rg], writes=[hkr])
                    sc.mm(banks[1][:, 0:128], Jm, hk, reads=[hkr, r_const], writes=[bres[1]])
                    sc.cp('dve', Etab[:, (2 * h + di) * 128:(2 * h + di + 1) * 128], banks[1][:, 0:128],
                          reads=[bres[1]], writes=[r_const])
            sc.barrier()

        def load_cast(dst, src, n, stage_ring, rdst):
            o = 0
            while o < n:
                w = min(2048, n - o)
                sg, rs = stage_ring.next()
                sc.dma('sp', sg[:, 0:w], src[:, o:o + w], writes=[rs])
                sc.cp('pool', dst[:, o:o + w], sg[:, 0:w], reads=[rs], writes=[rdst])
                o += w

        def rmsnorm_tm(xs, rx, gB, rg, hb, rh, ntile, sq, ssq, rstd, rtmp):
            for t in range(ntile):
                sc.act(sq, xs[:, t, :], AF.Square, reads=[rx], writes=[rtmp])
                sc.op('dve', lambda e, t=t: e.tensor_reduce(ssq[:, t:t + 1], sq, AX.X, ALU.add), reads=[rtmp], writes=[rtmp])
            sc.act(rstd[:, 0:ntile], ssq[:, 0:ntile], AF.Ln, reads=[rtmp], writes=[rtmp], scale=1.0 / D, bias=EPS)
            sc.act(rstd[:, 0:ntile], rstd[:, 0:ntile], AF.Exp, reads=[rtmp], writes=[rtmp], scale=-0.5)
            for t in range(ntile):
                sc.stt('dve', hb[:, t, :], xs[:, t, :], rstd[:, t:t + 1], gB, ALU.mult, ALU.mult,
                       reads=[rx, rtmp, rg], writes=[rh])

        def transpose_to(hb, rh, hT, rhT, ntile, bank_ring, evac_engs):
            for c in range(8):
                bk, rb = bank_ring.next()
                pb = bk.bitcast(BF16)
                for t in range(ntile):
                    sc.tr(pb[:, t * 128:(t + 1) * 128], hb[:, t, c * 128:(c + 1) * 128], ident,
                          reads=[rh, r_const], writes=[rb])
                sc.cp(evac_engs[c % len(evac_engs)], hT[:, c, :], pb[:, 0:ntile * 128], reads=[rb], writes=[rhT])

        def phase_A(l, xsrc):
            ar.top = base_top
            W = v3(ar.bf(8 * NQKV), 8)
            gA = ar.f32(D)
            gcol = ar.f32(8)
            rW = Res('W')
            mark = ar.top
            stg = [(ar.f32(2048), Res('stg%d' % i)) for i in range(2)]
            sring = Ring(stg)
            for c in range(8):
                load_cast(W[:, c, :], w_in[l, c * 128:(c + 1) * 128, 0:NQKV], NQKV, sring, rW)
            sc.dma('sp', gA, AP(attn_norm.tensor, l * D, [[0, 128], [1, D]]), writes=[rW])
            for gi in range(6):
                for half in range(2):
                    sc.dma('sp', gcol[half * 64:(half + 1) * 64, gi:gi + 1],
                           AP(gains.tensor, (l * 6 + gi) * 64, [[1, 64], [1, 1]]), writes=[rW])
            sc.barrier()
            if dbg == 'W':
                return
            ar.top = mark
            xs_r = Ring([(v3(ar.f32(4 * D), 4), Res('xs%d' % i)) for i in range(2)])
            hb = v3(ar.bf(4 * D), 4)
            rhb = Res('hb')
            sq = ar.f32(D)
            ssq = ar.f32(4)
            rstd = ar.f32(4)
            rtmp = Res('tmp')
            hT_r = Ring([(v3(ar.bf(8 * 512), 8), Res('hT%d' % i)) for i in range(2)])
            sqb_r = Ring([(ar.bf(512), Res('sqb%d' % i)) for i in range(2)])
            rs_r = Ring([(ar.f32(512), Res('rs%d' % i)) for i in range(2)])
            fst_r = Ring([(ar.bf(512), Res('fst%d' % i)) for i in range(3)])
            tst_r = Ring([(ar.bf(832), Res('tst%d' % i)) for i in range(2)])
            gst_r = Ring([(ar.f32(24), Res('gst%d' % i)) for i in range(2)])
            tb_r = Ring([(banks[i], bres[i]) for i in (0, 1)])
            pb_r = Ring([(banks[i], bres[i]) for i in (2, 3, 4)])
            nb_r = Ring([(banks[i], bres[i]) for i in (5,)])
            vb_r = Ring([(banks[i], bres[i]) for i in (6, 7)])
            r_hTd, r_fmd, r_tmvd, r_nsgd = Res(), Res(), Res(), Res()
            for ci in range(NC):
                xs, rx = xs_r.next()
                sc.dma('sp', xs, xsrc[ci * 512:(ci + 1) * 512, :].rearrange("(t p) d -> p t d", p=128), writes=[rx])
                if dbg == 'ld':
                    continue
                rmsnorm_tm(xs, rx, gA, rW, hb, rhb, 4, sq, ssq, rstd, rtmp)
                if dbg == 'norm':
                    continue
                hT, rhT = hT_r.next()
                transpose_to(hb, rhb, hT, rhT, 4, tb_r, ('dve', 'act'))
                if dbg == 'tr':
                    continue
                sc.dma(STQ, hTd.rearrange("(c p) s -> p c s", p=128)[:, :, ci * 512:(ci + 1) * 512], hT,
                       reads=[rhT], writes=[r_hTd])
                for j in range(16):
                    col = FM_COLS[j]
                    bk, rb = pb_r.next()
                    for c in range(8):
                        sc.mm(bk, W[:, c, col:col + 128], hT[:, c, :], start=(c == 0), stop=(c == 7),
                              reads=[rW, rhT], writes=[rb])
                    fs, rf = fst_r.next()
                    if FM_KIND[j] is None:
                        sc.cp('act' if j % 2 else 'dve', fs, bk, reads=[rb], writes=[rf])
                    else:
                        gi = FM_KIND[j]
                        sqb, rsq = sqb_r.next()
                        sc.act(sqb, bk, AF.Square, reads=[rb], writes=[rsq])
                        b2, rb2 = nb_r.next()
                        sc.mm(b2, blkones, sqb, reads=[rsq, r_const], writes=[rb2])
                        rs, rrs = rs_r.next()
                        sc.act(rs, b2, AF.Ln, reads=[rb2], writes=[rrs], scale=1.0 / 64, bias=EPS)
                        sc.act(rs, rs, AF.Exp, reads=[rrs], writes=[rrs], scale=-0.5)
                        sc.stt('dve', fs, bk, gcol[:, gi:gi + 1], rs, ALU.mult, ALU.mult, reads=[rb, rrs, rW], writes=[rf])
                    sc.dma(STQ, fmd[j * 128:(j + 1) * 128, ci * 512:(ci + 1) * 512], fs, reads=[rf], writes=[r_fmd])
                if dbg == 'fm':
                    continue
                for t in range(4):
                    b1, rb1 = vb_r.next()
                    b2, rb2 = vb_r.next()
                    lt = lambda c: hT[:, c, t * 128:(t + 1) * 128]
                    for (bk, rb, o, c0, w) in ((b1, rb1, 0, 512, 256), (b1, rb1, 256, 1280, 256),
                                               (b2, rb2, 0, 2432, 128), (b2, rb2, 128, 2688, 152)):
                        for c in range(8):
                            sc.mm(bk[:, o:o + w], lt(c), W[:, c, c0:c0 + w], start=(c == 0), stop=(c == 7),
                                  reads=[rW, rhT], writes=[rb])
                    ts_, rts = tst_r.next()
                    sc.cp('act', ts_[:, 0:512], b1, reads=[rb1], writes=[rts])
                    sc.cp('dve', ts_[:, 512:768], b2[:, 0:256], reads=[rb2], writes=[rts])
                    sc.act(ts_[:, 768:816].bitcast(F32), b2[:, 256:280], AF.Sigmoid, reads=[rb2], writes=[rts])
                    r0 = (ci * 4 + t) * 128
                    sc.dma(STQ, tmvd[r0:r0 + 128, 0:816], ts_[:, 0:816], reads=[rts], writes=[r_tmvd])
            sc.barrier()

        def phase_SB(l):
            ar.top = base_top
            kT = ar.bf(S)
            qT = ar.bf(S)
            v = v3(ar.bf(NT * 64), NT)
            rk = Res('kqv')
            R_r = Ring([(ar.bf(512), Res()) for i in range(LAG + 2)])
            zc_r = Ring([(ar.f32(512), Res()) for i in range(LAG + 2)])
            u_r = Ring([(ar.f32(512), Res()) for i in range(2)])
            sp_r = Ring([(ar.bf(512), Res()) for i in range(LAG + 2)])
            w_r = Ring([(ar.f32(512), Res()) for i in range(2)])
            a_r = Ring([(ar.bf(512), Res()) for i in range(LAG + 2)])
            os_r = Ring([(ar.bf(512), Res()) for i in range(2)])
            zb_r = Ring([(banks[i], bres[i]) for i in (0, 1, 6)])
            tb_r = Ring([(banks[i], bres[i]) for i in (2, 3, 7)])
            ob_r = Ring([(banks[i], bres[i]) for i in (4, 5)])
            r_oTd = Res()
            for h in range(4):
                sc.dma('sp', kT[0:64, :], fmd[256 + 64 * h:256 + 64 * h + 64, :], writes=[rk])
                sc.dma('sp', qT[0:64, :], fmd[64 * h:64 * h + 64, :], writes=[rk])
                sc.dma('sp', v, tmvd[:, 64 * h:64 * h + 64].rearrange("(t p) d -> p t d", p=128), writes=[rk])
                for c in range(NC):
                    ob, rob = ob_r.next()
                    sc.mm(ob[0:64, :], zerob[:, 0:64], zerob, start=True, stop=False, reads=[r_const], writes=[rob])
                    last = 4 * c + 3
                    n_t = last + 1
                    Rcur = [R_r.next()]
                    sc.memset('pool', Rcur[0][0], 0.0, writes=[Rcur[0][1]])

                    def s1(i, c=c, last=last, n_t=n_t, Rcur=Rcur):
                        kt = last - i
                        rel = kt - 4 * c
                        c0 = 128 * max(rel, 0)
                        zb, rzb = zb_r.next()
                        sc.mm(zb[:, c0:512], kT[0:64, kt * 128:(kt + 1) * 128], qT[0:64, c * 512 + c0:(c + 1) * 512],
                              reads=[rk], writes=[rzb])
                        zc, rzc = zc_r.next()
                        sc.ts('dve', zc[:, c0:512], zb[:, c0:512], 0.125, 40.0, ALU.mult, ALU.min, reads=[rzb], writes=[rzc])
                        u, ru = u_r.next()
                        sc.act(u[:, c0:512], zc[:, c0:512], AF.Exp, reads=[rzc], writes=[ru])
                        sp, rsp = sp_r.next()
                        sc.act(sp[:, c0:512], u[:, c0:512], AF.Ln, reads=[ru], writes=[rsp], bias=1.0)
                        if rel >= 0:
                            sc.tt('pool', sp[:, c0:c0 + 128], sp[:, c0:c0 + 128], maskL, ALU.mult,
                                  reads=[rsp, r_const], writes=[rsp])
                        if c0 > 0:
                            sc.memset('pool', sp[:, 0:c0], 0.0, writes=[rsp])
                        Ri, rRi = Rcur[0]
                        if i < n_t - 1:
                            Rn, rRn = R_r.next()
                            sc.tt('pool', Rn, Ri, sp, ALU.add, reads=[rRi, rsp], writes=[rRn])
                            Rcur[0] = (Rn, rRn)
                        return (kt, rel, c0, zc, rzc, sp, rsp, Ri, rRi)

                    def s2(i, stt_, ob=ob, rob=rob):
                        kt, rel, c0, zc, rzc, sp, rsp, Ri, rRi = stt_
                        tb, rtb = tb_r.next()
                        sc.mm(tb[:, c0:512], uincl, sp[:, c0:512], start=True, stop=(i == 0),
                              reads=[rsp, r_const], writes=[rtb])
                        if i > 0:
                            sc.mm(tb[:, c0:512], onesb, Ri[:, c0:512], start=False, stop=True,
                                  reads=[rRi, r_const], writes=[rtb])
                        w, rw = w_r.next()
                        sc.tt('dve', w[:, c0:512], zc[:, c0:512], tb[:, c0:512], ALU.subtract, reads=[rzc, rtb], writes=[rw])
                        if rel >= 0:
                            sc.tt('pool', w[:, c0:c0 + 128], w[:, c0:c0 + 128], negmask, ALU.add,
                                  reads=[rw, r_const], writes=[rw])
                        a, ra = a_r.next()
                        sc.act(a[:, c0:512], w[:, c0:512], AF.Exp, reads=[rw], writes=[ra])
                        return (kt, c0, a, ra)

                    def s3(i, stt_, ob=ob, rob=rob):
                        kt, c0, a, ra = stt_
                        sc.mm(ob[0:64, c0:512], v[:, kt, :], a[:, c0:512], start=False, stop=(kt == 0),
                              reads=[rk, ra], writes=[rob])

                    pipe(n_t, s1, s2, s3)
                    os_, ros = os_r.next()
                    sc.cp('act', os_[0:64, :], ob[0:64, :], reads=[rob], writes=[ros])
                    sc.dma('sp', oTd[64 * h:64 * h + 64, c * 512:(c + 1) * 512], os_[0:64, :], reads=[ros], writes=[r_oTd])
            sc.barrier()


        def phase_MB(l):
            ar.top = base_top
            NBLK = S // 256
            kT = ar.bf(S)
            qT = ar.bf(S)
            va = v3(ar.bf(NT * 65), NT)
            rk = Res('kqv')
            km = ar.f32(32)
            rkm = Res('km')
            qf_r = Ring([(ar.f32(512), Res()) for i in range(2)])
            scv = v3(ar.f32(4 * 32), 4)
            m8 = ar.f32(32)
            selw = v3(ar.f32(4 * 32), 4)
            rsel = Res('sel')
            acc = v3(ar.f32(4 * 65), 4)
            racc = [Res('acc%d' % j) for j in range(4)]
            rl = v3(ar.f32(4), 4)
            rrl = Res('rl')
            o_tok = v3(ar.bf(NT * 256), NT)
            rot = Res('otok')
            p_r = Ring([(ar.bf(512), Res()) for i in range(LAG + 2)])
            st_r = Ring([(ar.bf(512), Res()) for i in range(2)])
            lb_r = Ring([(banks[i], bres[i]) for i in (0, 1, 7)])
            ob_r = Ring([(banks[i], bres[i]) for i in (2, 3)])
            sb_r = Ring([(banks[i], bres[i]) for i in (4,)])
            tb_r = Ring([(banks[i], bres[i]) for i in (5, 6)])
            r_oTd = Res()
            sc.memset('pool', va[:, :, 64:65], 1.0, writes=[rk])
            for h in range(4):
                sc.dma('sp', kT[0:64, :], fmd[768 + 64 * h:768 + 64 * h + 64, :], writes=[rk])
                sc.dma('sp', qT[0:64, :], fmd[512 + 64 * h:512 + 64 * h + 64, :], writes=[rk])
                sc.dma('sp', va[:, :, 0:64], tmvd[:, 256 + 64 * h:256 + 64 * h + 64].rearrange("(t p) d -> p t d", p=128), writes=[rk])
                sc.op('dve', lambda e: e.tensor_reduce(km[0:64, 0:NBLK], kT[0:64, :].rearrange("p (n k) -> p n k", k=256), AX.X, ALU.add),
                      reads=[rk], writes=[rkm])
                sc.ts('dve', km[0:64, 0:NBLK], km[0:64, 0:NBLK], 1.0 / 256, None, ALU.mult, reads=[rkm], writes=[rkm])
                for c in range(NC):
                    qf, rqf = qf_r.next()
                    sc.cp('dve', qf[0:64, :], qT[0:64, c * 512:(c + 1) * 512], reads=[rk], writes=[rqf])
                    sc.memset('pool', scv, NEG, writes=[rsel])
                    sb, rsb = sb_r.next()
                    for j in range(4):
                        cur = (4 * c + j) // 2
                        if cur > 0:
                            sc.mm(sb[:, j * 32:j * 32 + cur], qf[0:64, j * 128:(j + 1) * 128], km[0:64, 0:cur],
                                  reads=[rqf, rkm], writes=[rsb])
                    for j in range(4):
                        cur = (4 * c + j) // 2
                        if cur > 0:
                            sc.cp('dve', scv[:, j, 0:cur], sb[:, j * 32:j * 32 + cur], reads=[rsb], writes=[rsel])
                    for j in range(4):
                        sc.op('dve', lambda e, j=j: e.max(out=m8[:, j * 8:(j + 1) * 8], in_=scv[:, j, :]), reads=[rsel], writes=[rsel])
                        sc.ts('dve', selw[:, j, :], scv[:, j, :], m8[:, j * 8 + 2:j * 8 + 3], None, ALU.is_ge, reads=[rsel], writes=[rsel])
                    sc.memset('pool', acc, 0.0, writes=racc)
                    obs = [None]

                    def s1(kt, c=c, h=h):
                        rel = kt - 4 * c
                        j0 = max(rel, 0)
                        c0 = 128 * j0
                        lb, rlb = lb_r.next()
                        sc.mm(lb[:, c0:512], kT[0:64, kt * 128:(kt + 1) * 128], qT[0:64, c * 512 + c0:(c + 1) * 512],
                              reads=[rk], writes=[rlb])
                        p, rp = p_r.next()
                        sc.act(p[:, c0:512], lb[:, c0:512], AF.Exp, reads=[rlb], writes=[rp], scale=0.125)
                        for j in range(j0, 4):
                            d = 4 * c + j - kt
                            if d == 0:
                                sc.tt('pool', p[:, j * 128:(j + 1) * 128], p[:, j * 128:(j + 1) * 128], E0(h), ALU.mult,
                                      reads=[rp, r_const], writes=[rp])
                            elif d == 1:
                                sc.tt('pool', p[:, j * 128:(j + 1) * 128], p[:, j * 128:(j + 1) * 128], E128(h), ALU.mult,
                                      reads=[rp, r_const], writes=[rp])
                        return (p, rp, j0)

                    def s2(kt, stt_, c=c, obs=obs):
                        p, rp, j0 = stt_
                        n = kt // 2
                        if kt % 2 == 0:
                            obs[0] = ob_r.next()
                        ob, rob = obs[0]
                        done = []
                        for j in range(j0, 4):
                            qt = 4 * c + j
                            stop = (kt % 2 == 1) or (kt == qt)
                            sc.mm(ob[:, j * 128:j * 128 + 65], p[:, j * 128:(j + 1) * 128], va[:, kt, :],
                                  start=(kt % 2 == 0 and j == j0), stop=stop, reads=[rp, rk], writes=[rob])
                            if stop:
                                done.append(j)
                        for j in done:
                            qt = 4 * c + j
                            if n == qt // 2:
                                sc.tt('dve', acc[:, j, :], ob[:, j * 128:j * 128 + 65], acc[:, j, :], ALU.add,
                                      reads=[rob, racc[j]], writes=[racc[j]])
                            else:
                                sc.stt('dve', acc[:, j, :], ob[:, j * 128:j * 128 + 65], selw[:, j, n:n + 1], acc[:, j, :],
                                       ALU.mult, ALU.add, reads=[rob, racc[j], rsel], writes=[racc[j]])

                    pipe(4 * c + 4, s1, s2)
                    sc.ts('dve', rl, acc[:, :, 64:65], TINY, None, ALU.max, reads=racc, writes=[rrl])
                    sc.op('dve', lambda e: e.reciprocal(rl, rl), reads=[rrl], writes=[rrl])
                    sc.tt('dve', o_tok[:, 4 * c:4 * c + 4, h * 64:(h + 1) * 64], acc[:, :, 0:64], rl.to_broadcast([128, 4, 64]), ALU.mult,
                          reads=racc + [rrl], writes=[rot])
            for c in range(NC):
                for fc in range(2):
                    tb, rtb = tb_r.next()
                    pb = tb.bitcast(BF16)
                    for j in range(4):
                        sc.tr(pb[:, j * 128:(j + 1) * 128], o_tok[:, 4 * c + j, fc * 128:(fc + 1) * 128], ident,
                              reads=[rot, r_const], writes=[rtb])
                    stg, rst = st_r.next()
                    sc.cp('act' if fc else 'dve', stg, pb[:, 0:512], reads=[rtb], writes=[rst])
                    sc.dma(STQ, oTd[256 + fc * 128:256 + (fc + 1) * 128, c * 512:(c + 1) * 512], stg, reads=[rst], writes=[r_oTd])
            sc.barrier()


        def phase_NSA(l):
            ar.top = base_top
            NCW = NCT * 128
            kcT2 = [ar.bf(NCW) for g in range(2)]
            vca = [v3(ar.bf(NCT * 65), NCT) for g in range(2)]
            gcol = ar.f32(8)
            r_cmp = Res('cmp')
            mark_p = ar.top
            w1sb = v3(ar.bf(32 * 256), 32)
            w2sb = v3(ar.bf(2 * 64), 2)
            TT = ar.bf(S)
            posf = ar.f32(64)
            posb = ar.bf(64)
            posT = ar.bf(32)
            pbias = ar.f32(2)
            xb = ar.f32(512)
            x2 = ar.f32(512)
            th = ar.f32(512)
            ghT = v3(ar.bf(2 * 512), 2)
            sqb = ar.bf(512)
            rs = ar.f32(512)
            sring = Ring([(ar.f32(2048), Res()) for i in range(2)])
            rw, rt, rx, rgh = Res('w'), Res('TT'), Res('x'), Res('gh')
            n = n_cmp
            for gi in range(6):
                for half in range(2):
                    sc.dma('sp', gcol[half * 64:(half + 1) * 64, gi:gi + 1],
                           AP(gains.tensor, (l * 6 + gi) * 64, [[1, 64], [1, 1]]), writes=[r_cmp])
            for g in range(2):
                sc.memset('pool', kcT2[g], 0.0, writes=[r_cmp])
                sc.memset('pool', vca[g], 0.0, writes=[r_cmp])
            for kv in range(2):
                w1v = cmp_w1[l, kv].rearrange("(i d) h -> d i h", d=64)
                for i0 in range(0, 32, 8):
                    sg, rsg = sring.next()
                    sc.dma('sp', v3(sg, 8)[0:64], w1v[:, i0:i0 + 8, :], writes=[rsg])
                    sc.cp('pool', w1sb[0:64, i0:i0 + 8, :], v3(sg, 8)[0:64], reads=[rsg], writes=[rw])
                sg, rsg = sring.next()
                sc.dma('sp', v3(sg[:, 0:128], 2), cmp_w2[l, kv].rearrange("(c p) d -> p c d", p=128), writes=[rsg])
                sc.cp('pool', w2sb, v3(sg[:, 0:128], 2), reads=[rsg], writes=[rw])
                sc.dma('sp', posf[0:32, :], cmp_pos[l, kv], writes=[rw])
                sc.cp('dve', posb[0:32, :], posf[0:32, :], reads=[rw], writes=[rw])
                pbk = banks[5].bitcast(BF16)
                sc.tr(pbk[0:64, 0:32], posb[0:32, 0:64], ident[0:32, 0:32], reads=[rw, r_const], writes=[bres[5]])
                sc.cp('dve', posT[0:64, :], pbk[0:64, 0:32], reads=[bres[5]], writes=[rw])
                for hc in range(2):
                    for i in range(32):
                        sc.mm(banks[6][:, hc:hc + 1], w1sb[0:64, i, hc * 128:(hc + 1) * 128], posT[0:64, i:i + 1],
                              start=(i == 0 and hc == 0), stop=(i == 31), reads=[rw], writes=[bres[6]])
                sc.cp('dve', pbias, banks[6][:, 0:2], reads=[bres[6]], writes=[rw])
                for g in range(2):
                    base = (1536 if kv == 0 else 1664) + 64 * g
                    sc.dma('sp', TT[0:64, :], fmd[base:base + 64, :], writes=[rt])
                    for hc in range(2):
                        bk, rb = banks[hc], bres[hc]
                        for i in range(32):
                            sc.mm(bk[:, 0:n], w1sb[0:64, i, hc * 128:(hc + 1) * 128], TT[0:64, i:i + 16 * (n - 1) + 1:16],
                                  start=(i == 0), stop=(i == 31), reads=[rw, rt], writes=[rb])
                        sc.ts('dve', xb[:, 0:n], bk[:, 0:n], pbias[:, hc:hc + 1], None, ALU.add, reads=[rb, rw], writes=[rx])
                        sc.tt('dve', x2[:, 0:n], xb[:, 0:n], xb[:, 0:n], ALU.mult, reads=[rx], writes=[rx])
                        sc.ts('dve', x2[:, 0:n], x2[:, 0:n], 0.044715, 1.0, ALU.mult, ALU.add, reads=[rx], writes=[rx])
                        sc.tt('dve', x2[:, 0:n], x2[:, 0:n], xb[:, 0:n], ALU.mult, reads=[rx], writes=[rx])
                        sc.act(th[:, 0:n], x2[:, 0:n], AF.Tanh, reads=[rx], writes=[rx], scale=0.7978845608028654)
                        sc.ts('dve', xb[:, 0:n], xb[:, 0:n], 0.5, None, ALU.mult, reads=[rx], writes=[rx])
                        sc.stt('dve', ghT[:, hc, 0:n], th[:, 0:n], 1.0, xb[:, 0:n], ALU.add, ALU.mult, reads=[rx], writes=[rgh])
                    if kv == 0:
                        for hc in range(2):
                            sc.mm(banks[2][0:64, 0:n], w2sb[:, hc, :], ghT[:, hc, 0:n], start=(hc == 0), stop=(hc == 1),
                                  reads=[rw, rgh], writes=[bres[2]])
                        sc.act(sqb[0:64, 0:n], banks[2][0:64, 0:n], AF.Square, reads=[bres[2]], writes=[rx])
                        sc.mm(banks[3][0:64, 0:n], onesb[0:64, 0:64], sqb[0:64, 0:n], reads=[rx, r_const], writes=[bres[3]])
                        sc.act(rs[0:64, 0:n], banks[3][0:64, 0:n], AF.Ln, reads=[bres[3]], writes=[rx], scale=1.0 / 64, bias=EPS)
                        sc.act(rs[0:64, 0:n], rs[0:64, 0:n], AF.Exp, reads=[rx], writes=[rx], scale=-0.5)
                        sc.stt('dve', kcT2[g][0:64, 0:n], banks[2][0:64, 0:n], gcol[0:64, 3:4], rs[0:64, 0:n], ALU.mult, ALU.mult,
                               reads=[bres[2], rx, r_cmp], writes=[r_cmp])
                        sc.dma('sp', kcT2[g][64:128, :], kcT2[g][0:64, :], reads=[r_cmp], writes=[r_cmp])
                    else:
                        for nt in range(NCT):
                            rows = min(128, n - nt * 128)
                            for hc in range(2):
                                sc.mm(banks[2][0:rows, 0:64], ghT[:, hc, nt * 128:nt * 128 + rows], w2sb[:, hc, :],
                                      start=(hc == 0), stop=(hc == 1), reads=[rw, rgh], writes=[bres[2]])
                            sc.cp('dve', vca[g][0:rows, nt, 0:64], banks[2][0:rows, 0:64], reads=[bres[2]], writes=[r_cmp])
                            sc.memset('pool', vca[g][0:rows, nt, 64:65], 1.0, writes=[r_cmp])
            sc.barrier()
            for g in range(2):
                ar.top = mark_p
                qT = v3(ar.bf(2 * S), 2)
                ksT2 = ar.bf(S)
                kwT2 = ar.bf(S)
                vsa = v3(ar.bf(NT * 65), NT)
                vwa = v3(ar.bf(NT * 65), NT)
                G = [ar.bf(2560) for hh in range(4)]
                Bm = ar.bf(S)
                ovl = v3(ar.bf(NCW), NCT)
                rk = Res('kqv')
                hk_r = Ring([(ar.bf(512), Res()) for i in range(2)])
                gtb = ar.bf(4 * 48)
                gt3 = v3(gtb.bitcast(F32), 4)
                rgt = Res('gt')
                impacc = v3(ar.f32(512), 4)
                rimp = Res('imp')
                itmp = v3(ar.f32(512), 4)
                ritmp = Res()
                score = v3(ar.f32(512), 4)
                wk = v3(ar.f32(512), 4)
                m8a = ar.f32(32)
                m8b = ar.f32(32)
                selb = v3(ar.bf(512), 4)
                selT = ar.bf(512)
                rsel = Res('sel')
                oc = ar.f32(4 * 4 * 64)
                roc = Res('oc')
                rlc = ar.f32(16)
                rls = v3(ar.f32(4), 4)
                rlw = v3(ar.f32(4), 4)
                cfs = v3(ar.f32(4), 4)
                cfw = v3(ar.f32(4), 4)
                rcoef = Res('coef')
                ot1 = v3(ar.f32(256), 4)
                ot2 = v3(ar.f32(256), 4)
                ot3 = v3(ar.f32(256), 4)
                rot1, rot2, rot3 = Res(), Res(), Res()
                o_tok = v3(ar.bf(4 * 256), 4)
                rotk = Res('otok')
                p_r = Ring([(ar.bf(512), Res()) for i in range(LAG + 2)])
                st_r = Ring([(ar.bf(512), Res()) for i in range(2)])
                lb_r = Ring([(banks[i], bres[i]) for i in (0, 1)])
                mb_r = Ring([(banks[i], bres[i]) for i in (2, 5)])
                a1_r = Ring([(banks[i], bres[i]) for i in (3, 6)])
                a2_r = Ring([(banks[i], bres[i]) for i in (4, 7)])
                tb_r = Ring([(banks[i], bres[i]) for i in (5,)])
                r_oTd = Res()
                for hh in range(4):
                    half, a = hh // 2, hh % 2
                    r0 = 1024 + 64 * (4 * g + hh)
                    sc.dma('sp', qT[half * 64:(half + 1) * 64, a, :], fmd[r0:r0 + 64, :], writes=[rk])
                for half in range(2):
                    sc.dma('sp', ksT2[half * 64:(half + 1) * 64, :], fmd[1792 + 64 * g:1792 + 64 * g + 64, :], writes=[rk])
                    sc.dma('sp', kwT2[half * 64:(half + 1) * 64, :], fmd[1920 + 64 * g:1920 + 64 * g + 64, :], writes=[rk])
                sc.dma('sp', vsa[:, :, 0:64], tmvd[:, 512 + 64 * g:512 + 64 * g + 64].rearrange("(t p) d -> p t d", p=128), writes=[rk])
                sc.dma('sp', vwa[:, :, 0:64], tmvd[:, 640 + 64 * g:640 + 64 * g + 64].rearrange("(t p) d -> p t d", p=128), writes=[rk])
                sc.memset('pool', vsa[:, :, 64:65], 1.0, writes=[rk])
                sc.memset('pool', vwa[:, :, 64:65], 1.0, writes=[rk])
                sc.dma('sp', ovl, v3(cb_in[:, CB_GLOBAL:CB_GLOBAL + NCW], NCT), writes=[rk])
                sc.dma('sp', Bm, cb_in[:, CB_GLOBAL + NCW:CB_GLOBAL + NCW + S], writes=[rk])
                for hh in range(4):
                    hrow = 4 + 4 * g + hh
                    for idx in range(5):
                        hk, rhk = hk_r.next()
                        src = AP(gvd.tensor, hrow * GL + GOFF - 2063 + 512 * idx, [[16, 128], [1, 512]])
                        sc.dma('sp', hk, src, writes=[rhk])
                        tb, rtb = tb_r.next()
                        sc.mm(tb, Jm, hk, reads=[rhk, r_const], writes=[rtb])
                        sc.cp('dve', G[hh][:, idx * 512:(idx + 1) * 512], tb, reads=[rtb], writes=[rk])
                for c in range(NC):
                    sc.dma('sp', v3(gtb, 4), tmvd[c * 512:(c + 1) * 512, 768:816].rearrange("(t p) w -> p t w", p=128), writes=[rgt])
                    sc.memset('pool', impacc, 0.0, writes=[rimp])
                    nts = [nt for nt in range(NCT) if c - 4 * nt >= 0]
                    oc4 = oc.rearrange("p (h j d) -> p h j d", h=4, j=4)
                    rlc4 = rlc.rearrange("p (h j o) -> p h j o", h=4, o=1)
                    for hh in range(4):
                        half, a = hh // 2, hh % 2
                        hs = slice(half * 64, half * 64 + 64)
                        ocb, rocb = a1_r.next()
                        ib, rib = a2_r.next()
                        def s1(ii, c=c, hh=hh, hs=hs, a=a, nts=nts):
                            nt = nts[ii]
                            lb, rlb = lb_r.next()
                            sc.mm(lb, kcT2[g][hs, nt * 128:(nt + 1) * 128], qT[hs, a, c * 512:(c + 1) * 512], reads=[r_cmp, rk], writes=[rlb])
                            pc, rpc = p_r.next()
                            sc.act(pc, lb, AF.Exp, reads=[rlb], writes=[rpc], scale=0.125)
                            idx = c - 4 * nt
                            if idx < 5:
                                sc.tt('pool', pc, pc, G[hh][:, idx * 512:(idx + 1) * 512], ALU.mult, reads=[rpc, rk], writes=[rpc])
                            return (pc, rpc)

                        def s2(ii, stt_, nts=nts, ocb=ocb, rocb=rocb, ib=ib, rib=rib):
                            pc, rpc = stt_
                            nt = nts[ii]
                            for j in range(4):
                                sc.mm(ocb[:, j * 128:j * 128 + 65], pc[:, j * 128:(j + 1) * 128], vca[g][:, nt, :],
                                      start=(ii == 0 and j == 0), stop=(nt == nts[-1]), reads=[rpc, r_cmp], writes=[rocb])
                            for j in range(4):
                                sc.mm(ib[:, j * 128:(j + 1) * 128], pc[:, j * 128:(j + 1) * 128], ovl[:, nt, :],
                                      start=(ii == 0 and j == 0), stop=(nt == nts[-1]), reads=[rpc, rk], writes=[rib])

                        pipe(len(nts), s1, s2)
                        ocb3 = v3(ocb, 4)
                        sc.ts('dve', rlc4[:, hh], ocb3[:, :, 64:65], TINY, None, ALU.max, reads=[rocb], writes=[roc])
                        sc.op('dve', lambda e, hh=hh: e.reciprocal(rlc4[:, hh], rlc4[:, hh]), reads=[roc], writes=[roc])
                        sc.tt('dve', oc4[:, hh], ocb3[:, :, 0:64], rlc4[:, hh].to_broadcast([128, 4, 64]), ALU.mult,
                              reads=[rocb, roc], writes=[roc])
                        sc.tt('dve', itmp, v3(ib, 4), rlc4[:, hh].to_broadcast([128, 4, 128]), ALU.mult, reads=[rib, roc], writes=[ritmp])
                        sc.tt('pool', impacc, impacc, itmp, ALU.add, reads=[rimp, ritmp], writes=[rimp])
                    for j in range(4):
                        off = 126 - 2 * (4 * c + j)
                        sc.tt('dve', score[:, j, :], impacc[:, j, :], cslide[:, off:off + 128], ALU.add, reads=[rimp, r_const], writes=[rsel])
                    sc.memset('dve', score[:, :, 0:1], BIG, writes=[rsel])
                    for j in range(4):
                        sc.op('dve', lambda e, j=j: e.max(out=m8a[:, j * 8:(j + 1) * 8], in_=score[:, j, :]), reads=[rsel], writes=[rsel])
                        sc.op('dve', lambda e, j=j: e.match_replace(out=wk[:, j, :], in_to_replace=m8a[:, j * 8:(j + 1) * 8],
                                                                    in_values=score[:, j, :], imm_value=-3e38), reads=[rsel], writes=[rsel])
                        sc.op('dve', lambda e, j=j: e.max(out=m8b[:, j * 8:(j + 1) * 8], in_=wk[:, j, :]), reads=[rsel], writes=[rsel])
                        sc.ts('dve', selb[:, j, :], score[:, j, :], m8b[:, j * 8 + 7:j * 8 + 8], None, ALU.is_ge, reads=[rsel], writes=[rsel])
                    tb, rtb = tb_r.next()
                    pb = tb.bitcast(BF16)
                    for j in range(4):
                        sc.tr(pb[:, j * 128:(j + 1) * 128], selb[:, j, :], ident, reads=[rsel, r_const], writes=[rtb])
                    sc.cp('dve', selT, pb[:, 0:512], reads=[rtb], writes=[rsel])
                    for hh in range(4):
                        half, a = hh // 2, hh % 2
                        hs = slice(half * 64, half * 64 + 64)
                        hrow = 4 + 4 * g + hh
                        osb, rosb = a1_r.next()
                        owb, rowb = a2_r.next()
                        def s1(kt, c=c, hs=hs, a=a, hrow=hrow):
                            j0 = max(kt - 4 * c, 0)
                            c0 = 128 * j0
                            lb, rlb = lb_r.next()
                            sc.mm(lb[:, c0:512], ksT2[hs, kt * 128:(kt + 1) * 128], qT[hs, a, c * 512 + c0:(c + 1) * 512], reads=[rk], writes=[rlb])
                            mb, rmb = mb_r.next()
                            sc.mm(mb[:, c0:512], Bm[:, kt * 128:(kt + 1) * 128], selT[:, c0:512], reads=[rk, rsel], writes=[rmb])
                            ps, rps = p_r.next()
                            sc.act(ps[:, c0:512], lb[:, c0:512], AF.Exp, reads=[rlb], writes=[rps], scale=0.125)
                            sc.tt('dve', ps[:, c0:512], ps[:, c0:512], mb[:, c0:512], ALU.mult, reads=[rps, rmb], writes=[rps])
                            for j in range(j0, 4):
                                d = 4 * c + j - kt
                                if d in (0, 1):
                                    sc.tt('pool', ps[:, j * 128:(j + 1) * 128], ps[:, j * 128:(j + 1) * 128],
                                          E0(hrow) if d == 0 else E128(hrow), ALU.mult, reads=[rps, r_const], writes=[rps])
                            return (ps, rps, j0)

                        def s2(kt, stt_, c=c, osb=osb, rosb=rosb):
                            ps, rps, j0 = stt_
                            for j in range(j0, 4):
                                sc.mm(osb[:, j * 128:j * 128 + 65], ps[:, j * 128:(j + 1) * 128], vsa[:, kt, :],
                                      start=(kt == 0 and j == j0), stop=(kt == 4 * c + j), reads=[rps, rk], writes=[rosb])

                        pipe(4 * c + 4, s1, s2)
                        kts = list(range(max(4 * c - 4, 0), 4 * c + 4))

                        def s1(ii, c=c, hs=hs, a=a, hrow=hrow, kts=kts):
                            kt = kts[ii]
                            jlo = max(kt - 4 * c, 0)
                            jhi = min(kt + 4 - 4 * c, 3)
                            cs_ = slice(128 * jlo, 128 * (jhi + 1))
                            lb, rlb = lb_r.next()
                            sc.mm(lb[:, cs_], kwT2[hs, kt * 128:(kt + 1) * 128], qT[hs, a, c * 512 + 128 * jlo:c * 512 + 128 * (jhi + 1)],
                                  reads=[rk], writes=[rlb])
                            pw, rpw = p_r.next()
                            sc.act(pw[:, cs_], lb[:, cs_], AF.Exp, reads=[rlb], writes=[rpw], scale=0.125)
                            for j in range(jlo, jhi + 1):
                                d = 4 * c + j - kt
                                if d in (0, 1, 4):
                                    mk = E0(hrow) if d == 0 else (E128(hrow) if d == 1 else m512)
                                    sc.tt('pool', pw[:, j * 128:(j + 1) * 128], pw[:, j * 128:(j + 1) * 128], mk, ALU.mult,
                                          reads=[rpw, r_const], writes=[rpw])
                            return (pw, rpw, jlo, jhi)

                        def s2(ii, stt_, c=c, kts=kts, owb=owb, rowb=rowb):
                            pw, rpw, jlo, jhi = stt_
                            kt = kts[ii]
                            for j in range(jlo, jhi + 1):
                                sc.mm(owb[:, j * 128:j * 128 + 65], pw[:, j * 128:(j + 1) * 128], vwa[:, kt, :],
                                      start=(ii == 0 and j == jlo), stop=(kt == 4 * c + j), reads=[rpw, rk], writes=[rowb])

                        pipe(len(kts), s1, s2)
                        osb3 = v3(osb, 4)
                        owb3 = v3(owb, 4)
                        sc.ts('dve', rls, osb3[:, :, 64:65], TINY, None, ALU.max, reads=[rosb], writes=[rcoef])
                        sc.op('dve', lambda e: e.reciprocal(rls, rls), reads=[rcoef], writes=[rcoef])
                        sc.ts('dve', rlw, owb3[:, :, 64:65], TINY, None, ALU.max, reads=[rowb], writes=[rcoef])
                        sc.op('dve', lambda e: e.reciprocal(rlw, rlw), reads=[rcoef], writes=[rcoef])
                        gi = g * 4 + hh
                        sc.tt('dve', cfs, rls, gt3[:, :, 8 + gi:9 + gi], ALU.mult, reads=[rcoef, rgt], writes=[rcoef])
                        sc.tt('dve', cfw, rlw, gt3[:, :, 16 + gi:17 + gi], ALU.mult, reads=[rcoef, rgt], writes=[rcoef])
                        sc.tt('dve', ot1, oc4[:, hh], gt3[:, :, gi:gi + 1].to_broadcast([128, 4, 64]), ALU.mult, reads=[roc, rgt], writes=[rot1])
                        sc.tt('dve', ot2, osb3[:, :, 0:64], cfs.to_broadcast([128, 4, 64]), ALU.mult, reads=[rosb, rcoef], writes=[rot2])
                        sc.tt('dve', ot3, owb3[:, :, 0:64], cfw.to_broadcast([128, 4, 64]), ALU.mult, reads=[rowb, rcoef], writes=[rot3])
                        sc.tt('pool', ot1, ot1, ot2, ALU.add, reads=[rot1, rot2], writes=[rot1])
                        sc.tt('pool', o_tok[:, :, hh * 64:(hh + 1) * 64], ot1, ot3, ALU.add, reads=[rot1, rot3], writes=[rotk])
                    for fc in range(2):
                        tb, rtb = tb_r.next()
                        pb = tb.bitcast(BF16)
                        for j in range(4):
                            sc.tr(pb[:, j * 128:(j + 1) * 128], o_tok[:, j, fc * 128:(fc + 1) * 128], ident, reads=[rotk, r_const], writes=[rtb])
                        stg, rst = st_r.next()
                        sc.cp('act', stg, pb[:, 0:512], reads=[rtb], writes=[rst])
                        r0 = 512 + g * 256 + fc * 128
                        sc.dma(STQ, oTd[r0:r0 + 128, c * 512:(c + 1) * 512], stg, reads=[rst], writes=[r_oTd])
                sc.barrier()

        def phase_C1(l, xsrc):
            ar.top = base_top
            Wg = v3(ar.bf(8 * 3072), 8)
            Wb = v3(ar.bf(8 * D), 8)
            Wo = v3(ar.bf(8 * D), 8)
            rW = Res('W')
            mark = ar.top
            sring = Ring([(ar.f32(2048), Res()) for i in range(2)])
            for c in range(8):
                load_cast(Wg[:, c, :], w_in[l, c * 128:(c + 1) * 128, NQKV:NIN], 3072, sring, rW)
                load_cast(Wb[:, c, :], w_branch[l, c * 128:(c + 1) * 128, :], D, sring, rW)
                load_cast(Wo[:, c, :], w_out[l, c * 128:(c + 1) * 128, :], D, sring, rW)
            sc.barrier()
            ar.top = mark
            xs_r = Ring([(v3(ar.f32(4 * D), 4), Res()) for i in range(2)])
            hT_r = Ring([(v3(ar.bf(8 * 512), 8), Res()) for i in range(2)])
            oT_r = Ring([(v3(ar.bf(8 * 512), 8), Res()) for i in range(2)])
            mixT = v3(ar.bf(8 * 512), 8)
            rmix = Res('mix')
            g_r = Ring([(ar.f32(512), Res()) for i in range(3)])
            t_r = Ring([(ar.f32(512), Res()) for i in range(4)])
            gb_r = Ring([(banks[i], bres[i]) for i in (0, 1, 2)])
            bb_r = Ring([(banks[i], bres[i]) for i in (3, 4, 5)])
            yb_r = Ring([(banks[i], bres[i]) for i in (6, 7)])
            r_x1d = Res()
            branch_k = ((0, 1), (2, 3), (4, 5, 6, 7))
            for ci in range(NC):
                xs, rx = xs_r.next()
                sc.dma('sp', xs, xsrc[ci * 512:(ci + 1) * 512, :].rearrange("(t p) d -> p t d", p=128), writes=[rx])
                hT, rhT = hT_r.next()
                sc.dma('sp', hT, hTd.rearrange("(c p) s -> p c s", p=128)[:, :, ci * 512:(ci + 1) * 512], writes=[rhT])
                oT, roT = oT_r.next()
                sc.dma('sp', oT, oTd.rearrange("(c p) s -> p c s", p=128)[:, :, ci * 512:(ci + 1) * 512], writes=[roT])
                for dm in range(8):
                    ts_ = []
                    for br in range(3):
                        gb, rgb = gb_r.next()
                        col = br * D + dm * 128
                        for c in range(8):
                            sc.mm(gb, Wg[:, c, col:col + 128], hT[:, c, :], start=(c == 0), stop=(c == 7),
                                  reads=[rW, rhT], writes=[rgb])
                        g, rg = g_r.next()
                        sc.act(g, gb, AF.Sigmoid, reads=[rgb], writes=[rg])
                        bb, rbb = bb_r.next()
                        ks = branch_k[br]
                        for i, k in enumerate(ks):
                            sc.mm(bb, Wb[:, k, dm * 128:(dm + 1) * 128], oT[:, k, :], start=(i == 0), stop=(i == len(ks) - 1),
                                  reads=[rW, roT], writes=[rbb])
                        t, rt = t_r.next()
                        sc.tt('dve', t, bb, g, ALU.mult, reads=[rbb, rg], writes=[rt])
                        ts_.append((t, rt))
                    sc.tt('pool', ts_[0][0], ts_[0][0], ts_[1][0], ALU.add, reads=[ts_[0][1], ts_[1][1]], writes=[ts_[0][1]])
                    sc.tt('pool', mixT[:, dm, :], ts_[0][0], ts_[2][0], ALU.add, reads=[ts_[0][1], ts_[2][1]], writes=[rmix])
                for t in range(4):
                    for half in range(2):
                        yb, ryb = yb_r.next()
                        for k in range(8):
                            sc.mm(yb, mixT[:, k, t * 128:(t + 1) * 128], Wo[:, k, half * 512:(half + 1) * 512],
                                  start=(k == 0), stop=(k == 7), reads=[rmix, rW], writes=[ryb])
                        sc.tt('dve', xs[:, t, half * 512:(half + 1) * 512], xs[:, t, half * 512:(half + 1) * 512], yb, ALU.add,
                              reads=[rx, ryb], writes=[rx])
                sc.dma(STQ, x1d[ci * 512:(ci + 1) * 512, :].rearrange("(t p) d -> p t d", p=128), xs, reads=[rx], writes=[r_x1d])
            sc.barrier()

        def phase_C2(l, xdst):
            ar.top = base_top
            Wgu = v3(ar.bf(8 * 2 * DFF), 8)
            Wd = v3(ar.bf(22 * D), 22)
            gF = ar.f32(D)
            rW = Res('W')
            mark = ar.top
            sring = Ring([(ar.f32(2048), Res()) for i in range(2)])
            for c in range(8):
                load_cast(Wgu[:, c, :], w_gate_up[l, c * 128:(c + 1) * 128, :], 2 * DFF, sring, rW)
            for f in range(22):
                load_cast(Wd[:, f, :], w_down[l, f * 128:(f + 1) * 128, :], D, sring, rW)
            sc.dma('sp', gF, AP(ffn_norm.tensor, l * D, [[0, 128], [1, D]]), writes=[rW])
            sc.barrier()
            ar.top = mark
            xs_r = Ring([(v3(ar.f32(2 * D), 2), Res()) for i in range(2)])
            hb = v3(ar.bf(2 * D), 2)
            rhb = Res()
            sq = ar.f32(D)
            ssq = ar.f32(4)
            rstd = ar.f32(4)
            rtmp = Res()
            hT = v3(ar.bf(8 * 256), 8)
            rhT = Res()
            actT = v3(ar.bf(22 * 256), 22)
            ract = Res()
            sg_r = Ring([(ar.f32(256), Res()) for i in range(2)])
            tb_r = Ring([(banks[i], bres[i]) for i in (0, 1)])
            fb_r = Ring([(banks[i], bres[i]) for i in (2, 3, 4)])
            yb_r = Ring([(banks[i], bres[i]) for i in (5, 6, 7)])
            r_out = Res()
            for ci in range(S // 256):
                xs, rx = xs_r.next()
                sc.dma('sp', xs, x1d[ci * 256:(ci + 1) * 256, :].rearrange("(t p) d -> p t d", p=128), writes=[rx])
                rmsnorm_tm(xs, rx, gF, rW, hb, rhb, 2, sq, ssq, rstd, rtmp)
                transpose_to(hb, rhb, hT, rhT, 2, tb_r, ('dve', 'act'))
                for f in range(22):
                    fb, rfb = fb_r.next()
                    for half, col in ((0, f * 128), (1, DFF + f * 128)):
                        for c in range(8):
                            sc.mm(fb[:, half * 256:(half + 1) * 256], Wgu[:, c, col:col + 128], hT[:, c, :],
                                  start=(c == 0), stop=(c == 7), reads=[rW, rhT], writes=[rfb])
                    sg, rsg = sg_r.next()
                    sc.act(sg, fb[:, 0:256], AF.Silu, reads=[rfb], writes=[rsg])
                    sc.tt('dve', actT[:, f, :], sg, fb[:, 256:512], ALU.mult, reads=[rsg, rfb], writes=[ract])
                for t in range(2):
                    for half in range(2):
                        yb, ryb = yb_r.next()
                        for f in range(22):
                            sc.mm(yb, actT[:, f, t * 128:(t + 1) * 128], Wd[:, f, half * 512:(half + 1) * 512],
                                  start=(f == 0), stop=(f == 21), reads=[ract, rW], writes=[ryb])
                        sc.tt('dve', xs[:, t, half * 512:(half + 1) * 512], xs[:, t, half * 512:(half + 1) * 512], yb, ALU.add,
                              reads=[rx, ryb], writes=[rx])
                sc.dma(STQ, xdst[ci * 256:(ci + 1) * 256, :].rearrange("(t p) d -> p t d", p=128), xs, reads=[rx], writes=[r_out])
            sc.barrier()

        PH = {}
        exec_phases = phases
        prologue()
        for l in range(depth):
            xsrc = x_in if l == 0 else xmd
            xdst = out if l == depth - 1 else xmd
            if exec_phases is None or 'A' in exec_phases:
                phase_A(l, xsrc)
            if exec_phases is None or 'SB' in exec_phases:
                phase_SB(l)
            if exec_phases is None or 'MB' in exec_phases:
                phase_MB(l)
            if exec_phases is None or 'NSA' in exec_phases:
                phase_NSA(l)
            if inject:
                sc.dma('sp', oTd, oT_in, writes=[Res()])
                sc.barrier()
            if exec_phases is None or 'C1' in exec_phases:
                phase_C1(l, xsrc)
            if exec_phases is None or 'C2' in exec_phases:
                phase_C2(l, xdst)
        sc.barrier()
        sc.emit()
    return nc


def core_inputs(inputs, b, S, depth, cbh, cfh):
    f = lambda a: np.ascontiguousarray(np.asarray(a, dtype=np.float32))
    gains = np.stack([f(inputs['moba_q_norm']), f(inputs['moba_k_norm']), f(inputs['nsa_q_norm']),
                      f(inputs['nsa_k_norm'])[:, 0], f(inputs['nsa_k_norm'])[:, 1], f(inputs['nsa_k_norm'])[:, 2]], axis=1)
    m = {
        'x': f(inputs['x'][b, :S]),
        'rel_bias': f(inputs['rel_bias']),
        'attn_norm': f(inputs['attn_norm'])[:depth],
        'w_in': f(inputs['w_in'])[:depth],
        'gains': np.ascontiguousarray(gains[:depth]),
        'nsa_cmp_pos': f(inputs['nsa_cmp_pos'])[:depth],
        'nsa_cmp_w1': f(inputs['nsa_cmp_w1'])[:depth],
        'nsa_cmp_w2': f(inputs['nsa_cmp_w2'])[:depth],
        'w_branch': f(inputs['w_branch'])[:depth],
        'w_out': f(inputs['w_out'])[:depth],
        'ffn_norm': f(inputs['ffn_norm'])[:depth],
        'w_gate_up': f(inputs['w_gate_up'])[:depth],
        'w_down': f(inputs['w_down'])[:depth],
        'cb': cbh,
        'cf': cfh,
    }
    return m


def kernel(**inputs):
    x = np.asarray(inputs['x'])
    B, S, _ = x.shape
    depth = int(np.asarray(inputs['w_in']).shape[0])
    cbh, cfh, _, _ = host_consts(S)
    nc = build(S, depth)
    in_maps = [core_inputs(inputs, b, S, depth, cbh, cfh) for b in range(B)]
    res = run_bass_kernel_spmd(nc, in_maps, core_ids=list(range(B)))
    return np.stack([np.asarray(r['out'], dtype=np.float32) for r in res.results], axis=0)
```

```python
import math
from contextlib import ExitStack

import numpy as np
import ml_dtypes
import concourse.bass as bass
import concourse.mybir as mybir
from concourse.bass_types import AP
from concourse.bass_utils import run_bass_kernel_spmd

F32 = mybir.dt.float32
BF16 = mybir.dt.bfloat16
AF = mybir.ActivationFunctionType
ALU = mybir.AluOpType
AX = mybir.AxisListType
ENGS = ('pe', 'act', 'dve', 'pool', 'sp')
NDMA = 8
STQ = 'sp'

D = 1024
NIN = 5912
NQKV = 2840
DFF = 2816
EPS = 1e-6
NEG = -1e30
BIG = 1e30
TINY = 1e-30
GOFF = 2304
GL = 4864
FM_COLS = [0, 128, 256, 384, 768, 896, 1024, 1152, 1536, 1664, 1792, 1920, 2048, 2176, 2304, 2560]
FM_KIND = [None, None, None, None, 0, 0, 1, 1, 2, 2, 2, 2, None, None, 4, 5]


class Res:
    __slots__ = ('name', 'lw', 'rd')

    def __init__(self, name=''):
        self.name = name
        self.lw = None
        self.rd = {}


class Ring:
    def __init__(self, items):
        self.items = items
        self.i = 0

    def next(self):
        it = self.items[self.i]
        self.i = (self.i + 1) % len(self.items)
        return it


class Sched:
    def __init__(self, nc, stack):
        self.nc = nc
        self.q = {e: [] for e in ENGS}
        self.cnt = {e: 0 for e in ENGS}
        self.waited = {e: {} for e in ENGS}
        self.sem = {}
        for e in ENGS:
            self.sem[('e', e)] = stack.enter_context(nc.semaphore('s_' + e))
        self.dcnt = {}
        self.drr = {}
        for e in ('sp', 'act', 'pool'):
            self.drr[e] = 0
            for i in range(NDMA):
                k = ('d', e, i)
                self.sem[k] = stack.enter_context(nc.semaphore('d_%s%d' % (e, i)))
                self.dcnt[k] = 0

    def _waits(self, eng, reads, writes, extra=()):
        deps = {}

        def add(ev):
            if ev is None:
                return
            k, v = ev
            if deps.get(k, 0) < v:
                deps[k] = v
        for r in reads:
            add(r.lw)
        for w in writes:
            add(w.lw)
            for k, v in w.rd.items():
                add((k, v))
        for ev in extra:
            add(ev)
        out = []
        wd = self.waited[eng]
        for k, v in deps.items():
            if k == ('e', 'pe') and eng == 'pe':
                continue
            if wd.get(k, 0) >= v:
                continue
            wd[k] = v
            out.append((k, v))
        return out

    def _mark(self, ev, reads, writes):
        k, v = ev
        for r in reads:
            if r.rd.get(k, 0) < v:
                r.rd[k] = v
        for w in writes:
            w.lw = ev
            w.rd = {}

    def op(self, eng, fn, reads=(), writes=()):
        waits = self._waits(eng, reads, writes)
        self.cnt[eng] += 1
        ev = (('e', eng), self.cnt[eng])
        self.q[eng].append((fn, waits, ('e', eng), 1))
        self._mark(ev, reads, writes)
        return ev

    def dma(self, eng, out, in_, reads=(), writes=(), **kw):
        i = self.drr[eng]
        self.drr[eng] = (i + 1) % NDMA
        k = ('d', eng, i)
        prev = (k, 16 * self.dcnt[k]) if self.dcnt[k] else None
        waits = self._waits(eng, reads, writes, extra=(prev,) if prev else ())
        self.dcnt[k] += 1
        ev = (k, 16 * self.dcnt[k])
        self.q[eng].append((lambda e: e.dma_start(out=out, in_=in_, **kw), waits, k, 16))
        self._mark(ev, reads, writes)
        return ev

    def barrier(self):
        evs = [(('e', e), self.cnt[e]) for e in ENGS if self.cnt[e] > 0]
        evs += [(k, 16 * c) for k, c in self.dcnt.items() if c > 0]
        for e in ENGS:
            wd = self.waited[e]
            waits = []
            for k, v in evs:
                if k == ('e', e):
                    continue
                if wd.get(k, 0) >= v:
                    continue
                wd[k] = v
                waits.append((k, v))
            self.q[e].append((None, waits, None, 0))

    def emit(self):
        nc = self.nc
        with nc.Block() as block:
            def run(name):
                def f(e):
                    for fn, waits, k, inc in self.q[name]:
                        for wk, wv in waits:
                            e.wait_ge(self.sem[wk], wv)
                        if fn is not None:
                            fn(e).then_inc(self.sem[k], inc)
                return f
            block.tensor(run('pe'))
            block.scalar(run('act'))
            block.vector(run('dve'))
            block.gpsimd(run('pool'))
            block.sync(run('sp'))

    def mm(self, out, lhsT, rhs, start=True, stop=True, reads=(), writes=()):
        return self.op('pe', lambda e: e.matmul(out, lhsT, rhs, start=start, stop=stop), reads, writes)

    def tr(self, out, in_, ident, reads=(), writes=()):
        return self.op('pe', lambda e: e.transpose(out, in_, ident), reads, writes)

    def act(self, out, in_, func, reads=(), writes=(), **kw):
        return self.op('act', lambda e: e.activation(out, in_, func, **kw), reads, writes)

    def tt(self, eng, out, in0, in1, op, reads=(), writes=()):
        return self.op(eng, lambda e: e.tensor_tensor(out, in0, in1, op), reads, writes)

    def ts(self, eng, out, in0, s1, s2, op0, op1=None, reads=(), writes=()):
        if op1 is None:
            return self.op(eng, lambda e: e.tensor_scalar(out, in0, s1, s2, op0), reads, writes)
        return self.op(eng, lambda e: e.tensor_scalar(out, in0, s1, s2, op0, op1), reads, writes)

    def stt(self, eng, out, in0, scalar, in1, op0, op1, reads=(), writes=()):
        return self.op(eng, lambda e: e.scalar_tensor_tensor(out, in0, scalar, in1, op0, op1), reads, writes)

    def cp(self, eng, out, in_, reads=(), writes=()):
        if eng == 'act':
            return self.op('act', lambda e: e.copy(out, in_), reads, writes)
        return self.op(eng, lambda e: e.tensor_copy(out, in_), reads, writes)

    def memset(self, eng, ap, val, writes=()):
        return self.op(eng, lambda e: e.memset(ap, val), (), writes)


class Arena:
    def __init__(self, ap, size):
        self.ap = ap
        self.size = size
        self.top = 0

    def bf(self, n):
        a = self.top
        self.top += (n + 31) // 32 * 32
        assert self.top <= self.size, ('arena overflow', self.top, self.size)
        return self.ap[:, a:a + n]

    def f32(self, n):
        a = self.top
        self.top += (2 * n + 31) // 32 * 32
        assert self.top <= self.size, ('arena overflow', self.top, self.size)
        return self.ap[:, a:a + 2 * n].bitcast(F32)


def v3(ap, a):
    return ap.rearrange("p (a b) -> p a b", a=a)


LAG = 2
WARM = 0
NDUMMY = 0


def pipe(n, *stages, lag=LAG):
    ns = len(stages)
    st = [None] * n
    for t in range(n + lag * (ns - 1)):
        for k, f in enumerate(stages):
            i = t - k * lag
            if 0 <= i < n:
                st[i] = f(i, st[i]) if k else f(i)


def t5_bucket_np(d):
    n = np.maximum(d, 0)
    nf = np.maximum(n, 1).astype(np.float32)
    large = 16 + (np.log(nf / np.float32(16)) / np.float32(math.log(128 / 16)) * np.float32(16)).astype(np.int32)
    large = np.minimum(large, 31)
    return np.where(n < 16, n, large)


def host_consts(S):
    k = np.arange(128)[:, None]
    q = np.arange(128)[None, :]
    ident = (k == q)
    J = (k + q == 127)
    uincl = (k >= q)
    ones = np.ones((128, 128), bool)
    blk = (k // 64 == q // 64)
    maskL = (k < q)
    m512 = (q < k)
    zeros = np.zeros((128, 512), bool)
    n_cmp = (S - 32) // 16 + 1
    nct = (n_cmp + 127) // 128
    n = np.arange(nct * 128)[:, None]
    m = np.arange(128)[None, :]
    ovl = ((16 * n < 64 * m + 64) & (16 * n + 32 > 64 * m) & (n < n_cmp))
    ovl = ovl.reshape(nct, 128, 128).transpose(1, 0, 2).reshape(128, nct * 128)
    mm_ = np.arange(128)[:, None]
    xx = np.arange(S)[None, :]
    B = (xx // 64 == mm_)
    cb = np.concatenate([ident, J, uincl, ones, blk, maskL, m512, zeros, ovl, B], axis=1).astype(np.float32)
    cb = cb.astype(ml_dtypes.bfloat16)
    negmask = np.where(k >= q, -1e4, 0.0).astype(np.float32)
    y = np.arange(256)[None, :]
    rel = y - 126 - (k >= 64)
    cs = np.where((rel == 0) | (rel == -1), BIG, np.where(rel > 0, NEG, 0.0)).astype(np.float32)
    oh = np.zeros((128, 128), np.float32)
    b = t5_bucket_np(np.arange(128))
    oh[b, np.arange(128)] = 1.0
    cf = np.concatenate([negmask, cs, oh], axis=1).astype(np.float32)
    return cb, cf, n_cmp, nct


def build(S, depth, debug=False, phases=None, dbg=None, inject=False):
    NT = S // 128
    NC = S // 512
    cbh, cfh, n_cmp, NCT = host_consts(S)
    NCB = cbh.shape[1]
    NCF = cfh.shape[1]
    nc = bass.Bass("TRN2", target_bir_lowering=False)
    dt_in = lambda name, shape, dt=F32: nc.dram_tensor(name, shape, dt, kind="ExternalInput").ap()
    dbg_kind = "ExternalOutput" if debug else "Internal"
    scr = lambda name, shape, dt: nc.dram_tensor(name, shape, dt, kind=dbg_kind).ap()
    x_in = dt_in("x", [S, D])
    rel_bias = dt_in("rel_bias", [32, 12])
    attn_norm = dt_in("attn_norm", [depth, D])
    w_in = dt_in("w_in", [depth, D, NIN])
    gains = dt_in("gains", [depth, 6, 64])
    cmp_pos = dt_in("nsa_cmp_pos", [depth, 2, 32, 64])
    cmp_w1 = dt_in("nsa_cmp_w1", [depth, 2, 2048, 256])
    cmp_w2 = dt_in("nsa_cmp_w2", [depth, 2, 256, 64])
    w_branch = dt_in("w_branch", [depth, D, D])
    w_out = dt_in("w_out", [depth, D, D])
    ffn_norm = dt_in("ffn_norm", [depth, D])
    w_gate_up = dt_in("w_gate_up", [depth, D, 2 * DFF])
    w_down = dt_in("w_down", [depth, DFF, D])
    cb_in = dt_in("cb", [128, NCB], BF16)
    cf_in = dt_in("cf", [128, NCF])
    out = nc.dram_tensor("out", [S, D], F32, kind="ExternalOutput").ap()

    hTd = scr("hTd", [D, S], BF16)
    fmd = scr("fmd", [2048, S], BF16)
    tmvd = scr("tmvd", [S, 832], BF16)
    nsgd = scr("nsgd", [S, 32], F32)
    oTd = scr("oTd", [D, S], BF16)
    x1d = scr("x1d", [S, D], F32)
    xmd = scr("xmd", [S, D], F32)
    gvd = scr("gvd", [12, GL], BF16)
    oT_in = dt_in("oT_in", [D, S], BF16) if inject else None

    with ExitStack() as st:
        sc = Sched(nc, st)
        ASZ = 98304
        arena_ap = nc.alloc_sbuf_tensor("arena", [128, ASZ], BF16).ap()
        ar = Arena(arena_ap, ASZ)
        banks = [nc.alloc_psum_tensor("bank%d" % i, [128, 512], F32).ap() for i in range(8)]
        bres = [Res("bank%d" % i) for i in range(8)]

        CB_GLOBAL = 8 * 128 + 384
        cb = ar.bf(CB_GLOBAL)
        ident = cb[:, 0:128]
        Jm = cb[:, 128:256]
        uincl = cb[:, 256:384]
        onesb = cb[:, 384:512]
        blkones = cb[:, 512:640]
        maskL = cb[:, 640:768]
        m512 = cb[:, 768:896]
        zerob = cb[:, 896:1408]
        cf = ar.f32(384)
        negmask = cf[:, 0:128]
        cslide = cf[:, 128:384]
        Etab = ar.bf(24 * 128)
        r_const = Res('const')
        sc.dma('sp', cb, cb_in[:, 0:CB_GLOBAL], writes=[r_const])
        sc.dma('sp', cf, cf_in[:, 0:384], writes=[r_const])
        base_top = ar.top

        def E0(h):
            return Etab[:, (2 * h) * 128:(2 * h + 1) * 128]

        def E128(h):
            return Etab[:, (2 * h + 1) * 128:(2 * h + 2) * 128]

        def prologue():
            ar.top = base_top
            tb = ar.f32(12)
            oh = ar.f32(128)
            br = ar.f32(128)
            negc = ar.f32(1)
            gsb = ar.bf(GL)
            hk = ar.bf(128)
            r = Res('pro')
            rb = bres[0]
            sc.dma('sp', tb[0:32, :], rel_bias, writes=[r])
            sc.dma('sp', oh[0:32, :], cf_in[0:32, 384:512], writes=[r])
            sc.mm(banks[0][0:12, 0:128], tb[0:32, 0:12], oh[0:32, 0:128], reads=[r], writes=[rb])
            sc.cp('dve', br[0:12, :], banks[0][0:12, 0:128], reads=[rb], writes=[r])
            sc.ts('dve', negc[0:12, :], br[0:12, 127:128], -1.0, None, ALU.mult, reads=[r], writes=[r])
            sc.memset('pool', gsb[0:12, 0:GOFF], 0.0, writes=[r])
            sc.memset('pool', gsb[0:12, GOFF + 128:GL], 1.0, writes=[r])
            sc.act(gsb[0:12, GOFF:GOFF + 128], br[0:12, :], AF.Exp, reads=[r], writes=[r], bias=negc[0:12, :])
            rg = Res('gvd')
            sc.dma('sp', gvd, gsb[0:12, :], reads=[r], writes=[rg])
            hkr = Res('hk')
            for h in range(12):
                for di, dl in enumerate((0, 128)):
                    src = AP(gvd.tensor, h * GL + GOFF + dl - 127, [[1, 128], [1, 128]])
                    sc.dma('sp', hk, src, reads=[rg], writes=[hkr])
                    sc.mm(banks[1][:, 0:128], Jm, hk, reads=[hkr, r_const], writes=[bres[1]])
                    sc.cp('dve', Etab[:, (2 * h + di) * 128:(2 * h + di + 1) * 128], banks[1][:, 0:128],
                          reads=[bres[1]], writes=[r_const])
            sc.barrier()

        CAST_ENG = ('pool', 'dve', 'act')
        cast_i = [0]
        def load_cast(dst, src, n, stage_ring, rdst):
            o = 0
            while o < n:
                w = min(2048, n - o)
                sg, rs = stage_ring.next()
                sc.dma('sp', sg[:, 0:w], src[:, o:o + w], writes=[rs])
                sc.cp(CAST_ENG[cast_i[0] % 3], dst[:, o:o + w], sg[:, 0:w], reads=[rs], writes=[rdst])
                cast_i[0] += 1
                o += w

        def rmsnorm_tm(xs, rx, gB, rg, hb, rh, ntile, sq, ssq, rstd, rtmp):
            for t in range(ntile):
                sc.act(sq, xs[:, t, :], AF.Square, reads=[rx], writes=[rtmp])
                sc.op('dve', lambda e, t=t: e.tensor_reduce(ssq[:, t:t + 1], sq, AX.X, ALU.add), reads=[rtmp], writes=[rtmp])
            sc.act(rstd[:, 0:ntile], ssq[:, 0:ntile], AF.Ln, reads=[rtmp], writes=[rtmp], scale=1.0 / D, bias=EPS)
            sc.act(rstd[:, 0:ntile], rstd[:, 0:ntile], AF.Exp, reads=[rtmp], writes=[rtmp], scale=-0.5)
            for t in range(ntile):
                sc.stt('dve', hb[:, t, :], xs[:, t, :], rstd[:, t:t + 1], gB, ALU.mult, ALU.mult,
                       reads=[rx, rtmp, rg], writes=[rh])

        def transpose_to(hb, rh, hT, rhT, ntile, bank_ring, evac_engs):
            for c in range(8):
                bk, rb = bank_ring.next()
                pb = bk.bitcast(BF16)
                for t in range(ntile):
                    sc.tr(pb[:, t * 128:(t + 1) * 128], hb[:, t, c * 128:(c + 1) * 128], ident,
                          reads=[rh, r_const], writes=[rb])
                sc.cp(evac_engs[c % len(evac_engs)], hT[:, c, :], pb[:, 0:ntile * 128], reads=[rb], writes=[rhT])

        def phase_A(l, xsrc):
            ar.top = base_top
            W = v3(ar.bf(8 * NQKV), 8)
            gA = ar.f32(D)
            gcol = ar.f32(8)
            rW = Res('W')
            mark = ar.top
            stg = [(ar.f32(2048), Res('stg%d' % i)) for i in range(4)]
            sring = Ring(stg)
            for c in range(8):
                load_cast(W[:, c, :], w_in[l, c * 128:(c + 1) * 128, 0:NQKV], NQKV, sring, rW)
            sc.dma('sp', gA, AP(attn_norm.tensor, l * D, [[0, 128], [1, D]]), writes=[rW])
            for gi in range(6):
                for half in range(2):
                    sc.dma('sp', gcol[half * 64:(half + 1) * 64, gi:gi + 1],
                           AP(gains.tensor, (l * 6 + gi) * 64, [[1, 64], [1, 1]]), writes=[rW])
            sc.barrier()
            if dbg == 'W':
                return
            ar.top = mark
            xs_r = Ring([(v3(ar.f32(4 * D), 4), Res('xs%d' % i)) for i in range(2)])
            hb = v3(ar.bf(4 * D), 4)
            rhb = Res('hb')
            sq = ar.f32(D)
            ssq = ar.f32(4)
            rstd = ar.f32(4)
            rtmp = Res('tmp')
            hT_r = Ring([(v3(ar.bf(8 * 512), 8), Res('hT%d' % i)) for i in range(2)])
            sqb_r = Ring([(ar.bf(512), Res('sqb%d' % i)) for i in range(2)])
            rs_r = Ring([(ar.f32(512), Res('rs%d' % i)) for i in range(2)])
            fst_r = Ring([(ar.bf(512), Res('fst%d' % i)) for i in range(3)])
            tst_r = Ring([(ar.bf(832), Res('tst%d' % i)) for i in range(2)])
            gst_r = Ring([(ar.f32(24), Res('gst%d' % i)) for i in range(2)])
            tb_r = Ring([(banks[i], bres[i]) for i in (0, 1)])
            pb_r = Ring([(banks[i], bres[i]) for i in (2, 3, 4)])
            nb_r = Ring([(banks[i], bres[i]) for i in (5,)])
            vb_r = Ring([(banks[i], bres[i]) for i in (6, 7)])
            r_hTd, r_fmd, r_tmvd, r_nsgd = Res(), Res(), Res(), Res()
            for ci in range(NC):
                xs, rx = xs_r.next()
                sc.dma('sp', xs, xsrc[ci * 512:(ci + 1) * 512, :].rearrange("(t p) d -> p t d", p=128), writes=[rx])
                if dbg == 'ld':
                    continue
                rmsnorm_tm(xs, rx, gA, rW, hb, rhb, 4, sq, ssq, rstd, rtmp)
                if dbg == 'norm':
                    continue
                hT, rhT = hT_r.next()
                transpose_to(hb, rhb, hT, rhT, 4, tb_r, ('dve', 'act'))
                if dbg == 'tr':
                    continue
                sc.dma(STQ, hTd.rearrange("(c p) s -> p c s", p=128)[:, :, ci * 512:(ci + 1) * 512], hT,
                       reads=[rhT], writes=[r_hTd])
                for j in range(16):
                    col = FM_COLS[j]
                    bk, rb = pb_r.next()
                    for c in range(8):
                        sc.mm(bk, W[:, c, col:col + 128], hT[:, c, :], start=(c == 0), stop=(c == 7),
                              reads=[rW, rhT], writes=[rb])
                    fs, rf = fst_r.next()
                    if FM_KIND[j] is None:
                        sc.cp('act' if j % 2 else 'dve', fs, bk, reads=[rb], writes=[rf])
                    else:
                        gi = FM_KIND[j]
                        sqb, rsq = sqb_r.next()
                        sc.act(sqb, bk, AF.Square, reads=[rb], writes=[rsq])
                        b2, rb2 = nb_r.next()
                        sc.mm(b2, blkones, sqb, reads=[rsq, r_const], writes=[rb2])
                        rs, rrs = rs_r.next()
                        sc.act(rs, b2, AF.Ln, reads=[rb2], writes=[rrs], scale=1.0 / 64, bias=EPS)
                        sc.act(rs, rs, AF.Exp, reads=[rrs], writes=[rrs], scale=-0.5)
                        sc.stt('dve', fs, bk, gcol[:, gi:gi + 1], rs, ALU.mult, ALU.mult, reads=[rb, rrs, rW], writes=[rf])
                    sc.dma(STQ, fmd[j * 128:(j + 1) * 128, ci * 512:(ci + 1) * 512], fs, reads=[rf], writes=[r_fmd])
                if dbg == 'fm':
                    continue
                for t in range(4):
                    b1, rb1 = vb_r.next()
                    b2, rb2 = vb_r.next()
                    lt = lambda c: hT[:, c, t * 128:(t + 1) * 128]
                    for (bk, rb, o, c0, w) in ((b1, rb1, 0, 512, 256), (b1, rb1, 256, 1280, 256),
                                               (b2, rb2, 0, 2432, 128), (b2, rb2, 128, 2688, 152)):
                        for c in range(8):
                            sc.mm(bk[:, o:o + w], lt(c), W[:, c, c0:c0 + w], start=(c == 0), stop=(c == 7),
                                  reads=[rW, rhT], writes=[rb])
                    ts_, rts = tst_r.next()
                    sc.cp('act', ts_[:, 0:512], b1, reads=[rb1], writes=[rts])
                    sc.cp('dve', ts_[:, 512:768], b2[:, 0:256], reads=[rb2], writes=[rts])
                    sc.act(ts_[:, 768:816].bitcast(F32), b2[:, 256:280], AF.Sigmoid, reads=[rb2], writes=[rts])
                    r0 = (ci * 4 + t) * 128
                    sc.dma(STQ, tmvd[r0:r0 + 128, 0:816], ts_[:, 0:816], reads=[rts], writes=[r_tmvd])
            sc.barrier()

        def phase_SB(l):
            ar.top = base_top
            kT = ar.bf(S)
            qT = ar.bf(S)
            v = v3(ar.bf(NT * 64), NT)
            rk = Res('kqv')
            R_r = Ring([(ar.bf(512), Res()) for i in range(LAG + 2)])
            zc_r = Ring([(ar.f32(512), Res()) for i in range(LAG + 2)])
            u_r = Ring([(ar.f32(512), Res()) for i in range(2)])
            sp_r = Ring([(ar.bf(512), Res()) for i in range(LAG + 2)])
            w_r = Ring([(ar.f32(512), Res()) for i in range(2)])
            a_r = Ring([(ar.bf(512), Res()) for i in range(LAG + 2)])
            os_r = Ring([(ar.bf(512), Res()) for i in range(2)])
            zb_r = Ring([(banks[i], bres[i]) for i in (0, 1)])
            tb_r = Ring([(banks[i], bres[i]) for i in (2, 3, 7)])
            rdum = Res('dummy')
            ob_r = Ring([(banks[i], bres[i]) for i in (4, 5)])
            r_oTd = Res()
            for h in range(4):
                sc.dma('sp', kT[0:64, :], fmd[256 + 64 * h:256 + 64 * h + 64, :], writes=[rk])
                sc.dma('sp', qT[0:64, :], fmd[64 * h:64 * h + 64, :], writes=[rk])
                sc.dma('sp', v, tmvd[:, 64 * h:64 * h + 64].rearrange("(t p) d -> p t d", p=128), writes=[rk])
                for c in range(NC):
                    ob, rob = ob_r.next()
                    sc.mm(ob[0:64, :], zerob[:, 0:64], zerob, start=True, stop=False, reads=[r_const], writes=[rob])
                    for _ in range(WARM):
                        sc.mm(banks[6], onesb, zerob, reads=[r_const], writes=[rdum])
                    last = 4 * c + 3
                    n_t = last + 1
                    Rcur = [R_r.next()]
                    sc.memset('pool', Rcur[0][0], 0.0, writes=[Rcur[0][1]])

                    def s1(i, c=c, last=last, n_t=n_t, Rcur=Rcur):
                        kt = last - i
                        rel = kt - 4 * c
                        c0 = 128 * max(rel, 0)
                        zb, rzb = zb_r.next()
                        sc.mm(zb[:, c0:512], kT[0:64, kt * 128:(kt + 1) * 128], qT[0:64, c * 512 + c0:(c + 1) * 512],
                              reads=[rk], writes=[rzb])
                        for _ in range(NDUMMY):
                            sc.mm(banks[6], onesb, zerob, reads=[r_const], writes=[rdum])
                        zc, rzc = zc_r.next()
                        sc.ts('dve', zc[:, c0:512], zb[:, c0:512], 0.125, 40.0, ALU.mult, ALU.min, reads=[rzb], writes=[rzc])
                        u, ru = u_r.next()
                        sc.act(u[:, c0:512], zc[:, c0:512], AF.Exp, reads=[rzc], writes=[ru])
                        sp, rsp = sp_r.next()
                        sc.act(sp[:, c0:512], u[:, c0:512], AF.Ln, reads=[ru], writes=[rsp], bias=1.0)
                        if rel >= 0:
                            sc.tt('pool', sp[:, c0:c0 + 128], sp[:, c0:c0 + 128], maskL, ALU.mult,
                                  reads=[rsp, r_const], writes=[rsp])
                        if c0 > 0:
                            sc.memset('pool', sp[:, 0:c0], 0.0, writes=[rsp])
                        Ri, rRi = Rcur[0]
                        if i < n_t - 1:
                            Rn, rRn = R_r.next()
                            sc.tt('pool', Rn, Ri, sp, ALU.add, reads=[rRi, rsp], writes=[rRn])
                            Rcur[0] = (Rn, rRn)
                        return (kt, rel, c0, zc, rzc, sp, rsp, Ri, rRi)

                    def s2(i, stt_, ob=ob, rob=rob):
                        kt, rel, c0, zc, rzc, sp, rsp, Ri, rRi = stt_
                        tb, rtb = tb_r.next()
                        sc.mm(tb[:, c0:512], uincl, sp[:, c0:512], start=True, stop=(i == 0),
                              reads=[rsp, r_const], writes=[rtb])
                        if i > 0:
                            sc.mm(tb[:, c0:512], onesb, Ri[:, c0:512], start=False, stop=True,
                                  reads=[rRi, r_const], writes=[rtb])
                        w, rw = w_r.next()
                        sc.tt('dve', w[:, c0:512], zc[:, c0:512], tb[:, c0:512], ALU.subtract, reads=[rzc, rtb], writes=[rw])
                        if rel >= 0:
                            sc.tt('pool', w[:, c0:c0 + 128], w[:, c0:c0 + 128], negmask, ALU.add,
                                  reads=[rw, r_const], writes=[rw])
                        a, ra = a_r.next()
                        sc.act(a[:, c0:512], w[:, c0:512], AF.Exp, reads=[rw], writes=[ra])
                        return (kt, c0, a, ra)

                    def s3(i, stt_, ob=ob, rob=rob):
                        kt, c0, a, ra = stt_
                        sc.mm(ob[0:64, c0:512], v[:, kt, :], a[:, c0:512], start=False, stop=(kt == 0),
                              reads=[rk, ra], writes=[rob])

                    pipe(n_t, s1, s2, s3)
                    os_, ros = os_r.next()
                    sc.cp('act', os_[0:64, :], ob[0:64, :], reads=[rob], writes=[ros])
                    sc.dma('sp', oTd[64 * h:64 * h + 64, c * 512:(c + 1) * 512], os_[0:64, :], reads=[ros], writes=[r_oTd])
            sc.barrier()


        def phase_MB(l):
            ar.top = base_top
            NBLK = S // 256
            kT = ar.bf(S)
            qT = ar.bf(S)
            va = v3(ar.bf(NT * 65), NT)
            rk = Res('kqv')
            km = ar.f32(32)
            rkm = Res('km')
            qf_r = Ring([(ar.f32(512), Res()) for i in range(2)])
            scv = v3(ar.f32(4 * 32), 4)
            m8 = ar.f32(32)
            selw = v3(ar.f32(4 * 32), 4)
            rsel = Res('sel')
            acc = v3(ar.f32(4 * 65), 4)
            racc = [Res('acc%d' % j) for j in range(4)]
            rl = v3(ar.f32(4), 4)
            rrl = Res('rl')
            o_tok = v3(ar.bf(NT * 256), NT)
            rot = Res('otok')
            p_r = Ring([(ar.bf(512), Res()) for i in range(LAG + 2)])
            st_r = Ring([(ar.bf(512), Res()) for i in range(2)])
            lb_r = Ring([(banks[i], bres[i]) for i in (0, 1, 7)])
            ob_r = Ring([(banks[i], bres[i]) for i in (2, 3)])
            sb_r = Ring([(banks[i], bres[i]) for i in (4,)])
            tb_r = Ring([(banks[i], bres[i]) for i in (5, 6)])
            r_oTd = Res()
            sc.memset('pool', va[:, :, 64:65], 1.0, writes=[rk])
            for h in range(4):
                sc.dma('sp', kT[0:64, :], fmd[768 + 64 * h:768 + 64 * h + 64, :], writes=[rk])
                sc.dma('sp', qT[0:64, :], fmd[512 + 64 * h:512 + 64 * h + 64, :], writes=[rk])
                sc.dma('sp', va[:, :, 0:64], tmvd[:, 256 + 64 * h:256 + 64 * h + 64].rearrange("(t p) d -> p t d", p=128), writes=[rk])
                sc.op('dve', lambda e: e.tensor_reduce(km[0:64, 0:NBLK], kT[0:64, :].rearrange("p (n k) -> p n k", k=256), AX.X, ALU.add),
                      reads=[rk], writes=[rkm])
                sc.ts('dve', km[0:64, 0:NBLK], km[0:64, 0:NBLK], 1.0 / 256, None, ALU.mult, reads=[rkm], writes=[rkm])
                for c in range(NC):
                    qf, rqf = qf_r.next()
                    sc.cp('dve', qf[0:64, :], qT[0:64, c * 512:(c + 1) * 512], reads=[rk], writes=[rqf])
                    sc.memset('pool', scv, NEG, writes=[rsel])
                    sb, rsb = sb_r.next()
                    for j in range(4):
                        cur = (4 * c + j) // 2
                        if cur > 0:
                            sc.mm(sb[:, j * 32:j * 32 + cur], qf[0:64, j * 128:(j + 1) * 128], km[0:64, 0:cur],
                                  reads=[rqf, rkm], writes=[rsb])
                    for j in range(4):
                        cur = (4 * c + j) // 2
                        if cur > 0:
                            sc.cp('dve', scv[:, j, 0:cur], sb[:, j * 32:j * 32 + cur], reads=[rsb], writes=[rsel])
                    for j in range(4):
                        sc.op('dve', lambda e, j=j: e.max(out=m8[:, j * 8:(j + 1) * 8], in_=scv[:, j, :]), reads=[rsel], writes=[rsel])
                        sc.ts('dve', selw[:, j, :], scv[:, j, :], m8[:, j * 8 + 2:j * 8 + 3], None, ALU.is_ge, reads=[rsel], writes=[rsel])
                    sc.memset('pool', acc, 0.0, writes=racc)
                    obs = [None]

                    def s1(kt, c=c, h=h):
                        rel = kt - 4 * c
                        j0 = max(rel, 0)
                        c0 = 128 * j0
                        lb, rlb = lb_r.next()
                        sc.mm(lb[:, c0:512], kT[0:64, kt * 128:(kt + 1) * 128], qT[0:64, c * 512 + c0:(c + 1) * 512],
                              reads=[rk], writes=[rlb])
                        p, rp = p_r.next()
                        sc.act(p[:, c0:512], lb[:, c0:512], AF.Exp, reads=[rlb], writes=[rp], scale=0.125)
                        for j in range(j0, 4):
                            d = 4 * c + j - kt
                            if d == 0:
                                sc.tt('pool', p[:, j * 128:(j + 1) * 128], p[:, j * 128:(j + 1) * 128], E0(h), ALU.mult,
                                      reads=[rp, r_const], writes=[rp])
                            elif d == 1:
                                sc.tt('pool', p[:, j * 128:(j + 1) * 128], p[:, j * 128:(j + 1) * 128], E128(h), ALU.mult,
                                      reads=[rp, r_const], writes=[rp])
                        return (p, rp, j0)

                    def s2(kt, stt_, c=c, obs=obs):
                        p, rp, j0 = stt_
                        n = kt // 2
                        if kt % 2 == 0:
                            obs[0] = ob_r.next()
                        ob, rob = obs[0]
                        done = []
                        for j in range(j0, 4):
                            qt = 4 * c + j
                            stop = (kt % 2 == 1) or (kt == qt)
                            sc.mm(ob[:, j * 128:j * 128 + 65], p[:, j * 128:(j + 1) * 128], va[:, kt, :],
                                  start=(kt % 2 == 0 and j == j0), stop=stop, reads=[rp, rk], writes=[rob])
                            if stop:
                                done.append(j)
                        for j in done:
                            qt = 4 * c + j
                            if n == qt // 2:
                                sc.tt('dve', acc[:, j, :], ob[:, j * 128:j * 128 + 65], acc[:, j, :], ALU.add,
                                      reads=[rob, racc[j]], writes=[racc[j]])
                            else:
                                sc.stt('dve', acc[:, j, :], ob[:, j * 128:j * 128 + 65], selw[:, j, n:n + 1], acc[:, j, :],
                                       ALU.mult, ALU.add, reads=[rob, racc[j], rsel], writes=[racc[j]])

                    pipe(4 * c + 4, s1, s2)
                    sc.ts('dve', rl, acc[:, :, 64:65], TINY, None, ALU.max, reads=racc, writes=[rrl])
                    sc.op('dve', lambda e: e.reciprocal(rl, rl), reads=[rrl], writes=[rrl])
                    sc.tt('dve', o_tok[:, 4 * c:4 * c + 4, h * 64:(h + 1) * 64], acc[:, :, 0:64], rl.to_broadcast([128, 4, 64]), ALU.mult,
                          reads=racc + [rrl], writes=[rot])
            for c in range(NC):
                for fc in range(2):
                    tb, rtb = tb_r.next()
                    pb = tb.bitcast(BF16)
                    for j in range(4):
                        sc.tr(pb[:, j * 128:(j + 1) * 128], o_tok[:, 4 * c + j, fc * 128:(fc + 1) * 128], ident,
                              reads=[rot, r_const], writes=[rtb])
                    stg, rst = st_r.next()
                    sc.cp('act' if fc else 'dve', stg, pb[:, 0:512], reads=[rtb], writes=[rst])
                    sc.dma(STQ, oTd[256 + fc * 128:256 + (fc + 1) * 128, c * 512:(c + 1) * 512], stg, reads=[rst], writes=[r_oTd])
            sc.barrier()


        def phase_NSA(l):
            ar.top = base_top
            NCW = NCT * 128
            kcT2 = [ar.bf(NCW) for g in range(2)]
            vca = [v3(ar.bf(NCT * 65), NCT) for g in range(2)]
            gcol = ar.f32(8)
            r_cmp = Res('cmp')
            mark_p = ar.top
            w1sb = v3(ar.bf(32 * 256), 32)
            w2sb = v3(ar.bf(2 * 64), 2)
            TT = ar.bf(S)
            posf = ar.f32(64)
            posb = ar.bf(64)
            posT = ar.bf(32)
            pbias = ar.f32(2)
            xb = ar.f32(512)
            x2 = ar.f32(512)
            th = ar.f32(512)
            ghT = v3(ar.bf(2 * 512), 2)
            sqb = ar.bf(512)
            rs = ar.f32(512)
            sring = Ring([(ar.f32(2048), Res()) for i in range(2)])
            rw, rt, rx, rgh = Res('w'), Res('TT'), Res('x'), Res('gh')
            n = n_cmp
            for gi in range(6):
                for half in range(2):
                    sc.dma('sp', gcol[half * 64:(half + 1) * 64, gi:gi + 1],
                           AP(gains.tensor, (l * 6 + gi) * 64, [[1, 64], [1, 1]]), writes=[r_cmp])
            for g in range(2):
                sc.memset('pool', kcT2[g], 0.0, writes=[r_cmp])
                sc.memset('pool', vca[g], 0.0, writes=[r_cmp])
            for kv in range(2):
                w1v = cmp_w1[l, kv].rearrange("(i d) h -> d i h", d=64)
                for i0 in range(0, 32, 8):
                    sg, rsg = sring.next()
                    sc.dma('sp', v3(sg, 8)[0:64], w1v[:, i0:i0 + 8, :], writes=[rsg])
                    sc.cp('pool', w1sb[0:64, i0:i0 + 8, :], v3(sg, 8)[0:64], reads=[rsg], writes=[rw])
                sg, rsg = sring.next()
                sc.dma('sp', v3(sg[:, 0:128], 2), cmp_w2[l, kv].rearrange("(c p) d -> p c d", p=128), writes=[rsg])
                sc.cp('pool', w2sb, v3(sg[:, 0:128], 2), reads=[rsg], writes=[rw])
                sc.dma('sp', posf[0:32, :], cmp_pos[l, kv], writes=[rw])
                sc.cp('dve', posb[0:32, :], posf[0:32, :], reads=[rw], writes=[rw])
                pbk = banks[5].bitcast(BF16)
                sc.tr(pbk[0:64, 0:32], posb[0:32, 0:64], ident[0:32, 0:32], reads=[rw, r_const], writes=[bres[5]])
                sc.cp('dve', posT[0:64, :], pbk[0:64, 0:32], reads=[bres[5]], writes=[rw])
                for hc in range(2):
                    for i in range(32):
                        sc.mm(banks[6][:, hc:hc + 1], w1sb[0:64, i, hc * 128:(hc + 1) * 128], posT[0:64, i:i + 1],
                              start=(i == 0 and hc == 0), stop=(i == 31), reads=[rw], writes=[bres[6]])
                sc.cp('dve', pbias, banks[6][:, 0:2], reads=[bres[6]], writes=[rw])
                for g in range(2):
                    base = (1536 if kv == 0 else 1664) + 64 * g
                    sc.dma('sp', TT[0:64, :], fmd[base:base + 64, :], writes=[rt])
                    for hc in range(2):
                        bk, rb = banks[hc], bres[hc]
                        for i in range(32):
                            sc.mm(bk[:, 0:n], w1sb[0:64, i, hc * 128:(hc + 1) * 128], TT[0:64, i:i + 16 * (n - 1) + 1:16],
                                  start=(i == 0), stop=(i == 31), reads=[rw, rt], writes=[rb])
                        sc.ts('dve', xb[:, 0:n], bk[:, 0:n], pbias[:, hc:hc + 1], None, ALU.add, reads=[rb, rw], writes=[rx])
                        sc.tt('dve', x2[:, 0:n], xb[:, 0:n], xb[:, 0:n], ALU.mult, reads=[rx], writes=[rx])
                        sc.ts('dve', x2[:, 0:n], x2[:, 0:n], 0.044715, 1.0, ALU.mult, ALU.add, reads=[rx], writes=[rx])
                        sc.tt('dve', x2[:, 0:n], x2[:, 0:n], xb[:, 0:n], ALU.mult, reads=[rx], writes=[rx])
                        sc.act(th[:, 0:n], x2[:, 0:n], AF.Tanh, reads=[rx], writes=[rx], scale=0.7978845608028654)
                        sc.ts('dve', xb[:, 0:n], xb[:, 0:n], 0.5, None, ALU.mult, reads=[rx], writes=[rx])
                        sc.stt('dve', ghT[:, hc, 0:n], th[:, 0:n], 1.0, xb[:, 0:n], ALU.add, ALU.mult, reads=[rx], writes=[rgh])
                    if kv == 0:
                        for hc in range(2):
                            sc.mm(banks[2][0:64, 0:n], w2sb[:, hc, :], ghT[:, hc, 0:n], start=(hc == 0), stop=(hc == 1),
                                  reads=[rw, rgh], writes=[bres[2]])
                        sc.act(sqb[0:64, 0:n], banks[2][0:64, 0:n], AF.Square, reads=[bres[2]], writes=[rx])
                        sc.mm(banks[3][0:64, 0:n], onesb[0:64, 0:64], sqb[0:64, 0:n], reads=[rx, r_const], writes=[bres[3]])
                        sc.act(rs[0:64, 0:n], banks[3][0:64, 0:n], AF.Ln, reads=[bres[3]], writes=[rx], scale=1.0 / 64, bias=EPS)
                        sc.act(rs[0:64, 0:n], rs[0:64, 0:n], AF.Exp, reads=[rx], writes=[rx], scale=-0.5)
                        sc.stt('dve', kcT2[g][0:64, 0:n], banks[2][0:64, 0:n], gcol[0:64, 3:4], rs[0:64, 0:n], ALU.mult, ALU.mult,
                               reads=[bres[2], rx, r_cmp], writes=[r_cmp])
                        sc.dma('sp', kcT2[g][64:128, :], kcT2[g][0:64, :], reads=[r_cmp], writes=[r_cmp])
                    else:
                        for nt in range(NCT):
                            rows = min(128, n - nt * 128)
                            for hc in range(2):
                                sc.mm(banks[2][0:rows, 0:64], ghT[:, hc, nt * 128:nt * 128 + rows], w2sb[:, hc, :],
                                      start=(hc == 0), stop=(hc == 1), reads=[rw, rgh], writes=[bres[2]])
                            sc.cp('dve', vca[g][0:rows, nt, 0:64], banks[2][0:rows, 0:64], reads=[bres[2]], writes=[r_cmp])
                            sc.memset('pool', vca[g][0:rows, nt, 64:65], 1.0, writes=[r_cmp])
            sc.barrier()
            for g in range(2):
                ar.top = mark_p
                qT = v3(ar.bf(2 * S), 2)
                ksT2 = ar.bf(S)
                kwT2 = ar.bf(S)
                vsa = v3(ar.bf(NT * 65), NT)
                vwa = v3(ar.bf(NT * 65), NT)
                G = [ar.bf(2560) for hh in range(4)]
                Bm = ar.bf(S)
                ovl = v3(ar.bf(NCW), NCT)
                rk = Res('kqv')
                hk_r = Ring([(ar.bf(512), Res()) for i in range(2)])
                gtb = ar.bf(4 * 48)
                gt3 = v3(gtb.bitcast(F32), 4)
                rgt = Res('gt')
                impacc = v3(ar.f32(512), 4)
                rimp = Res('imp')
                itmp = v3(ar.f32(512), 4)
                ritmp = Res()
                score = v3(ar.f32(512), 4)
                wk = v3(ar.f32(512), 4)
                m8a = ar.f32(32)
                m8b = ar.f32(32)
                selb = v3(ar.bf(512), 4)
                selT = ar.bf(512)
                rsel = Res('sel')
                oc = ar.f32(4 * 4 * 64)
                roc = Res('oc')
                rlc = ar.f32(16)
                rls = v3(ar.f32(4), 4)
                rlw = v3(ar.f32(4), 4)
                cfs = v3(ar.f32(4), 4)
                cfw = v3(ar.f32(4), 4)
                rcoef = Res('coef')
                ot1 = v3(ar.f32(256), 4)
                ot2 = v3(ar.f32(256), 4)
                ot3 = v3(ar.f32(256), 4)
                rot1, rot2, rot3 = Res(), Res(), Res()
                o_tok = v3(ar.bf(4 * 256), 4)
                rotk = Res('otok')
                p_r = Ring([(ar.bf(512), Res()) for i in range(LAG + 2)])
                ps_r = Ring([(ar.bf(512), Res()) for i in range(4 * (LAG + 2))])
                mk_r = Ring([(ar.bf(512), Res()) for i in range(3)])
                st_r = Ring([(ar.bf(512), Res()) for i in range(2)])
                lb_r = Ring([(banks[i], bres[i]) for i in (0, 1)])
                mb_r = Ring([(banks[i], bres[i]) for i in (2,)])
                a1_r = Ring([(banks[i], bres[i]) for i in (3, 6)])
                a2_r = Ring([(banks[i], bres[i]) for i in (4, 7)])
                tb_r = Ring([(banks[i], bres[i]) for i in (5,)])
                r_oTd = Res()
                for hh in range(4):
                    half, a = hh // 2, hh % 2
                    r0 = 1024 + 64 * (4 * g + hh)
                    sc.dma('sp', qT[half * 64:(half + 1) * 64, a, :], fmd[r0:r0 + 64, :], writes=[rk])
                for half in range(2):
                    sc.dma('sp', ksT2[half * 64:(half + 1) * 64, :], fmd[1792 + 64 * g:1792 + 64 * g + 64, :], writes=[rk])
                    sc.dma('sp', kwT2[half * 64:(half + 1) * 64, :], fmd[1920 + 64 * g:1920 + 64 * g + 64, :], writes=[rk])
                sc.dma('sp', vsa[:, :, 0:64], tmvd[:, 512 + 64 * g:512 + 64 * g + 64].rearrange("(t p) d -> p t d", p=128), writes=[rk])
                sc.dma('sp', vwa[:, :, 0:64], tmvd[:, 640 + 64 * g:640 + 64 * g + 64].rearrange("(t p) d -> p t d", p=128), writes=[rk])
                sc.memset('pool', vsa[:, :, 64:65], 1.0, writes=[rk])
                sc.memset('pool', vwa[:, :, 64:65], 1.0, writes=[rk])
                sc.dma('sp', ovl, v3(cb_in[:, CB_GLOBAL:CB_GLOBAL + NCW], NCT), writes=[rk])
                sc.dma('sp', Bm, cb_in[:, CB_GLOBAL + NCW:CB_GLOBAL + NCW + S], writes=[rk])
                for hh in range(4):
                    hrow = 4 + 4 * g + hh
                    for idx in range(5):
                        hk, rhk = hk_r.next()
                        src = AP(gvd.tensor, hrow * GL + GOFF - 2063 + 512 * idx, [[16, 128], [1, 512]])
                        sc.dma('sp', hk, src, writes=[rhk])
                        tb, rtb = tb_r.next()
                        sc.mm(tb, Jm, hk, reads=[rhk, r_const], writes=[rtb])
                        sc.cp('dve', G[hh][:, idx * 512:(idx + 1) * 512], tb, reads=[rtb], writes=[rk])
                for c in range(NC):
                    sc.dma('sp', v3(gtb, 4), tmvd[c * 512:(c + 1) * 512, 768:816].rearrange("(t p) w -> p t w", p=128), writes=[rgt])
                    sc.memset('pool', impacc, 0.0, writes=[rimp])
                    nts = [nt for nt in range(NCT) if c - 4 * nt >= 0]
                    oc4 = oc.rearrange("p (h j d) -> p h j d", h=4, j=4)
                    rlc4 = rlc.rearrange("p (h j o) -> p h j o", h=4, o=1)
                    for hh in range(4):
                        half, a = hh // 2, hh % 2
                        hs = slice(half * 64, half * 64 + 64)
                        ocb, rocb = a1_r.next()
                        ib, rib = a2_r.next()
                        def s1(ii, c=c, hh=hh, hs=hs, a=a, nts=nts):
                            nt = nts[ii]
                            lb, rlb = lb_r.next()
                            sc.mm(lb, kcT2[g][hs, nt * 128:(nt + 1) * 128], qT[hs, a, c * 512:(c + 1) * 512], reads=[r_cmp, rk], writes=[rlb])
                            pc, rpc = p_r.next()
                            sc.act(pc, lb, AF.Exp, reads=[rlb], writes=[rpc], scale=0.125)
                            idx = c - 4 * nt
                            if idx < 5:
                                sc.tt('pool', pc, pc, G[hh][:, idx * 512:(idx + 1) * 512], ALU.mult, reads=[rpc, rk], writes=[rpc])
                            return (pc, rpc)

                        def s2(ii, stt_, nts=nts, ocb=ocb, rocb=rocb, ib=ib, rib=rib):
                            pc, rpc = stt_
                            nt = nts[ii]
                            for j in range(4):
                                sc.mm(ocb[:, j * 128:j * 128 + 65], pc[:, j * 128:(j + 1) * 128], vca[g][:, nt, :],
                                      start=(ii == 0 and j == 0), stop=(nt == nts[-1]), reads=[rpc, r_cmp], writes=[rocb])
                            for j in range(4):
                                sc.mm(ib[:, j * 128:(j + 1) * 128], pc[:, j * 128:(j + 1) * 128], ovl[:, nt, :],
                                      start=(ii == 0 and j == 0), stop=(nt == nts[-1]), reads=[rpc, rk], writes=[rib])

                        pipe(len(nts), s1, s2)
                        ocb3 = v3(ocb, 4)
                        sc.ts('dve', rlc4[:, hh], ocb3[:, :, 64:65], TINY, None, ALU.max, reads=[rocb], writes=[roc])
                        sc.op('dve', lambda e, hh=hh: e.reciprocal(rlc4[:, hh], rlc4[:, hh]), reads=[roc], writes=[roc])
                        sc.tt('dve', oc4[:, hh], ocb3[:, :, 0:64], rlc4[:, hh].to_broadcast([128, 4, 64]), ALU.mult,
                              reads=[rocb, roc], writes=[roc])
                        sc.tt('dve', itmp, v3(ib, 4), rlc4[:, hh].to_broadcast([128, 4, 128]), ALU.mult, reads=[rib, roc], writes=[ritmp])
                        sc.tt('pool', impacc, impacc, itmp, ALU.add, reads=[rimp, ritmp], writes=[rimp])
                    for j in range(4):
                        off = 126 - 2 * (4 * c + j)
                        sc.tt('dve', score[:, j, :], impacc[:, j, :], cslide[:, off:off + 128], ALU.add, reads=[rimp, r_const], writes=[rsel])
                    sc.memset('dve', score[:, :, 0:1], BIG, writes=[rsel])
                    for j in range(4):
                        sc.op('dve', lambda e, j=j: e.max(out=m8a[:, j * 8:(j + 1) * 8], in_=score[:, j, :]), reads=[rsel], writes=[rsel])
                        sc.op('dve', lambda e, j=j: e.match_replace(out=wk[:, j, :], in_to_replace=m8a[:, j * 8:(j + 1) * 8],
                                                                    in_values=score[:, j, :], imm_value=-3e38), reads=[rsel], writes=[rsel])
                        sc.op('dve', lambda e, j=j: e.max(out=m8b[:, j * 8:(j + 1) * 8], in_=wk[:, j, :]), reads=[rsel], writes=[rsel])
                        sc.ts('dve', selb[:, j, :], score[:, j, :], m8b[:, j * 8 + 7:j * 8 + 8], None, ALU.is_ge, reads=[rsel], writes=[rsel])
                    tb, rtb = tb_r.next()
                    pb = tb.bitcast(BF16)
                    for j in range(4):
                        sc.tr(pb[:, j * 128:(j + 1) * 128], selb[:, j, :], ident, reads=[rsel, r_const], writes=[rtb])
                    sc.cp('dve', selT, pb[:, 0:512], reads=[rtb], writes=[rsel])
                    osbs = [(banks[i], bres[i]) for i in (3, 4, 6, 7)]

                    def s1(kt, c=c):
                        j0 = max(kt - 4 * c, 0)
                        c0 = 128 * j0
                        mb, rmb = mb_r.next()
                        sc.mm(mb[:, c0:512], Bm[:, kt * 128:(kt + 1) * 128], selT[:, c0:512], reads=[rk, rsel], writes=[rmb])
                        mk, rmk = mk_r.next()
                        sc.cp('act', mk[:, c0:512], mb[:, c0:512], reads=[rmb], writes=[rmk])
                        outs = []
                        for hh in range(4):
                            half, a = hh // 2, hh % 2
                            hs = slice(half * 64, half * 64 + 64)
                            hrow = 4 + 4 * g + hh
                            lb, rlb = lb_r.next()
                            sc.mm(lb[:, c0:512], ksT2[hs, kt * 128:(kt + 1) * 128], qT[hs, a, c * 512 + c0:(c + 1) * 512], reads=[rk], writes=[rlb])
                            ps, rps = ps_r.next()
                            sc.act(ps[:, c0:512], lb[:, c0:512], AF.Exp, reads=[rlb], writes=[rps], scale=0.125)
                            sc.tt('dve', ps[:, c0:512], ps[:, c0:512], mk[:, c0:512], ALU.mult, reads=[rps, rmk], writes=[rps])
                            for j in range(j0, 4):
                                d = 4 * c + j - kt
                                if d in (0, 1):
                                    sc.tt('pool', ps[:, j * 128:(j + 1) * 128], ps[:, j * 128:(j + 1) * 128],
                                          E0(hrow) if d == 0 else E128(hrow), ALU.mult, reads=[rps, r_const], writes=[rps])
                            outs.append((ps, rps))
                        return (outs, j0)

                    def s2(kt, stt_, c=c):
                        outs, j0 = stt_
                        for hh in range(4):
                            ps, rps = outs[hh]
                            osb, rosb = osbs[hh]
                            for j in range(j0, 4):
                                sc.mm(osb[:, j * 128:j * 128 + 65], ps[:, j * 128:(j + 1) * 128], vsa[:, kt, :],
                                      start=(kt == 0 and j == j0), stop=(kt == 4 * c + j), reads=[rps, rk], writes=[rosb])

                    pipe(4 * c + 4, s1, s2)
                    for hh in range(4):
                        half, a = hh // 2, hh % 2
                        hs = slice(half * 64, half * 64 + 64)
                        hrow = 4 + 4 * g + hh
                        osb, rosb = osbs[hh]
                        owb, rowb = banks[5], bres[5]
                        kts = list(range(max(4 * c - 4, 0), 4 * c + 4))

                        def s1(ii, c=c, hs=hs, a=a, hrow=hrow, kts=kts):
                            kt = kts[ii]
                            jlo = max(kt - 4 * c, 0)
                            jhi = min(kt + 4 - 4 * c, 3)
                            cs_ = slice(128 * jlo, 128 * (jhi + 1))
                            lb, rlb = lb_r.next()
                            sc.mm(lb[:, cs_], kwT2[hs, kt * 128:(kt + 1) * 128], qT[hs, a, c * 512 + 128 * jlo:c * 512 + 128 * (jhi + 1)],
                                  reads=[rk], writes=[rlb])
                            pw, rpw = p_r.next()
                            sc.act(pw[:, cs_], lb[:, cs_], AF.Exp, reads=[rlb], writes=[rpw], scale=0.125)
                            for j in range(jlo, jhi + 1):
                                d = 4 * c + j - kt
                                if d in (0, 1, 4):
                                    mk = E0(hrow) if d == 0 else (E128(hrow) if d == 1 else m512)
                                    sc.tt('pool', pw[:, j * 128:(j + 1) * 128], pw[:, j * 128:(j + 1) * 128], mk, ALU.mult,
                                          reads=[rpw, r_const], writes=[rpw])
                            return (pw, rpw, jlo, jhi)

                        def s2(ii, stt_, c=c, kts=kts, owb=owb, rowb=rowb):
                            pw, rpw, jlo, jhi = stt_
                            kt = kts[ii]
                            for j in range(jlo, jhi + 1):
                                sc.mm(owb[:, j * 128:j * 128 + 65], pw[:, j * 128:(j + 1) * 128], vwa[:, kt, :],
                                      start=(ii == 0 and j == jlo), stop=(kt == 4 * c + j), reads=[rpw, rk], writes=[rowb])

                        pipe(len(kts), s1, s2)
                        osb3 = v3(osb, 4)
                        owb3 = v3(owb, 4)
                        sc.ts('dve', rls, osb3[:, :, 64:65], TINY, None, ALU.max, reads=[rosb], writes=[rcoef])
                        sc.op('dve', lambda e: e.reciprocal(rls, rls), reads=[rcoef], writes=[rcoef])
                        sc.ts('dve', rlw, owb3[:, :, 64:65], TINY, None, ALU.max, reads=[rowb], writes=[rcoef])
                        sc.op('dve', lambda e: e.reciprocal(rlw, rlw), reads=[rcoef], writes=[rcoef])
                        gi = g * 4 + hh
                        sc.tt('dve', cfs, rls, gt3[:, :, 8 + gi:9 + gi], ALU.mult, reads=[rcoef, rgt], writes=[rcoef])
                        sc.tt('dve', cfw, rlw, gt3[:, :, 16 + gi:17 + gi], ALU.mult, reads=[rcoef, rgt], writes=[rcoef])
                        sc.tt('dve', ot1, oc4[:, hh], gt3[:, :, gi:gi + 1].to_broadcast([128, 4, 64]), ALU.mult, reads=[roc, rgt], writes=[rot1])
                        sc.tt('dve', ot2, osb3[:, :, 0:64], cfs.to_broadcast([128, 4, 64]), ALU.mult, reads=[rosb, rcoef], writes=[rot2])
                        sc.tt('dve', ot3, owb3[:, :, 0:64], cfw.to_broadcast([128, 4, 64]), ALU.mult, reads=[rowb, rcoef], writes=[rot3])
                        sc.tt('pool', ot1, ot1, ot2, ALU.add, reads=[rot1, rot2], writes=[rot1])
                        sc.tt('pool', o_tok[:, :, hh * 64:(hh + 1) * 64], ot1, ot3, ALU.add, reads=[rot1, rot3], writes=[rotk])
                    for fc in range(2):
                        tb, rtb = tb_r.next()
                        pb = tb.bitcast(BF16)
                        for j in range(4):
                            sc.tr(pb[:, j * 128:(j + 1) * 128], o_tok[:, j, fc * 128:(fc + 1) * 128], ident, reads=[rotk, r_const], writes=[rtb])
                        stg, rst = st_r.next()
                        sc.cp('act', stg, pb[:, 0:512], reads=[rtb], writes=[rst])
                        r0 = 512 + g * 256 + fc * 128
                        sc.dma(STQ, oTd[r0:r0 + 128, c * 512:(c + 1) * 512], stg, reads=[rst], writes=[r_oTd])
                sc.barrier()

        def phase_C1(l, xsrc):
            ar.top = base_top
            Wg = v3(ar.bf(8 * 3072), 8)
            Wb = v3(ar.bf(8 * D), 8)
            Wo = v3(ar.bf(8 * D), 8)
            rW = Res('W')
            mark = ar.top
            sring = Ring([(ar.f32(2048), Res()) for i in range(4)])
            for c in range(8):
                load_cast(Wg[:, c, :], w_in[l, c * 128:(c + 1) * 128, NQKV:NIN], 3072, sring, rW)
                load_cast(Wb[:, c, :], w_branch[l, c * 128:(c + 1) * 128, :], D, sring, rW)
                load_cast(Wo[:, c, :], w_out[l, c * 128:(c + 1) * 128, :], D, sring, rW)
            sc.barrier()
            ar.top = mark
            xs_r = Ring([(v3(ar.f32(4 * D), 4), Res()) for i in range(2)])
            hT_r = Ring([(v3(ar.bf(8 * 512), 8), Res()) for i in range(2)])
            oT_r = Ring([(v3(ar.bf(8 * 512), 8), Res()) for i in range(2)])
            mixT = v3(ar.bf(8 * 512), 8)
            rmix = Res('mix')
            g_r = Ring([(ar.f32(512), Res()) for i in range(3)])
            t_r = Ring([(ar.f32(512), Res()) for i in range(4)])
            gb_r = Ring([(banks[i], bres[i]) for i in (0, 1, 2)])
            bb_r = Ring([(banks[i], bres[i]) for i in (3, 4, 5)])
            yb_r = Ring([(banks[i], bres[i]) for i in (6, 7)])
            r_x1d = Res()
            branch_k = ((0, 1), (2, 3), (4, 5, 6, 7))
            for ci in range(NC):
                xs, rx = xs_r.next()
                sc.dma('sp', xs, xsrc[ci * 512:(ci + 1) * 512, :].rearrange("(t p) d -> p t d", p=128), writes=[rx])
                hT, rhT = hT_r.next()
                sc.dma('sp', hT, hTd.rearrange("(c p) s -> p c s", p=128)[:, :, ci * 512:(ci + 1) * 512], writes=[rhT])
                oT, roT = oT_r.next()
                sc.dma('sp', oT, oTd.rearrange("(c p) s -> p c s", p=128)[:, :, ci * 512:(ci + 1) * 512], writes=[roT])
                for dm in range(8):
                    ts_ = []
                    for br in range(3):
                        gb, rgb = gb_r.next()
                        col = br * D + dm * 128
                        for c in range(8):
                            sc.mm(gb, Wg[:, c, col:col + 128], hT[:, c, :], start=(c == 0), stop=(c == 7),
                                  reads=[rW, rhT], writes=[rgb])
                        g, rg = g_r.next()
                        sc.act(g, gb, AF.Sigmoid, reads=[rgb], writes=[rg])
                        bb, rbb = bb_r.next()
                        ks = branch_k[br]
                        for i, k in enumerate(ks):
                            sc.mm(bb, Wb[:, k, dm * 128:(dm + 1) * 128], oT[:, k, :], start=(i == 0), stop=(i == len(ks) - 1),
                                  reads=[rW, roT], writes=[rbb])
                        t, rt = t_r.next()
                        sc.tt('dve', t, bb, g, ALU.mult, reads=[rbb, rg], writes=[rt])
                        ts_.append((t, rt))
                    sc.tt('pool', ts_[0][0], ts_[0][0], ts_[1][0], ALU.add, reads=[ts_[0][1], ts_[1][1]], writes=[ts_[0][1]])
                    sc.tt('pool', mixT[:, dm, :], ts_[0][0], ts_[2][0], ALU.add, reads=[ts_[0][1], ts_[2][1]], writes=[rmix])
                for t in range(4):
                    for half in range(2):
                        yb, ryb = yb_r.next()
                        for k in range(8):
                            sc.mm(yb, mixT[:, k, t * 128:(t + 1) * 128], Wo[:, k, half * 512:(half + 1) * 512],
                                  start=(k == 0), stop=(k == 7), reads=[rmix, rW], writes=[ryb])
                        sc.tt('dve', xs[:, t, half * 512:(half + 1) * 512], xs[:, t, half * 512:(half + 1) * 512], yb, ALU.add,
                              reads=[rx, ryb], writes=[rx])
                sc.dma(STQ, x1d[ci * 512:(ci + 1) * 512, :].rearrange("(t p) d -> p t d", p=128), xs, reads=[rx], writes=[r_x1d])
            sc.barrier()

        def phase_C2(l, xdst):
            ar.top = base_top
            Wgu = v3(ar.bf(8 * 2 * DFF), 8)
            Wd = v3(ar.bf(22 * D), 22)
            gF = ar.f32(D)
            rW = Res('W')
            mark = ar.top
            sring = Ring([(ar.f32(2048), Res()) for i in range(4)])
            for c in range(8):
                load_cast(Wgu[:, c, :], w_gate_up[l, c * 128:(c + 1) * 128, :], 2 * DFF, sring, rW)
            for f in range(22):
                load_cast(Wd[:, f, :], w_down[l, f * 128:(f + 1) * 128, :], D, sring, rW)
            sc.dma('sp', gF, AP(ffn_norm.tensor, l * D, [[0, 128], [1, D]]), writes=[rW])
            sc.barrier()
            ar.top = mark
            xs_r = Ring([(v3(ar.f32(2 * D), 2), Res()) for i in range(2)])
            hb = v3(ar.bf(2 * D), 2)
            rhb = Res()
            sq = ar.f32(D)
            ssq = ar.f32(4)
            rstd = ar.f32(4)
            rtmp = Res()
            hT = v3(ar.bf(8 * 256), 8)
            rhT = Res()
            actT = v3(ar.bf(22 * 256), 22)
            ract = Res()
            sg_r = Ring([(ar.f32(256), Res()) for i in range(2)])
            tb_r = Ring([(banks[i], bres[i]) for i in (0, 1)])
            fb_r = Ring([(banks[i], bres[i]) for i in (2, 3, 4)])
            yb_r = Ring([(banks[i], bres[i]) for i in (5, 6, 7)])
            r_out = Res()
            for ci in range(S // 256):
                xs, rx = xs_r.next()
                sc.dma('sp', xs, x1d[ci * 256:(ci + 1) * 256, :].rearrange("(t p) d -> p t d", p=128), writes=[rx])
                rmsnorm_tm(xs, rx, gF, rW, hb, rhb, 2, sq, ssq, rstd, rtmp)
                transpose_to(hb, rhb, hT, rhT, 2, tb_r, ('dve', 'act'))
                for f in range(22):
                    fb, rfb = fb_r.next()
                    for half, col in ((0, f * 128), (1, DFF + f * 128)):
                        for c in range(8):
                            sc.mm(fb[:, half * 256:(half + 1) * 256], Wgu[:, c, col:col + 128], hT[:, c, :],
                                  start=(c == 0), stop=(c == 7), reads=[rW, rhT], writes=[rfb])
                    sg, rsg = sg_r.next()
                    sc.act(sg, fb[:, 0:256], AF.Silu, reads=[rfb], writes=[rsg])
                    sc.tt('dve', actT[:, f, :], sg, fb[:, 256:512], ALU.mult, reads=[rsg, rfb], writes=[ract])
                for t in range(2):
                    for half in range(2):
                        yb, ryb = yb_r.next()
                        for f in range(22):
                            sc.mm(yb, actT[:, f, t * 128:(t + 1) * 128], Wd[:, f, half * 512:(half + 1) * 512],
                                  start=(f == 0), stop=(f == 21), reads=[ract, rW], writes=[ryb])
                        sc.tt('dve', xs[:, t, half * 512:(half + 1) * 512], xs[:, t, half * 512:(half + 1) * 512], yb, ALU.add,
                              reads=[rx, ryb], writes=[rx])
                sc.dma(STQ, xdst[ci * 256:(ci + 1) * 256, :].rearrange("(t p) d -> p t d", p=128), xs, reads=[rx], writes=[r_out])
            sc.barrier()

        PH = {}
        exec_phases = phases
        prologue()
        for l in range(depth):
            xsrc = x_in if l == 0 else xmd
            xdst = out if l == depth - 1 else xmd
            if exec_phases is None or 'A' in exec_phases:
                phase_A(l, xsrc)
            if exec_phases is None or 'SB' in exec_phases:
                phase_SB(l)
            if exec_phases is None or 'MB' in exec_phases:
                phase_MB(l)
            if exec_phases is None or 'NSA' in exec_phases:
                phase_NSA(l)
            if inject:
                sc.dma('sp', oTd, oT_in, writes=[Res()])
                sc.barrier()
            if exec_phases is None or 'C1' in exec_phases:
                phase_C1(l, xsrc)
            if exec_phases is None or 'C2' in exec_phases:
                phase_C2(l, xdst)
        sc.barrier()
        sc.emit()
    return nc


def core_inputs(inputs, b, S, depth, cbh, cfh):
    f = lambda a: np.ascontiguousarray(np.asarray(a, dtype=np.float32))
    gains = np.stack([f(inputs['moba_q_norm']), f(inputs['moba_k_norm']), f(inputs['nsa_q_norm']),
                      f(inputs['nsa_k_norm'])[:, 0], f(inputs['nsa_k_norm'])[:, 1], f(inputs['nsa_k_norm'])[:, 2]], axis=1)
    m = {
        'x': f(inputs['x'][b, :S]),
        'rel_bias': f(inputs['rel_bias']),
        'attn_norm': f(inputs['attn_norm'])[:depth],
        'w_in': f(inputs['w_in'])[:depth],
        'gains': np.ascontiguousarray(gains[:depth]),
        'nsa_cmp_pos': f(inputs['nsa_cmp_pos'])[:depth],
        'nsa_cmp_w1': f(inputs['nsa_cmp_w1'])[:depth],
        'nsa_cmp_w2': f(inputs['nsa_cmp_w2'])[:depth],
        'w_branch': f(inputs['w_branch'])[:depth],
        'w_out': f(inputs['w_out'])[:depth],
        'ffn_norm': f(inputs['ffn_norm'])[:depth],
        'w_gate_up': f(inputs['w_gate_up'])[:depth],
        'w_down': f(inputs['w_down'])[:depth],
        'cb': cbh,
        'cf': cfh,
    }
    return m


def kernel(**inputs):
    x = np.asarray(inputs['x'])
    B, S, _ = x.shape
    depth = int(np.asarray(inputs['w_in']).shape[0])
    cbh, cfh, _, _ = host_consts(S)
    nc = build(S, depth)
    in_maps = [core_inputs(inputs, b, S, depth, cbh, cfh) for b in range(B)]
    res = run_bass_kernel_spmd(nc, in_maps, core_ids=list(range(B)))
    return np.stack([np.asarray(r['out'], dtype=np.float32) for r in res.results], axis=0)
```

```python
import math
from contextlib import ExitStack

import numpy as np
import ml_dtypes
import concourse.bass as bass
import concourse.mybir as mybir
from concourse.bass_types import AP
from concourse.bass_utils import run_bass_kernel_spmd

F32 = mybir.dt.float32
BF16 = mybir.dt.bfloat16
AF = mybir.ActivationFunctionType
ALU = mybir.AluOpType
AX = mybir.AxisListType
ENGS = ('pe', 'act', 'dve', 'pool', 'sp')
NDMA = 8
STQ = 'sp'

D = 1024
NIN = 5912
NQKV = 2840
DFF = 2816
EPS = 1e-6
NEG = -1e30
BIG = 1e30
TINY = 1e-30
GOFF = 2304
GL = 4864
FM_COLS = [0, 128, 256, 384, 768, 896, 1024, 1152, 1536, 1664, 1792, 1920, 2048, 2176, 2304, 2560]
FM_KIND = [None, None, None, None, 0, 0, 1, 1, 2, 2, 2, 2, None, None, 4, 5]


class Res:
    __slots__ = ('name', 'lw', 'rd')

    def __init__(self, name=''):
        self.name = name
        self.lw = None
        self.rd = {}


class Ring:
    def __init__(self, items):
        self.items = items
        self.i = 0

    def next(self):
        it = self.items[self.i]
        self.i = (self.i + 1) % len(self.items)
        return it


class Sched:
    def __init__(self, nc, stack):
        self.nc = nc
        self.q = {e: [] for e in ENGS}
        self.cnt = {e: 0 for e in ENGS}
        self.waited = {e: {} for e in ENGS}
        self.sem = {}
        for e in ENGS:
            self.sem[('e', e)] = stack.enter_context(nc.semaphore('s_' + e))
        self.dcnt = {}
        self.drr = {}
        for e in ('sp', 'act', 'pool'):
            self.drr[e] = 0
            for i in range(NDMA):
                k = ('d', e, i)
                self.sem[k] = stack.enter_context(nc.semaphore('d_%s%d' % (e, i)))
                self.dcnt[k] = 0

    def _waits(self, eng, reads, writes, extra=()):
        deps = {}

        def add(ev):
            if ev is None:
                return
            k, v = ev
            if deps.get(k, 0) < v:
                deps[k] = v
        for r in reads:
            add(r.lw)
        for w in writes:
            add(w.lw)
            for k, v in w.rd.items():
                add((k, v))
        for ev in extra:
            add(ev)
        out = []
        wd = self.waited[eng]
        for k, v in deps.items():
            if k == ('e', 'pe') and eng == 'pe':
                continue
            if wd.get(k, 0) >= v:
                continue
            wd[k] = v
            out.append((k, v))
        return out

    def _mark(self, ev, reads, writes):
        k, v = ev
        for r in reads:
            if r.rd.get(k, 0) < v:
                r.rd[k] = v
        for w in writes:
            w.lw = ev
            w.rd = {}

    def op(self, eng, fn, reads=(), writes=()):
        waits = self._waits(eng, reads, writes)
        self.cnt[eng] += 1
        ev = (('e', eng), self.cnt[eng])
        self.q[eng].append((fn, waits, ('e', eng), 1))
        self._mark(ev, reads, writes)
        return ev

    def dma(self, eng, out, in_, reads=(), writes=(), **kw):
        i = self.drr[eng]
        self.drr[eng] = (i + 1) % NDMA
        k = ('d', eng, i)
        prev = (k, 16 * self.dcnt[k]) if self.dcnt[k] else None
        waits = self._waits(eng, reads, writes, extra=(prev,) if prev else ())
        self.dcnt[k] += 1
        ev = (k, 16 * self.dcnt[k])
        self.q[eng].append((lambda e: e.dma_start(out=out, in_=in_, **kw), waits, k, 16))
        self._mark(ev, reads, writes)
        return ev

    def barrier(self):
        evs = [(('e', e), self.cnt[e]) for e in ENGS if self.cnt[e] > 0]
        evs += [(k, 16 * c) for k, c in self.dcnt.items() if c > 0]
        for e in ENGS:
            wd = self.waited[e]
            waits = []
            for k, v in evs:
                if k == ('e', e):
                    continue
                if wd.get(k, 0) >= v:
                    continue
                wd[k] = v
                waits.append((k, v))
            self.q[e].append((None, waits, None, 0))

    def emit(self):
        nc = self.nc
        with nc.Block() as block:
            def run(name):
                def f(e):
                    for fn, waits, k, inc in self.q[name]:
                        for wk, wv in waits:
                            e.wait_ge(self.sem[wk], wv)
                        if fn is not None:
                            fn(e).then_inc(self.sem[k], inc)
                return f
            block.tensor(run('pe'))
            block.scalar(run('act'))
            block.vector(run('dve'))
            block.gpsimd(run('pool'))
            block.sync(run('sp'))

    def mm(self, out, lhsT, rhs, start=True, stop=True, reads=(), writes=()):
        return self.op('pe', lambda e: e.matmul(out, lhsT, rhs, start=start, stop=stop), reads, writes)

    def tr(self, out, in_, ident, reads=(), writes=()):
        return self.op('pe', lambda e: e.transpose(out, in_, ident), reads, writes)

    def act(self, out, in_, func, reads=(), writes=(), **kw):
        return self.op('act', lambda e: e.activation(out, in_, func, **kw), reads, writes)

    def tt(self, eng, out, in0, in1, op, reads=(), writes=()):
        return self.op(eng, lambda e: e.tensor_tensor(out, in0, in1, op), reads, writes)

    def ts(self, eng, out, in0, s1, s2, op0, op1=None, reads=(), writes=()):
        if op1 is None:
            return self.op(eng, lambda e: e.tensor_scalar(out, in0, s1, s2, op0), reads, writes)
        return self.op(eng, lambda e: e.tensor_scalar(out, in0, s1, s2, op0, op1), reads, writes)

    def stt(self, eng, out, in0, scalar, in1, op0, op1, reads=(), writes=()):
        return self.op(eng, lambda e: e.scalar_tensor_tensor(out, in0, scalar, in1, op0, op1), reads, writes)

    def cp(self, eng, out, in_, reads=(), writes=()):
        if eng == 'act':
            return self.op('act', lambda e: e.copy(out, in_), reads, writes)
        return self.op(eng, lambda e: e.tensor_copy(out, in_), reads, writes)

    def memset(self, eng, ap, val, writes=()):
        return self.op(eng, lambda e: e.memset(ap, val), (), writes)


class Arena:
    def __init__(self, ap, size):
        self.ap = ap
        self.size = size
        self.top = 0

    def bf(self, n):
        a = self.top
        self.top += (n + 31) // 32 * 32
        assert self.top <= self.size, ('arena overflow', self.top, self.size)
        return self.ap[:, a:a + n]

    def f32(self, n):
        a = self.top
        self.top += (2 * n + 31) // 32 * 32
        assert self.top <= self.size, ('arena overflow', self.top, self.size)
        return self.ap[:, a:a + 2 * n].bitcast(F32)


def v3(ap, a):
    return ap.rearrange("p (a b) -> p a b", a=a)


LAG = 2
WARM = 0
NDUMMY = 0


def pipe(n, *stages, lag=LAG):
    ns = len(stages)
    st = [None] * n
    for t in range(n + lag * (ns - 1)):
        for k, f in enumerate(stages):
            i = t - k * lag
            if 0 <= i < n:
                st[i] = f(i, st[i]) if k else f(i)


def t5_bucket_np(d):
    n = np.maximum(d, 0)
    nf = np.maximum(n, 1).astype(np.float32)
    large = 16 + (np.log(nf / np.float32(16)) / np.float32(math.log(128 / 16)) * np.float32(16)).astype(np.int32)
    large = np.minimum(large, 31)
    return np.where(n < 16, n, large)


def host_consts(S):
    k = np.arange(128)[:, None]
    q = np.arange(128)[None, :]
    ident = (k == q)
    J = (k + q == 127)
    uincl = (k >= q)
    ones = np.ones((128, 128), bool)
    blk = (k // 64 == q // 64)
    maskL = (k < q)
    m512 = (q < k)
    zeros = np.zeros((128, 512), bool)
    n_cmp = (S - 32) // 16 + 1
    nct = (n_cmp + 127) // 128
    n = np.arange(nct * 128)[:, None]
    m = np.arange(128)[None, :]
    ovl = ((16 * n < 64 * m + 64) & (16 * n + 32 > 64 * m) & (n < n_cmp))
    ovl = ovl.reshape(nct, 128, 128).transpose(1, 0, 2).reshape(128, nct * 128)
    mm_ = np.arange(128)[:, None]
    xx = np.arange(S)[None, :]
    B = (xx // 64 == mm_)
    cb = np.concatenate([ident, J, uincl, ones, blk, maskL, m512, zeros, ovl, B], axis=1).astype(np.float32)
    cb = cb.astype(ml_dtypes.bfloat16)
    negmask = np.where(k >= q, -1e4, 0.0).astype(np.float32)
    y = np.arange(256)[None, :]
    rel = y - 126 - (k >= 64)
    cs = np.where((rel == 0) | (rel == -1), BIG, np.where(rel > 0, NEG, 0.0)).astype(np.float32)
    oh = np.zeros((128, 128), np.float32)
    b = t5_bucket_np(np.arange(128))
    oh[b, np.arange(128)] = 1.0
    cf = np.concatenate([negmask, cs, oh], axis=1).astype(np.float32)
    return cb, cf, n_cmp, nct


def build(S, depth, debug=False, phases=None, dbg=None, inject=False):
    NT = S // 128
    NC = S // 512
    cbh, cfh, n_cmp, NCT = host_consts(S)
    NCB = cbh.shape[1]
    NCF = cfh.shape[1]
    nc = bass.Bass("TRN2", target_bir_lowering=False)
    dt_in = lambda name, shape, dt=F32: nc.dram_tensor(name, shape, dt, kind="ExternalInput").ap()
    dbg_kind = "ExternalOutput" if debug else "Internal"
    scr = lambda name, shape, dt: nc.dram_tensor(name, shape, dt, kind=dbg_kind).ap()
    x_in = dt_in("x", [S, D])
    rel_bias = dt_in("rel_bias", [32, 12])
    attn_norm = dt_in("attn_norm", [depth, D])
    w_in = dt_in("w_in", [depth, D, NIN])
    gains = dt_in("gains", [depth, 6, 64])
    cmp_pos = dt_in("nsa_cmp_pos", [depth, 2, 32, 64])
    cmp_w1 = dt_in("nsa_cmp_w1", [depth, 2, 2048, 256])
    cmp_w2 = dt_in("nsa_cmp_w2", [depth, 2, 256, 64])
    w_branch = dt_in("w_branch", [depth, D, D])
    w_out = dt_in("w_out", [depth, D, D])
    ffn_norm = dt_in("ffn_norm", [depth, D])
    w_gate_up = dt_in("w_gate_up", [depth, D, 2 * DFF])
    w_down = dt_in("w_down", [depth, DFF, D])
    cb_in = dt_in("cb", [128, NCB], BF16)
    cf_in = dt_in("cf", [128, NCF])
    out = nc.dram_tensor("out", [S, D], F32, kind="ExternalOutput").ap()

    hTd = scr("hTd", [D, S], BF16)
    fmd = scr("fmd", [2048, S], BF16)
    tmvd = scr("tmvd", [S, 832], BF16)
    nsgd = scr("nsgd", [S, 32], F32)
    oTd = scr("oTd", [D, S], BF16)
    x1d = scr("x1d", [S, D], F32)
    xmd = scr("xmd", [S, D], F32)
    gvd = scr("gvd", [12, GL], BF16)
    oT_in = dt_in("oT_in", [D, S], BF16) if inject else None

    with ExitStack() as st:
        sc = Sched(nc, st)
        ASZ = 98304
        arena_ap = nc.alloc_sbuf_tensor("arena", [128, ASZ], BF16).ap()
        ar = Arena(arena_ap, ASZ)
        banks = [nc.alloc_psum_tensor("bank%d" % i, [128, 512], F32).ap() for i in range(8)]
        bres = [Res("bank%d" % i) for i in range(8)]

        CB_GLOBAL = 8 * 128 + 384
        cb = ar.bf(CB_GLOBAL)
        ident = cb[:, 0:128]
        Jm = cb[:, 128:256]
        uincl = cb[:, 256:384]
        onesb = cb[:, 384:512]
        blkones = cb[:, 512:640]
        maskL = cb[:, 640:768]
        m512 = cb[:, 768:896]
        zerob = cb[:, 896:1408]
        cf = ar.f32(384)
        negmask = cf[:, 0:128]
        cslide = cf[:, 128:384]
        Etab = ar.bf(24 * 128)
        r_const = Res('const')
        sc.dma('sp', cb, cb_in[:, 0:CB_GLOBAL], writes=[r_const])
        sc.dma('sp', cf, cf_in[:, 0:384], writes=[r_const])
        base_top = ar.top

        def E0(h):
            return Etab[:, (2 * h) * 128:(2 * h + 1) * 128]

        def E128(h):
            return Etab[:, (2 * h + 1) * 128:(2 * h + 2) * 128]

        def prologue():
            ar.top = base_top
            tb = ar.f32(12)
            oh = ar.f32(128)
            br = ar.f32(128)
            negc = ar.f32(1)
            gsb = ar.bf(GL)
            hk = ar.bf(128)
            r = Res('pro')
            rb = bres[0]
            sc.dma('sp', tb[0:32, :], rel_bias, writes=[r])
            sc.dma('sp', oh[0:32, :], cf_in[0:32, 384:512], writes=[r])
            sc.mm(banks[0][0:12, 0:128], tb[0:32, 0:12], oh[0:32, 0:128], reads=[r], writes=[rb])
            sc.cp('dve', br[0:12, :], banks[0][0:12, 0:128], reads=[rb], writes=[r])
            sc.ts('dve', negc[0:12, :], br[0:12, 127:128], -1.0, None, ALU.mult, reads=[r], writes=[r])
            sc.memset('pool', gsb[0:12, 0:GOFF], 0.0, writes=[r])
            sc.memset('pool', gsb[0:12, GOFF + 128:GL], 1.0, writes=[r])
            sc.act(gsb[0:12, GOFF:GOFF + 128], br[0:12, :], AF.Exp, reads=[r], writes=[r], bias=negc[0:12, :])
            rg = Res('gvd')
            sc.dma('sp', gvd, gsb[0:12, :], reads=[r], writes=[rg])
            hkr = Res('hk')
            for h in range(12):
                for di, dl in enumerate((0, 128)):
                    src = AP(gvd.tensor, h * GL + GOFF + dl - 127, [[1, 128], [1, 128]])
                    sc.dma('sp', hk, src, reads=[rg], writes=[hkr])
                    sc.mm(banks[1][:, 0:128], Jm, hk, reads=[hkr, r_const], writes=[bres[1]])
                    sc.cp('dve', Etab[:, (2 * h + di) * 128:(2 * h + di + 1) * 128], banks[1][:, 0:128],
                          reads=[bres[1]], writes=[r_const])
            sc.barrier()

        def load_v(dst, src, writes):
            step = 8
            for t0 in range(0, NT, step):
                t1 = min(NT, t0 + step)
                sc.dma('sp', dst[:, t0:t1, :], src[t0 * 128:t1 * 128, :].rearrange("(t p) d -> p t d", p=128), writes=writes)

        CAST_ENG = ('pool', 'dve', 'act')
        cast_i = [0]
        def load_cast(dst, src, n, stage_ring, rdst):
            o = 0
            while o < n:
                w = min(2048, n - o)
                sg, rs = stage_ring.next()
                sc.dma('sp', sg[:, 0:w], src[:, o:o + w], writes=[rs])
                sc.cp(CAST_ENG[cast_i[0] % 3], dst[:, o:o + w], sg[:, 0:w], reads=[rs], writes=[rdst])
                cast_i[0] += 1
                o += w

        def rmsnorm_tm(xs, rx, gB, rg, hb, rh, ntile, sq, ssq, rstd, rtmp):
            for t in range(ntile):
                sc.act(sq, xs[:, t, :], AF.Square, reads=[rx], writes=[rtmp])
                sc.op('dve', lambda e, t=t: e.tensor_reduce(ssq[:, t:t + 1], sq, AX.X, ALU.add), reads=[rtmp], writes=[rtmp])
            sc.act(rstd[:, 0:ntile], ssq[:, 0:ntile], AF.Ln, reads=[rtmp], writes=[rtmp], scale=1.0 / D, bias=EPS)
            sc.act(rstd[:, 0:ntile], rstd[:, 0:ntile], AF.Exp, reads=[rtmp], writes=[rtmp], scale=-0.5)
            for t in range(ntile):
                sc.stt('dve', hb[:, t, :], xs[:, t, :], rstd[:, t:t + 1], gB, ALU.mult, ALU.mult,
                       reads=[rx, rtmp, rg], writes=[rh])

        def transpose_to(hb, rh, hT, rhT, ntile, bank_ring, evac_engs):
            for c in range(8):
                bk, rb = bank_ring.next()
                pb = bk.bitcast(BF16)
                for t in range(ntile):
                    sc.tr(pb[:, t * 128:(t + 1) * 128], hb[:, t, c * 128:(c + 1) * 128], ident,
                          reads=[rh, r_const], writes=[rb])
                sc.cp(evac_engs[c % len(evac_engs)], hT[:, c, :], pb[:, 0:ntile * 128], reads=[rb], writes=[rhT])

        def phase_A(l, xsrc):
            ar.top = base_top
            W = v3(ar.bf(8 * NQKV), 8)
            gA = ar.f32(D)
            gcol = ar.f32(8)
            rW = Res('W')
            mark = ar.top
            stg = [(ar.f32(2048), Res('stg%d' % i)) for i in range(4)]
            sring = Ring(stg)
            for c in range(8):
                load_cast(W[:, c, :], w_in[l, c * 128:(c + 1) * 128, 0:NQKV], NQKV, sring, rW)
            sc.dma('sp', gA, AP(attn_norm.tensor, l * D, [[0, 128], [1, D]]), writes=[rW])
            for gi in range(6):
                for half in range(2):
                    sc.dma('sp', gcol[half * 64:(half + 1) * 64, gi:gi + 1],
                           AP(gains.tensor, (l * 6 + gi) * 64, [[1, 64], [1, 1]]), writes=[rW])
            sc.barrier()
            if dbg == 'W':
                return
            ar.top = mark
            xs_r = Ring([(v3(ar.f32(4 * D), 4), Res('xs%d' % i)) for i in range(2)])
            hb = v3(ar.bf(4 * D), 4)
            rhb = Res('hb')
            sq = ar.f32(D)
            ssq = ar.f32(4)
            rstd = ar.f32(4)
            rtmp = Res('tmp')
            hT_r = Ring([(v3(ar.bf(8 * 512), 8), Res('hT%d' % i)) for i in range(2)])
            sqb_r = Ring([(ar.bf(512), Res('sqb%d' % i)) for i in range(2)])
            rs_r = Ring([(ar.f32(512), Res('rs%d' % i)) for i in range(2)])
            fst_r = Ring([(ar.bf(512), Res('fst%d' % i)) for i in range(3)])
            tst_r = Ring([(ar.bf(832), Res('tst%d' % i)) for i in range(2)])
            gst_r = Ring([(ar.f32(24), Res('gst%d' % i)) for i in range(2)])
            tb_r = Ring([(banks[i], bres[i]) for i in (0, 1)])
            pb_r = Ring([(banks[i], bres[i]) for i in (2, 3, 4)])
            nb_r = Ring([(banks[i], bres[i]) for i in (5,)])
            vb_r = Ring([(banks[i], bres[i]) for i in (6, 7)])
            r_hTd, r_fmd, r_tmvd, r_nsgd = Res(), Res(), Res(), Res()
            for ci in range(NC):
                xs, rx = xs_r.next()
                sc.dma('sp', xs, xsrc[ci * 512:(ci + 1) * 512, :].rearrange("(t p) d -> p t d", p=128), writes=[rx])
                if dbg == 'ld':
                    continue
                rmsnorm_tm(xs, rx, gA, rW, hb, rhb, 4, sq, ssq, rstd, rtmp)
                if dbg == 'norm':
                    continue
                hT, rhT = hT_r.next()
                transpose_to(hb, rhb, hT, rhT, 4, tb_r, ('dve', 'act'))
                if dbg == 'tr':
                    continue
                sc.dma(STQ, hTd.rearrange("(c p) s -> p c s", p=128)[:, :, ci * 512:(ci + 1) * 512], hT,
                       reads=[rhT], writes=[r_hTd])
                for j in range(16):
                    col = FM_COLS[j]
                    bk, rb = pb_r.next()
                    for c in range(8):
                        sc.mm(bk, W[:, c, col:col + 128], hT[:, c, :], start=(c == 0), stop=(c == 7),
                              reads=[rW, rhT], writes=[rb])
                    fs, rf = fst_r.next()
                    if FM_KIND[j] is None:
                        sc.cp('act' if j % 2 else 'dve', fs, bk, reads=[rb], writes=[rf])
                    else:
                        gi = FM_KIND[j]
                        sqb, rsq = sqb_r.next()
                        sc.act(sqb, bk, AF.Square, reads=[rb], writes=[rsq])
                        b2, rb2 = nb_r.next()
                        sc.mm(b2, blkones, sqb, reads=[rsq, r_const], writes=[rb2])
                        rs, rrs = rs_r.next()
                        sc.act(rs, b2, AF.Ln, reads=[rb2], writes=[rrs], scale=1.0 / 64, bias=EPS)
                        sc.act(rs, rs, AF.Exp, reads=[rrs], writes=[rrs], scale=-0.5)
                        sc.stt('dve', fs, bk, gcol[:, gi:gi + 1], rs, ALU.mult, ALU.mult, reads=[rb, rrs, rW], writes=[rf])
                    sc.dma(STQ, fmd[j * 128:(j + 1) * 128, ci * 512:(ci + 1) * 512], fs, reads=[rf], writes=[r_fmd])
                if dbg == 'fm':
                    continue
                for t in range(4):
                    b1, rb1 = vb_r.next()
                    b2, rb2 = vb_r.next()
                    lt = lambda c: hT[:, c, t * 128:(t + 1) * 128]
                    for (bk, rb, o, c0, w) in ((b1, rb1, 0, 512, 256), (b1, rb1, 256, 1280, 256),
                                               (b2, rb2, 0, 2432, 128), (b2, rb2, 128, 2688, 152)):
                        for c in range(8):
                            sc.mm(bk[:, o:o + w], lt(c), W[:, c, c0:c0 + w], start=(c == 0), stop=(c == 7),
                                  reads=[rW, rhT], writes=[rb])
                    ts_, rts = tst_r.next()
                    sc.cp('act', ts_[:, 0:512], b1, reads=[rb1], writes=[rts])
                    sc.cp('dve', ts_[:, 512:768], b2[:, 0:256], reads=[rb2], writes=[rts])
                    sc.act(ts_[:, 768:816].bitcast(F32), b2[:, 256:280], AF.Sigmoid, reads=[rb2], writes=[rts])
                    r0 = (ci * 4 + t) * 128
                    sc.dma(STQ, tmvd[r0:r0 + 128, 0:816], ts_[:, 0:816], reads=[rts], writes=[r_tmvd])
            sc.barrier()

        def phase_SB(l):
            ar.top = base_top
            kT = ar.bf(S)
            qT = ar.bf(S)
            v = v3(ar.bf(NT * 64), NT)
            rk = Res('kqv')
            R_r = Ring([(ar.bf(512), Res()) for i in range(LAG + 2)])
            zc_r = Ring([(ar.f32(512), Res()) for i in range(LAG + 2)])
            u_r = Ring([(ar.f32(512), Res()) for i in range(2)])
            sp_r = Ring([(ar.bf(512), Res()) for i in range(LAG + 2)])
            w_r = Ring([(ar.f32(512), Res()) for i in range(2)])
            a_r = Ring([(ar.bf(512), Res()) for i in range(LAG + 2)])
            os_r = Ring([(ar.bf(512), Res()) for i in range(2)])
            zb_r = Ring([(banks[i], bres[i]) for i in (0, 1)])
            tb_r = Ring([(banks[i], bres[i]) for i in (2, 3, 7)])
            rdum = Res('dummy')
            ob_r = Ring([(banks[i], bres[i]) for i in (4, 5)])
            r_oTd = Res()
            for h in range(4):
                sc.dma('sp', kT[0:64, :], fmd[256 + 64 * h:256 + 64 * h + 64, :], writes=[rk])
                sc.dma('sp', qT[0:64, :], fmd[64 * h:64 * h + 64, :], writes=[rk])
                load_v(v, tmvd[:, 64 * h:64 * h + 64], [rk])
                for c in range(NC):
                    ob, rob = ob_r.next()
                    sc.mm(ob[0:64, :], zerob[:, 0:64], zerob, start=True, stop=False, reads=[r_const], writes=[rob])
                    for _ in range(WARM):
                        sc.mm(banks[6], onesb, zerob, reads=[r_const], writes=[rdum])
                    last = 4 * c + 3
                    n_t = last + 1
                    Rcur = [R_r.next()]
                    sc.memset('pool', Rcur[0][0], 0.0, writes=[Rcur[0][1]])

                    def s1(i, c=c, last=last, n_t=n_t, Rcur=Rcur):
                        kt = last - i
                        rel = kt - 4 * c
                        c0 = 128 * max(rel, 0)
                        zb, rzb = zb_r.next()
                        sc.mm(zb[:, c0:512], kT[0:64, kt * 128:(kt + 1) * 128], qT[0:64, c * 512 + c0:(c + 1) * 512],
                              reads=[rk], writes=[rzb])
                        for _ in range(NDUMMY):
                            sc.mm(banks[6], onesb, zerob, reads=[r_const], writes=[rdum])
                        zc, rzc = zc_r.next()
                        sc.ts('dve', zc[:, c0:512], zb[:, c0:512], 0.125, 40.0, ALU.mult, ALU.min, reads=[rzb], writes=[rzc])
                        u, ru = u_r.next()
                        sc.act(u[:, c0:512], zc[:, c0:512], AF.Exp, reads=[rzc], writes=[ru])
                        sp, rsp = sp_r.next()
                        sc.act(sp[:, c0:512], u[:, c0:512], AF.Ln, reads=[ru], writes=[rsp], bias=1.0)
                        if rel >= 0:
                            sc.tt('pool', sp[:, c0:c0 + 128], sp[:, c0:c0 + 128], maskL, ALU.mult,
                                  reads=[rsp, r_const], writes=[rsp])
                        if c0 > 0:
                            sc.memset('pool', sp[:, 0:c0], 0.0, writes=[rsp])
                        Ri, rRi = Rcur[0]
                        if i < n_t - 1:
                            Rn, rRn = R_r.next()
                            sc.tt('pool', Rn, Ri, sp, ALU.add, reads=[rRi, rsp], writes=[rRn])
                            Rcur[0] = (Rn, rRn)
                        return (kt, rel, c0, zc, rzc, sp, rsp, Ri, rRi)

                    def s2(i, stt_, ob=ob, rob=rob):
                        kt, rel, c0, zc, rzc, sp, rsp, Ri, rRi = stt_
                        tb, rtb = tb_r.next()
                        sc.mm(tb[:, c0:512], uincl, sp[:, c0:512], start=True, stop=(i == 0),
                              reads=[rsp, r_const], writes=[rtb])
                        if i > 0:
                            sc.mm(tb[:, c0:512], onesb, Ri[:, c0:512], start=False, stop=True,
                                  reads=[rRi, r_const], writes=[rtb])
                        w, rw = w_r.next()
                        sc.tt('dve', w[:, c0:512], zc[:, c0:512], tb[:, c0:512], ALU.subtract, reads=[rzc, rtb], writes=[rw])
                        if rel >= 0:
                            sc.tt('pool', w[:, c0:c0 + 128], w[:, c0:c0 + 128], negmask, ALU.add,
                                  reads=[rw, r_const], writes=[rw])
                        a, ra = a_r.next()
                        sc.act(a[:, c0:512], w[:, c0:512], AF.Exp, reads=[rw], writes=[ra])
                        return (kt, c0, a, ra)

                    def s3(i, stt_, ob=ob, rob=rob):
                        kt, c0, a, ra = stt_
                        sc.mm(ob[0:64, c0:512], v[:, kt, :], a[:, c0:512], start=False, stop=(kt == 0),
                              reads=[rk, ra], writes=[rob])

                    pipe(n_t, s1, s2, s3)
                    os_, ros = os_r.next()
                    sc.cp('act', os_[0:64, :], ob[0:64, :], reads=[rob], writes=[ros])
                    sc.dma('sp', oTd[64 * h:64 * h + 64, c * 512:(c + 1) * 512], os_[0:64, :], reads=[ros], writes=[r_oTd])
            sc.barrier()


        def phase_MB(l):
            ar.top = base_top
            NBLK = S // 256
            kT = ar.bf(S)
            qT = ar.bf(S)
            va = v3(ar.bf(NT * 65), NT)
            rk = Res('kqv')
            km = ar.f32(32)
            rkm = Res('km')
            qf_r = Ring([(ar.f32(512), Res()) for i in range(2)])
            scv = v3(ar.f32(4 * 32), 4)
            m8 = ar.f32(32)
            selw = v3(ar.f32(4 * 32), 4)
            rsel = Res('sel')
            acc = v3(ar.f32(4 * 65), 4)
            racc = [Res('acc%d' % j) for j in range(4)]
            rl = v3(ar.f32(4), 4)
            rrl = Res('rl')
            o_tok = v3(ar.bf(NT * 256), NT)
            rot = Res('otok')
            p_r = Ring([(ar.bf(512), Res()) for i in range(LAG + 2)])
            st_r = Ring([(ar.bf(512), Res()) for i in range(2)])
            lb_r = Ring([(banks[i], bres[i]) for i in (0, 1, 7)])
            ob_r = Ring([(banks[i], bres[i]) for i in (2, 3)])
            sb_r = Ring([(banks[i], bres[i]) for i in (4,)])
            tb_r = Ring([(banks[i], bres[i]) for i in (5, 6)])
            r_oTd = Res()
            sc.memset('pool', va[:, :, 64:65], 1.0, writes=[rk])
            for h in range(4):
                sc.dma('sp', kT[0:64, :], fmd[768 + 64 * h:768 + 64 * h + 64, :], writes=[rk])
                sc.dma('sp', qT[0:64, :], fmd[512 + 64 * h:512 + 64 * h + 64, :], writes=[rk])
                load_v(va[:, :, 0:64], tmvd[:, 256 + 64 * h:256 + 64 * h + 64], [rk])
                sc.op('dve', lambda e: e.tensor_reduce(km[0:64, 0:NBLK], kT[0:64, :].rearrange("p (n k) -> p n k", k=256), AX.X, ALU.add),
                      reads=[rk], writes=[rkm])
                sc.ts('dve', km[0:64, 0:NBLK], km[0:64, 0:NBLK], 1.0 / 256, None, ALU.mult, reads=[rkm], writes=[rkm])
                for c in range(NC):
                    qf, rqf = qf_r.next()
                    sc.cp('dve', qf[0:64, :], qT[0:64, c * 512:(c + 1) * 512], reads=[rk], writes=[rqf])
                    sc.memset('pool', scv, NEG, writes=[rsel])
                    sb, rsb = sb_r.next()
                    for j in range(4):
                        cur = (4 * c + j) // 2
                        if cur > 0:
                            sc.mm(sb[:, j * 32:j * 32 + cur], qf[0:64, j * 128:(j + 1) * 128], km[0:64, 0:cur],
                                  reads=[rqf, rkm], writes=[rsb])
                    for j in range(4):
                        cur = (4 * c + j) // 2
                        if cur > 0:
                            sc.cp('dve', scv[:, j, 0:cur], sb[:, j * 32:j * 32 + cur], reads=[rsb], writes=[rsel])
                    for j in range(4):
                        sc.op('dve', lambda e, j=j: e.max(out=m8[:, j * 8:(j + 1) * 8], in_=scv[:, j, :]), reads=[rsel], writes=[rsel])
                        sc.ts('dve', selw[:, j, :], scv[:, j, :], m8[:, j * 8 + 2:j * 8 + 3], None, ALU.is_ge, reads=[rsel], writes=[rsel])
                    sc.memset('pool', acc, 0.0, writes=racc)
                    obs = [None]

                    def s1(kt, c=c, h=h):
                        rel = kt - 4 * c
                        j0 = max(rel, 0)
                        c0 = 128 * j0
                        lb, rlb = lb_r.next()
                        sc.mm(lb[:, c0:512], kT[0:64, kt * 128:(kt + 1) * 128], qT[0:64, c * 512 + c0:(c + 1) * 512],
                              reads=[rk], writes=[rlb])
                        p, rp = p_r.next()
                        sc.act(p[:, c0:512], lb[:, c0:512], AF.Exp, reads=[rlb], writes=[rp], scale=0.125)
                        for j in range(j0, 4):
                            d = 4 * c + j - kt
                            if d == 0:
                                sc.tt('pool', p[:, j * 128:(j + 1) * 128], p[:, j * 128:(j + 1) * 128], E0(h), ALU.mult,
                                      reads=[rp, r_const], writes=[rp])
                            elif d == 1:
                                sc.tt('pool', p[:, j * 128:(j + 1) * 128], p[:, j * 128:(j + 1) * 128], E128(h), ALU.mult,
                                      reads=[rp, r_const], writes=[rp])
                        return (p, rp, j0)

                    def s2(kt, stt_, c=c, obs=obs):
                        p, rp, j0 = stt_
                        n = kt // 2
                        if kt % 2 == 0:
                            obs[0] = ob_r.next()
                        ob, rob = obs[0]
                        done = []
                        for j in range(j0, 4):
                            qt = 4 * c + j
                            stop = (kt % 2 == 1) or (kt == qt)
                            sc.mm(ob[:, j * 128:j * 128 + 65], p[:, j * 128:(j + 1) * 128], va[:, kt, :],
                                  start=(kt % 2 == 0 and j == j0), stop=stop, reads=[rp, rk], writes=[rob])
                            if stop:
                                done.append(j)
                        for j in done:
                            qt = 4 * c + j
                            if n == qt // 2:
                                sc.tt('dve', acc[:, j, :], ob[:, j * 128:j * 128 + 65], acc[:, j, :], ALU.add,
                                      reads=[rob, racc[j]], writes=[racc[j]])
                            else:
                                sc.stt('dve', acc[:, j, :], ob[:, j * 128:j * 128 + 65], selw[:, j, n:n + 1], acc[:, j, :],
                                       ALU.mult, ALU.add, reads=[rob, racc[j], rsel], writes=[racc[j]])

                    pipe(4 * c + 4, s1, s2)
                    sc.ts('dve', rl, acc[:, :, 64:65], TINY, None, ALU.max, reads=racc, writes=[rrl])
                    sc.op('dve', lambda e: e.reciprocal(rl, rl), reads=[rrl], writes=[rrl])
                    sc.tt('dve', o_tok[:, 4 * c:4 * c + 4, h * 64:(h + 1) * 64], acc[:, :, 0:64], rl.to_broadcast([128, 4, 64]), ALU.mult,
                          reads=racc + [rrl], writes=[rot])
            for c in range(NC):
                for fc in range(2):
                    tb, rtb = tb_r.next()
                    pb = tb.bitcast(BF16)
                    for j in range(4):
                        sc.tr(pb[:, j * 128:(j + 1) * 128], o_tok[:, 4 * c + j, fc * 128:(fc + 1) * 128], ident,
                              reads=[rot, r_const], writes=[rtb])
                    stg, rst = st_r.next()
                    sc.cp('act' if fc else 'dve', stg, pb[:, 0:512], reads=[rtb], writes=[rst])
                    sc.dma(STQ, oTd[256 + fc * 128:256 + (fc + 1) * 128, c * 512:(c + 1) * 512], stg, reads=[rst], writes=[r_oTd])
            sc.barrier()


        def phase_NSA(l):
            ar.top = base_top
            NCW = NCT * 128
            kcT2 = [ar.bf(NCW) for g in range(2)]
            vca = [v3(ar.bf(NCT * 65), NCT) for g in range(2)]
            gcol = ar.f32(8)
            r_cmp = Res('cmp')
            mark_p = ar.top
            w1sb = v3(ar.bf(32 * 256), 32)
            w2sb = v3(ar.bf(2 * 64), 2)
            TT = ar.bf(S)
            posf = ar.f32(64)
            posb = ar.bf(64)
            posT = ar.bf(32)
            pbias = ar.f32(2)
            xb = ar.f32(512)
            x2 = ar.f32(512)
            th = ar.f32(512)
            ghT = v3(ar.bf(2 * 512), 2)
            sqb = ar.bf(512)
            rs = ar.f32(512)
            sring = Ring([(ar.f32(2048), Res()) for i in range(2)])
            rw, rt, rx, rgh = Res('w'), Res('TT'), Res('x'), Res('gh')
            n = n_cmp
            for gi in range(6):
                for half in range(2):
                    sc.dma('sp', gcol[half * 64:(half + 1) * 64, gi:gi + 1],
                           AP(gains.tensor, (l * 6 + gi) * 64, [[1, 64], [1, 1]]), writes=[r_cmp])
            for g in range(2):
                sc.memset('pool', kcT2[g], 0.0, writes=[r_cmp])
                sc.memset('pool', vca[g], 0.0, writes=[r_cmp])
            for kv in range(2):
                w1v = cmp_w1[l, kv].rearrange("(i d) h -> d i h", d=64)
                for i0 in range(0, 32, 8):
                    sg, rsg = sring.next()
                    sc.dma('sp', v3(sg, 8)[0:64], w1v[:, i0:i0 + 8, :], writes=[rsg])
                    sc.cp('pool', w1sb[0:64, i0:i0 + 8, :], v3(sg, 8)[0:64], reads=[rsg], writes=[rw])
                sg, rsg = sring.next()
                sc.dma('sp', v3(sg[:, 0:128], 2), cmp_w2[l, kv].rearrange("(c p) d -> p c d", p=128), writes=[rsg])
                sc.cp('pool', w2sb, v3(sg[:, 0:128], 2), reads=[rsg], writes=[rw])
                sc.dma('sp', posf[0:32, :], cmp_pos[l, kv], writes=[rw])
                sc.cp('dve', posb[0:32, :], posf[0:32, :], reads=[rw], writes=[rw])
                pbk = banks[5].bitcast(BF16)
                sc.tr(pbk[0:64, 0:32], posb[0:32, 0:64], ident[0:32, 0:32], reads=[rw, r_const], writes=[bres[5]])
                sc.cp('dve', posT[0:64, :], pbk[0:64, 0:32], reads=[bres[5]], writes=[rw])
                for hc in range(2):
                    for i in range(32):
                        sc.mm(banks[6][:, hc:hc + 1], w1sb[0:64, i, hc * 128:(hc + 1) * 128], posT[0:64, i:i + 1],
                              start=(i == 0 and hc == 0), stop=(i == 31), reads=[rw], writes=[bres[6]])
                sc.cp('dve', pbias, banks[6][:, 0:2], reads=[bres[6]], writes=[rw])
                for g in range(2):
                    base = (1536 if kv == 0 else 1664) + 64 * g
                    sc.dma('sp', TT[0:64, :], fmd[base:base + 64, :], writes=[rt])
                    for hc in range(2):
                        bk, rb = banks[hc], bres[hc]
                        for i in range(32):
                            sc.mm(bk[:, 0:n], w1sb[0:64, i, hc * 128:(hc + 1) * 128], TT[0:64, i:i + 16 * (n - 1) + 1:16],
                                  start=(i == 0), stop=(i == 31), reads=[rw, rt], writes=[rb])
                        sc.ts('dve', xb[:, 0:n], bk[:, 0:n], pbias[:, hc:hc + 1], None, ALU.add, reads=[rb, rw], writes=[rx])
                        sc.tt('dve', x2[:, 0:n], xb[:, 0:n], xb[:, 0:n], ALU.mult, reads=[rx], writes=[rx])
                        sc.ts('dve', x2[:, 0:n], x2[:, 0:n], 0.044715, 1.0, ALU.mult, ALU.add, reads=[rx], writes=[rx])
                        sc.tt('dve', x2[:, 0:n], x2[:, 0:n], xb[:, 0:n], ALU.mult, reads=[rx], writes=[rx])
                        sc.act(th[:, 0:n], x2[:, 0:n], AF.Tanh, reads=[rx], writes=[rx], scale=0.7978845608028654)
                        sc.ts('dve', xb[:, 0:n], xb[:, 0:n], 0.5, None, ALU.mult, reads=[rx], writes=[rx])
                        sc.stt('dve', ghT[:, hc, 0:n], th[:, 0:n], 1.0, xb[:, 0:n], ALU.add, ALU.mult, reads=[rx], writes=[rgh])
                    if kv == 0:
                        for hc in range(2):
                            sc.mm(banks[2][0:64, 0:n], w2sb[:, hc, :], ghT[:, hc, 0:n], start=(hc == 0), stop=(hc == 1),
                                  reads=[rw, rgh], writes=[bres[2]])
                        sc.act(sqb[0:64, 0:n], banks[2][0:64, 0:n], AF.Square, reads=[bres[2]], writes=[rx])
                        sc.mm(banks[3][0:64, 0:n], onesb[0:64, 0:64], sqb[0:64, 0:n], reads=[rx, r_const], writes=[bres[3]])
                        sc.act(rs[0:64, 0:n], banks[3][0:64, 0:n], AF.Ln, reads=[bres[3]], writes=[rx], scale=1.0 / 64, bias=EPS)
                        sc.act(rs[0:64, 0:n], rs[0:64, 0:n], AF.Exp, reads=[rx], writes=[rx], scale=-0.5)
                        sc.stt('dve', kcT2[g][0:64, 0:n], banks[2][0:64, 0:n], gcol[0:64, 3:4], rs[0:64, 0:n], ALU.mult, ALU.mult,
                               reads=[bres[2], rx, r_cmp], writes=[r_cmp])
                        sc.dma('sp', kcT2[g][64:128, :], kcT2[g][0:64, :], reads=[r_cmp], writes=[r_cmp])
                    else:
                        for nt in range(NCT):
                            rows = min(128, n - nt * 128)
                            for hc in range(2):
                                sc.mm(banks[2][0:rows, 0:64], ghT[:, hc, nt * 128:nt * 128 + rows], w2sb[:, hc, :],
                                      start=(hc == 0), stop=(hc == 1), reads=[rw, rgh], writes=[bres[2]])
                            sc.cp('dve', vca[g][0:rows, nt, 0:64], banks[2][0:rows, 0:64], reads=[bres[2]], writes=[r_cmp])
                            sc.memset('pool', vca[g][0:rows, nt, 64:65], 1.0, writes=[r_cmp])
            sc.barrier()
            for g in range(2):
                ar.top = mark_p
                qT = v3(ar.bf(2 * S), 2)
                ksT2 = ar.bf(S)
                kwT2 = ar.bf(S)
                vsa = v3(ar.bf(NT * 65), NT)
                vwa = v3(ar.bf(NT * 65), NT)
                G = [ar.bf(2560) for hh in range(4)]
                Bm = ar.bf(S)
                ovl = v3(ar.bf(NCW), NCT)
                rk = Res('kqv')
                hk_r = Ring([(ar.bf(512), Res()) for i in range(2)])
                gtb = ar.bf(4 * 48)
                gt3 = v3(gtb.bitcast(F32), 4)
                rgt = Res('gt')
                impacc = v3(ar.f32(512), 4)
                rimp = Res('imp')
                itmp = v3(ar.f32(512), 4)
                ritmp = Res()
                score = v3(ar.f32(512), 4)
                wk = v3(ar.f32(512), 4)
                m8a = ar.f32(32)
                m8b = ar.f32(32)
                selb = v3(ar.bf(512), 4)
                selT = ar.bf(512)
                rsel = Res('sel')
                oc = ar.f32(4 * 4 * 64)
                roc = Res('oc')
                rlc = ar.f32(16)
                rls = v3(ar.f32(4), 4)
                rlw = v3(ar.f32(4), 4)
                cfs = v3(ar.f32(4), 4)
                cfw = v3(ar.f32(4), 4)
                rcoef = Res('coef')
                ot1 = v3(ar.f32(256), 4)
                ot2 = v3(ar.f32(256), 4)
                ot3 = v3(ar.f32(256), 4)
                rot1, rot2, rot3 = Res(), Res(), Res()
                o_tok = v3(ar.bf(4 * 256), 4)
                rotk = Res('otok')
                p_r = Ring([(ar.bf(512), Res()) for i in range(LAG + 2)])
                ps_r = Ring([(ar.bf(512), Res()) for i in range(4 * (LAG + 2))])
                mk_r = Ring([(ar.bf(512), Res()) for i in range(3)])
                st_r = Ring([(ar.bf(512), Res()) for i in range(2)])
                lb_r = Ring([(banks[i], bres[i]) for i in (0, 1)])
                mb_r = Ring([(banks[i], bres[i]) for i in (5,)])
                lb3_r = Ring([(banks[i], bres[i]) for i in (0, 1, 2)])
                a1_r = Ring([(banks[i], bres[i]) for i in (3, 6)])
                a2_r = Ring([(banks[i], bres[i]) for i in (4, 7)])
                tb_r = Ring([(banks[i], bres[i]) for i in (5,)])
                r_oTd = Res()
                for hh in range(4):
                    half, a = hh // 2, hh % 2
                    r0 = 1024 + 64 * (4 * g + hh)
                    sc.dma('sp', qT[half * 64:(half + 1) * 64, a, :], fmd[r0:r0 + 64, :], writes=[rk])
                for half in range(2):
                    sc.dma('sp', ksT2[half * 64:(half + 1) * 64, :], fmd[1792 + 64 * g:1792 + 64 * g + 64, :], writes=[rk])
                    sc.dma('sp', kwT2[half * 64:(half + 1) * 64, :], fmd[1920 + 64 * g:1920 + 64 * g + 64, :], writes=[rk])
                load_v(vsa[:, :, 0:64], tmvd[:, 512 + 64 * g:512 + 64 * g + 64], [rk])
                load_v(vwa[:, :, 0:64], tmvd[:, 640 + 64 * g:640 + 64 * g + 64], [rk])
                sc.memset('pool', vsa[:, :, 64:65], 1.0, writes=[rk])
                sc.memset('pool', vwa[:, :, 64:65], 1.0, writes=[rk])
                sc.dma('sp', ovl, v3(cb_in[:, CB_GLOBAL:CB_GLOBAL + NCW], NCT), writes=[rk])
                sc.dma('sp', Bm, cb_in[:, CB_GLOBAL + NCW:CB_GLOBAL + NCW + S], writes=[rk])
                for hh in range(4):
                    hrow = 4 + 4 * g + hh
                    for idx in range(5):
                        hk, rhk = hk_r.next()
                        src = AP(gvd.tensor, hrow * GL + GOFF - 2063 + 512 * idx, [[16, 128], [1, 512]])
                        sc.dma('sp', hk, src, writes=[rhk])
                        tb, rtb = tb_r.next()
                        sc.mm(tb, Jm, hk, reads=[rhk, r_const], writes=[rtb])
                        sc.cp('dve', G[hh][:, idx * 512:(idx + 1) * 512], tb, reads=[rtb], writes=[rk])
                for c in range(NC):
                    sc.dma('sp', v3(gtb, 4), tmvd[c * 512:(c + 1) * 512, 768:816].rearrange("(t p) w -> p t w", p=128), writes=[rgt])
                    sc.memset('pool', impacc, 0.0, writes=[rimp])
                    nts = [nt for nt in range(NCT) if c - 4 * nt >= 0]
                    oc4 = oc.rearrange("p (h j d) -> p h j d", h=4, j=4)
                    rlc4 = rlc.rearrange("p (h j o) -> p h j o", h=4, o=1)
                    for hh in range(4):
                        half, a = hh // 2, hh % 2
                        hs = slice(half * 64, half * 64 + 64)
                        ocb, rocb = a1_r.next()
                        ib, rib = a2_r.next()
                        def s1(ii, c=c, hh=hh, hs=hs, a=a, nts=nts):
                            nt = nts[ii]
                            lb, rlb = lb_r.next()
                            sc.mm(lb, kcT2[g][hs, nt * 128:(nt + 1) * 128], qT[hs, a, c * 512:(c + 1) * 512], reads=[r_cmp, rk], writes=[rlb])
                            pc, rpc = p_r.next()
                            sc.act(pc, lb, AF.Exp, reads=[rlb], writes=[rpc], scale=0.125)
                            idx = c - 4 * nt
                            if idx < 5:
                                sc.tt('pool', pc, pc, G[hh][:, idx * 512:(idx + 1) * 512], ALU.mult, reads=[rpc, rk], writes=[rpc])
                            return (pc, rpc)

                        def s2(ii, stt_, nts=nts, ocb=ocb, rocb=rocb, ib=ib, rib=rib):
                            pc, rpc = stt_
                            nt = nts[ii]
                            for j in range(4):
                                sc.mm(ocb[:, j * 128:j * 128 + 65], pc[:, j * 128:(j + 1) * 128], vca[g][:, nt, :],
                                      start=(ii == 0 and j == 0), stop=(nt == nts[-1]), reads=[rpc, r_cmp], writes=[rocb])
                            for j in range(4):
                                sc.mm(ib[:, j * 128:(j + 1) * 128], pc[:, j * 128:(j + 1) * 128], ovl[:, nt, :],
                                      start=(ii == 0 and j == 0), stop=(nt == nts[-1]), reads=[rpc, rk], writes=[rib])

                        pipe(len(nts), s1, s2)
                        ocb3 = v3(ocb, 4)
                        sc.ts('dve', rlc4[:, hh], ocb3[:, :, 64:65], TINY, None, ALU.max, reads=[rocb], writes=[roc])
                        sc.op('dve', lambda e, hh=hh: e.reciprocal(rlc4[:, hh], rlc4[:, hh]), reads=[roc], writes=[roc])
                        sc.tt('dve', oc4[:, hh], ocb3[:, :, 0:64], rlc4[:, hh].to_broadcast([128, 4, 64]), ALU.mult,
                              reads=[rocb, roc], writes=[roc])
                        sc.tt('dve', itmp, v3(ib, 4), rlc4[:, hh].to_broadcast([128, 4, 128]), ALU.mult, reads=[rib, roc], writes=[ritmp])
                        sc.tt('pool', impacc, impacc, itmp, ALU.add, reads=[rimp, ritmp], writes=[rimp])
                    for j in range(4):
                        off = 126 - 2 * (4 * c + j)
                        sc.tt('dve', score[:, j, :], impacc[:, j, :], cslide[:, off:off + 128], ALU.add, reads=[rimp, r_const], writes=[rsel])
                    sc.memset('dve', score[:, :, 0:1], BIG, writes=[rsel])
                    for j in range(4):
                        sc.op('dve', lambda e, j=j: e.max(out=m8a[:, j * 8:(j + 1) * 8], in_=score[:, j, :]), reads=[rsel], writes=[rsel])
                        sc.op('dve', lambda e, j=j: e.match_replace(out=wk[:, j, :], in_to_replace=m8a[:, j * 8:(j + 1) * 8],
                                                                    in_values=score[:, j, :], imm_value=-3e38), reads=[rsel], writes=[rsel])
                        sc.op('dve', lambda e, j=j: e.max(out=m8b[:, j * 8:(j + 1) * 8], in_=wk[:, j, :]), reads=[rsel], writes=[rsel])
                        sc.ts('dve', selb[:, j, :], score[:, j, :], m8b[:, j * 8 + 7:j * 8 + 8], None, ALU.is_ge, reads=[rsel], writes=[rsel])
                    tb, rtb = tb_r.next()
                    pb = tb.bitcast(BF16)
                    for j in range(4):
                        sc.tr(pb[:, j * 128:(j + 1) * 128], selb[:, j, :], ident, reads=[rsel, r_const], writes=[rtb])
                    sc.cp('dve', selT, pb[:, 0:512], reads=[rtb], writes=[rsel])
                    osbs = [(banks[i], bres[i]) for i in (3, 4, 6, 7)]

                    def s1(kt, c=c):
                        j0 = max(kt - 4 * c, 0)
                        c0 = 128 * j0
                        mb, rmb = mb_r.next()
                        sc.mm(mb[:, c0:512], Bm[:, kt * 128:(kt + 1) * 128], selT[:, c0:512], reads=[rk, rsel], writes=[rmb])
                        mk, rmk = mk_r.next()
                        sc.cp('act', mk[:, c0:512], mb[:, c0:512], reads=[rmb], writes=[rmk])
                        outs = [None] * 4
                        for pair in ((0, 2), (1, 3)):
                            lbs = {}
                            for hh in pair:
                                half, a = hh // 2, hh % 2
                                hs = slice(half * 64, half * 64 + 64)
                                lb, rlb = lb3_r.next()
                                sc.mm(lb[:, c0:512], ksT2[hs, kt * 128:(kt + 1) * 128], qT[hs, a, c * 512 + c0:(c + 1) * 512], reads=[rk], writes=[rlb])
                                lbs[hh] = (lb, rlb)
                            for hh in pair:
                                hrow = 4 + 4 * g + hh
                                lb, rlb = lbs[hh]
                                ps, rps = ps_r.next()
                                sc.act(ps[:, c0:512], lb[:, c0:512], AF.Exp, reads=[rlb], writes=[rps], scale=0.125)
                                sc.tt('dve', ps[:, c0:512], ps[:, c0:512], mk[:, c0:512], ALU.mult, reads=[rps, rmk], writes=[rps])
                                for j in range(j0, 4):
                                    d = 4 * c + j - kt
                                    if d in (0, 1):
                                        sc.tt('pool', ps[:, j * 128:(j + 1) * 128], ps[:, j * 128:(j + 1) * 128],
                                              E0(hrow) if d == 0 else E128(hrow), ALU.mult, reads=[rps, r_const], writes=[rps])
                                outs[hh] = (ps, rps)
                        return (outs, j0)

                    def s2(kt, stt_, c=c):
                        outs, j0 = stt_
                        for hh in range(4):
                            ps, rps = outs[hh]
                            osb, rosb = osbs[hh]
                            for j in range(j0, 4):
                                sc.mm(osb[:, j * 128:j * 128 + 65], ps[:, j * 128:(j + 1) * 128], vsa[:, kt, :],
                                      start=(kt == 0 and j == j0), stop=(kt == 4 * c + j), reads=[rps, rk], writes=[rosb])

                    pipe(4 * c + 4, s1, s2)
                    for hh in range(4):
                        half, a = hh // 2, hh % 2
                        hs = slice(half * 64, half * 64 + 64)
                        hrow = 4 + 4 * g + hh
                        osb, rosb = osbs[hh]
                        owb, rowb = banks[5], bres[5]
                        kts = list(range(max(4 * c - 4, 0), 4 * c + 4))

                        def s1(ii, c=c, hs=hs, a=a, hrow=hrow, kts=kts):
                            kt = kts[ii]
                            jlo = max(kt - 4 * c, 0)
                            jhi = min(kt + 4 - 4 * c, 3)
                            cs_ = slice(128 * jlo, 128 * (jhi + 1))
                            lb, rlb = lb_r.next()
                            sc.mm(lb[:, cs_], kwT2[hs, kt * 128:(kt + 1) * 128], qT[hs, a, c * 512 + 128 * jlo:c * 512 + 128 * (jhi + 1)],
                                  reads=[rk], writes=[rlb])
                            pw, rpw = p_r.next()
                            sc.act(pw[:, cs_], lb[:, cs_], AF.Exp, reads=[rlb], writes=[rpw], scale=0.125)
                            for j in range(jlo, jhi + 1):
                                d = 4 * c + j - kt
                                if d in (0, 1, 4):
                                    mk = E0(hrow) if d == 0 else (E128(hrow) if d == 1 else m512)
                                    sc.tt('pool', pw[:, j * 128:(j + 1) * 128], pw[:, j * 128:(j + 1) * 128], mk, ALU.mult,
                                          reads=[rpw, r_const], writes=[rpw])
                            return (pw, rpw, jlo, jhi)

                        def s2(ii, stt_, c=c, kts=kts, owb=owb, rowb=rowb):
                            pw, rpw, jlo, jhi = stt_
                            kt = kts[ii]
                            for j in range(jlo, jhi + 1):
                                sc.mm(owb[:, j * 128:j * 128 + 65], pw[:, j * 128:(j + 1) * 128], vwa[:, kt, :],
                                      start=(ii == 0 and j == jlo), stop=(kt == 4 * c + j), reads=[rpw, rk], writes=[rowb])

                        pipe(len(kts), s1, s2)
                        osb3 = v3(osb, 4)
                        owb3 = v3(owb, 4)
                        sc.ts('dve', rls, osb3[:, :, 64:65], TINY, None, ALU.max, reads=[rosb], writes=[rcoef])
                        sc.op('dve', lambda e: e.reciprocal(rls, rls), reads=[rcoef], writes=[rcoef])
                        sc.ts('dve', rlw, owb3[:, :, 64:65], TINY, None, ALU.max, reads=[rowb], writes=[rcoef])
                        sc.op('dve', lambda e: e.reciprocal(rlw, rlw), reads=[rcoef], writes=[rcoef])
                        gi = g * 4 + hh
                        sc.tt('dve', cfs, rls, gt3[:, :, 8 + gi:9 + gi], ALU.mult, reads=[rcoef, rgt], writes=[rcoef])
                        sc.tt('dve', cfw, rlw, gt3[:, :, 16 + gi:17 + gi], ALU.mult, reads=[rcoef, rgt], writes=[rcoef])
                        sc.tt('dve', ot1, oc4[:, hh], gt3[:, :, gi:gi + 1].to_broadcast([128, 4, 64]), ALU.mult, reads=[roc, rgt], writes=[rot1])
                        sc.tt('dve', ot2, osb3[:, :, 0:64], cfs.to_broadcast([128, 4, 64]), ALU.mult, reads=[rosb, rcoef], writes=[rot2])
                        sc.tt('dve', ot3, owb3[:, :, 0:64], cfw.to_broadcast([128, 4, 64]), ALU.mult, reads=[rowb, rcoef], writes=[rot3])
                        sc.tt('pool', ot1, ot1, ot2, ALU.add, reads=[rot1, rot2], writes=[rot1])
                        sc.tt('pool', o_tok[:, :, hh * 64:(hh + 1) * 64], ot1, ot3, ALU.add, reads=[rot1, rot3], writes=[rotk])
                    for fc in range(2):
                        tb, rtb = tb_r.next()
                        pb = tb.bitcast(BF16)
                        for j in range(4):
                            sc.tr(pb[:, j * 128:(j + 1) * 128], o_tok[:, j, fc * 128:(fc + 1) * 128], ident, reads=[rotk, r_const], writes=[rtb])
                        stg, rst = st_r.next()
                        sc.cp('act', stg, pb[:, 0:512], reads=[rtb], writes=[rst])
                        r0 = 512 + g * 256 + fc * 128
                        sc.dma(STQ, oTd[r0:r0 + 128, c * 512:(c + 1) * 512], stg, reads=[rst], writes=[r_oTd])
                sc.barrier()

        def phase_C1(l, xsrc):
            ar.top = base_top
            Wg = v3(ar.bf(8 * 3072), 8)
            Wb = v3(ar.bf(8 * D), 8)
            Wo = v3(ar.bf(8 * D), 8)
            rW = Res('W')
            mark = ar.top
            sring = Ring([(ar.f32(2048), Res()) for i in range(4)])
            for c in range(8):
                load_cast(Wg[:, c, :], w_in[l, c * 128:(c + 1) * 128, NQKV:NIN], 3072, sring, rW)
                load_cast(Wb[:, c, :], w_branch[l, c * 128:(c + 1) * 128, :], D, sring, rW)
                load_cast(Wo[:, c, :], w_out[l, c * 128:(c + 1) * 128, :], D, sring, rW)
            sc.barrier()
            ar.top = mark
            xs_r = Ring([(v3(ar.f32(4 * D), 4), Res()) for i in range(2)])
            hT_r = Ring([(v3(ar.bf(8 * 512), 8), Res()) for i in range(2)])
            oT_r = Ring([(v3(ar.bf(8 * 512), 8), Res()) for i in range(2)])
            mixT = v3(ar.bf(8 * 512), 8)
            rmix = Res('mix')
            g_r = Ring([(ar.f32(512), Res()) for i in range(3)])
            t_r = Ring([(ar.f32(512), Res()) for i in range(4)])
            gb_r = Ring([(banks[i], bres[i]) for i in (0, 1, 2)])
            bb_r = Ring([(banks[i], bres[i]) for i in (3, 4, 5)])
            yb_r = Ring([(banks[i], bres[i]) for i in (6, 7)])
            r_x1d = Res()
            branch_k = ((0, 1), (2, 3), (4, 5, 6, 7))
            for ci in range(NC):
                xs, rx = xs_r.next()
                sc.dma('sp', xs, xsrc[ci * 512:(ci + 1) * 512, :].rearrange("(t p) d -> p t d", p=128), writes=[rx])
                hT, rhT = hT_r.next()
                sc.dma('sp', hT, hTd.rearrange("(c p) s -> p c s", p=128)[:, :, ci * 512:(ci + 1) * 512], writes=[rhT])
                oT, roT = oT_r.next()
                sc.dma('sp', oT, oTd.rearrange("(c p) s -> p c s", p=128)[:, :, ci * 512:(ci + 1) * 512], writes=[roT])
                for dm in range(8):
                    ts_ = []
                    for br in range(3):
                        gb, rgb = gb_r.next()
                        col = br * D + dm * 128
                        for c in range(8):
                            sc.mm(gb, Wg[:, c, col:col + 128], hT[:, c, :], start=(c == 0), stop=(c == 7),
                                  reads=[rW, rhT], writes=[rgb])
                        g, rg = g_r.next()
                        sc.act(g, gb, AF.Sigmoid, reads=[rgb], writes=[rg])
                        bb, rbb = bb_r.next()
                        ks = branch_k[br]
                        for i, k in enumerate(ks):
                            sc.mm(bb, Wb[:, k, dm * 128:(dm + 1) * 128], oT[:, k, :], start=(i == 0), stop=(i == len(ks) - 1),
                                  reads=[rW, roT], writes=[rbb])
                        t, rt = t_r.next()
                        sc.tt('dve', t, bb, g, ALU.mult, reads=[rbb, rg], writes=[rt])
                        ts_.append((t, rt))
                    sc.tt('pool', ts_[0][0], ts_[0][0], ts_[1][0], ALU.add, reads=[ts_[0][1], ts_[1][1]], writes=[ts_[0][1]])
                    sc.tt('pool', mixT[:, dm, :], ts_[0][0], ts_[2][0], ALU.add, reads=[ts_[0][1], ts_[2][1]], writes=[rmix])
                for t in range(4):
                    for half in range(2):
                        yb, ryb = yb_r.next()
                        for k in range(8):
                            sc.mm(yb, mixT[:, k, t * 128:(t + 1) * 128], Wo[:, k, half * 512:(half + 1) * 512],
                                  start=(k == 0), stop=(k == 7), reads=[rmix, rW], writes=[ryb])
                        sc.tt('dve', xs[:, t, half * 512:(half + 1) * 512], xs[:, t, half * 512:(half + 1) * 512], yb, ALU.add,
                              reads=[rx, ryb], writes=[rx])
                sc.dma(STQ, x1d[ci * 512:(ci + 1) * 512, :].rearrange("(t p) d -> p t d", p=128), xs, reads=[rx], writes=[r_x1d])
            sc.barrier()

        def phase_C2(l, xdst):
            ar.top = base_top
            Wgu = v3(ar.bf(8 * 2 * DFF), 8)
            Wd = v3(ar.bf(22 * D), 22)
            gF = ar.f32(D)
            rW = Res('W')
            mark = ar.top
            sring = Ring([(ar.f32(2048), Res()) for i in range(4)])
            for c in range(8):
                load_cast(Wgu[:, c, :], w_gate_up[l, c * 128:(c + 1) * 128, :], 2 * DFF, sring, rW)
            for f in range(22):
                load_cast(Wd[:, f, :], w_down[l, f * 128:(f + 1) * 128, :], D, sring, rW)
            sc.dma('sp', gF, AP(ffn_norm.tensor, l * D, [[0, 128], [1, D]]), writes=[rW])
            sc.barrier()
            ar.top = mark
            xs_r = Ring([(v3(ar.f32(2 * D), 2), Res()) for i in range(2)])
            hb = v3(ar.bf(2 * D), 2)
            rhb = Res()
            sq = ar.f32(D)
            ssq = ar.f32(4)
            rstd = ar.f32(4)
            rtmp = Res()
            hT = v3(ar.bf(8 * 256), 8)
            rhT = Res()
            actT = v3(ar.bf(22 * 256), 22)
            ract = Res()
            sg_r = Ring([(ar.f32(256), Res()) for i in range(2)])
            tb_r = Ring([(banks[i], bres[i]) for i in (0, 1)])
            fb_r = Ring([(banks[i], bres[i]) for i in (2, 3, 4)])
            yb_r = Ring([(banks[i], bres[i]) for i in (5, 6, 7)])
            r_out = Res()
            for ci in range(S // 256):
                xs, rx = xs_r.next()
                sc.dma('sp', xs, x1d[ci * 256:(ci + 1) * 256, :].rearrange("(t p) d -> p t d", p=128), writes=[rx])
                rmsnorm_tm(xs, rx, gF, rW, hb, rhb, 2, sq, ssq, rstd, rtmp)
                transpose_to(hb, rhb, hT, rhT, 2, tb_r, ('dve', 'act'))
                for f in range(22):
                    fb, rfb = fb_r.next()
                    for half, col in ((0, f * 128), (1, DFF + f * 128)):
                        for c in range(8):
                            sc.mm(fb[:, half * 256:(half + 1) * 256], Wgu[:, c, col:col + 128], hT[:, c, :],
                                  start=(c == 0), stop=(c == 7), reads=[rW, rhT], writes=[rfb])
                    sg, rsg = sg_r.next()
                    sc.act(sg, fb[:, 0:256], AF.Silu, reads=[rfb], writes=[rsg])
                    sc.tt('dve', actT[:, f, :], sg, fb[:, 256:512], ALU.mult, reads=[rsg, rfb], writes=[ract])
                for t in range(2):
                    for half in range(2):
                        yb, ryb = yb_r.next()
                        for f in range(22):
                            sc.mm(yb, actT[:, f, t * 128:(t + 1) * 128], Wd[:, f, half * 512:(half + 1) * 512],
                                  start=(f == 0), stop=(f == 21), reads=[ract, rW], writes=[ryb])
                        sc.tt('dve', xs[:, t, half * 512:(half + 1) * 512], xs[:, t, half * 512:(half + 1) * 512], yb, ALU.add,
                              reads=[rx, ryb], writes=[rx])
                sc.dma(STQ, xdst[ci * 256:(ci + 1) * 256, :].rearrange("(t p) d -> p t d", p=128), xs, reads=[rx], writes=[r_out])
            sc.barrier()

        PH = {}
        exec_phases = phases
        prologue()
        for l in range(depth):
            xsrc = x_in if l == 0 else xmd
            xdst = out if l == depth - 1 else xmd
            if exec_phases is None or 'A' in exec_phases:
                phase_A(l, xsrc)
            if exec_phases is None or 'SB' in exec_phases:
                phase_SB(l)
            if exec_phases is None or 'MB' in exec_phases:
                phase_MB(l)
            if exec_phases is None or 'NSA' in exec_phases:
                phase_NSA(l)
            if inject:
                sc.dma('sp', oTd, oT_in, writes=[Res()])
                sc.barrier()
            if exec_phases is None or 'C1' in exec_phases:
                phase_C1(l, xsrc)
            if exec_phases is None or 'C2' in exec_phases:
                phase_C2(l, xdst)
        sc.barrier()
        sc.emit()
    return nc


def core_inputs(inputs, b, S, depth, cbh, cfh):
    f = lambda a: np.ascontiguousarray(np.asarray(a, dtype=np.float32))
    gains = np.stack([f(inputs['moba_q_norm']), f(inputs['moba_k_norm']), f(inputs['nsa_q_norm']),
                      f(inputs['nsa_k_norm'])[:, 0], f(inputs['nsa_k_norm'])[:, 1], f(inputs['nsa_k_norm'])[:, 2]], axis=1)
    m = {
        'x': f(inputs['x'][b, :S]),
        'rel_bias': f(inputs['rel_bias']),
        'attn_norm': f(inputs['attn_norm'])[:depth],
        'w_in': f(inputs['w_in'])[:depth],
        'gains': np.ascontiguousarray(gains[:depth]),
        'nsa_cmp_pos': f(inputs['nsa_cmp_pos'])[:depth],
        'nsa_cmp_w1': f(inputs['nsa_cmp_w1'])[:depth],
        'nsa_cmp_w2': f(inputs['nsa_cmp_w2'])[:depth],
        'w_branch': f(inputs['w_branch'])[:depth],
        'w_out': f(inputs['w_out'])[:depth],
        'ffn_norm': f(inputs['ffn_norm'])[:depth],
        'w_gate_up': f(inputs['w_gate_up'])[:depth],
        'w_down': f(inputs['w_down'])[:depth],
        'cb': cbh,
        'cf': cfh,
    }
    return m


def kernel(**inputs):
    x = np.asarray(inputs['x'])
    B, S, _ = x.shape
    depth = int(np.asarray(inputs['w_in']).shape[0])
    cbh, cfh, _, _ = host_consts(S)
    nc = build(S, depth)
    in_maps = [core_inputs(inputs, b, S, depth, cbh, cfh) for b in range(B)]
    res = run_bass_kernel_spmd(nc, in_maps, core_ids=list(range(B)))
    return np.stack([np.asarray(r['out'], dtype=np.float32) for r in res.results], axis=0)
```

```python
import math
from contextlib import ExitStack

import numpy as np
import ml_dtypes
import concourse.bass as bass
import concourse.mybir as mybir
from concourse.bass_types import AP
from concourse.bass_utils import run_bass_kernel_spmd

F32 = mybir.dt.float32
BF16 = mybir.dt.bfloat16
AF = mybir.ActivationFunctionType
ALU = mybir.AluOpType
AX = mybir.AxisListType
ENGS = ('pe', 'act', 'dve', 'pool', 'sp')
NDMA = 8
STQ = 'sp'

D = 1024
NIN = 5912
NQKV = 2840
DFF = 2816
EPS = 1e-6
NEG = -1e30
BIG = 1e30
TINY = 1e-30
GOFF = 2304
GL = 4864
FM_COLS = [0, 128, 256, 384, 768, 896, 1024, 1152, 1536, 1664, 1792, 1920, 2048, 2176, 2304, 2560]
FM_KIND = [None, None, None, None, 0, 0, 1, 1, 2, 2, 2, 2, None, None, 4, 5]


class Res:
    __slots__ = ('name', 'lw', 'rd')

    def __init__(self, name=''):
        self.name = name
        self.lw = None
        self.rd = {}


class Ring:
    def __init__(self, items):
        self.items = items
        self.i = 0

    def next(self):
        it = self.items[self.i]
        self.i = (self.i + 1) % len(self.items)
        return it


class Sched:
    def __init__(self, nc, stack):
        self.nc = nc
        self.q = {e: [] for e in ENGS}
        self.cnt = {e: 0 for e in ENGS}
        self.waited = {e: {} for e in ENGS}
        self.sem = {}
        for e in ENGS:
            self.sem[('e', e)] = stack.enter_context(nc.semaphore('s_' + e))
        self.dcnt = {}
        self.drr = {}
        for e in ('sp', 'act', 'pool'):
            self.drr[e] = 0
            for i in range(NDMA):
                k = ('d', e, i)
                self.sem[k] = stack.enter_context(nc.semaphore('d_%s%d' % (e, i)))
                self.dcnt[k] = 0

    def _waits(self, eng, reads, writes, extra=()):
        deps = {}

        def add(ev):
            if ev is None:
                return
            k, v = ev
            if deps.get(k, 0) < v:
                deps[k] = v
        for r in reads:
            add(r.lw)
        for w in writes:
            add(w.lw)
            for k, v in w.rd.items():
                add((k, v))
        for ev in extra:
            add(ev)
        out = []
        wd = self.waited[eng]
        for k, v in deps.items():
            if k == ('e', 'pe') and eng == 'pe':
                continue
            if wd.get(k, 0) >= v:
                continue
            wd[k] = v
            out.append((k, v))
        return out

    def _mark(self, ev, reads, writes):
        k, v = ev
        for r in reads:
            if r.rd.get(k, 0) < v:
                r.rd[k] = v
        for w in writes:
            w.lw = ev
            w.rd = {}

    def op(self, eng, fn, reads=(), writes=()):
        waits = self._waits(eng, reads, writes)
        self.cnt[eng] += 1
        ev = (('e', eng), self.cnt[eng])
        self.q[eng].append((fn, waits, ('e', eng), 1))
        self._mark(ev, reads, writes)
        return ev

    def dma(self, eng, out, in_, reads=(), writes=(), **kw):
        i = self.drr[eng]
        self.drr[eng] = (i + 1) % NDMA
        k = ('d', eng, i)
        prev = (k, 16 * self.dcnt[k]) if self.dcnt[k] else None
        waits = self._waits(eng, reads, writes, extra=(prev,) if prev else ())
        self.dcnt[k] += 1
        ev = (k, 16 * self.dcnt[k])
        self.q[eng].append((lambda e: e.dma_start(out=out, in_=in_, **kw), waits, k, 16))
        self._mark(ev, reads, writes)
        return ev

    def barrier(self):
        evs = [(('e', e), self.cnt[e]) for e in ENGS if self.cnt[e] > 0]
        evs += [(k, 16 * c) for k, c in self.dcnt.items() if c > 0]
        for e in ENGS:
            wd = self.waited[e]
            waits = []
            for k, v in evs:
                if k == ('e', e):
                    continue
                if wd.get(k, 0) >= v:
                    continue
                wd[k] = v
                waits.append((k, v))
            self.q[e].append((None, waits, None, 0))

    def emit(self):
        nc = self.nc
        with nc.Block() as block:
            def run(name):
                def f(e):
                    for fn, waits, k, inc in self.q[name]:
                        for wk, wv in waits:
                            e.wait_ge(self.sem[wk], wv)
                        if fn is not None:
                            fn(e).then_inc(self.sem[k], inc)
                return f
            block.tensor(run('pe'))
            block.scalar(run('act'))
            block.vector(run('dve'))
            block.gpsimd(run('pool'))
            block.sync(run('sp'))

    def mm(self, out, lhsT, rhs, start=True, stop=True, reads=(), writes=()):
        return self.op('pe', lambda e: e.matmul(out, lhsT, rhs, start=start, stop=stop), reads, writes)

    def tr(self, out, in_, ident, reads=(), writes=()):
        return self.op('pe', lambda e: e.transpose(out, in_, ident), reads, writes)

    def act(self, out, in_, func, reads=(), writes=(), **kw):
        return self.op('act', lambda e: e.activation(out, in_, func, **kw), reads, writes)

    def tt(self, eng, out, in0, in1, op, reads=(), writes=()):
        return self.op(eng, lambda e: e.tensor_tensor(out, in0, in1, op), reads, writes)

    def ts(self, eng, out, in0, s1, s2, op0, op1=None, reads=(), writes=()):
        if op1 is None:
            return self.op(eng, lambda e: e.tensor_scalar(out, in0, s1, s2, op0), reads, writes)
        return self.op(eng, lambda e: e.tensor_scalar(out, in0, s1, s2, op0, op1), reads, writes)

    def stt(self, eng, out, in0, scalar, in1, op0, op1, reads=(), writes=()):
        return self.op(eng, lambda e: e.scalar_tensor_tensor(out, in0, scalar, in1, op0, op1), reads, writes)

    def cp(self, eng, out, in_, reads=(), writes=()):
        if eng == 'act':
            return self.op('act', lambda e: e.copy(out, in_), reads, writes)
        return self.op(eng, lambda e: e.tensor_copy(out, in_), reads, writes)

    def memset(self, eng, ap, val, writes=()):
        return self.op(eng, lambda e: e.memset(ap, val), (), writes)


class Arena:
    def __init__(self, ap, size):
        self.ap = ap
        self.size = size
        self.top = 0

    def bf(self, n):
        a = self.top
        self.top += (n + 31) // 32 * 32
        assert self.top <= self.size, ('arena overflow', self.top, self.size)
        return self.ap[:, a:a + n]

    def f32(self, n):
        a = self.top
        self.top += (2 * n + 31) // 32 * 32
        assert self.top <= self.size, ('arena overflow', self.top, self.size)
        return self.ap[:, a:a + 2 * n].bitcast(F32)


def v3(ap, a):
    return ap.rearrange("p (a b) -> p a b", a=a)


LAG = 2
WARM = 0
NDUMMY = 0


def pipe(n, *stages, lag=LAG):
    ns = len(stages)
    st = [None] * n
    for t in range(n + lag * (ns - 1)):
        for k, f in enumerate(stages):
            i = t - k * lag
            if 0 <= i < n:
                st[i] = f(i, st[i]) if k else f(i)


def t5_bucket_np(d):
    n = np.maximum(d, 0)
    nf = np.maximum(n, 1).astype(np.float32)
    large = 16 + (np.log(nf / np.float32(16)) / np.float32(math.log(128 / 16)) * np.float32(16)).astype(np.int32)
    large = np.minimum(large, 31)
    return np.where(n < 16, n, large)


def host_consts(S):
    k = np.arange(128)[:, None]
    q = np.arange(128)[None, :]
    ident = (k == q)
    J = (k + q == 127)
    uincl = (k >= q)
    ones = np.ones((128, 128), bool)
    blk = (k // 64 == q // 64)
    maskL = (k < q)
    m512 = (q < k)
    zeros = np.zeros((128, 512), bool)
    n_cmp = (S - 32) // 16 + 1
    nct = (n_cmp + 127) // 128
    n = np.arange(nct * 128)[:, None]
    m = np.arange(128)[None, :]
    ovl = ((16 * n < 64 * m + 64) & (16 * n + 32 > 64 * m) & (n < n_cmp))
    ovl = ovl.reshape(nct, 128, 128).transpose(1, 0, 2).reshape(128, nct * 128)
    mm_ = np.arange(128)[:, None]
    xx = np.arange(S)[None, :]
    B = (xx // 64 == mm_)
    cb = np.concatenate([ident, J, uincl, ones, blk, maskL, m512, zeros, ovl, B], axis=1).astype(np.float32)
    cb = cb.astype(ml_dtypes.bfloat16)
    negmask = np.where(k >= q, -1e4, 0.0).astype(np.float32)
    y = np.arange(256)[None, :]
    rel = y - 126 - (k >= 64)
    cs = np.where((rel == 0) | (rel == -1), BIG, np.where(rel > 0, NEG, 0.0)).astype(np.float32)
    oh = np.zeros((128, 128), np.float32)
    b = t5_bucket_np(np.arange(128))
    oh[b, np.arange(128)] = 1.0
    cf = np.concatenate([negmask, cs, oh], axis=1).astype(np.float32)
    return cb, cf, n_cmp, nct


def build(S, depth, debug=False, phases=None, dbg=None, inject=False):
    NT = S // 128
    NC = S // 512
    cbh, cfh, n_cmp, NCT = host_consts(S)
    NCB = cbh.shape[1]
    NCF = cfh.shape[1]
    nc = bass.Bass("TRN2", target_bir_lowering=False)
    dt_in = lambda name, shape, dt=F32: nc.dram_tensor(name, shape, dt, kind="ExternalInput").ap()
    dbg_kind = "ExternalOutput" if debug else "Internal"
    scr = lambda name, shape, dt: nc.dram_tensor(name, shape, dt, kind=dbg_kind).ap()
    x_in = dt_in("x", [S, D])
    rel_bias = dt_in("rel_bias", [32, 12])
    attn_norm = dt_in("attn_norm", [depth, D])
    w_in = dt_in("w_in", [depth, D, NIN])
    gains = dt_in("gains", [depth, 6, 64])
    cmp_pos = dt_in("nsa_cmp_pos", [depth, 2, 32, 64])
    cmp_w1 = dt_in("nsa_cmp_w1", [depth, 2, 2048, 256])
    cmp_w2 = dt_in("nsa_cmp_w2", [depth, 2, 256, 64])
    w_branch = dt_in("w_branch", [depth, D, D])
    w_out = dt_in("w_out", [depth, D, D])
    ffn_norm = dt_in("ffn_norm", [depth, D])
    w_gate_up = dt_in("w_gate_up", [depth, D, 2 * DFF])
    w_down = dt_in("w_down", [depth, DFF, D])
    cb_in = dt_in("cb", [128, NCB], BF16)
    cf_in = dt_in("cf", [128, NCF])
    out = nc.dram_tensor("out", [S, D], F32, kind="ExternalOutput").ap()

    hTd = scr("hTd", [D, S], BF16)
    fmd = scr("fmd", [2048, S], BF16)
    tmvd = scr("tmvd", [S, 832], BF16)
    nsgd = scr("nsgd", [S, 32], F32)
    oTd = scr("oTd", [D, S], BF16)
    x1d = scr("x1d", [S, D], F32)
    xmd = scr("xmd", [S, D], F32)
    gvd = scr("gvd", [12, GL], BF16)
    oT_in = dt_in("oT_in", [D, S], BF16) if inject else None

    with ExitStack() as st:
        sc = Sched(nc, st)
        ASZ = 98304
        arena_ap = nc.alloc_sbuf_tensor("arena", [128, ASZ], BF16).ap()
        ar = Arena(arena_ap, ASZ)
        banks = [nc.alloc_psum_tensor("bank%d" % i, [128, 512], F32).ap() for i in range(8)]
        bres = [Res("bank%d" % i) for i in range(8)]

        CB_GLOBAL = 8 * 128 + 384
        cb = ar.bf(CB_GLOBAL)
        ident = cb[:, 0:128]
        Jm = cb[:, 128:256]
        uincl = cb[:, 256:384]
        onesb = cb[:, 384:512]
        blkones = cb[:, 512:640]
        maskL = cb[:, 640:768]
        m512 = cb[:, 768:896]
        zerob = cb[:, 896:1408]
        cf = ar.f32(384)
        negmask = cf[:, 0:128]
        cslide = cf[:, 128:384]
        Etab = ar.bf(24 * 128)
        r_const = Res('const')
        sc.dma('sp', cb, cb_in[:, 0:CB_GLOBAL], writes=[r_const])
        sc.dma('sp', cf, cf_in[:, 0:384], writes=[r_const])
        base_top = ar.top

        def E0(h):
            return Etab[:, (2 * h) * 128:(2 * h + 1) * 128]

        def E128(h):
            return Etab[:, (2 * h + 1) * 128:(2 * h + 2) * 128]

        def prologue():
            ar.top = base_top
            tb = ar.f32(12)
            oh = ar.f32(128)
            br = ar.f32(128)
            negc = ar.f32(1)
            gsb = ar.bf(GL)
            hk = ar.bf(128)
            r = Res('pro')
            rb = bres[0]
            sc.dma('sp', tb[0:32, :], rel_bias, writes=[r])
            sc.dma('sp', oh[0:32, :], cf_in[0:32, 384:512], writes=[r])
            sc.mm(banks[0][0:12, 0:128], tb[0:32, 0:12], oh[0:32, 0:128], reads=[r], writes=[rb])
            sc.cp('dve', br[0:12, :], banks[0][0:12, 0:128], reads=[rb], writes=[r])
            sc.ts('dve', negc[0:12, :], br[0:12, 127:128], -1.0, None, ALU.mult, reads=[r], writes=[r])
            sc.memset('pool', gsb[0:12, 0:GOFF], 0.0, writes=[r])
            sc.memset('pool', gsb[0:12, GOFF + 128:GL], 1.0, writes=[r])
            sc.act(gsb[0:12, GOFF:GOFF + 128], br[0:12, :], AF.Exp, reads=[r], writes=[r], bias=negc[0:12, :])
            rg = Res('gvd')
            sc.dma('sp', gvd, gsb[0:12, :], reads=[r], writes=[rg])
            hkr = Res('hk')
            for h in range(12):
                for di, dl in enumerate((0, 128)):
                    src = AP(gvd.tensor, h * GL + GOFF + dl - 127, [[1, 128], [1, 128]])
                    sc.dma('sp', hk, src, reads=[rg], writes=[hkr])
                    sc.mm(banks[1][:, 0:128], Jm, hk, reads=[hkr, r_const], writes=[bres[1]])
                    sc.cp('dve', Etab[:, (2 * h + di) * 128:(2 * h + di + 1) * 128], banks[1][:, 0:128],
                          reads=[bres[1]], writes=[r_const])
            sc.barrier()

        def load_v(dst, src, writes):
            step = 8
            for t0 in range(0, NT, step):
                t1 = min(NT, t0 + step)
                sc.dma('sp', dst[:, t0:t1, :], src[t0 * 128:t1 * 128, :].rearrange("(t p) d -> p t d", p=128), writes=writes)

        CAST_ENG = ('pool', 'dve', 'act')
        cast_i = [0]
        def load_cast(dst, src, n, stage_ring, rdst):
            o = 0
            while o < n:
                w = min(2048, n - o)
                sg, rs = stage_ring.next()
                sc.dma('sp', sg[:, 0:w], src[:, o:o + w], writes=[rs])
                sc.cp(CAST_ENG[cast_i[0] % 3], dst[:, o:o + w], sg[:, 0:w], reads=[rs], writes=[rdst])
                cast_i[0] += 1
                o += w

        def rmsnorm_tm(xs, rx, gB, rg, hb, rh, ntile, sq, ssq, rstd, rtmp):
            for t in range(ntile):
                sc.act(sq, xs[:, t, :], AF.Square, reads=[rx], writes=[rtmp])
                sc.op('dve', lambda e, t=t: e.tensor_reduce(ssq[:, t:t + 1], sq, AX.X, ALU.add), reads=[rtmp], writes=[rtmp])
            sc.act(rstd[:, 0:ntile], ssq[:, 0:ntile], AF.Ln, reads=[rtmp], writes=[rtmp], scale=1.0 / D, bias=EPS)
            sc.act(rstd[:, 0:ntile], rstd[:, 0:ntile], AF.Exp, reads=[rtmp], writes=[rtmp], scale=-0.5)
            for t in range(ntile):
                sc.stt('dve', hb[:, t, :], xs[:, t, :], rstd[:, t:t + 1], gB, ALU.mult, ALU.mult,
                       reads=[rx, rtmp, rg], writes=[rh])

        def transpose_to(hb, rh, hT, rhT, ntile, bank_ring, evac_engs):
            for c in range(8):
                bk, rb = bank_ring.next()
                pb = bk.bitcast(BF16)
                for t in range(ntile):
                    sc.tr(pb[:, t * 128:(t + 1) * 128], hb[:, t, c * 128:(c + 1) * 128], ident,
                          reads=[rh, r_const], writes=[rb])
                sc.cp(evac_engs[c % len(evac_engs)], hT[:, c, :], pb[:, 0:ntile * 128], reads=[rb], writes=[rhT])

        def phase_A(l, xsrc):
            ar.top = base_top
            W = v3(ar.bf(8 * NQKV), 8)
            gA = ar.f32(D)
            gcol = ar.f32(8)
            rW = Res('W')
            mark = ar.top
            stg = [(ar.f32(2048), Res('stg%d' % i)) for i in range(4)]
            sring = Ring(stg)
            for c in range(8):
                load_cast(W[:, c, :], w_in[l, c * 128:(c + 1) * 128, 0:NQKV], NQKV, sring, rW)
            sc.dma('sp', gA, AP(attn_norm.tensor, l * D, [[0, 128], [1, D]]), writes=[rW])
            for gi in range(6):
                for half in range(2):
                    sc.dma('sp', gcol[half * 64:(half + 1) * 64, gi:gi + 1],
                           AP(gains.tensor, (l * 6 + gi) * 64, [[1, 64], [1, 1]]), writes=[rW])
            sc.barrier()
            if dbg == 'W':
                return
            ar.top = mark
            xs_r = Ring([(v3(ar.f32(4 * D), 4), Res('xs%d' % i)) for i in range(2)])
            hb = v3(ar.bf(4 * D), 4)
            rhb = Res('hb')
            sq = ar.f32(D)
            ssq = ar.f32(4)
            rstd = ar.f32(4)
            rtmp = Res('tmp')
            hT_r = Ring([(v3(ar.bf(8 * 512), 8), Res('hT%d' % i)) for i in range(2)])
            sqb_r = Ring([(ar.bf(512), Res('sqb%d' % i)) for i in range(2)])
            rs_r = Ring([(ar.f32(512), Res('rs%d' % i)) for i in range(2)])
            fst_r = Ring([(ar.bf(512), Res('fst%d' % i)) for i in range(3)])
            tst_r = Ring([(ar.bf(832), Res('tst%d' % i)) for i in range(2)])
            gst_r = Ring([(ar.f32(24), Res('gst%d' % i)) for i in range(2)])
            tb_r = Ring([(banks[i], bres[i]) for i in (0, 1)])
            pb_r = Ring([(banks[i], bres[i]) for i in (2, 3, 4)])
            nb_r = Ring([(banks[i], bres[i]) for i in (5,)])
            vb_r = Ring([(banks[i], bres[i]) for i in (6, 7)])
            r_hTd, r_fmd, r_tmvd, r_nsgd = Res(), Res(), Res(), Res()
            for ci in range(NC):
                xs, rx = xs_r.next()
                sc.dma('sp', xs, xsrc[ci * 512:(ci + 1) * 512, :].rearrange("(t p) d -> p t d", p=128), writes=[rx])
                if dbg == 'ld':
                    continue
                rmsnorm_tm(xs, rx, gA, rW, hb, rhb, 4, sq, ssq, rstd, rtmp)
                if dbg == 'norm':
                    continue
                hT, rhT = hT_r.next()
                transpose_to(hb, rhb, hT, rhT, 4, tb_r, ('dve', 'act'))
                if dbg == 'tr':
                    continue
                sc.dma(STQ, hTd.rearrange("(c p) s -> p c s", p=128)[:, :, ci * 512:(ci + 1) * 512], hT,
                       reads=[rhT], writes=[r_hTd])
                for j in range(16):
                    col = FM_COLS[j]
                    bk, rb = pb_r.next()
                    for c in range(8):
                        sc.mm(bk, W[:, c, col:col + 128], hT[:, c, :], start=(c == 0), stop=(c == 7),
                              reads=[rW, rhT], writes=[rb])
                    fs, rf = fst_r.next()
                    if FM_KIND[j] is None:
                        sc.cp('act' if j % 2 else 'dve', fs, bk, reads=[rb], writes=[rf])
                    else:
                        gi = FM_KIND[j]
                        sqb, rsq = sqb_r.next()
                        sc.act(sqb, bk, AF.Square, reads=[rb], writes=[rsq])
                        b2, rb2 = nb_r.next()
                        sc.mm(b2, blkones, sqb, reads=[rsq, r_const], writes=[rb2])
                        rs, rrs = rs_r.next()
                        sc.act(rs, b2, AF.Ln, reads=[rb2], writes=[rrs], scale=1.0 / 64, bias=EPS)
                        sc.act(rs, rs, AF.Exp, reads=[rrs], writes=[rrs], scale=-0.5)
                        sc.stt('dve', fs, bk, gcol[:, gi:gi + 1], rs, ALU.mult, ALU.mult, reads=[rb, rrs, rW], writes=[rf])
                    sc.dma(STQ, fmd[j * 128:(j + 1) * 128, ci * 512:(ci + 1) * 512], fs, reads=[rf], writes=[r_fmd])
                if dbg == 'fm':
                    continue
                for t in range(4):
                    b1, rb1 = vb_r.next()
                    b2, rb2 = vb_r.next()
                    lt = lambda c: hT[:, c, t * 128:(t + 1) * 128]
                    for (bk, rb, o, c0, w) in ((b1, rb1, 0, 512, 256), (b1, rb1, 256, 1280, 256),
                                               (b2, rb2, 0, 2432, 128), (b2, rb2, 128, 2688, 152)):
                        for c in range(8):
                            sc.mm(bk[:, o:o + w], lt(c), W[:, c, c0:c0 + w], start=(c == 0), stop=(c == 7),
                                  reads=[rW, rhT], writes=[rb])
                    ts_, rts = tst_r.next()
                    sc.cp('act', ts_[:, 0:512], b1, reads=[rb1], writes=[rts])
                    sc.cp('dve', ts_[:, 512:768], b2[:, 0:256], reads=[rb2], writes=[rts])
                    sc.act(ts_[:, 768:816].bitcast(F32), b2[:, 256:280], AF.Sigmoid, reads=[rb2], writes=[rts])
                    r0 = (ci * 4 + t) * 128
                    sc.dma(STQ, tmvd[r0:r0 + 128, 0:816], ts_[:, 0:816], reads=[rts], writes=[r_tmvd])
            sc.barrier()

        def phase_SB(l):
            ar.top = base_top
            kT = ar.bf(S)
            qT = ar.bf(S)
            v = v3(ar.bf(NT * 64), NT)
            rk = Res('kqv')
            R_r = Ring([(ar.bf(512), Res()) for i in range(LAG + 2)])
            zc_r = Ring([(ar.f32(512), Res()) for i in range(LAG + 2)])
            u_r = Ring([(ar.f32(512), Res()) for i in range(2)])
            sp_r = Ring([(ar.bf(512), Res()) for i in range(LAG + 2)])
            w_r = Ring([(ar.f32(512), Res()) for i in range(2)])
            a_r = Ring([(ar.bf(512), Res()) for i in range(LAG + 2)])
            os_r = Ring([(ar.bf(512), Res()) for i in range(2)])
            zb_r = Ring([(banks[i], bres[i]) for i in (0, 1)])
            tb_r = Ring([(banks[i], bres[i]) for i in (2, 3, 7)])
            rdum = Res('dummy')
            ob_r = Ring([(banks[i], bres[i]) for i in (4, 5)])
            r_oTd = Res()
            for h in range(4):
                sc.dma('sp', kT[0:64, :], fmd[256 + 64 * h:256 + 64 * h + 64, :], writes=[rk])
                sc.dma('sp', qT[0:64, :], fmd[64 * h:64 * h + 64, :], writes=[rk])
                load_v(v, tmvd[:, 64 * h:64 * h + 64], [rk])
                for c in range(NC):
                    ob, rob = ob_r.next()
                    sc.mm(ob[0:64, :], zerob[:, 0:64], zerob, start=True, stop=False, reads=[r_const], writes=[rob])
                    for _ in range(WARM):
                        sc.mm(banks[6], onesb, zerob, reads=[r_const], writes=[rdum])
                    last = 4 * c + 3
                    n_t = last + 1
                    Rcur = [R_r.next()]
                    sc.memset('pool', Rcur[0][0], 0.0, writes=[Rcur[0][1]])

                    def s1(i, c=c, last=last, n_t=n_t, Rcur=Rcur):
                        kt = last - i
                        rel = kt - 4 * c
                        c0 = 128 * max(rel, 0)
                        zb, rzb = zb_r.next()
                        sc.mm(zb[:, c0:512], kT[0:64, kt * 128:(kt + 1) * 128], qT[0:64, c * 512 + c0:(c + 1) * 512],
                              reads=[rk], writes=[rzb])
                        for _ in range(NDUMMY):
                            sc.mm(banks[6], onesb, zerob, reads=[r_const], writes=[rdum])
                        zc, rzc = zc_r.next()
                        sc.ts('dve', zc[:, c0:512], zb[:, c0:512], 0.125, 40.0, ALU.mult, ALU.min, reads=[rzb], writes=[rzc])
                        u, ru = u_r.next()
                        sc.act(u[:, c0:512], zc[:, c0:512], AF.Exp, reads=[rzc], writes=[ru])
                        sp, rsp = sp_r.next()
                        sc.act(sp[:, c0:512], u[:, c0:512], AF.Ln, reads=[ru], writes=[rsp], bias=1.0)
                        if rel >= 0:
                            sc.tt('pool', sp[:, c0:c0 + 128], sp[:, c0:c0 + 128], maskL, ALU.mult,
                                  reads=[rsp, r_const], writes=[rsp])
                        if c0 > 0:
                            sc.memset('pool', sp[:, 0:c0], 0.0, writes=[rsp])
                        Ri, rRi = Rcur[0]
                        if i < n_t - 1:
                            Rn, rRn = R_r.next()
                            sc.tt('pool', Rn, Ri, sp, ALU.add, reads=[rRi, rsp], writes=[rRn])
                            Rcur[0] = (Rn, rRn)
                        return (kt, rel, c0, zc, rzc, sp, rsp, Ri, rRi)

                    def s2(i, stt_, ob=ob, rob=rob):
                        kt, rel, c0, zc, rzc, sp, rsp, Ri, rRi = stt_
                        tb, rtb = tb_r.next()
                        sc.mm(tb[:, c0:512], uincl, sp[:, c0:512], start=True, stop=(i == 0),
                              reads=[rsp, r_const], writes=[rtb])
                        if i > 0:
                            sc.mm(tb[:, c0:512], onesb, Ri[:, c0:512], start=False, stop=True,
                                  reads=[rRi, r_const], writes=[rtb])
                        w, rw = w_r.next()
                        sc.tt('dve', w[:, c0:512], zc[:, c0:512], tb[:, c0:512], ALU.subtract, reads=[rzc, rtb], writes=[rw])
                        if rel >= 0:
                            sc.tt('pool', w[:, c0:c0 + 128], w[:, c0:c0 + 128], negmask, ALU.add,
                                  reads=[rw, r_const], writes=[rw])
                        a, ra = a_r.next()
                        sc.act(a[:, c0:512], w[:, c0:512], AF.Exp, reads=[rw], writes=[ra])
                        return (kt, c0, a, ra)

                    def s3(i, stt_, ob=ob, rob=rob):
                        kt, c0, a, ra = stt_
                        sc.mm(ob[0:64, c0:512], v[:, kt, :], a[:, c0:512], start=False, stop=(kt == 0),
                              reads=[rk, ra], writes=[rob])

                    pipe(n_t, s1, s2, s3)
                    os_, ros = os_r.next()
                    sc.cp('act', os_[0:64, :], ob[0:64, :], reads=[rob], writes=[ros])
                    sc.dma('sp', oTd[64 * h:64 * h + 64, c * 512:(c + 1) * 512], os_[0:64, :], reads=[ros], writes=[r_oTd])
            sc.barrier()


        def phase_MB(l):
            ar.top = base_top
            NBLK = S // 256
            kT = ar.bf(S)
            qT = ar.bf(S)
            va = ar.bf(NT * 130).rearrange("p (t e d) -> p t e d", e=2, d=65)
            rk = Res('kqv')
            km = ar.f32(32)
            kmz = v3(ar.f32(64), 2)
            rkm = Res('km')
            qf_r = Ring([(ar.f32(512), Res()) for i in range(2)])
            scv = ar.f32(256).rearrange("p (e j n) -> p e j n", e=2, j=4)
            m8 = ar.f32(64)
            selw = ar.f32(256).rearrange("p (e j n) -> p e j n", e=2, j=4)
            rsel = Res('sel')
            acc_r = Ring([(ar.f32(2 * 4 * 65).rearrange("p (e j d) -> p e j d", e=2, j=4),
                           [[Res() for j in range(4)] for e in range(2)]) for i in range(2)])
            rl = ar.f32(8).rearrange("p (e j o) -> p e j o", e=2, o=1)
            rrl = Res('rl')
            o_tok = v3(ar.bf(NT * 256), NT)
            rot = Res('otok')
            p_r = Ring([(ar.bf(512), Res()) for i in range(2 * (LAG + 2))])
            st_r = Ring([(ar.bf(512), Res()) for i in range(2)])
            lb_r = Ring([(banks[i], bres[i]) for i in (0, 1, 7)])
            ob_r = Ring([(banks[i], bres[i]) for i in (2, 3, 4, 6)])
            sb_r = Ring([(banks[i], bres[i]) for i in (5,)])
            tb_r = Ring([(banks[i], bres[i]) for i in (5, 6)])
            r_oTd = Res()
            sc.memset('pool', va[:, :, :, 64:65], 1.0, writes=[rk])
            for hp in range(2):
                sc.dma('sp', kT, fmd[768 + 128 * hp:768 + 128 * hp + 128, :], writes=[rk])
                sc.dma('sp', qT, fmd[512 + 128 * hp:512 + 128 * hp + 128, :], writes=[rk])
                for e in range(2):
                    h = 2 * hp + e
                    load_v(va[:, :, e, 0:64], tmvd[:, 256 + 64 * h:256 + 64 * h + 64], [rk])
                sc.op('dve', lambda e_: e_.tensor_reduce(km[:, 0:NBLK], kT.rearrange("p (n k) -> p n k", k=256), AX.X, ALU.add),
                      reads=[rk], writes=[rkm])
                sc.ts('dve', km[:, 0:NBLK], km[:, 0:NBLK], 1.0 / 256, None, ALU.mult, reads=[rkm], writes=[rkm])
                sc.memset('pool', kmz, 0.0, writes=[rkm])
                for e in range(2):
                    sc.cp('dve', kmz[64 * e:64 * e + 64, e, 0:NBLK], km[64 * e:64 * e + 64, 0:NBLK], reads=[rkm], writes=[rkm])
                for c in range(NC):
                    qf, rqf = qf_r.next()
                    sc.cp('dve', qf, qT[:, c * 512:(c + 1) * 512], reads=[rk], writes=[rqf])
                    sc.memset('pool', scv, NEG, writes=[rsel])
                    sb, rsb = sb_r.next()
                    for e in range(2):
                        es = slice(64 * e, 64 * e + 64)
                        for j in range(4):
                            cur = (4 * c + j) // 2
                            if cur > 0:
                                o_ = (e * 4 + j) * 32
                                sc.mm(sb[:, o_:o_ + cur], qf[:, j * 128:(j + 1) * 128], kmz[:, e, 0:cur],
                                      start=True, stop=True,
                                      reads=[rqf, rkm], writes=[rsb])
                    for e in range(2):
                        for j in range(4):
                            cur = (4 * c + j) // 2
                            if cur > 0:
                                o_ = (e * 4 + j) * 32
                                sc.cp('dve', scv[:, e, j, 0:cur], sb[:, o_:o_ + cur], reads=[rsb], writes=[rsel])
                    for e in range(2):
                        for j in range(4):
                            o8 = (e * 4 + j) * 8
                            sc.op('dve', lambda e_, e=e, j=j, o8=o8: e_.max(out=m8[:, o8:o8 + 8], in_=scv[:, e, j, :]), reads=[rsel], writes=[rsel])
                            sc.ts('dve', selw[:, e, j, :], scv[:, e, j, :], m8[:, o8 + 2:o8 + 3], None, ALU.is_ge, reads=[rsel], writes=[rsel])
                    acc, racc = acc_r.next()
                    sc.memset('pool', acc, 0.0, writes=[r_ for re_ in racc for r_ in re_])
                    obs = [None, None]

                    def s1(kt, c=c, hp=hp):
                        rel = kt - 4 * c
                        j0 = max(rel, 0)
                        c0 = 128 * j0
                        lbs = []
                        for e in range(2):
                            es = slice(64 * e, 64 * e + 64)
                            lb, rlb = lb_r.next()
                            sc.mm(lb[:, c0:512], kT[es, kt * 128:(kt + 1) * 128], qT[es, c * 512 + c0:(c + 1) * 512],
                                  reads=[rk], writes=[rlb])
                            lbs.append((lb, rlb))
                        ps = []
                        for e in range(2):
                            h = 2 * hp + e
                            lb, rlb = lbs[e]
                            p, rp = p_r.next()
                            sc.act(p[:, c0:512], lb[:, c0:512], AF.Exp, reads=[rlb], writes=[rp], scale=0.125)
                            for j in range(j0, 4):
                                d = 4 * c + j - kt
                                if d in (0, 1):
                                    sc.tt('pool', p[:, j * 128:(j + 1) * 128], p[:, j * 128:(j + 1) * 128],
                                          E0(h) if d == 0 else E128(h), ALU.mult, reads=[rp, r_const], writes=[rp])
                            ps.append((p, rp))
                        return (ps, j0)

                    def s2(kt, stt_, c=c, obs=obs, acc=acc, racc=racc):
                        ps, j0 = stt_
                        n = kt // 2
                        for e in range(2):
                            p, rp = ps[e]
                            if kt % 2 == 0:
                                obs[e] = ob_r.next()
                            ob, rob = obs[e]
                            done = []
                            for j in range(j0, 4):
                                qt = 4 * c + j
                                stop = (kt % 2 == 1) or (kt == qt)
                                sc.mm(ob[:, j * 128:j * 128 + 65], p[:, j * 128:(j + 1) * 128], va[:, kt, e, :],
                                      start=(kt % 2 == 0 and j == j0), stop=stop, reads=[rp, rk], writes=[rob])
                                if stop:
                                    done.append(j)
                            for j in done:
                                qt = 4 * c + j
                                if n == qt // 2:
                                    sc.tt('dve', acc[:, e, j, :], ob[:, j * 128:j * 128 + 65], acc[:, e, j, :], ALU.add,
                                          reads=[rob, racc[e][j]], writes=[racc[e][j]])
                                else:
                                    sc.stt('dve', acc[:, e, j, :], ob[:, j * 128:j * 128 + 65], selw[:, e, j, n:n + 1], acc[:, e, j, :],
                                           ALU.mult, ALU.add, reads=[rob, racc[e][j], rsel], writes=[racc[e][j]])

                    pipe(4 * c + 4, s1, s2)
                    allacc = [r_ for re_ in racc for r_ in re_]
                    sc.ts('dve', rl, acc[:, :, :, 64:65], TINY, None, ALU.max, reads=allacc, writes=[rrl])
                    sc.op('dve', lambda e_: e_.reciprocal(rl, rl), reads=[rrl], writes=[rrl])
                    for e in range(2):
                        h = 2 * hp + e
                        sc.tt('dve', o_tok[:, 4 * c:4 * c + 4, h * 64:(h + 1) * 64], acc[:, e, :, 0:64],
                              rl[:, e].to_broadcast([128, 4, 64]), ALU.mult, reads=allacc + [rrl], writes=[rot])
            for c in range(NC):
                for fc in range(2):
                    tb, rtb = tb_r.next()
                    pb = tb.bitcast(BF16)
                    for j in range(4):
                        sc.tr(pb[:, j * 128:(j + 1) * 128], o_tok[:, 4 * c + j, fc * 128:(fc + 1) * 128], ident,
                              reads=[rot, r_const], writes=[rtb])
                    stg, rst = st_r.next()
                    sc.cp('act' if fc else 'dve', stg, pb[:, 0:512], reads=[rtb], writes=[rst])
                    sc.dma(STQ, oTd[256 + fc * 128:256 + (fc + 1) * 128, c * 512:(c + 1) * 512], stg, reads=[rst], writes=[r_oTd])
            sc.barrier()

        def phase_NSA(l):
            ar.top = base_top
            NCW = NCT * 128
            kcT2 = [ar.bf(NCW) for g in range(2)]
            vca = [v3(ar.bf(NCT * 65), NCT) for g in range(2)]
            gcol = ar.f32(8)
            r_cmp = Res('cmp')
            mark_p = ar.top
            w1sb = v3(ar.bf(32 * 256), 32)
            w2sb = v3(ar.bf(2 * 64), 2)
            TT = ar.bf(S)
            posf = ar.f32(64)
            posb = ar.bf(64)
            posT = ar.bf(32)
            pbias = ar.f32(2)
            xb = ar.f32(512)
            x2 = ar.f32(512)
            th = ar.f32(512)
            ghT = v3(ar.bf(2 * 512), 2)
            sqb = ar.bf(512)
            rs = ar.f32(512)
            sring = Ring([(ar.f32(2048), Res()) for i in range(2)])
            rw, rt, rx, rgh = Res('w'), Res('TT'), Res('x'), Res('gh')
            n = n_cmp
            for gi in range(6):
                for half in range(2):
                    sc.dma('sp', gcol[half * 64:(half + 1) * 64, gi:gi + 1],
                           AP(gains.tensor, (l * 6 + gi) * 64, [[1, 64], [1, 1]]), writes=[r_cmp])
            for g in range(2):
                sc.memset('pool', kcT2[g], 0.0, writes=[r_cmp])
                sc.memset('pool', vca[g], 0.0, writes=[r_cmp])
            for kv in range(2):
                w1v = cmp_w1[l, kv].rearrange("(i d) h -> d i h", d=64)
                for i0 in range(0, 32, 8):
                    sg, rsg = sring.next()
                    sc.dma('sp', v3(sg, 8)[0:64], w1v[:, i0:i0 + 8, :], writes=[rsg])
                    sc.cp('pool', w1sb[0:64, i0:i0 + 8, :], v3(sg, 8)[0:64], reads=[rsg], writes=[rw])
                sg, rsg = sring.next()
                sc.dma('sp', v3(sg[:, 0:128], 2), cmp_w2[l, kv].rearrange("(c p) d -> p c d", p=128), writes=[rsg])
                sc.cp('pool', w2sb, v3(sg[:, 0:128], 2), reads=[rsg], writes=[rw])
                sc.dma('sp', posf[0:32, :], cmp_pos[l, kv], writes=[rw])
                sc.cp('dve', posb[0:32, :], posf[0:32, :], reads=[rw], writes=[rw])
                pbk = banks[5].bitcast(BF16)
                sc.tr(pbk[0:64, 0:32], posb[0:32, 0:64], ident[0:32, 0:32], reads=[rw, r_const], writes=[bres[5]])
                sc.cp('dve', posT[0:64, :], pbk[0:64, 0:32], reads=[bres[5]], writes=[rw])
                for hc in range(2):
                    for i in range(32):
                        sc.mm(banks[6][:, hc:hc + 1], w1sb[0:64, i, hc * 128:(hc + 1) * 128], posT[0:64, i:i + 1],
                              start=(i == 0 and hc == 0), stop=(i == 31), reads=[rw], writes=[bres[6]])
                sc.cp('dve', pbias, banks[6][:, 0:2], reads=[bres[6]], writes=[rw])
                for g in range(2):
                    base = (1536 if kv == 0 else 1664) + 64 * g
                    sc.dma('sp', TT[0:64, :], fmd[base:base + 64, :], writes=[rt])
                    for hc in range(2):
                        bk, rb = banks[hc], bres[hc]
                        for i in range(32):
                            sc.mm(bk[:, 0:n], w1sb[0:64, i, hc * 128:(hc + 1) * 128], TT[0:64, i:i + 16 * (n - 1) + 1:16],
                                  start=(i == 0), stop=(i == 31), reads=[rw, rt], writes=[rb])
                        sc.ts('dve', xb[:, 0:n], bk[:, 0:n], pbias[:, hc:hc + 1], None, ALU.add, reads=[rb, rw], writes=[rx])
                        sc.tt('dve', x2[:, 0:n], xb[:, 0:n], xb[:, 0:n], ALU.mult, reads=[rx], writes=[rx])
                        sc.ts('dve', x2[:, 0:n], x2[:, 0:n], 0.044715, 1.0, ALU.mult, ALU.add, reads=[rx], writes=[rx])
                        sc.tt('dve', x2[:, 0:n], x2[:, 0:n], xb[:, 0:n], ALU.mult, reads=[rx], writes=[rx])
                        sc.act(th[:, 0:n], x2[:, 0:n], AF.Tanh, reads=[rx], writes=[rx], scale=0.7978845608028654)
                        sc.ts('dve', xb[:, 0:n], xb[:, 0:n], 0.5, None, ALU.mult, reads=[rx], writes=[rx])
                        sc.stt('dve', ghT[:, hc, 0:n], th[:, 0:n], 1.0, xb[:, 0:n], ALU.add, ALU.mult, reads=[rx], writes=[rgh])
                    if kv == 0:
                        for hc in range(2):
                            sc.mm(banks[2][0:64, 0:n], w2sb[:, hc, :], ghT[:, hc, 0:n], start=(hc == 0), stop=(hc == 1),
                                  reads=[rw, rgh], writes=[bres[2]])
                        sc.act(sqb[0:64, 0:n], banks[2][0:64, 0:n], AF.Square, reads=[bres[2]], writes=[rx])
                        sc.mm(banks[3][0:64, 0:n], onesb[0:64, 0:64], sqb[0:64, 0:n], reads=[rx, r_const], writes=[bres[3]])
                        sc.act(rs[0:64, 0:n], banks[3][0:64, 0:n], AF.Ln, reads=[bres[3]], writes=[rx], scale=1.0 / 64, bias=EPS)
                        sc.act(rs[0:64, 0:n], rs[0:64, 0:n], AF.Exp, reads=[rx], writes=[rx], scale=-0.5)
                        sc.stt('dve', kcT2[g][0:64, 0:n], banks[2][0:64, 0:n], gcol[0:64, 3:4], rs[0:64, 0:n], ALU.mult, ALU.mult,
                               reads=[bres[2], rx, r_cmp], writes=[r_cmp])
                        sc.dma('sp', kcT2[g][64:128, :], kcT2[g][0:64, :], reads=[r_cmp], writes=[r_cmp])
                    else:
                        for nt in range(NCT):
                            rows = min(128, n - nt * 128)
                            for hc in range(2):
                                sc.mm(banks[2][0:rows, 0:64], ghT[:, hc, nt * 128:nt * 128 + rows], w2sb[:, hc, :],
                                      start=(hc == 0), stop=(hc == 1), reads=[rw, rgh], writes=[bres[2]])
                            sc.cp('dve', vca[g][0:rows, nt, 0:64], banks[2][0:rows, 0:64], reads=[bres[2]], writes=[r_cmp])
                            sc.memset('pool', vca[g][0:rows, nt, 64:65], 1.0, writes=[r_cmp])
            sc.barrier()
            for g in range(2):
                ar.top = mark_p
                qT = v3(ar.bf(2 * S), 2)
                ksT2 = ar.bf(S)
                kwT2 = ar.bf(S)
                vsa = v3(ar.bf(NT * 65), NT)
                vwa = v3(ar.bf(NT * 65), NT)
                G = [ar.bf(2560) for hh in range(4)]
                Bm = ar.bf(S)
                ovl = v3(ar.bf(NCW), NCT)
                rk = Res('kqv')
                hk_r = Ring([(ar.bf(512), Res()) for i in range(2)])
                gtb = ar.bf(4 * 48)
                gt3 = v3(gtb.bitcast(F32), 4)
                rgt = Res('gt')
                impacc = v3(ar.f32(512), 4)
                rimp = Res('imp')
                itmp = v3(ar.f32(512), 4)
                ritmp = Res()
                score = v3(ar.f32(512), 4)
                wk = v3(ar.f32(512), 4)
                m8a = ar.f32(32)
                m8b = ar.f32(32)
                selb = v3(ar.bf(512), 4)
                selT = ar.bf(512)
                rsel = Res('sel')
                oc = ar.f32(4 * 4 * 64)
                roc = Res('oc')
                rlc = ar.f32(16)
                rls = v3(ar.f32(4), 4)
                rlw = v3(ar.f32(4), 4)
                cfs = v3(ar.f32(4), 4)
                cfw = v3(ar.f32(4), 4)
                rcoef = Res('coef')
                ot1 = v3(ar.f32(256), 4)
                ot2 = v3(ar.f32(256), 4)
                ot3 = v3(ar.f32(256), 4)
                rot1, rot2, rot3 = Res(), Res(), Res()
                o_tok = v3(ar.bf(4 * 256), 4)
                rotk = Res('otok')
                p_r = Ring([(ar.bf(512), Res()) for i in range(LAG + 2)])
                ps_r = Ring([(ar.bf(512), Res()) for i in range(4 * (LAG + 2))])
                mk_r = Ring([(ar.bf(512), Res()) for i in range(3)])
                st_r = Ring([(ar.bf(512), Res()) for i in range(2)])
                lb_r = Ring([(banks[i], bres[i]) for i in (0, 1)])
                mb_r = Ring([(banks[i], bres[i]) for i in (5,)])
                lb3_r = Ring([(banks[i], bres[i]) for i in (0, 1, 2)])
                a1_r = Ring([(banks[i], bres[i]) for i in (3, 6)])
                a2_r = Ring([(banks[i], bres[i]) for i in (4, 7)])
                tb_r = Ring([(banks[i], bres[i]) for i in (5,)])
                r_oTd = Res()
                for hh in range(4):
                    half, a = hh // 2, hh % 2
                    r0 = 1024 + 64 * (4 * g + hh)
                    sc.dma('sp', qT[half * 64:(half + 1) * 64, a, :], fmd[r0:r0 + 64, :], writes=[rk])
                for half in range(2):
                    sc.dma('sp', ksT2[half * 64:(half + 1) * 64, :], fmd[1792 + 64 * g:1792 + 64 * g + 64, :], writes=[rk])
                    sc.dma('sp', kwT2[half * 64:(half + 1) * 64, :], fmd[1920 + 64 * g:1920 + 64 * g + 64, :], writes=[rk])
                load_v(vsa[:, :, 0:64], tmvd[:, 512 + 64 * g:512 + 64 * g + 64], [rk])
                load_v(vwa[:, :, 0:64], tmvd[:, 640 + 64 * g:640 + 64 * g + 64], [rk])
                sc.memset('pool', vsa[:, :, 64:65], 1.0, writes=[rk])
                sc.memset('pool', vwa[:, :, 64:65], 1.0, writes=[rk])
                sc.dma('sp', ovl, v3(cb_in[:, CB_GLOBAL:CB_GLOBAL + NCW], NCT), writes=[rk])
                sc.dma('sp', Bm, cb_in[:, CB_GLOBAL + NCW:CB_GLOBAL + NCW + S], writes=[rk])
                for hh in range(4):
                    hrow = 4 + 4 * g + hh
                    for idx in range(5):
                        hk, rhk = hk_r.next()
                        src = AP(gvd.tensor, hrow * GL + GOFF - 2063 + 512 * idx, [[16, 128], [1, 512]])
                        sc.dma('sp', hk, src, writes=[rhk])
                        tb, rtb = tb_r.next()
                        sc.mm(tb, Jm, hk, reads=[rhk, r_const], writes=[rtb])
                        sc.cp('dve', G[hh][:, idx * 512:(idx + 1) * 512], tb, reads=[rtb], writes=[rk])
                for c in range(NC):
                    sc.dma('sp', v3(gtb, 4), tmvd[c * 512:(c + 1) * 512, 768:816].rearrange("(t p) w -> p t w", p=128), writes=[rgt])
                    sc.memset('pool', impacc, 0.0, writes=[rimp])
                    nts = [nt for nt in range(NCT) if c - 4 * nt >= 0]
                    oc4 = oc.rearrange("p (h j d) -> p h j d", h=4, j=4)
                    rlc4 = rlc.rearrange("p (h j o) -> p h j o", h=4, o=1)
                    for hh in range(4):
                        half, a = hh // 2, hh % 2
                        hs = slice(half * 64, half * 64 + 64)
                        ocb, rocb = a1_r.next()
                        ib, rib = a2_r.next()
                        def s1(ii, c=c, hh=hh, hs=hs, a=a, nts=nts):
                            nt = nts[ii]
                            lb, rlb = lb_r.next()
                            sc.mm(lb, kcT2[g][hs, nt * 128:(nt + 1) * 128], qT[hs, a, c * 512:(c + 1) * 512], reads=[r_cmp, rk], writes=[rlb])
                            pc, rpc = p_r.next()
                            sc.act(pc, lb, AF.Exp, reads=[rlb], writes=[rpc], scale=0.125)
                            idx = c - 4 * nt
                            if idx < 5:
                                sc.tt('pool', pc, pc, G[hh][:, idx * 512:(idx + 1) * 512], ALU.mult, reads=[rpc, rk], writes=[rpc])
                            return (pc, rpc)

                        def s2(ii, stt_, nts=nts, ocb=ocb, rocb=rocb, ib=ib, rib=rib):
                            pc, rpc = stt_
                            nt = nts[ii]
                            for j in range(4):
                                sc.mm(ocb[:, j * 128:j * 128 + 65], pc[:, j * 128:(j + 1) * 128], vca[g][:, nt, :],
                                      start=(ii == 0 and j == 0), stop=(nt == nts[-1]), reads=[rpc, r_cmp], writes=[rocb])
                            for j in range(4):
                                sc.mm(ib[:, j * 128:(j + 1) * 128], pc[:, j * 128:(j + 1) * 128], ovl[:, nt, :],
                                      start=(ii == 0 and j == 0), stop=(nt == nts[-1]), reads=[rpc, rk], writes=[rib])

                        pipe(len(nts), s1, s2)
                        ocb3 = v3(ocb, 4)
                        sc.ts('dve', rlc4[:, hh], ocb3[:, :, 64:65], TINY, None, ALU.max, reads=[rocb], writes=[roc])
                        sc.op('dve', lambda e, hh=hh: e.reciprocal(rlc4[:, hh], rlc4[:, hh]), reads=[roc], writes=[roc])
                        sc.tt('dve', oc4[:, hh], ocb3[:, :, 0:64], rlc4[:, hh].to_broadcast([128, 4, 64]), ALU.mult,
                              reads=[rocb, roc], writes=[roc])
                        sc.tt('dve', itmp, v3(ib, 4), rlc4[:, hh].to_broadcast([128, 4, 128]), ALU.mult, reads=[rib, roc], writes=[ritmp])
                        sc.tt('pool', impacc, impacc, itmp, ALU.add, reads=[rimp, ritmp], writes=[rimp])
                    for j in range(4):
                        off = 126 - 2 * (4 * c + j)
                        sc.tt('dve', score[:, j, :], impacc[:, j, :], cslide[:, off:off + 128], ALU.add, reads=[rimp, r_const], writes=[rsel])
                    sc.memset('dve', score[:, :, 0:1], BIG, writes=[rsel])
                    for j in range(4):
                        sc.op('dve', lambda e, j=j: e.max(out=m8a[:, j * 8:(j + 1) * 8], in_=score[:, j, :]), reads=[rsel], writes=[rsel])
                        sc.op('dve', lambda e, j=j: e.match_replace(out=wk[:, j, :], in_to_replace=m8a[:, j * 8:(j + 1) * 8],
                                                                    in_values=score[:, j, :], imm_value=-3e38), reads=[rsel], writes=[rsel])
                        sc.op('dve', lambda e, j=j: e.max(out=m8b[:, j * 8:(j + 1) * 8], in_=wk[:, j, :]), reads=[rsel], writes=[rsel])
                        sc.ts('dve', selb[:, j, :], score[:, j, :], m8b[:, j * 8 + 7:j * 8 + 8], None, ALU.is_ge, reads=[rsel], writes=[rsel])
                    tb, rtb = tb_r.next()
                    pb = tb.bitcast(BF16)
                    for j in range(4):
                        sc.tr(pb[:, j * 128:(j + 1) * 128], selb[:, j, :], ident, reads=[rsel, r_const], writes=[rtb])
                    sc.cp('dve', selT, pb[:, 0:512], reads=[rtb], writes=[rsel])
                    osbs = [(banks[i], bres[i]) for i in (3, 4, 6, 7)]

                    def s1(kt, c=c):
                        j0 = max(kt - 4 * c, 0)
                        c0 = 128 * j0
                        mb, rmb = mb_r.next()
                        sc.mm(mb[:, c0:512], Bm[:, kt * 128:(kt + 1) * 128], selT[:, c0:512], reads=[rk, rsel], writes=[rmb])
                        mk, rmk = mk_r.next()
                        sc.cp('act', mk[:, c0:512], mb[:, c0:512], reads=[rmb], writes=[rmk])
                        outs = [None] * 4
                        for pair in ((0, 2), (1, 3)):
                            lbs = {}
                            for hh in pair:
                                half, a = hh // 2, hh % 2
                                hs = slice(half * 64, half * 64 + 64)
                                lb, rlb = lb3_r.next()
                                sc.mm(lb[:, c0:512], ksT2[hs, kt * 128:(kt + 1) * 128], qT[hs, a, c * 512 + c0:(c + 1) * 512], reads=[rk], writes=[rlb])
                                lbs[hh] = (lb, rlb)
                            for hh in pair:
                                hrow = 4 + 4 * g + hh
                                lb, rlb = lbs[hh]
                                ps, rps = ps_r.next()
                                sc.act(ps[:, c0:512], lb[:, c0:512], AF.Exp, reads=[rlb], writes=[rps], scale=0.125)
                                sc.tt('dve', ps[:, c0:512], ps[:, c0:512], mk[:, c0:512], ALU.mult, reads=[rps, rmk], writes=[rps])
                                for j in range(j0, 4):
                                    d = 4 * c + j - kt
                                    if d in (0, 1):
                                        sc.tt('pool', ps[:, j * 128:(j + 1) * 128], ps[:, j * 128:(j + 1) * 128],
                                              E0(hrow) if d == 0 else E128(hrow), ALU.mult, reads=[rps, r_const], writes=[rps])
                                outs[hh] = (ps, rps)
                        return (outs, j0)

                    def s2(kt, stt_, c=c):
                        outs, j0 = stt_
                        for hh in range(4):
                            ps, rps = outs[hh]
                            osb, rosb = osbs[hh]
                            for j in range(j0, 4):
                                sc.mm(osb[:, j * 128:j * 128 + 65], ps[:, j * 128:(j + 1) * 128], vsa[:, kt, :],
                                      start=(kt == 0 and j == j0), stop=(kt == 4 * c + j), reads=[rps, rk], writes=[rosb])

                    pipe(4 * c + 4, s1, s2)
                    for hh in range(4):
                        half, a = hh // 2, hh % 2
                        hs = slice(half * 64, half * 64 + 64)
                        hrow = 4 + 4 * g + hh
                        osb, rosb = osbs[hh]
                        owb, rowb = banks[5], bres[5]
                        kts = list(range(max(4 * c - 4, 0), 4 * c + 4))

                        def s1(ii, c=c, hs=hs, a=a, hrow=hrow, kts=kts):
                            kt = kts[ii]
                            jlo = max(kt - 4 * c, 0)
                            jhi = min(kt + 4 - 4 * c, 3)
                            cs_ = slice(128 * jlo, 128 * (jhi + 1))
                            lb, rlb = lb_r.next()
                            sc.mm(lb[:, cs_], kwT2[hs, kt * 128:(kt + 1) * 128], qT[hs, a, c * 512 + 128 * jlo:c * 512 + 128 * (jhi + 1)],
                                  reads=[rk], writes=[rlb])
                            pw, rpw = p_r.next()
                            sc.act(pw[:, cs_], lb[:, cs_], AF.Exp, reads=[rlb], writes=[rpw], scale=0.125)
                            for j in range(jlo, jhi + 1):
                                d = 4 * c + j - kt
                                if d in (0, 1, 4):
                                    mk = E0(hrow) if d == 0 else (E128(hrow) if d == 1 else m512)
                                    sc.tt('pool', pw[:, j * 128:(j + 1) * 128], pw[:, j * 128:(j + 1) * 128], mk, ALU.mult,
                                          reads=[rpw, r_const], writes=[rpw])
                            return (pw, rpw, jlo, jhi)

                        def s2(ii, stt_, c=c, kts=kts, owb=owb, rowb=rowb):
                            pw, rpw, jlo, jhi = stt_
                            kt = kts[ii]
                            for j in range(jlo, jhi + 1):
                                sc.mm(owb[:, j * 128:j * 128 + 65], pw[:, j * 128:(j + 1) * 128], vwa[:, kt, :],
                                      start=(ii == 0 and j == jlo), stop=(kt == 4 * c + j), reads=[rpw, rk], writes=[rowb])

                        pipe(len(kts), s1, s2)
                        osb3 = v3(osb, 4)
                        owb3 = v3(owb, 4)
                        sc.ts('dve', rls, osb3[:, :, 64:65], TINY, None, ALU.max, reads=[rosb], writes=[rcoef])
                        sc.op('dve', lambda e: e.reciprocal(rls, rls), reads=[rcoef], writes=[rcoef])
                        sc.ts('dve', rlw, owb3[:, :, 64:65], TINY, None, ALU.max, reads=[rowb], writes=[rcoef])
                        sc.op('dve', lambda e: e.reciprocal(rlw, rlw), reads=[rcoef], writes=[rcoef])
                        gi = g * 4 + hh
                        sc.tt('dve', cfs, rls, gt3[:, :, 8 + gi:9 + gi], ALU.mult, reads=[rcoef, rgt], writes=[rcoef])
                        sc.tt('dve', cfw, rlw, gt3[:, :, 16 + gi:17 + gi], ALU.mult, reads=[rcoef, rgt], writes=[rcoef])
                        sc.tt('dve', ot1, oc4[:, hh], gt3[:, :, gi:gi + 1].to_broadcast([128, 4, 64]), ALU.mult, reads=[roc, rgt], writes=[rot1])
                        sc.tt('dve', ot2, osb3[:, :, 0:64], cfs.to_broadcast([128, 4, 64]), ALU.mult, reads=[rosb, rcoef], writes=[rot2])
                        sc.tt('dve', ot3, owb3[:, :, 0:64], cfw.to_broadcast([128, 4, 64]), ALU.mult, reads=[rowb, rcoef], writes=[rot3])
                        sc.tt('pool', ot1, ot1, ot2, ALU.add, reads=[rot1, rot2], writes=[rot1])
                        sc.tt('pool', o_tok[:, :, hh * 64:(hh + 1) * 64], ot1, ot3, ALU.add, reads=[rot1, rot3], writes=[rotk])
                    for fc in range(2):
                        tb, rtb = tb_r.next()
                        pb = tb.bitcast(BF16)
                        for j in range(4):
                            sc.tr(pb[:, j * 128:(j + 1) * 128], o_tok[:, j, fc * 128:(fc + 1) * 128], ident, reads=[rotk, r_const], writes=[rtb])
                        stg, rst = st_r.next()
                        sc.cp('act', stg, pb[:, 0:512], reads=[rtb], writes=[rst])
                        r0 = 512 + g * 256 + fc * 128
                        sc.dma(STQ, oTd[r0:r0 + 128, c * 512:(c + 1) * 512], stg, reads=[rst], writes=[r_oTd])
                sc.barrier()

        def phase_C1(l, xsrc):
            ar.top = base_top
            Wg = v3(ar.bf(8 * 3072), 8)
            Wb = v3(ar.bf(8 * D), 8)
            Wo = v3(ar.bf(8 * D), 8)
            rW = Res('W')
            mark = ar.top
            sring = Ring([(ar.f32(2048), Res()) for i in range(4)])
            for c in range(8):
                load_cast(Wg[:, c, :], w_in[l, c * 128:(c + 1) * 128, NQKV:NIN], 3072, sring, rW)
                load_cast(Wb[:, c, :], w_branch[l, c * 128:(c + 1) * 128, :], D, sring, rW)
                load_cast(Wo[:, c, :], w_out[l, c * 128:(c + 1) * 128, :], D, sring, rW)
            sc.barrier()
            ar.top = mark
            xs_r = Ring([(v3(ar.f32(4 * D), 4), Res()) for i in range(2)])
            hT_r = Ring([(v3(ar.bf(8 * 512), 8), Res()) for i in range(2)])
            oT_r = Ring([(v3(ar.bf(8 * 512), 8), Res()) for i in range(2)])
            mixT = v3(ar.bf(8 * 512), 8)
            rmix = Res('mix')
            g_r = Ring([(ar.f32(512), Res()) for i in range(3)])
            t_r = Ring([(ar.f32(512), Res()) for i in range(4)])
            gb_r = Ring([(banks[i], bres[i]) for i in (0, 1, 2)])
            bb_r = Ring([(banks[i], bres[i]) for i in (3, 4, 5)])
            yb_r = Ring([(banks[i], bres[i]) for i in (6, 7)])
            r_x1d = Res()
            branch_k = ((0, 1), (2, 3), (4, 5, 6, 7))
            for ci in range(NC):
                xs, rx = xs_r.next()
                sc.dma('sp', xs, xsrc[ci * 512:(ci + 1) * 512, :].rearrange("(t p) d -> p t d", p=128), writes=[rx])
                hT, rhT = hT_r.next()
                sc.dma('sp', hT, hTd.rearrange("(c p) s -> p c s", p=128)[:, :, ci * 512:(ci + 1) * 512], writes=[rhT])
                oT, roT = oT_r.next()
                sc.dma('sp', oT, oTd.rearrange("(c p) s -> p c s", p=128)[:, :, ci * 512:(ci + 1) * 512], writes=[roT])
                for dm in range(8):
                    ts_ = []
                    for br in range(3):
                        gb, rgb = gb_r.next()
                        col = br * D + dm * 128
                        for c in range(8):
                            sc.mm(gb, Wg[:, c, col:col + 128], hT[:, c, :], start=(c == 0), stop=(c == 7),
                                  reads=[rW, rhT], writes=[rgb])
                        g, rg = g_r.next()
                        sc.act(g, gb, AF.Sigmoid, reads=[rgb], writes=[rg])
                        bb, rbb = bb_r.next()
                        ks = branch_k[br]
                        for i, k in enumerate(ks):
                            sc.mm(bb, Wb[:, k, dm * 128:(dm + 1) * 128], oT[:, k, :], start=(i == 0), stop=(i == len(ks) - 1),
                                  reads=[rW, roT], writes=[rbb])
                        t, rt = t_r.next()
                        sc.tt('dve', t, bb, g, ALU.mult, reads=[rbb, rg], writes=[rt])
                        ts_.append((t, rt))
                    sc.tt('pool', ts_[0][0], ts_[0][0], ts_[1][0], ALU.add, reads=[ts_[0][1], ts_[1][1]], writes=[ts_[0][1]])
                    sc.tt('pool', mixT[:, dm, :], ts_[0][0], ts_[2][0], ALU.add, reads=[ts_[0][1], ts_[2][1]], writes=[rmix])
                for t in range(4):
                    for half in range(2):
                        yb, ryb = yb_r.next()
                        for k in range(8):
                            sc.mm(yb, mixT[:, k, t * 128:(t + 1) * 128], Wo[:, k, half * 512:(half + 1) * 512],
                                  start=(k == 0), stop=(k == 7), reads=[rmix, rW], writes=[ryb])
                        sc.tt('dve', xs[:, t, half * 512:(half + 1) * 512], xs[:, t, half * 512:(half + 1) * 512], yb, ALU.add,
                              reads=[rx, ryb], writes=[rx])
                sc.dma(STQ, x1d[ci * 512:(ci + 1) * 512, :].rearrange("(t p) d -> p t d", p=128), xs, reads=[rx], writes=[r_x1d])
            sc.barrier()

        def phase_C2(l, xdst):
            ar.top = base_top
            Wgu = v3(ar.bf(8 * 2 * DFF), 8)
            Wd = v3(ar.bf(22 * D), 22)
            gF = ar.f32(D)
            rW = Res('W')
            mark = ar.top
            sring = Ring([(ar.f32(2048), Res()) for i in range(4)])
            for c in range(8):
                load_cast(Wgu[:, c, :], w_gate_up[l, c * 128:(c + 1) * 128, :], 2 * DFF, sring, rW)
            for f in range(22):
                load_cast(Wd[:, f, :], w_down[l, f * 128:(f + 1) * 128, :], D, sring, rW)
            sc.dma('sp', gF, AP(ffn_norm.tensor, l * D, [[0, 128], [1, D]]), writes=[rW])
            sc.barrier()
            ar.top = mark
            xs_r = Ring([(v3(ar.f32(2 * D), 2), Res()) for i in range(2)])
            hb = v3(ar.bf(2 * D), 2)
            rhb = Res()
            sq = ar.f32(D)
            ssq = ar.f32(4)
            rstd = ar.f32(4)
            rtmp = Res()
            hT = v3(ar.bf(8 * 256), 8)
            rhT = Res()
            actT = v3(ar.bf(22 * 256), 22)
            ract = Res()
            sg_r = Ring([(ar.f32(256), Res()) for i in range(2)])
            tb_r = Ring([(banks[i], bres[i]) for i in (0, 1)])
            fb_r = Ring([(banks[i], bres[i]) for i in (2, 3, 4)])
            yb_r = Ring([(banks[i], bres[i]) for i in (5, 6, 7)])
            r_out = Res()
            for ci in range(S // 256):
                xs, rx = xs_r.next()
                sc.dma('sp', xs, x1d[ci * 256:(ci + 1) * 256, :].rearrange("(t p) d -> p t d", p=128), writes=[rx])
                rmsnorm_tm(xs, rx, gF, rW, hb, rhb, 2, sq, ssq, rstd, rtmp)
                transpose_to(hb, rhb, hT, rhT, 2, tb_r, ('dve', 'act'))
                for f in range(22):
                    fb, rfb = fb_r.next()
                    for half, col in ((0, f * 128), (1, DFF + f * 128)):
                        for c in range(8):
                            sc.mm(fb[:, half * 256:(half + 1) * 256], Wgu[:, c, col:col + 128], hT[:, c, :],
                                  start=(c == 0), stop=(c == 7), reads=[rW, rhT], writes=[rfb])
                    sg, rsg = sg_r.next()
                    sc.act(sg, fb[:, 0:256], AF.Silu, reads=[rfb], writes=[rsg])
                    sc.tt('dve', actT[:, f, :], sg, fb[:, 256:512], ALU.mult, reads=[rsg, rfb], writes=[ract])
                for t in range(2):
                    for half in range(2):
                        yb, ryb = yb_r.next()
                        for f in range(22):
                            sc.mm(yb, actT[:, f, t * 128:(t + 1) * 128], Wd[:, f, half * 512:(half + 1) * 512],
                                  start=(f == 0), stop=(f == 21), reads=[ract, rW], writes=[ryb])
                        sc.tt('dve', xs[:, t, half * 512:(half + 1) * 512], xs[:, t, half * 512:(half + 1) * 512], yb, ALU.add,
                              reads=[rx, ryb], writes=[rx])
                sc.dma(STQ, xdst[ci * 256:(ci + 1) * 256, :].rearrange("(t p) d -> p t d", p=128), xs, reads=[rx], writes=[r_out])
            sc.barrier()

        PH = {}
        exec_phases = phases
        prologue()
        for l in range(depth):
            xsrc = x_in if l == 0 else xmd
            xdst = out if l == depth - 1 else xmd
            if exec_phases is None or 'A' in exec_phases:
                phase_A(l, xsrc)
            if exec_phases is None or 'SB' in exec_phases:
                phase_SB(l)
            if exec_phases is None or 'MB' in exec_phases:
                phase_MB(l)
            if exec_phases is None or 'NSA' in exec_phases:
                phase_NSA(l)
            if inject:
                sc.dma('sp', oTd, oT_in, writes=[Res()])
                sc.barrier()
            if exec_phases is None or 'C1' in exec_phases:
                phase_C1(l, xsrc)
            if exec_phases is None or 'C2' in exec_phases:
                phase_C2(l, xdst)
        sc.barrier()
        sc.emit()
    return nc


def core_inputs(inputs, b, S, depth, cbh, cfh):
    f = lambda a: np.ascontiguousarray(np.asarray(a, dtype=np.float32))
    gains = np.stack([f(inputs['moba_q_norm']), f(inputs['moba_k_norm']), f(inputs['nsa_q_norm']),
                      f(inputs['nsa_k_norm'])[:, 0], f(inputs['nsa_k_norm'])[:, 1], f(inputs['nsa_k_norm'])[:, 2]], axis=1)
    m = {
        'x': f(inputs['x'][b, :S]),
        'rel_bias': f(inputs['rel_bias']),
        'attn_norm': f(inputs['attn_norm'])[:depth],
        'w_in': f(inputs['w_in'])[:depth],
        'gains': np.ascontiguousarray(gains[:depth]),
        'nsa_cmp_pos': f(inputs['nsa_cmp_pos'])[:depth],
        'nsa_cmp_w1': f(inputs['nsa_cmp_w1'])[:depth],
        'nsa_cmp_w2': f(inputs['nsa_cmp_w2'])[:depth],
        'w_branch': f(inputs['w_branch'])[:depth],
        'w_out': f(inputs['w_out'])[:depth],
        'ffn_norm': f(inputs['ffn_norm'])[:depth],
        'w_gate_up': f(inputs['w_gate_up'])[:depth],
        'w_down': f(inputs['w_down'])[:depth],
        'cb': cbh,
        'cf': cfh,
    }
    return m


def kernel(**inputs):
    x = np.asarray(inputs['x'])
    B, S, _ = x.shape
    depth = int(np.asarray(inputs['w_in']).shape[0])
    cbh, cfh, _, _ = host_consts(S)
    nc = build(S, depth)
    in_maps = [core_inputs(inputs, b, S, depth, cbh, cfh) for b in range(B)]
    res = run_bass_kernel_spmd(nc, in_maps, core_ids=list(range(B)))
    return np.stack([np.asarray(r['out'], dtype=np.float32) for r in res.results], axis=0)
```

```python
import math
from contextlib import ExitStack

import numpy as np
import ml_dtypes
import concourse.bass as bass
import concourse.mybir as mybir
from concourse.bass_types import AP
from concourse.bass_utils import run_bass_kernel_spmd

F32 = mybir.dt.float32
BF16 = mybir.dt.bfloat16
AF = mybir.ActivationFunctionType
ALU = mybir.AluOpType
AX = mybir.AxisListType
ENGS = ('pe', 'act', 'dve', 'pool', 'sp')
NDMA = 8
STQ = 'pool'

D = 1024
NIN = 5912
NQKV = 2840
DFF = 2816
EPS = 1e-6
NEG = -1e30
BIG = 1e30
TINY = 1e-30
GOFF = 2304
GL = 4864
FM_COLS = [0, 128, 256, 384, 768, 896, 1024, 1152, 1536, 1664, 1792, 1920, 2048, 2176, 2304, 2560]
FM_KIND = [None, None, None, None, 0, 0, 1, 1, 2, 2, 2, 2, None, None, 4, 5]


class Res:
    __slots__ = ('name', 'lw', 'rd')

    def __init__(self, name=''):
        self.name = name
        self.lw = None
        self.rd = {}


class Ring:
    def __init__(self, items):
        self.items = items
        self.i = 0

    def next(self):
        it = self.items[self.i]
        self.i = (self.i + 1) % len(self.items)
        return it


class Sched:
    def __init__(self, nc, stack):
        self.nc = nc
        self.q = {e: [] for e in ENGS}
        self.cnt = {e: 0 for e in ENGS}
        self.waited = {e: {} for e in ENGS}
        self.sem = {}
        for e in ENGS:
            self.sem[('e', e)] = stack.enter_context(nc.semaphore('s_' + e))
        self.dcnt = {}
        self.drr = {}
        for e in ('sp', 'act', 'pool'):
            self.drr[e] = 0
            for i in range(NDMA):
                k = ('d', e, i)
                self.sem[k] = stack.enter_context(nc.semaphore('d_%s%d' % (e, i)))
                self.dcnt[k] = 0

    def _waits(self, eng, reads, writes, extra=()):
        deps = {}

        def add(ev):
            if ev is None:
                return
            k, v = ev
            if deps.get(k, 0) < v:
                deps[k] = v
        for r in reads:
            add(r.lw)
        for w in writes:
            add(w.lw)
            for k, v in w.rd.items():
                add((k, v))
        for ev in extra:
            add(ev)
        out = []
        wd = self.waited[eng]
        for k, v in deps.items():
            if k == ('e', 'pe') and eng == 'pe':
                continue
            if wd.get(k, 0) >= v:
                continue
            wd[k] = v
            out.append((k, v))
        return out

    def _mark(self, ev, reads, writes):
        k, v = ev
        for r in reads:
            if r.rd.get(k, 0) < v:
                r.rd[k] = v
        for w in writes:
            w.lw = ev
            w.rd = {}

    def op(self, eng, fn, reads=(), writes=()):
        waits = self._waits(eng, reads, writes)
        self.cnt[eng] += 1
        ev = (('e', eng), self.cnt[eng])
        self.q[eng].append((fn, waits, ('e', eng), 1))
        self._mark(ev, reads, writes)
        return ev

    def dma(self, eng, out, in_, reads=(), writes=(), **kw):
        i = self.drr[eng]
        self.drr[eng] = (i + 1) % NDMA
        k = ('d', eng, i)
        prev = (k, 16 * self.dcnt[k]) if self.dcnt[k] else None
        waits = self._waits(eng, reads, writes, extra=(prev,) if prev else ())
        self.dcnt[k] += 1
        ev = (k, 16 * self.dcnt[k])
        self.q[eng].append((lambda e: e.dma_start(out=out, in_=in_, **kw), waits, k, 16))
        self._mark(ev, reads, writes)
        return ev

    def barrier(self):
        evs = [(('e', e), self.cnt[e]) for e in ENGS if self.cnt[e] > 0]
        evs += [(k, 16 * c) for k, c in self.dcnt.items() if c > 0]
        for e in ENGS:
            wd = self.waited[e]
            waits = []
            for k, v in evs:
                if k == ('e', e):
                    continue
                if wd.get(k, 0) >= v:
                    continue
                wd[k] = v
                waits.append((k, v))
            self.q[e].append((None, waits, None, 0))

    def emit(self):
        nc = self.nc
        with nc.Block() as block:
            def run(name):
                def f(e):
                    for fn, waits, k, inc in self.q[name]:
                        for wk, wv in waits:
                            e.wait_ge(self.sem[wk], wv)
                        if fn is not None:
                            fn(e).then_inc(self.sem[k], inc)
                return f
            block.tensor(run('pe'))
            block.scalar(run('act'))
            block.vector(run('dve'))
            block.gpsimd(run('pool'))
            block.sync(run('sp'))

    def mm(self, out, lhsT, rhs, start=True, stop=True, reads=(), writes=()):
        return self.op('pe', lambda e: e.matmul(out, lhsT, rhs, start=start, stop=stop), reads, writes)

    def tr(self, out, in_, ident, reads=(), writes=()):
        return self.op('pe', lambda e: e.transpose(out, in_, ident), reads, writes)

    def act(self, out, in_, func, reads=(), writes=(), **kw):
        return self.op('act', lambda e: e.activation(out, in_, func, **kw), reads, writes)

    def tt(self, eng, out, in0, in1, op, reads=(), writes=()):
        return self.op(eng, lambda e: e.tensor_tensor(out, in0, in1, op), reads, writes)

    def ts(self, eng, out, in0, s1, s2, op0, op1=None, reads=(), writes=()):
        if op1 is None:
            return self.op(eng, lambda e: e.tensor_scalar(out, in0, s1, s2, op0), reads, writes)
        return self.op(eng, lambda e: e.tensor_scalar(out, in0, s1, s2, op0, op1), reads, writes)

    def stt(self, eng, out, in0, scalar, in1, op0, op1, reads=(), writes=()):
        return self.op(eng, lambda e: e.scalar_tensor_tensor(out, in0, scalar, in1, op0, op1), reads, writes)

    def cp(self, eng, out, in_, reads=(), writes=()):
        if eng == 'act':
            return self.op('act', lambda e: e.copy(out, in_), reads, writes)
        return self.op(eng, lambda e: e.tensor_copy(out, in_), reads, writes)

    def memset(self, eng, ap, val, writes=()):
        return self.op(eng, lambda e: e.memset(ap, val), (), writes)


class Arena:
    def __init__(self, ap, size):
        self.ap = ap
        self.size = size
        self.top = 0

    def bf(self, n):
        a = self.top
        self.top += (n + 31) // 32 * 32
        assert self.top <= self.size, ('arena overflow', self.top, self.size)
        return self.ap[:, a:a + n]

    def f32(self, n):
        a = self.top
        self.top += (2 * n + 31) // 32 * 32
        assert self.top <= self.size, ('arena overflow', self.top, self.size)
        return self.ap[:, a:a + 2 * n].bitcast(F32)


def v3(ap, a):
    return ap.rearrange("p (a b) -> p a b", a=a)


LAG = 3
WARM = 0
NDUMMY = 0


def pipe(n, *stages, lag=LAG):
    ns = len(stages)
    st = [None] * n
    for t in range(n + lag * (ns - 1)):
        for k, f in enumerate(stages):
            i = t - k * lag
            if 0 <= i < n:
                st[i] = f(i, st[i]) if k else f(i)


def t5_bucket_np(d):
    n = np.maximum(d, 0)
    nf = np.maximum(n, 1).astype(np.float32)
    large = 16 + (np.log(nf / np.float32(16)) / np.float32(math.log(128 / 16)) * np.float32(16)).astype(np.int32)
    large = np.minimum(large, 31)
    return np.where(n < 16, n, large)


def host_consts(S):
    k = np.arange(128)[:, None]
    q = np.arange(128)[None, :]
    ident = (k == q)
    J = (k + q == 127)
    uincl = (k >= q)
    ones = np.ones((128, 128), bool)
    blk = (k // 64 == q // 64)
    maskL = (k < q)
    m512 = (q < k)
    zeros = np.zeros((128, 512), bool)
    n_cmp = (S - 32) // 16 + 1
    nct = (n_cmp + 127) // 128
    n = np.arange(nct * 128)[:, None]
    m = np.arange(128)[None, :]
    ovl = ((16 * n < 64 * m + 64) & (16 * n + 32 > 64 * m) & (n < n_cmp))
    ovl = ovl.reshape(nct, 128, 128).transpose(1, 0, 2).reshape(128, nct * 128)
    mm_ = np.arange(128)[:, None]
    xx = np.arange(S)[None, :]
    B = (xx // 64 == mm_)
    cb = np.concatenate([ident, J, uincl, ones, blk, maskL, m512, zeros, ovl, B], axis=1).astype(np.float32)
    cb = cb.astype(ml_dtypes.bfloat16)
    negmask = np.where(k >= q, -1e4, 0.0).astype(np.float32)
    y = np.arange(256)[None, :]
    rel = y - 126 - (k >= 64)
    cs = np.where((rel == 0) | (rel == -1), BIG, np.where(rel > 0, NEG, 0.0)).astype(np.float32)
    oh = np.zeros((128, 128), np.float32)
    b = t5_bucket_np(np.arange(128))
    oh[b, np.arange(128)] = 1.0
    cf = np.concatenate([negmask, cs, oh], axis=1).astype(np.float32)
    return cb, cf, n_cmp, nct


def build(S, depth, debug=False, phases=None, dbg=None, inject=False):
    NT = S // 128
    NC = S // 512
    cbh, cfh, n_cmp, NCT = host_consts(S)
    NCB = cbh.shape[1]
    NCF = cfh.shape[1]
    nc = bass.Bass("TRN2", target_bir_lowering=False)
    dt_in = lambda name, shape, dt=F32: nc.dram_tensor(name, shape, dt, kind="ExternalInput").ap()
    dbg_kind = "ExternalOutput" if debug else "Internal"
    scr = lambda name, shape, dt: nc.dram_tensor(name, shape, dt, kind=dbg_kind).ap()
    x_in = dt_in("x", [S, D])
    rel_bias = dt_in("rel_bias", [32, 12])
    attn_norm = dt_in("attn_norm", [depth, D])
    w_in = dt_in("w_in", [depth, D, NIN])
    gains = dt_in("gains", [depth, 6, 64])
    cmp_pos = dt_in("nsa_cmp_pos", [depth, 2, 32, 64])
    cmp_w1 = dt_in("nsa_cmp_w1", [depth, 2, 2048, 256])
    cmp_w2 = dt_in("nsa_cmp_w2", [depth, 2, 256, 64])
    w_branch = dt_in("w_branch", [depth, D, D])
    w_out = dt_in("w_out", [depth, D, D])
    ffn_norm = dt_in("ffn_norm", [depth, D])
    w_gate_up = dt_in("w_gate_up", [depth, D, 2 * DFF])
    w_down = dt_in("w_down", [depth, DFF, D])
    cb_in = dt_in("cb", [128, NCB], BF16)
    cf_in = dt_in("cf", [128, NCF])
    out = nc.dram_tensor("out", [S, D], F32, kind="ExternalOutput").ap()

    hTd = scr("hTd", [D, S], BF16)
    fmd = scr("fmd", [2048, S], BF16)
    tmvd = scr("tmvd", [S, 832], BF16)
    nsgd = scr("nsgd", [S, 32], F32)
    oTd = scr("oTd", [D, S], BF16)
    x1d = scr("x1d", [S, D], F32)
    xmd = scr("xmd", [S, D], F32)
    gvd = scr("gvd", [12, GL], BF16)
    oT_in = dt_in("oT_in", [D, S], BF16) if inject else None

    with ExitStack() as st:
        sc = Sched(nc, st)
        ASZ = 98304
        arena_ap = nc.alloc_sbuf_tensor("arena", [128, ASZ], BF16).ap()
        ar = Arena(arena_ap, ASZ)
        banks = [nc.alloc_psum_tensor("bank%d" % i, [128, 512], F32).ap() for i in range(8)]
        bres = [Res("bank%d" % i) for i in range(8)]

        CB_GLOBAL = 8 * 128 + 384
        cb = ar.bf(CB_GLOBAL)
        ident = cb[:, 0:128]
        Jm = cb[:, 128:256]
        uincl = cb[:, 256:384]
        onesb = cb[:, 384:512]
        blkones = cb[:, 512:640]
        maskL = cb[:, 640:768]
        m512 = cb[:, 768:896]
        zerob = cb[:, 896:1408]
        cf = ar.f32(384)
        negmask = cf[:, 0:128]
        cslide = cf[:, 128:384]
        Etab = ar.bf(24 * 128)
        r_const = Res('const')
        sc.dma('sp', cb, cb_in[:, 0:CB_GLOBAL], writes=[r_const])
        sc.dma('sp', cf, cf_in[:, 0:384], writes=[r_const])
        base_top = ar.top

        def E0(h):
            return Etab[:, (2 * h) * 128:(2 * h + 1) * 128]

        def E128(h):
            return Etab[:, (2 * h + 1) * 128:(2 * h + 2) * 128]

        def prologue():
            ar.top = base_top
            tb = ar.f32(12)
            oh = ar.f32(128)
            br = ar.f32(128)
            negc = ar.f32(1)
            gsb = ar.bf(GL)
            hk = ar.bf(128)
            r = Res('pro')
            rb = bres[0]
            sc.dma('sp', tb[0:32, :], rel_bias, writes=[r])
            sc.dma('sp', oh[0:32, :], cf_in[0:32, 384:512], writes=[r])
            sc.mm(banks[0][0:12, 0:128], tb[0:32, 0:12], oh[0:32, 0:128], reads=[r], writes=[rb])
            sc.cp('dve', br[0:12, :], banks[0][0:12, 0:128], reads=[rb], writes=[r])
            sc.ts('dve', negc[0:12, :], br[0:12, 127:128], -1.0, None, ALU.mult, reads=[r], writes=[r])
            sc.memset('pool', gsb[0:12, 0:GOFF], 0.0, writes=[r])
            sc.memset('pool', gsb[0:12, GOFF + 128:GL], 1.0, writes=[r])
            sc.act(gsb[0:12, GOFF:GOFF + 128], br[0:12, :], AF.Exp, reads=[r], writes=[r], bias=negc[0:12, :])
            rg = Res('gvd')
            sc.dma('sp', gvd, gsb[0:12, :], reads=[r], writes=[rg])
            hkr = Res('hk')
            for h in range(12):
                for di, dl in enumerate((0, 128)):
                    src = AP(gvd.tensor, h * GL + GOFF + dl - 127, [[1, 128], [1, 128]])
                    sc.dma('sp', hk, src, reads=[rg], writes=[hkr])
                    sc.mm(banks[1][:, 0:128], Jm, hk, reads=[hkr, r_const], writes=[bres[1]])
                    sc.cp('dve', Etab[:, (2 * h + di) * 128:(2 * h + di + 1) * 128], banks[1][:, 0:128],
                          reads=[bres[1]], writes=[r_const])
            sc.barrier()

        def load_v(dst, src, writes):
            step = 8
            for t0 in range(0, NT, step):
                t1 = min(NT, t0 + step)
                sc.dma('sp', dst[:, t0:t1, :], src[t0 * 128:t1 * 128, :].rearrange("(t p) d -> p t d", p=128), writes=writes)

        CAST_ENG = ('pool', 'dve', 'act')
        cast_i = [0]
        def load_cast(dst, src, n, stage_ring, rdst):
            o = 0
            while o < n:
                w = min(2048, n - o)
                sg, rs = stage_ring.next()
                sc.dma('sp', sg[:, 0:w], src[:, o:o + w], writes=[rs])
                sc.cp(CAST_ENG[cast_i[0] % 3], dst[:, o:o + w], sg[:, 0:w], reads=[rs], writes=[rdst])
                cast_i[0] += 1
                o += w

        def rmsnorm_tm(xs, rx, gB, rg, hb, rh, ntile, sq, ssq, rstd, rtmp):
            for t in range(ntile):
                sc.act(sq, xs[:, t, :], AF.Square, reads=[rx], writes=[rtmp])
                sc.op('dve', lambda e, t=t: e.tensor_reduce(ssq[:, t:t + 1], sq, AX.X, ALU.add), reads=[rtmp], writes=[rtmp])
            sc.act(rstd[:, 0:ntile], ssq[:, 0:ntile], AF.Ln, reads=[rtmp], writes=[rtmp], scale=1.0 / D, bias=EPS)
            sc.act(rstd[:, 0:ntile], rstd[:, 0:ntile], AF.Exp, reads=[rtmp], writes=[rtmp], scale=-0.5)
            for t in range(ntile):
                sc.stt('dve', hb[:, t, :], xs[:, t, :], rstd[:, t:t + 1], gB, ALU.mult, ALU.mult,
                       reads=[rx, rtmp, rg], writes=[rh])

        def transpose_to(hb, rh, hT, rhT, ntile, bank_ring, evac_engs):
            for c in range(8):
                bk, rb = bank_ring.next()
                pb = bk.bitcast(BF16)
                for t in range(ntile):
                    sc.tr(pb[:, t * 128:(t + 1) * 128], hb[:, t, c * 128:(c + 1) * 128], ident,
                          reads=[rh, r_const], writes=[rb])
                sc.cp(evac_engs[c % len(evac_engs)], hT[:, c, :], pb[:, 0:ntile * 128], reads=[rb], writes=[rhT])

        def phase_A(l, xsrc):
            ar.top = base_top
            W = v3(ar.bf(8 * NQKV), 8)
            gA = ar.f32(D)
            gcol = ar.f32(8)
            rW = Res('W')
            mark = ar.top
            stg = [(ar.f32(2048), Res('stg%d' % i)) for i in range(4)]
            sring = Ring(stg)
            for c in range(8):
                load_cast(W[:, c, :], w_in[l, c * 128:(c + 1) * 128, 0:NQKV], NQKV, sring, rW)
            sc.dma('sp', gA, AP(attn_norm.tensor, l * D, [[0, 128], [1, D]]), writes=[rW])
            for gi in range(6):
                for half in range(2):
                    sc.dma('sp', gcol[half * 64:(half + 1) * 64, gi:gi + 1],
                           AP(gains.tensor, (l * 6 + gi) * 64, [[1, 64], [1, 1]]), writes=[rW])
            sc.barrier()
            if dbg == 'W':
                return
            ar.top = mark
            xs_r = Ring([(v3(ar.f32(4 * D), 4), Res('xs%d' % i)) for i in range(2)])
            hb = v3(ar.bf(4 * D), 4)
            rhb = Res('hb')
            sq = ar.f32(D)
            ssq = ar.f32(4)
            rstd = ar.f32(4)
            rtmp = Res('tmp')
            hT_r = Ring([(v3(ar.bf(8 * 512), 8), Res('hT%d' % i)) for i in range(2)])
            sqb_r = Ring([(ar.bf(512), Res('sqb%d' % i)) for i in range(2)])
            rs_r = Ring([(ar.f32(512), Res('rs%d' % i)) for i in range(2)])
            fst_r = Ring([(ar.bf(512), Res('fst%d' % i)) for i in range(3)])
            tst_r = Ring([(ar.bf(832), Res('tst%d' % i)) for i in range(2)])
            gst_r = Ring([(ar.f32(24), Res('gst%d' % i)) for i in range(2)])
            tb_r = Ring([(banks[i], bres[i]) for i in (0, 1)])
            pb_r = Ring([(banks[i], bres[i]) for i in (2, 3, 4)])
            nb_r = Ring([(banks[i], bres[i]) for i in (5,)])
            vb_r = Ring([(banks[i], bres[i]) for i in (6, 7)])
            r_hTd, r_fmd, r_tmvd, r_nsgd = Res(), Res(), Res(), Res()
            for ci in range(NC):
                xs, rx = xs_r.next()
                sc.dma('sp', xs, xsrc[ci * 512:(ci + 1) * 512, :].rearrange("(t p) d -> p t d", p=128), writes=[rx])
                if dbg == 'ld':
                    continue
                rmsnorm_tm(xs, rx, gA, rW, hb, rhb, 4, sq, ssq, rstd, rtmp)
                if dbg == 'norm':
                    continue
                hT, rhT = hT_r.next()
                transpose_to(hb, rhb, hT, rhT, 4, tb_r, ('dve', 'act'))
                if dbg == 'tr':
                    continue
                sc.dma(STQ, hTd.rearrange("(c p) s -> p c s", p=128)[:, :, ci * 512:(ci + 1) * 512], hT,
                       reads=[rhT], writes=[r_hTd])
                for j in range(16):
                    col = FM_COLS[j]
                    bk, rb = pb_r.next()
                    for c in range(8):
                        sc.mm(bk, W[:, c, col:col + 128], hT[:, c, :], start=(c == 0), stop=(c == 7),
                              reads=[rW, rhT], writes=[rb])
                    fs, rf = fst_r.next()
                    if FM_KIND[j] is None:
                        sc.cp('act' if j % 2 else 'dve', fs, bk, reads=[rb], writes=[rf])
                    else:
                        gi = FM_KIND[j]
                        sqb, rsq = sqb_r.next()
                        sc.act(sqb, bk, AF.Square, reads=[rb], writes=[rsq])
                        b2, rb2 = nb_r.next()
                        sc.mm(b2, blkones, sqb, reads=[rsq, r_const], writes=[rb2])
                        rs, rrs = rs_r.next()
                        sc.act(rs, b2, AF.Ln, reads=[rb2], writes=[rrs], scale=1.0 / 64, bias=EPS)
                        sc.act(rs, rs, AF.Exp, reads=[rrs], writes=[rrs], scale=-0.5)
                        sc.stt('dve', fs, bk, gcol[:, gi:gi + 1], rs, ALU.mult, ALU.mult, reads=[rb, rrs, rW], writes=[rf])
                    sc.dma(STQ, fmd[j * 128:(j + 1) * 128, ci * 512:(ci + 1) * 512], fs, reads=[rf], writes=[r_fmd])
                if dbg == 'fm':
                    continue
                for t in range(4):
                    b1, rb1 = vb_r.next()
                    b2, rb2 = vb_r.next()
                    lt = lambda c: hT[:, c, t * 128:(t + 1) * 128]
                    for (bk, rb, o, c0, w) in ((b1, rb1, 0, 512, 256), (b1, rb1, 256, 1280, 256),
                                               (b2, rb2, 0, 2432, 128), (b2, rb2, 128, 2688, 152)):
                        for c in range(8):
                            sc.mm(bk[:, o:o + w], lt(c), W[:, c, c0:c0 + w], start=(c == 0), stop=(c == 7),
                                  reads=[rW, rhT], writes=[rb])
                    ts_, rts = tst_r.next()
                    sc.cp('act', ts_[:, 0:512], b1, reads=[rb1], writes=[rts])
                    sc.cp('dve', ts_[:, 512:768], b2[:, 0:256], reads=[rb2], writes=[rts])
                    sc.act(ts_[:, 768:816].bitcast(F32), b2[:, 256:280], AF.Sigmoid, reads=[rb2], writes=[rts])
                    r0 = (ci * 4 + t) * 128
                    sc.dma(STQ, tmvd[r0:r0 + 128, 0:816], ts_[:, 0:816], reads=[rts], writes=[r_tmvd])
            sc.barrier()

        def phase_SB(l):
            ar.top = base_top
            kT = ar.bf(S)
            qT = ar.bf(S)
            v = v3(ar.bf(NT * 64), NT)
            rk = Res('kqv')
            R_r = Ring([(ar.bf(512), Res()) for i in range(LAG + 2)])
            zc_r = Ring([(ar.f32(512), Res()) for i in range(LAG + 2)])
            u_r = Ring([(ar.f32(512), Res()) for i in range(2)])
            sp_r = Ring([(ar.bf(512), Res()) for i in range(LAG + 2)])
            w_r = Ring([(ar.f32(512), Res()) for i in range(2)])
            a_r = Ring([(ar.bf(512), Res()) for i in range(LAG + 2)])
            os_r = Ring([(ar.bf(512), Res()) for i in range(2)])
            zb_r = Ring([(banks[i], bres[i]) for i in (0, 1)])
            tb_r = Ring([(banks[i], bres[i]) for i in (2, 3, 7)])
            rdum = Res('dummy')
            ob_r = Ring([(banks[i], bres[i]) for i in (4, 5)])
            r_oTd = Res()
            for h in range(4):
                sc.dma('sp', kT[0:64, :], fmd[256 + 64 * h:256 + 64 * h + 64, :], writes=[rk])
                sc.dma('sp', qT[0:64, :], fmd[64 * h:64 * h + 64, :], writes=[rk])
                load_v(v, tmvd[:, 64 * h:64 * h + 64], [rk])
                for c in range(NC):
                    ob, rob = ob_r.next()
                    sc.mm(ob[0:64, :], zerob[:, 0:64], zerob, start=True, stop=False, reads=[r_const], writes=[rob])
                    for _ in range(WARM):
                        sc.mm(banks[6], onesb, zerob, reads=[r_const], writes=[rdum])
                    last = 4 * c + 3
                    n_t = last + 1
                    Rcur = [R_r.next()]
                    sc.memset('pool', Rcur[0][0], 0.0, writes=[Rcur[0][1]])

                    def s1(i, c=c, last=last, n_t=n_t, Rcur=Rcur):
                        kt = last - i
                        rel = kt - 4 * c
                        c0 = 128 * max(rel, 0)
                        zb, rzb = zb_r.next()
                        sc.mm(zb[:, c0:512], kT[0:64, kt * 128:(kt + 1) * 128], qT[0:64, c * 512 + c0:(c + 1) * 512],
                              reads=[rk], writes=[rzb])
                        for _ in range(NDUMMY):
                            sc.mm(banks[6], onesb, zerob, reads=[r_const], writes=[rdum])
                        zc, rzc = zc_r.next()
                        sc.ts('dve', zc[:, c0:512], zb[:, c0:512], 0.125, 40.0, ALU.mult, ALU.min, reads=[rzb], writes=[rzc])
                        u, ru = u_r.next()
                        sc.act(u[:, c0:512], zc[:, c0:512], AF.Exp, reads=[rzc], writes=[ru])
                        sp, rsp = sp_r.next()
                        sc.act(sp[:, c0:512], u[:, c0:512], AF.Ln, reads=[ru], writes=[rsp], bias=1.0)
                        if rel >= 0:
                            sc.tt('pool', sp[:, c0:c0 + 128], sp[:, c0:c0 + 128], maskL, ALU.mult,
                                  reads=[rsp, r_const], writes=[rsp])
                        if c0 > 0:
                            sc.memset('pool', sp[:, 0:c0], 0.0, writes=[rsp])
                        Ri, rRi = Rcur[0]
                        if i < n_t - 1:
                            Rn, rRn = R_r.next()
                            sc.tt('pool', Rn, Ri, sp, ALU.add, reads=[rRi, rsp], writes=[rRn])
                            Rcur[0] = (Rn, rRn)
                        return (kt, rel, c0, zc, rzc, sp, rsp, Ri, rRi)

                    def s2(i, stt_, ob=ob, rob=rob):
                        kt, rel, c0, zc, rzc, sp, rsp, Ri, rRi = stt_
                        tb, rtb = tb_r.next()
                        sc.mm(tb[:, c0:512], uincl, sp[:, c0:512], start=True, stop=(i == 0),
                              reads=[rsp, r_const], writes=[rtb])
                        if i > 0:
                            sc.mm(tb[:, c0:512], onesb, Ri[:, c0:512], start=False, stop=True,
                                  reads=[rRi, r_const], writes=[rtb])
                        w, rw = w_r.next()
                        sc.tt('dve', w[:, c0:512], zc[:, c0:512], tb[:, c0:512], ALU.subtract, reads=[rzc, rtb], writes=[rw])
                        if rel >= 0:
                            sc.tt('pool', w[:, c0:c0 + 128], w[:, c0:c0 + 128], negmask, ALU.add,
                                  reads=[rw, r_const], writes=[rw])
                        a, ra = a_r.next()
                        sc.act(a[:, c0:512], w[:, c0:512], AF.Exp, reads=[rw], writes=[ra])
                        return (kt, c0, a, ra)

                    def s3(i, stt_, ob=ob, rob=rob):
                        kt, c0, a, ra = stt_
                        sc.mm(ob[0:64, c0:512], v[:, kt, :], a[:, c0:512], start=False, stop=(kt == 0),
                              reads=[rk, ra], writes=[rob])

                    pipe(n_t, s1, s2, s3)
                    os_, ros = os_r.next()
                    sc.cp('act', os_[0:64, :], ob[0:64, :], reads=[rob], writes=[ros])
                    sc.dma('sp', oTd[64 * h:64 * h + 64, c * 512:(c + 1) * 512], os_[0:64, :], reads=[ros], writes=[r_oTd])
            sc.barrier()


        def phase_MB(l):
            ar.top = base_top
            NBLK = S // 256
            kT = ar.bf(S)
            qT = ar.bf(S)
            va = ar.bf(NT * 130).rearrange("p (t e d) -> p t e d", e=2, d=65)
            rk = Res('kqv')
            km = ar.f32(32)
            kmz = v3(ar.f32(64), 2)
            rkm = Res('km')
            qf_r = Ring([(ar.f32(512), Res()) for i in range(2)])
            scv = ar.f32(256).rearrange("p (e j n) -> p e j n", e=2, j=4)
            m8 = ar.f32(64)
            selw = ar.f32(256).rearrange("p (e j n) -> p e j n", e=2, j=4)
            rsel = Res('sel')
            acc_r = Ring([(ar.f32(2 * 4 * 65).rearrange("p (e j d) -> p e j d", e=2, j=4),
                           [[Res() for j in range(4)] for e in range(2)]) for i in range(2)])
            rl = ar.f32(8).rearrange("p (e j o) -> p e j o", e=2, o=1)
            rrl = Res('rl')
            o_tok = v3(ar.bf(NT * 256), NT)
            rot = Res('otok')
            p_r = Ring([(ar.bf(512), Res()) for i in range(2 * (LAG + 2))])
            st_r = Ring([(ar.bf(512), Res()) for i in range(2)])
            lb_r = Ring([(banks[i], bres[i]) for i in (0, 1, 7)])
            ob_r = Ring([(banks[i], bres[i]) for i in (2, 3, 4, 6)])
            sb_r = Ring([(banks[i], bres[i]) for i in (5,)])
            tb_r = Ring([(banks[i], bres[i]) for i in (5, 6)])
            r_oTd = Res()
            sc.memset('pool', va[:, :, :, 64:65], 1.0, writes=[rk])
            for hp in range(2):
                sc.dma('sp', kT, fmd[768 + 128 * hp:768 + 128 * hp + 128, :], writes=[rk])
                sc.dma('sp', qT, fmd[512 + 128 * hp:512 + 128 * hp + 128, :], writes=[rk])
                for e in range(2):
                    h = 2 * hp + e
                    load_v(va[:, :, e, 0:64], tmvd[:, 256 + 64 * h:256 + 64 * h + 64], [rk])
                sc.op('dve', lambda e_: e_.tensor_reduce(km[:, 0:NBLK], kT.rearrange("p (n k) -> p n k", k=256), AX.X, ALU.add),
                      reads=[rk], writes=[rkm])
                sc.ts('dve', km[:, 0:NBLK], km[:, 0:NBLK], 1.0 / 256, None, ALU.mult, reads=[rkm], writes=[rkm])
                sc.memset('pool', kmz, 0.0, writes=[rkm])
                for e in range(2):
                    sc.cp('dve', kmz[64 * e:64 * e + 64, e, 0:NBLK], km[64 * e:64 * e + 64, 0:NBLK], reads=[rkm], writes=[rkm])
                for c in range(NC):
                    qf, rqf = qf_r.next()
                    sc.cp('dve', qf, qT[:, c * 512:(c + 1) * 512], reads=[rk], writes=[rqf])
                    sc.memset('pool', scv, NEG, writes=[rsel])
                    sb, rsb = sb_r.next()
                    for e in range(2):
                        es = slice(64 * e, 64 * e + 64)
                        for j in range(4):
                            cur = (4 * c + j) // 2
                            if cur > 0:
                                o_ = (e * 4 + j) * 32
                                sc.mm(sb[:, o_:o_ + cur], qf[:, j * 128:(j + 1) * 128], kmz[:, e, 0:cur],
                                      start=True, stop=True,
                                      reads=[rqf, rkm], writes=[rsb])
                    for e in range(2):
                        for j in range(4):
                            cur = (4 * c + j) // 2
                            if cur > 0:
                                o_ = (e * 4 + j) * 32
                                sc.cp('dve', scv[:, e, j, 0:cur], sb[:, o_:o_ + cur], reads=[rsb], writes=[rsel])
                    for e in range(2):
                        for j in range(4):
                            o8 = (e * 4 + j) * 8
                            sc.op('dve', lambda e_, e=e, j=j, o8=o8: e_.max(out=m8[:, o8:o8 + 8], in_=scv[:, e, j, :]), reads=[rsel], writes=[rsel])
                            sc.ts('dve', selw[:, e, j, :], scv[:, e, j, :], m8[:, o8 + 2:o8 + 3], None, ALU.is_ge, reads=[rsel], writes=[rsel])
                    acc, racc = acc_r.next()
                    sc.memset('pool', acc, 0.0, writes=[r_ for re_ in racc for r_ in re_])
                    obs = [None, None]

                    def s1(kt, c=c, hp=hp):
                        rel = kt - 4 * c
                        j0 = max(rel, 0)
                        c0 = 128 * j0
                        lbs = []
                        for e in range(2):
                            es = slice(64 * e, 64 * e + 64)
                            lb, rlb = lb_r.next()
                            sc.mm(lb[:, c0:512], kT[es, kt * 128:(kt + 1) * 128], qT[es, c * 512 + c0:(c + 1) * 512],
                                  reads=[rk], writes=[rlb])
                            lbs.append((lb, rlb))
                        ps = []
                        for e in range(2):
                            h = 2 * hp + e
                            lb, rlb = lbs[e]
                            p, rp = p_r.next()
                            sc.act(p[:, c0:512], lb[:, c0:512], AF.Exp, reads=[rlb], writes=[rp], scale=0.125)
                            for j in range(j0, 4):
                                d = 4 * c + j - kt
                                if d in (0, 1):
                                    sc.tt('pool', p[:, j * 128:(j + 1) * 128], p[:, j * 128:(j + 1) * 128],
                                          E0(h) if d == 0 else E128(h), ALU.mult, reads=[rp, r_const], writes=[rp])
                            ps.append((p, rp))
                        return (ps, j0)

                    def s2(kt, stt_, c=c, obs=obs, acc=acc, racc=racc):
                        ps, j0 = stt_
                        n = kt // 2
                        for e in range(2):
                            p, rp = ps[e]
                            if kt % 2 == 0:
                                obs[e] = ob_r.next()
                            ob, rob = obs[e]
                            done = []
                            for j in range(j0, 4):
                                qt = 4 * c + j
                                stop = (kt % 2 == 1) or (kt == qt)
                                sc.mm(ob[:, j * 128:j * 128 + 65], p[:, j * 128:(j + 1) * 128], va[:, kt, e, :],
                                      start=(kt % 2 == 0 and j == j0), stop=stop, reads=[rp, rk], writes=[rob])
                                if stop:
                                    done.append(j)
                            for j in done:
                                qt = 4 * c + j
                                if n == qt // 2:
                                    sc.tt('dve', acc[:, e, j, :], ob[:, j * 128:j * 128 + 65], acc[:, e, j, :], ALU.add,
                                          reads=[rob, racc[e][j]], writes=[racc[e][j]])
                                else:
                                    sc.stt('dve', acc[:, e, j, :], ob[:, j * 128:j * 128 + 65], selw[:, e, j, n:n + 1], acc[:, e, j, :],
                                           ALU.mult, ALU.add, reads=[rob, racc[e][j], rsel], writes=[racc[e][j]])

                    pipe(4 * c + 4, s1, s2)
                    allacc = [r_ for re_ in racc for r_ in re_]
                    sc.ts('dve', rl, acc[:, :, :, 64:65], TINY, None, ALU.max, reads=allacc, writes=[rrl])
                    sc.op('dve', lambda e_: e_.reciprocal(rl, rl), reads=[rrl], writes=[rrl])
                    for e in range(2):
                        h = 2 * hp + e
                        sc.tt('dve', o_tok[:, 4 * c:4 * c + 4, h * 64:(h + 1) * 64], acc[:, e, :, 0:64],
                              rl[:, e].to_broadcast([128, 4, 64]), ALU.mult, reads=allacc + [rrl], writes=[rot])
            for c in range(NC):
                for fc in range(2):
                    tb, rtb = tb_r.next()
                    pb = tb.bitcast(BF16)
                    for j in range(4):
                        sc.tr(pb[:, j * 128:(j + 1) * 128], o_tok[:, 4 * c + j, fc * 128:(fc + 1) * 128], ident,
                              reads=[rot, r_const], writes=[rtb])
                    stg, rst = st_r.next()
                    sc.cp('act' if fc else 'dve', stg, pb[:, 0:512], reads=[rtb], writes=[rst])
                    sc.dma(STQ, oTd[256 + fc * 128:256 + (fc + 1) * 128, c * 512:(c + 1) * 512], stg, reads=[rst], writes=[r_oTd])
            sc.barrier()

        def phase_NSA(l):
            ar.top = base_top
            NCW = NCT * 128
            kcT2 = [ar.bf(NCW) for g in range(2)]
            vca = [v3(ar.bf(NCT * 65), NCT) for g in range(2)]
            gcol = ar.f32(8)
            r_cmp = Res('cmp')
            mark_p = ar.top
            w1sb = v3(ar.bf(32 * 256), 32)
            w2sb = v3(ar.bf(2 * 64), 2)
            TT = ar.bf(S)
            posf = ar.f32(64)
            posb = ar.bf(64)
            posT = ar.bf(32)
            pbias = ar.f32(2)
            xb = ar.f32(512)
            x2 = ar.f32(512)
            th = ar.f32(512)
            ghT = v3(ar.bf(2 * 512), 2)
            sqb = ar.bf(512)
            rs = ar.f32(512)
            sring = Ring([(ar.f32(2048), Res()) for i in range(2)])
            rw, rt, rx, rgh = Res('w'), Res('TT'), Res('x'), Res('gh')
            n = n_cmp
            for gi in range(6):
                for half in range(2):
                    sc.dma('sp', gcol[half * 64:(half + 1) * 64, gi:gi + 1],
                           AP(gains.tensor, (l * 6 + gi) * 64, [[1, 64], [1, 1]]), writes=[r_cmp])
            for g in range(2):
                sc.memset('pool', kcT2[g], 0.0, writes=[r_cmp])
                sc.memset('pool', vca[g], 0.0, writes=[r_cmp])
            for kv in range(2):
                w1v = cmp_w1[l, kv].rearrange("(i d) h -> d i h", d=64)
                for i0 in range(0, 32, 8):
                    sg, rsg = sring.next()
                    sc.dma('sp', v3(sg, 8)[0:64], w1v[:, i0:i0 + 8, :], writes=[rsg])
                    sc.cp('pool', w1sb[0:64, i0:i0 + 8, :], v3(sg, 8)[0:64], reads=[rsg], writes=[rw])
                sg, rsg = sring.next()
                sc.dma('sp', v3(sg[:, 0:128], 2), cmp_w2[l, kv].rearrange("(c p) d -> p c d", p=128), writes=[rsg])
                sc.cp('pool', w2sb, v3(sg[:, 0:128], 2), reads=[rsg], writes=[rw])
                sc.dma('sp', posf[0:32, :], cmp_pos[l, kv], writes=[rw])
                sc.cp('dve', posb[0:32, :], posf[0:32, :], reads=[rw], writes=[rw])
                pbk = banks[5].bitcast(BF16)
                sc.tr(pbk[0:64, 0:32], posb[0:32, 0:64], ident[0:32, 0:32], reads=[rw, r_const], writes=[bres[5]])
                sc.cp('dve', posT[0:64, :], pbk[0:64, 0:32], reads=[bres[5]], writes=[rw])
                for hc in range(2):
                    for i in range(32):
                        sc.mm(banks[6][:, hc:hc + 1], w1sb[0:64, i, hc * 128:(hc + 1) * 128], posT[0:64, i:i + 1],
                              start=(i == 0 and hc == 0), stop=(i == 31), reads=[rw], writes=[bres[6]])
                sc.cp('dve', pbias, banks[6][:, 0:2], reads=[bres[6]], writes=[rw])
                for g in range(2):
                    base = (1536 if kv == 0 else 1664) + 64 * g
                    sc.dma('sp', TT[0:64, :], fmd[base:base + 64, :], writes=[rt])
                    for hc in range(2):
                        bk, rb = banks[hc], bres[hc]
                        for i in range(32):
                            sc.mm(bk[:, 0:n], w1sb[0:64, i, hc * 128:(hc + 1) * 128], TT[0:64, i:i + 16 * (n - 1) + 1:16],
                                  start=(i == 0), stop=(i == 31), reads=[rw, rt], writes=[rb])
                        sc.ts('dve', xb[:, 0:n], bk[:, 0:n], pbias[:, hc:hc + 1], None, ALU.add, reads=[rb, rw], writes=[rx])
                        sc.tt('dve', x2[:, 0:n], xb[:, 0:n], xb[:, 0:n], ALU.mult, reads=[rx], writes=[rx])
                        sc.ts('dve', x2[:, 0:n], x2[:, 0:n], 0.044715, 1.0, ALU.mult, ALU.add, reads=[rx], writes=[rx])
                        sc.tt('dve', x2[:, 0:n], x2[:, 0:n], xb[:, 0:n], ALU.mult, reads=[rx], writes=[rx])
                        sc.act(th[:, 0:n], x2[:, 0:n], AF.Tanh, reads=[rx], writes=[rx], scale=0.7978845608028654)
                        sc.ts('dve', xb[:, 0:n], xb[:, 0:n], 0.5, None, ALU.mult, reads=[rx], writes=[rx])
                        sc.stt('dve', ghT[:, hc, 0:n], th[:, 0:n], 1.0, xb[:, 0:n], ALU.add, ALU.mult, reads=[rx], writes=[rgh])
                    if kv == 0:
                        for hc in range(2):
                            sc.mm(banks[2][0:64, 0:n], w2sb[:, hc, :], ghT[:, hc, 0:n], start=(hc == 0), stop=(hc == 1),
                                  reads=[rw, rgh], writes=[bres[2]])
                        sc.act(sqb[0:64, 0:n], banks[2][0:64, 0:n], AF.Square, reads=[bres[2]], writes=[rx])
                        sc.mm(banks[3][0:64, 0:n], onesb[0:64, 0:64], sqb[0:64, 0:n], reads=[rx, r_const], writes=[bres[3]])
                        sc.act(rs[0:64, 0:n], banks[3][0:64, 0:n], AF.Ln, reads=[bres[3]], writes=[rx], scale=1.0 / 64, bias=EPS)
                        sc.act(rs[0:64, 0:n], rs[0:64, 0:n], AF.Exp, reads=[rx], writes=[rx], scale=-0.5)
                        sc.stt('dve', kcT2[g][0:64, 0:n], banks[2][0:64, 0:n], gcol[0:64, 3:4], rs[0:64, 0:n], ALU.mult, ALU.mult,
                               reads=[bres[2], rx, r_cmp], writes=[r_cmp])
                        sc.dma('sp', kcT2[g][64:128, :], kcT2[g][0:64, :], reads=[r_cmp], writes=[r_cmp])
                    else:
                        for nt in range(NCT):
                            rows = min(128, n - nt * 128)
                            for hc in range(2):
                                sc.mm(banks[2][0:rows, 0:64], ghT[:, hc, nt * 128:nt * 128 + rows], w2sb[:, hc, :],
                                      start=(hc == 0), stop=(hc == 1), reads=[rw, rgh], writes=[bres[2]])
                            sc.cp('dve', vca[g][0:rows, nt, 0:64], banks[2][0:rows, 0:64], reads=[bres[2]], writes=[r_cmp])
                            sc.memset('pool', vca[g][0:rows, nt, 64:65], 1.0, writes=[r_cmp])
            sc.barrier()
            for g in range(2):
                ar.top = mark_p
                qT = v3(ar.bf(2 * S), 2)
                ksT2 = ar.bf(S)
                kwT2 = ar.bf(S)
                vsa = v3(ar.bf(NT * 65), NT)
                vwa = v3(ar.bf(NT * 65), NT)
                G = [ar.bf(2560) for hh in range(4)]
                Bm = ar.bf(S)
                ovl = v3(ar.bf(NCW), NCT)
                rk = Res('kqv')
                hk_r = Ring([(ar.bf(512), Res()) for i in range(2)])
                gtb = ar.bf(4 * 48)
                gt3 = v3(gtb.bitcast(F32), 4)
                rgt = Res('gt')
                impacc = v3(ar.f32(512), 4)
                rimp = Res('imp')
                itmp = v3(ar.f32(512), 4)
                ritmp = Res()
                score = v3(ar.f32(512), 4)
                wk = v3(ar.f32(512), 4)
                m8a = ar.f32(32)
                m8b = ar.f32(32)
                selb = v3(ar.bf(512), 4)
                selT = ar.bf(512)
                rsel = Res('sel')
                oc = ar.f32(4 * 4 * 64)
                roc = Res('oc')
                rlc = ar.f32(16)
                rls = v3(ar.f32(4), 4)
                rlw = v3(ar.f32(4), 4)
                cfs = v3(ar.f32(4), 4)
                cfw = v3(ar.f32(4), 4)
                rcoef = Res('coef')
                ot1 = v3(ar.f32(256), 4)
                ot2 = v3(ar.f32(256), 4)
                ot3 = v3(ar.f32(256), 4)
                rot1, rot2, rot3 = Res(), Res(), Res()
                o_tok = v3(ar.bf(4 * 256), 4)
                rotk = Res('otok')
                p_r = Ring([(ar.bf(512), Res()) for i in range(LAG + 2)])
                ps_r = Ring([(ar.bf(512), Res()) for i in range(4 * (LAG + 2))])
                mk_r = Ring([(ar.bf(512), Res()) for i in range(3)])
                st_r = Ring([(ar.bf(512), Res()) for i in range(2)])
                lb_r = Ring([(banks[i], bres[i]) for i in (0, 1)])
                mb_r = Ring([(banks[i], bres[i]) for i in (5,)])
                lb3_r = Ring([(banks[i], bres[i]) for i in (0, 1, 2)])
                a1_r = Ring([(banks[i], bres[i]) for i in (3, 6)])
                a2_r = Ring([(banks[i], bres[i]) for i in (4, 7)])
                tb_r = Ring([(banks[i], bres[i]) for i in (5,)])
                r_oTd = Res()
                for hh in range(4):
                    half, a = hh // 2, hh % 2
                    r0 = 1024 + 64 * (4 * g + hh)
                    sc.dma('sp', qT[half * 64:(half + 1) * 64, a, :], fmd[r0:r0 + 64, :], writes=[rk])
                for half in range(2):
                    sc.dma('sp', ksT2[half * 64:(half + 1) * 64, :], fmd[1792 + 64 * g:1792 + 64 * g + 64, :], writes=[rk])
                    sc.dma('sp', kwT2[half * 64:(half + 1) * 64, :], fmd[1920 + 64 * g:1920 + 64 * g + 64, :], writes=[rk])
                load_v(vsa[:, :, 0:64], tmvd[:, 512 + 64 * g:512 + 64 * g + 64], [rk])
                load_v(vwa[:, :, 0:64], tmvd[:, 640 + 64 * g:640 + 64 * g + 64], [rk])
                sc.memset('pool', vsa[:, :, 64:65], 1.0, writes=[rk])
                sc.memset('pool', vwa[:, :, 64:65], 1.0, writes=[rk])
                sc.dma('sp', ovl, v3(cb_in[:, CB_GLOBAL:CB_GLOBAL + NCW], NCT), writes=[rk])
                sc.dma('sp', Bm, cb_in[:, CB_GLOBAL + NCW:CB_GLOBAL + NCW + S], writes=[rk])
                for hh in range(4):
                    hrow = 4 + 4 * g + hh
                    for idx in range(5):
                        hk, rhk = hk_r.next()
                        src = AP(gvd.tensor, hrow * GL + GOFF - 2063 + 512 * idx, [[16, 128], [1, 512]])
                        sc.dma('sp', hk, src, writes=[rhk])
                        tb, rtb = tb_r.next()
                        sc.mm(tb, Jm, hk, reads=[rhk, r_const], writes=[rtb])
                        sc.cp('dve', G[hh][:, idx * 512:(idx + 1) * 512], tb, reads=[rtb], writes=[rk])
                for c in range(NC):
                    sc.dma('sp', v3(gtb, 4), tmvd[c * 512:(c + 1) * 512, 768:816].rearrange("(t p) w -> p t w", p=128), writes=[rgt])
                    sc.memset('pool', impacc, 0.0, writes=[rimp])
                    nts = [nt for nt in range(NCT) if c - 4 * nt >= 0]
                    oc4 = oc.rearrange("p (h j d) -> p h j d", h=4, j=4)
                    rlc4 = rlc.rearrange("p (h j o) -> p h j o", h=4, o=1)
                    for hh in range(4):
                        half, a = hh // 2, hh % 2
                        hs = slice(half * 64, half * 64 + 64)
                        ocb, rocb = a1_r.next()
                        ib, rib = a2_r.next()
                        def s1(ii, c=c, hh=hh, hs=hs, a=a, nts=nts):
                            nt = nts[ii]
                            lb, rlb = lb_r.next()
                            sc.mm(lb, kcT2[g][hs, nt * 128:(nt + 1) * 128], qT[hs, a, c * 512:(c + 1) * 512], reads=[r_cmp, rk], writes=[rlb])
                            pc, rpc = p_r.next()
                            sc.act(pc, lb, AF.Exp, reads=[rlb], writes=[rpc], scale=0.125)
                            idx = c - 4 * nt
                            if idx < 5:
                                sc.tt('pool', pc, pc, G[hh][:, idx * 512:(idx + 1) * 512], ALU.mult, reads=[rpc, rk], writes=[rpc])
                            return (pc, rpc)

                        def s2(ii, stt_, nts=nts, ocb=ocb, rocb=rocb, ib=ib, rib=rib):
                            pc, rpc = stt_
                            nt = nts[ii]
                            for j in range(4):
                                sc.mm(ocb[:, j * 128:j * 128 + 65], pc[:, j * 128:(j + 1) * 128], vca[g][:, nt, :],
                                      start=(ii == 0 and j == 0), stop=(nt == nts[-1]), reads=[rpc, r_cmp], writes=[rocb])
                            for j in range(4):
                                sc.mm(ib[:, j * 128:(j + 1) * 128], pc[:, j * 128:(j + 1) * 128], ovl[:, nt, :],
                                      start=(ii == 0 and j == 0), stop=(nt == nts[-1]), reads=[rpc, rk], writes=[rib])

                        pipe(len(nts), s1, s2)
                        ocb3 = v3(ocb, 4)
                        sc.ts('dve', rlc4[:, hh], ocb3[:, :, 64:65], TINY, None, ALU.max, reads=[rocb], writes=[roc])
                        sc.op('dve', lambda e, hh=hh: e.reciprocal(rlc4[:, hh], rlc4[:, hh]), reads=[roc], writes=[roc])
                        sc.tt('dve', oc4[:, hh], ocb3[:, :, 0:64], rlc4[:, hh].to_broadcast([128, 4, 64]), ALU.mult,
                              reads=[rocb, roc], writes=[roc])
                        sc.tt('dve', itmp, v3(ib, 4), rlc4[:, hh].to_broadcast([128, 4, 128]), ALU.mult, reads=[rib, roc], writes=[ritmp])
                        sc.tt('pool', impacc, impacc, itmp, ALU.add, reads=[rimp, ritmp], writes=[rimp])
                    for j in range(4):
                        off = 126 - 2 * (4 * c + j)
                        sc.tt('dve', score[:, j, :], impacc[:, j, :], cslide[:, off:off + 128], ALU.add, reads=[rimp, r_const], writes=[rsel])
                    sc.memset('dve', score[:, :, 0:1], BIG, writes=[rsel])
                    for j in range(4):
                        sc.op('dve', lambda e, j=j: e.max(out=m8a[:, j * 8:(j + 1) * 8], in_=score[:, j, :]), reads=[rsel], writes=[rsel])
                        sc.op('dve', lambda e, j=j: e.match_replace(out=wk[:, j, :], in_to_replace=m8a[:, j * 8:(j + 1) * 8],
                                                                    in_values=score[:, j, :], imm_value=-3e38), reads=[rsel], writes=[rsel])
                        sc.op('dve', lambda e, j=j: e.max(out=m8b[:, j * 8:(j + 1) * 8], in_=wk[:, j, :]), reads=[rsel], writes=[rsel])
                        sc.ts('dve', selb[:, j, :], score[:, j, :], m8b[:, j * 8 + 7:j * 8 + 8], None, ALU.is_ge, reads=[rsel], writes=[rsel])
                    tb, rtb = tb_r.next()
                    pb = tb.bitcast(BF16)
                    for j in range(4):
                        sc.tr(pb[:, j * 128:(j + 1) * 128], selb[:, j, :], ident, reads=[rsel, r_const], writes=[rtb])
                    sc.cp('dve', selT, pb[:, 0:512], reads=[rtb], writes=[rsel])
                    osbs = [(banks[i], bres[i]) for i in (3, 4, 6, 7)]

                    def s1(kt, c=c):
                        j0 = max(kt - 4 * c, 0)
                        c0 = 128 * j0
                        mb, rmb = mb_r.next()
                        sc.mm(mb[:, c0:512], Bm[:, kt * 128:(kt + 1) * 128], selT[:, c0:512], reads=[rk, rsel], writes=[rmb])
                        mk, rmk = mk_r.next()
                        sc.cp('act', mk[:, c0:512], mb[:, c0:512], reads=[rmb], writes=[rmk])
                        outs = [None] * 4
                        for pair in ((0, 2), (1, 3)):
                            lbs = {}
                            for hh in pair:
                                half, a = hh // 2, hh % 2
                                hs = slice(half * 64, half * 64 + 64)
                                lb, rlb = lb3_r.next()
                                sc.mm(lb[:, c0:512], ksT2[hs, kt * 128:(kt + 1) * 128], qT[hs, a, c * 512 + c0:(c + 1) * 512], reads=[rk], writes=[rlb])
                                lbs[hh] = (lb, rlb)
                            for hh in pair:
                                hrow = 4 + 4 * g + hh
                                lb, rlb = lbs[hh]
                                ps, rps = ps_r.next()
                                sc.act(ps[:, c0:512], lb[:, c0:512], AF.Exp, reads=[rlb], writes=[rps], scale=0.125)
                                sc.tt('dve', ps[:, c0:512], ps[:, c0:512], mk[:, c0:512], ALU.mult, reads=[rps, rmk], writes=[rps])
                                for j in range(j0, 4):
                                    d = 4 * c + j - kt
                                    if d in (0, 1):
                                        sc.tt('pool', ps[:, j * 128:(j + 1) * 128], ps[:, j * 128:(j + 1) * 128],
                                              E0(hrow) if d == 0 else E128(hrow), ALU.mult, reads=[rps, r_const], writes=[rps])
                                outs[hh] = (ps, rps)
                        return (outs, j0)

                    def s2(kt, stt_, c=c):
                        outs, j0 = stt_
                        for hh in range(4):
                            ps, rps = outs[hh]
                            osb, rosb = osbs[hh]
                            for j in range(j0, 4):
                                sc.mm(osb[:, j * 128:j * 128 + 65], ps[:, j * 128:(j + 1) * 128], vsa[:, kt, :],
                                      start=(kt == 0 and j == j0), stop=(kt == 4 * c + j), reads=[rps, rk], writes=[rosb])

                    pipe(4 * c + 4, s1, s2)
                    for hh in range(4):
                        half, a = hh // 2, hh % 2
                        hs = slice(half * 64, half * 64 + 64)
                        hrow = 4 + 4 * g + hh
                        osb, rosb = osbs[hh]
                        owb, rowb = banks[5], bres[5]
                        kts = list(range(max(4 * c - 4, 0), 4 * c + 4))

                        def s1(ii, c=c, hs=hs, a=a, hrow=hrow, kts=kts):
                            kt = kts[ii]
                            jlo = max(kt - 4 * c, 0)
                            jhi = min(kt + 4 - 4 * c, 3)
                            cs_ = slice(128 * jlo, 128 * (jhi + 1))
                            lb, rlb = lb_r.next()
                            sc.mm(lb[:, cs_], kwT2[hs, kt * 128:(kt + 1) * 128], qT[hs, a, c * 512 + 128 * jlo:c * 512 + 128 * (jhi + 1)],
                                  reads=[rk], writes=[rlb])
                            pw, rpw = p_r.next()
                            sc.act(pw[:, cs_], lb[:, cs_], AF.Exp, reads=[rlb], writes=[rpw], scale=0.125)
                            for j in range(jlo, jhi + 1):
                                d = 4 * c + j - kt
                                if d in (0, 1, 4):
                                    mk = E0(hrow) if d == 0 else (E128(hrow) if d == 1 else m512)
                                    sc.tt('pool', pw[:, j * 128:(j + 1) * 128], pw[:, j * 128:(j + 1) * 128], mk, ALU.mult,
                                          reads=[rpw, r_const], writes=[rpw])
                            return (pw, rpw, jlo, jhi)

                        def s2(ii, stt_, c=c, kts=kts, owb=owb, rowb=rowb):
                            pw, rpw, jlo, jhi = stt_
                            kt = kts[ii]
                            for j in range(jlo, jhi + 1):
                                sc.mm(owb[:, j * 128:j * 128 + 65], pw[:, j * 128:(j + 1) * 128], vwa[:, kt, :],
                                      start=(ii == 0 and j == jlo), stop=(kt == 4 * c + j), reads=[rpw, rk], writes=[rowb])

                        pipe(len(kts), s1, s2)
                        osb3 = v3(osb, 4)
                        owb3 = v3(owb, 4)
                        sc.ts('dve', rls, osb3[:, :, 64:65], TINY, None, ALU.max, reads=[rosb], writes=[rcoef])
                        sc.op('dve', lambda e: e.reciprocal(rls, rls), reads=[rcoef], writes=[rcoef])
                        sc.ts('dve', rlw, owb3[:, :, 64:65], TINY, None, ALU.max, reads=[rowb], writes=[rcoef])
                        sc.op('dve', lambda e: e.reciprocal(rlw, rlw), reads=[rcoef], writes=[rcoef])
                        gi = g * 4 + hh
                        sc.tt('dve', cfs, rls, gt3[:, :, 8 + gi:9 + gi], ALU.mult, reads=[rcoef, rgt], writes=[rcoef])
                        sc.tt('dve', cfw, rlw, gt3[:, :, 16 + gi:17 + gi], ALU.mult, reads=[rcoef, rgt], writes=[rcoef])
                        sc.tt('dve', ot1, oc4[:, hh], gt3[:, :, gi:gi + 1].to_broadcast([128, 4, 64]), ALU.mult, reads=[roc, rgt], writes=[rot1])
                        sc.tt('dve', ot2, osb3[:, :, 0:64], cfs.to_broadcast([128, 4, 64]), ALU.mult, reads=[rosb, rcoef], writes=[rot2])
                        sc.tt('dve', ot3, owb3[:, :, 0:64], cfw.to_broadcast([128, 4, 64]), ALU.mult, reads=[rowb, rcoef], writes=[rot3])
                        sc.tt('pool', ot1, ot1, ot2, ALU.add, reads=[rot1, rot2], writes=[rot1])
                        sc.tt('pool', o_tok[:, :, hh * 64:(hh + 1) * 64], ot1, ot3, ALU.add, reads=[rot1, rot3], writes=[rotk])
                    for fc in range(2):
                        tb, rtb = tb_r.next()
                        pb = tb.bitcast(BF16)
                        for j in range(4):
                            sc.tr(pb[:, j * 128:(j + 1) * 128], o_tok[:, j, fc * 128:(fc + 1) * 128], ident, reads=[rotk, r_const], writes=[rtb])
                        stg, rst = st_r.next()
                        sc.cp('act', stg, pb[:, 0:512], reads=[rtb], writes=[rst])
                        r0 = 512 + g * 256 + fc * 128
                        sc.dma(STQ, oTd[r0:r0 + 128, c * 512:(c + 1) * 512], stg, reads=[rst], writes=[r_oTd])
                sc.barrier()

        def phase_C1(l, xsrc):
            ar.top = base_top
            Wg = v3(ar.bf(8 * 3072), 8)
            Wb = v3(ar.bf(8 * D), 8)
            Wo = v3(ar.bf(8 * D), 8)
            rW = Res('W')
            mark = ar.top
            sring = Ring([(ar.f32(2048), Res()) for i in range(4)])
            for c in range(8):
                load_cast(Wg[:, c, :], w_in[l, c * 128:(c + 1) * 128, NQKV:NIN], 3072, sring, rW)
                load_cast(Wb[:, c, :], w_branch[l, c * 128:(c + 1) * 128, :], D, sring, rW)
                load_cast(Wo[:, c, :], w_out[l, c * 128:(c + 1) * 128, :], D, sring, rW)
            sc.barrier()
            ar.top = mark
            xs_r = Ring([(v3(ar.f32(4 * D), 4), Res()) for i in range(2)])
            hT_r = Ring([(v3(ar.bf(8 * 512), 8), Res()) for i in range(2)])
            oT_r = Ring([(v3(ar.bf(8 * 512), 8), Res()) for i in range(2)])
            mixT = v3(ar.bf(8 * 512), 8)
            rmix = Res('mix')
            g_r = Ring([(ar.f32(512), Res()) for i in range(3)])
            t_r = Ring([(ar.f32(512), Res()) for i in range(4)])
            gb_r = Ring([(banks[i], bres[i]) for i in (0, 1, 2)])
            bb_r = Ring([(banks[i], bres[i]) for i in (3, 4, 5)])
            yb_r = Ring([(banks[i], bres[i]) for i in (6, 7)])
            r_x1d = Res()
            branch_k = ((0, 1), (2, 3), (4, 5, 6, 7))
            for ci in range(NC):
                xs, rx = xs_r.next()
                sc.dma('sp', xs, xsrc[ci * 512:(ci + 1) * 512, :].rearrange("(t p) d -> p t d", p=128), writes=[rx])
                hT, rhT = hT_r.next()
                sc.dma('sp', hT, hTd.rearrange("(c p) s -> p c s", p=128)[:, :, ci * 512:(ci + 1) * 512], writes=[rhT])
                oT, roT = oT_r.next()
                sc.dma('sp', oT, oTd.rearrange("(c p) s -> p c s", p=128)[:, :, ci * 512:(ci + 1) * 512], writes=[roT])
                for dm in range(8):
                    ts_ = []
                    for br in range(3):
                        gb, rgb = gb_r.next()
                        col = br * D + dm * 128
                        for c in range(8):
                            sc.mm(gb, Wg[:, c, col:col + 128], hT[:, c, :], start=(c == 0), stop=(c == 7),
                                  reads=[rW, rhT], writes=[rgb])
                        g, rg = g_r.next()
                        sc.act(g, gb, AF.Sigmoid, reads=[rgb], writes=[rg])
                        bb, rbb = bb_r.next()
                        ks = branch_k[br]
                        for i, k in enumerate(ks):
                            sc.mm(bb, Wb[:, k, dm * 128:(dm + 1) * 128], oT[:, k, :], start=(i == 0), stop=(i == len(ks) - 1),
                                  reads=[rW, roT], writes=[rbb])
                        t, rt = t_r.next()
                        sc.tt('dve', t, bb, g, ALU.mult, reads=[rbb, rg], writes=[rt])
                        ts_.append((t, rt))
                    sc.tt('pool', ts_[0][0], ts_[0][0], ts_[1][0], ALU.add, reads=[ts_[0][1], ts_[1][1]], writes=[ts_[0][1]])
                    sc.tt('pool', mixT[:, dm, :], ts_[0][0], ts_[2][0], ALU.add, reads=[ts_[0][1], ts_[2][1]], writes=[rmix])
                for t in range(4):
                    for half in range(2):
                        yb, ryb = yb_r.next()
                        for k in range(8):
                            sc.mm(yb, mixT[:, k, t * 128:(t + 1) * 128], Wo[:, k, half * 512:(half + 1) * 512],
                                  start=(k == 0), stop=(k == 7), reads=[rmix, rW], writes=[ryb])
                        sc.tt('dve', xs[:, t, half * 512:(half + 1) * 512], xs[:, t, half * 512:(half + 1) * 512], yb, ALU.add,
                              reads=[rx, ryb], writes=[rx])
                sc.dma(STQ, x1d[ci * 512:(ci + 1) * 512, :].rearrange("(t p) d -> p t d", p=128), xs, reads=[rx], writes=[r_x1d])
            sc.barrier()

        def phase_C2(l, xdst):
            ar.top = base_top
            Wgu = v3(ar.bf(8 * 2 * DFF), 8)
            Wd = v3(ar.bf(22 * D), 22)
            gF = ar.f32(D)
            rW = Res('W')
            mark = ar.top
            sring = Ring([(ar.f32(2048), Res()) for i in range(4)])
            for c in range(8):
                load_cast(Wgu[:, c, :], w_gate_up[l, c * 128:(c + 1) * 128, :], 2 * DFF, sring, rW)
            for f in range(22):
                load_cast(Wd[:, f, :], w_down[l, f * 128:(f + 1) * 128, :], D, sring, rW)
            sc.dma('sp', gF, AP(ffn_norm.tensor, l * D, [[0, 128], [1, D]]), writes=[rW])
            sc.barrier()
            ar.top = mark
            xs_r = Ring([(v3(ar.f32(2 * D), 2), Res()) for i in range(2)])
            hb = v3(ar.bf(2 * D), 2)
            rhb = Res()
            sq = ar.f32(D)
            ssq = ar.f32(4)
            rstd = ar.f32(4)
            rtmp = Res()
            hT = v3(ar.bf(8 * 256), 8)
            rhT = Res()
            actT = v3(ar.bf(22 * 256), 22)
            ract = Res()
            sg_r = Ring([(ar.f32(256), Res()) for i in range(2)])
            tb_r = Ring([(banks[i], bres[i]) for i in (0, 1)])
            fb_r = Ring([(banks[i], bres[i]) for i in (2, 3, 4)])
            yb_r = Ring([(banks[i], bres[i]) for i in (5, 6, 7)])
            r_out = Res()
            for ci in range(S // 256):
                xs, rx = xs_r.next()
                sc.dma('sp', xs, x1d[ci * 256:(ci + 1) * 256, :].rearrange("(t p) d -> p t d", p=128), writes=[rx])
                rmsnorm_tm(xs, rx, gF, rW, hb, rhb, 2, sq, ssq, rstd, rtmp)
                transpose_to(hb, rhb, hT, rhT, 2, tb_r, ('dve', 'act'))
                for f in range(22):
                    fb, rfb = fb_r.next()
                    for half, col in ((0, f * 128), (1, DFF + f * 128)):
                        for c in range(8):
                            sc.mm(fb[:, half * 256:(half + 1) * 256], Wgu[:, c, col:col + 128], hT[:, c, :],
                                  start=(c == 0), stop=(c == 7), reads=[rW, rhT], writes=[rfb])
                    sg, rsg = sg_r.next()
                    sc.act(sg, fb[:, 0:256], AF.Silu, reads=[rfb], writes=[rsg])
                    sc.tt('dve', actT[:, f, :], sg, fb[:, 256:512], ALU.mult, reads=[rsg, rfb], writes=[ract])
                for t in range(2):
                    for half in range(2):
                        yb, ryb = yb_r.next()
                        for f in range(22):
                            sc.mm(yb, actT[:, f, t * 128:(t + 1) * 128], Wd[:, f, half * 512:(half + 1) * 512],
                                  start=(f == 0), stop=(f == 21), reads=[ract, rW], writes=[ryb])
                        sc.tt('dve', xs[:, t, half * 512:(half + 1) * 512], xs[:, t, half * 512:(half + 1) * 512], yb, ALU.add,
                              reads=[rx, ryb], writes=[rx])
                sc.dma(STQ, xdst[ci * 256:(ci + 1) * 256, :].rearrange("(t p) d -> p t d", p=128), xs, reads=[rx], writes=[r_out])
            sc.barrier()

        PH = {}
        exec_phases = phases
        prologue()
        for l in range(depth):
            xsrc = x_in if l == 0 else xmd
            xdst = out if l == depth - 1 else xmd
            if exec_phases is None or 'A' in exec_phases:
                phase_A(l, xsrc)
            if exec_phases is None or 'SB' in exec_phases:
                phase_SB(l)
            if exec_phases is None or 'MB' in exec_phases:
                phase_MB(l)
            if exec_phases is None or 'NSA' in exec_phases:
                phase_NSA(l)
            if inject:
                sc.dma('sp', oTd, oT_in, writes=[Res()])
                sc.barrier()
            if exec_phases is None or 'C1' in exec_phases:
                phase_C1(l, xsrc)
            if exec_phases is None or 'C2' in exec_phases:
                phase_C2(l, xdst)
        sc.barrier()
        sc.emit()
    return nc


def core_inputs(inputs, b, S, depth, cbh, cfh):
    f = lambda a: np.ascontiguousarray(np.asarray(a, dtype=np.float32))
    gains = np.stack([f(inputs['moba_q_norm']), f(inputs['moba_k_norm']), f(inputs['nsa_q_norm']),
                      f(inputs['nsa_k_norm'])[:, 0], f(inputs['nsa_k_norm'])[:, 1], f(inputs['nsa_k_norm'])[:, 2]], axis=1)
    m = {
        'x': f(inputs['x'][b, :S]),
        'rel_bias': f(inputs['rel_bias']),
        'attn_norm': f(inputs['attn_norm'])[:depth],
        'w_in': f(inputs['w_in'])[:depth],
        'gains': np.ascontiguousarray(gains[:depth]),
        'nsa_cmp_pos': f(inputs['nsa_cmp_pos'])[:depth],
        'nsa_cmp_w1': f(inputs['nsa_cmp_w1'])[:depth],
        'nsa_cmp_w2': f(inputs['nsa_cmp_w2'])[:depth],
        'w_branch': f(inputs['w_branch'])[:depth],
        'w_out': f(inputs['w_out'])[:depth],
        'ffn_norm': f(inputs['ffn_norm'])[:depth],
        'w_gate_up': f(inputs['w_gate_up'])[:depth],
        'w_down': f(inputs['w_down'])[:depth],
        'cb': cbh,
        'cf': cfh,
    }
    return m


def kernel(**inputs):
    x = np.asarray(inputs['x'])
    B, S, _ = x.shape
    depth = int(np.asarray(inputs['w_in']).shape[0])
    cbh, cfh, _, _ = host_consts(S)
    nc = build(S, depth)
    in_maps = [core_inputs(inputs, b, S, depth, cbh, cfh) for b in range(B)]
    res = run_bass_kernel_spmd(nc, in_maps, core_ids=list(range(B)))
    return np.stack([np.asarray(r['out'], dtype=np.float32) for r in res.results], axis=0)
```
